# Optimizing a Trainium2 kernel written in Bass

```python
import jax, jax.numpy as jnp
from jax import lax
import numpy as np

D_MODEL = 1024
BATCH = 8
SEQ = 8192
DEPTH = 1

GRID_W = 64
CTX_LEN = 256
EPS = 1e-6
M_HEADS = 4
M_HEAD_DIM = D_MODEL // M_HEADS
M_WIDTH = M_HEADS * M_HEAD_DIM
M_CHUNK = 128
CONV_K = 5
A_HEADS = 8
A_KV_HEADS = 2
A_HEAD_DIM = D_MODEL // A_HEADS
A_GROUPS = A_HEADS // A_KV_HEADS
Q_BLOCK = 128
ROPE_THETA = 10000.0
ROPE_PAIRS = A_HEAD_DIM // 4
N_BRANCH = 2
D_FF = -(-8 * D_MODEL // (3 * 256)) * 256
IN_SPLITS = (M_WIDTH, M_WIDTH, M_WIDTH, M_WIDTH, 4 * M_HEADS, A_HEADS * A_HEAD_DIM, A_KV_HEADS * A_HEAD_DIM, A_KV_HEADS * A_HEAD_DIM, N_BRANCH * D_MODEL)
D_IN = sum(IN_SPLITS)

kernel_name = 'hybrid_mlstm_gqa_prefix_block'


def rmsnorm(x, g):
    xf = x.astype(jnp.float32)
    y = xf * lax.rsqrt(jnp.mean(xf * xf, axis=-1, keepdims=True) + EPS)
    return (y * g.astype(jnp.float32)).astype(x.dtype)


def split_cols(z):
    idx = np.cumsum(IN_SPLITS)[:-1].tolist()
    return jnp.split(z, idx, axis=-1)


def short_conv(u, w, b):
    pad = CONV_K // 2
    y = lax.conv_general_dilated(u, w[:, None, :].astype(u.dtype), window_strides=(1,), padding=[(pad, pad)], dimension_numbers=('NWC', 'WIO', 'NWC'), feature_group_count=u.shape[-1])
    return y + b.astype(u.dtype)


def axial_rope_tables(rows, dtype):
    row = jnp.repeat(jnp.arange(rows, dtype=jnp.float32), GRID_W)
    col = jnp.tile(jnp.arange(GRID_W, dtype=jnp.float32), rows)
    inv = ROPE_THETA ** (-jnp.arange(ROPE_PAIRS, dtype=jnp.float32) / ROPE_PAIRS)
    ang = jnp.concatenate([row[:, None] * inv, col[:, None] * inv], axis=-1)
    return jnp.cos(ang).astype(dtype)[:, None, :], jnp.sin(ang).astype(dtype)[:, None, :]


def apply_rope(x, cos, sin):
    x1, x2 = x[..., 0::2], x[..., 1::2]
    return jnp.stack([x1 * cos - x2 * sin, x1 * sin + x2 * cos], axis=-1).reshape(x.shape)


def mlstm_scan(q, k, v, log_i, log_f, state):
    B, T, H, _ = q.shape
    nc = T // M_CHUNK

    def to_chunks(a):
        a = a.reshape((B, nc, M_CHUNK) + a.shape[2:])
        return jnp.moveaxis(a, (1, 3), (0, 2))

    tri = jnp.tril(jnp.ones((M_CHUNK, M_CHUNK), dtype=bool))

    def step(carry, blk):
        C, n, m = carry
        qc, kc, vc, li, lf = blk
        F = jnp.cumsum(lf, axis=-1)
        g = F + m[..., None]
        Dm = jnp.where(tri, F[..., :, None] - F[..., None, :] + li[..., None, :], -jnp.inf)
        mt = jnp.maximum(g, jnp.max(Dm, axis=-1))
        w_inter = jnp.exp(g - mt)
        s = jnp.einsum('bhtd,bhsd->bhts', qc, kc) * jnp.exp(Dm - mt[..., None])
        num = w_inter[..., None] * jnp.einsum('bhtd,bhde->bhte', qc, C) + jnp.einsum('bhts,bhse->bhte', s, vc)
        den = w_inter * jnp.einsum('bhtd,bhd->bht', qc, n) + jnp.sum(s, axis=-1)
        h = num / jnp.maximum(jnp.abs(den), jnp.exp(-mt))[..., None]
        FL = F[..., -1]
        dec = FL[..., None] - F + li
        m_new = jnp.maximum(FL + m, jnp.max(dec, axis=-1))
        a_prev = jnp.exp(FL + m - m_new)
        w_s = jnp.exp(dec - m_new[..., None])
        C_new = a_prev[..., None, None] * C + jnp.einsum('bhs,bhsd,bhse->bhde', w_s, kc, vc)
        n_new = a_prev[..., None] * n + jnp.einsum('bhs,bhsd->bhd', w_s, kc)
        return (C_new, n_new, m_new), h

    blocks = (to_chunks(q), to_chunks(k), to_chunks(v), to_chunks(log_i), to_chunks(log_f))
    final, h = lax.scan(step, state, blocks)
    h = jnp.moveaxis(h, (0, 2), (1, 3)).reshape(B, T, H, -1)
    return final, h


def mlstm_prep(qm, km, vm, zg, conv_w, conv_b, gate_b):
    B, T, _ = qm.shape
    qk = jax.nn.silu(short_conv(jnp.concatenate([qm, km], axis=-1), conv_w, conv_b))
    q, k = jnp.split(qk, 2, axis=-1)
    heads = lambda a: a.reshape(B, T, M_HEADS, M_HEAD_DIM).astype(jnp.float32)
    g = (zg + gate_b).astype(jnp.float32).reshape(B, T, 4, M_HEADS)
    return heads(q), heads(k) * (M_HEAD_DIM ** -0.5), heads(vm), g


def mlstm_two_way(lat, ctx):
    B = lat[0].shape[0]
    init = (jnp.zeros((B, M_HEADS, M_HEAD_DIM, M_HEAD_DIM), jnp.float32), jnp.zeros((B, M_HEADS, M_HEAD_DIM), jnp.float32), jnp.full((B, M_HEADS), -jnp.inf, jnp.float32))

    def run(stream, d, state, reverse):
        q, k, v, g = stream
        li = g[:, :, 2 * d]
        lf = jax.nn.log_sigmoid(g[:, :, 2 * d + 1])
        if reverse:
            q, k, v, li, lf = (jnp.flip(a, axis=1) for a in (q, k, v, li, lf))
        st, h = mlstm_scan(q, k, v, li, lf, state)
        if reverse:
            h = jnp.flip(h, axis=1)
        return st, h

    st_f, hc_f = run(ctx, 0, init, False)
    _, hl_f = run(lat, 0, st_f, False)
    st_b, hc_b = run(ctx, 1, init, True)
    _, hl_b = run(lat, 1, st_b, True)
    return hl_f + hl_b, hc_f + hc_b


def mlstm_out(h, o, m_norm_g, dtype):
    B, T = h.shape[:2]
    hn = rmsnorm(h, m_norm_g.reshape(M_HEADS, M_HEAD_DIM)).reshape(B, T, M_WIDTH).astype(dtype)
    return jax.nn.sigmoid(o) * hn


def attn_heads(qa, ka, va, q_norm_g, k_norm_g):
    B, T, _ = qa.shape
    q = rmsnorm(qa.reshape(B, T, A_HEADS, A_HEAD_DIM), q_norm_g)
    k = rmsnorm(ka.reshape(B, T, A_KV_HEADS, A_HEAD_DIM), k_norm_g)
    v = va.reshape(B, T, A_KV_HEADS, A_HEAD_DIM)
    return q, k, v


def attend(qb, k, v):
    s = jnp.einsum('bkgqd,bksd->bkgqs', qb, k).astype(jnp.float32) * (A_HEAD_DIM ** -0.5)
    p = jax.nn.softmax(s, axis=-1).astype(v.dtype)
    return jnp.einsum('bkgqs,bksd->bkgqd', p, v)


def gqa_latent(q, k_all, v_all):
    B, S = q.shape[:2]
    nb = S // Q_BLOCK
    qb = q.reshape(B, nb, Q_BLOCK, A_KV_HEADS, A_GROUPS, A_HEAD_DIM).transpose(1, 0, 3, 4, 2, 5)
    kt = k_all.transpose(0, 2, 1, 3)
    vt = v_all.transpose(0, 2, 1, 3)
    ob = lax.map(lambda blk: attend(blk, kt, vt), qb)
    return ob.transpose(1, 0, 4, 2, 3, 5).reshape(B, S, A_HEADS * A_HEAD_DIM)


def gqa_context(q, k, v):
    B, C = q.shape[:2]
    qb = q.reshape(B, C, A_KV_HEADS, A_GROUPS, A_HEAD_DIM).transpose(0, 2, 3, 1, 4)
    o = attend(qb, k.transpose(0, 2, 1, 3), v.transpose(0, 2, 1, 3))
    return o.transpose(0, 3, 1, 2, 4).reshape(B, C, A_HEADS * A_HEAD_DIM)


def merge_branches(zg, ym, ya, w_pa, w_pb, w_o):
    g_m, g_a = jnp.split(jax.nn.sigmoid(zg), N_BRANCH, axis=-1)
    return (g_m * (ym @ w_pa) + g_a * (ya @ w_pb)) @ w_o


def token_mix(h, hc, w_in, gate_b, conv_w, conv_b, m_norm_g, q_norm_g, k_norm_g, w_pa, w_pb, w_o, cos, sin, need_ctx):
    zl = split_cols(h @ w_in)
    zc = split_cols(hc @ w_in)
    lat_m = mlstm_prep(zl[0], zl[1], zl[2], zl[4], conv_w, conv_b, gate_b)
    ctx_m = mlstm_prep(zc[0], zc[1], zc[2], zc[4], conv_w, conv_b, gate_b)
    hl, hcm = mlstm_two_way(lat_m, ctx_m)
    ym = mlstm_out(hl, zl[3], m_norm_g, h.dtype)
    ql, kl, vl = attn_heads(zl[5], zl[6], zl[7], q_norm_g, k_norm_g)
    qc, kc, vc = attn_heads(zc[5], zc[6], zc[7], q_norm_g, k_norm_g)
    ql = apply_rope(ql, cos, sin)
    kl = apply_rope(kl, cos, sin)
    ya = gqa_latent(ql, jnp.concatenate([kl, kc], axis=1), jnp.concatenate([vl, vc], axis=1))
    y = merge_branches(zl[8], ym, ya, w_pa, w_pb, w_o)
    if not need_ctx:
        return y, None
    ymc = mlstm_out(hcm, zc[3], m_norm_g, hc.dtype)
    yac = gqa_context(qc, kc, vc)
    return y, merge_branches(zc[8], ymc, yac, w_pa, w_pb, w_o)


def swiglu(h, w_g, w_u, w_d):
    return (jax.nn.silu(h @ w_g) * (h @ w_u)) @ w_d


def setup_inputs(seed: int = 0) -> dict:
    key = jax.random.key(seed)
    ks = jax.random.split(key, 24)
    nrm = lambda k, shape, scale: jax.random.normal(k, shape, jnp.float32) * scale
    D, L = D_MODEL, DEPTH
    i_bias = nrm(ks[9], (L, 2, M_HEADS), 0.1)
    f_bias = jnp.linspace(3.0, 6.0, M_HEADS, dtype=jnp.float32)[None, None, :] + nrm(ks[10], (L, 2, M_HEADS), 0.1)
    gate_b = jnp.stack([i_bias, f_bias], axis=2).reshape(L, 4 * M_HEADS)
    return {
        'x': nrm(ks[0], (BATCH, SEQ, D), 1.0),
        'c': nrm(ks[1], (BATCH, D), 1.0),
        'ctx': nrm(ks[2], (BATCH, CTX_LEN, D), 1.0),
        'c_ctx': nrm(ks[3], (D,), 1.0),
        'w_mod': nrm(ks[4], (L, D, 6 * D), 0.5 * D ** -0.5),
        'b_mod': nrm(ks[5], (L, 6 * D), 0.02),
        'norm1_g': 1.0 + nrm(ks[6], (L, D), 0.02),
        'norm2_g': 1.0 + nrm(ks[7], (L, D), 0.02),
        'w_in': nrm(ks[8], (L, D, D_IN), D ** -0.5),
        'gate_b': gate_b,
        'conv_w': nrm(ks[11], (L, CONV_K, 2 * M_WIDTH), CONV_K ** -0.5),
        'conv_b': nrm(ks[12], (L, 2 * M_WIDTH), 0.02),
        'm_norm_g': 1.0 + nrm(ks[13], (L, M_WIDTH), 0.02),
        'q_norm_g': 1.0 + nrm(ks[14], (L, A_HEAD_DIM), 0.02),
        'k_norm_g': 1.0 + nrm(ks[15], (L, A_HEAD_DIM), 0.02),
        'w_pa': nrm(ks[16], (L, M_WIDTH, D), M_WIDTH ** -0.5),
        'w_pb': nrm(ks[17], (L, A_HEADS * A_HEAD_DIM, D), (A_HEADS * A_HEAD_DIM) ** -0.5),
        'w_o': nrm(ks[18], (L, D, D), D ** -0.5),
        'w_ffn_gate': nrm(ks[19], (L, D, D_FF), D ** -0.5),
        'w_ffn_up': nrm(ks[20], (L, D, D_FF), D ** -0.5),
        'w_ffn_down': nrm(ks[21], (L, D_FF, D), D_FF ** -0.5),
        'final_g': 1.0 + nrm(ks[22], (D,), 0.02),
    }


def reference(x, c, ctx, c_ctx, w_mod, b_mod, norm1_g, norm2_g, w_in, gate_b, conv_w, conv_b, m_norm_g, q_norm_g, k_norm_g, w_pa, w_pb, w_o, w_ffn_gate, w_ffn_up, w_ffn_down, final_g):
    B, S, _ = x.shape
    ROWS = S // GRID_W
    cos, sin = axial_rope_tables(ROWS, x.dtype)
    silu_c = jax.nn.silu(c)
    silu_cc = jax.nn.silu(c_ctx)
    for l in range(DEPTH):
        last = l == DEPTH - 1
        mod = (silu_c @ w_mod[l] + b_mod[l])[:, None, :]
        mod_c = silu_cc @ w_mod[l] + b_mod[l]
        sh1, sc1, g1, sh2, sc2, g2 = jnp.split(mod, 6, axis=-1)
        csh1, csc1, cg1, csh2, csc2, cg2 = jnp.split(mod_c, 6, axis=-1)
        h = rmsnorm(x, norm1_g[l]) * (1.0 + sc1) + sh1
        hc = rmsnorm(ctx, norm1_g[l]) * (1.0 + csc1) + csh1
        y, yc = token_mix(h, hc, w_in[l], gate_b[l], conv_w[l], conv_b[l], m_norm_g[l], q_norm_g[l], k_norm_g[l], w_pa[l], w_pb[l], w_o[l], cos, sin, not last)
        x = x + g1 * y
        x = x + g2 * swiglu(rmsnorm(x, norm2_g[l]) * (1.0 + sc2) + sh2, w_ffn_gate[l], w_ffn_up[l], w_ffn_down[l])
        if not last:
            ctx = ctx + cg1 * yc
            ctx = ctx + cg2 * swiglu(rmsnorm(ctx, norm2_g[l]) * (1.0 + csc2) + csh2, w_ffn_gate[l], w_ffn_up[l], w_ffn_down[l])
    return rmsnorm(x, final_g)
```

```python
import numpy as np
from contextlib import ExitStack
import concourse.bass as bass
import concourse.mybir as mybir
from concourse.bass_utils import run_bass_kernel_spmd

F32 = mybir.dt.float32
BF16 = mybir.dt.bfloat16
AF = mybir.ActivationFunctionType
ALU = mybir.AluOpType
AX = mybir.AxisListType

SEM_EPOCH = 12000
DMA_EPOCH = 1500


class T:
    def __init__(self, name, h=None):
        self.name = name
        self.h = h
        self.last_w = None
        self.readers = []
        self.epochs = []

    def ap(self):
        return self.h if isinstance(self.h, bass.AP) else self.h[:]


class TV:
    def __init__(self, base, h):
        self.base = base
        self.h = h
        self.name = base.name

    def ap(self):
        return self.h

    last_w = property(lambda self: self.base.last_w, lambda self, v: setattr(self.base, "last_w", v))
    readers = property(lambda self: self.base.readers, lambda self, v: setattr(self.base, "readers", v))
    epochs = property(lambda self: self.base.epochs)


def _shape(v, shape):
    if len(shape) == 2:
        return v
    if len(shape) == 3:
        return v.rearrange("p (a b) -> p a b", a=shape[1])
    if len(shape) == 4:
        return v.rearrange("p (a b c) -> p a b c", a=shape[1], b=shape[2])
    raise ValueError(shape)


class Op:
    __slots__ = ("eng", "fn", "deps", "need_inc", "sig", "dma_key", "waits", "idx", "phase")


class Prog:
    ENGS = ("pe", "act", "dve", "pool", "sp")

    def __init__(self, nc, arena_bytes=0):
        self.nc = nc
        self.stack = ExitStack()
        self.ops = {e: [] for e in self.ENGS}
        self.nsem = 0
        self.sem_names = []
        self.all_ops = 0
        self.keys = []
        self.free_sw = []
        self.free_hw = []
        self.scopes = False
        self.bar = {e: None for e in self.ENGS}
        self.arena = None
        if arena_bytes:
            self.arena = self.stack.enter_context(nc.sbuf_tensor("arena", [128, arena_bytes], mybir.dt.uint8))
            self.arena_bytes = arena_bytes
            self.off = 0
            self.mark = 0
            self.banks = [self.stack.enter_context(nc.psum_tensor("bank%d" % i, [128, 512], F32)) for i in range(8)]

    def sb(self, name, shape, dtype):
        if self.arena is None:
            h = self.stack.enter_context(self.nc.sbuf_tensor(name, list(shape), dtype))
            return T(name, h)
        esz = 4 if dtype == F32 else 2
        n = 1
        for d in shape[1:]:
            n *= d
        nb = (n * esz + 31) // 32 * 32
        assert self.off + nb <= self.arena_bytes, ("SBUF arena overflow", name, self.off, nb)
        v = self.arena[0:shape[0], self.off:self.off + n * esz].bitcast(dtype)
        self.off += nb
        v = _shape(v, shape)
        return T(name, v)

    def ps(self, name, shape, dtype, bank=None, off=0):
        if bank is None:
            h = self.stack.enter_context(self.nc.psum_tensor(name, list(shape), dtype))
            return T(name, h)
        n = 1
        for d in shape[1:]:
            n *= d
        esz = 4 if dtype == F32 else 2
        nf = (n * esz + 3) // 4
        assert off + nf <= 512
        v = self.banks[bank][0:shape[0], off:off + nf]
        if dtype != F32:
            v = v.bitcast(dtype)
        return T(name, _shape(v, shape))

    def set_mark(self):
        self.mark = self.off

    def reset(self):
        self.off = self.mark

    def barrier(self):
        last = [self.ops[e][-1] for e in self.ENGS if self.ops[e]]
        pairs = []
        for k in self.keys:
            for slot, cnt in k.epochs:
                pairs.append((slot, 16 * cnt))
            if k.epochs and not k.name.startswith("OUT"):
                (self.free_sw if k.name.endswith("_sw") else self.free_hw).append(tuple(k.epochs[-1]))
                k.epochs = []
        self.keys = [k for k in self.keys if k.epochs]
        for e in self.ENGS:
            self.bar[e] = (last, pairs)

    def T(self, name):
        return T(name)

    def _new_sem(self, name):
        self.sem_names.append(name)
        self.nsem += 1
        return self.nsem - 1

    def _record(self, eng, fn, r, w, dma_key=None):
        op = Op()
        op.eng = eng
        op.fn = fn
        op.need_inc = False
        op.sig = None
        op.dma_key = dma_key
        op.idx = self.all_ops
        op.phase = getattr(self, "phase", "p")
        self.all_ops += 1
        waits = {}
        deps = {}

        def add_dep(d, raw):
            if d is None:
                return
            if d.dma_key is not None:
                k = d.dma_key
                for slot, cnt in k.epochs:
                    v = 16 * cnt
                    if waits.get(slot, 0) < v:
                        waits[slot] = v
                return
            if d.eng == eng and eng == "pe":
                return
            deps[id(d)] = d

        if self.bar[eng] is not None:
            last, keys = self.bar[eng]
            self.bar[eng] = None
            for d in last:
                if d.dma_key is None:
                    add_dep(d, True)
            for slot, v in keys:
                waits[slot] = max(waits.get(slot, 0), v)
        for t in r:
            add_dep(t.last_w, True)
        for t in w:
            add_dep(t.last_w, False)
            for rd in t.readers:
                add_dep(rd, False)
        for t in r:
            t.readers.append(op)
        for t in w:
            t.last_w = op
            t.readers = []
        for d in deps.values():
            d.need_inc = True
        op.deps = list(deps.values())
        op.waits = waits
        if dma_key is not None:
            if not dma_key.epochs:
                self.keys.append(dma_key)
                free = self.free_sw if dma_key.name.endswith("_sw") else self.free_hw
                if free and free[-1][1] < DMA_EPOCH:
                    slot, base = free.pop()
                    dma_key.epochs.append([slot, base])
            if not dma_key.epochs or dma_key.epochs[-1][1] >= 2 * DMA_EPOCH:
                dma_key.epochs.append([self._new_sem("d_" + dma_key.name), 0])
            dma_key.epochs[-1][1] += 1
            op.sig = dma_key.epochs[-1][0]
        self.ops[eng].append(op)
        return op

    def op(self, eng, fn, r=(), w=()):
        return self._record(eng, fn, r, w)

    def dma(self, eng, out, in_, r=(), w=(), key=None, slow=False):
        if key is None:
            key = w[0]
        if eng == "pool":
            base = key.base if isinstance(key, TV) else key
            if not hasattr(base, "_sw"):
                base._sw = T(base.name + "_sw")
            key = base._sw
        if slow:
            return self._record(eng, lambda e: e.dma_start(out=out, in_=in_, allow_slow_non_contiguous=True), r, w, dma_key=key)
        return self._record(eng, lambda e: e.dma_start(out=out, in_=in_), r, w, dma_key=key)

    def finish(self):
        nc = self.nc
        for eng in ("pe", "act", "dve", "pool"):
            slot = None
            cnt = SEM_EPOCH
            for op in self.ops[eng]:
                if op.dma_key is not None or not op.need_inc:
                    continue
                if cnt >= SEM_EPOCH:
                    slot = self._new_sem("e_%s" % eng)
                    cnt = 0
                cnt += 1
                op.sig = (slot, cnt)
        sems = [self.stack.enter_context(nc.semaphore(n + "_%d" % i)) for i, n in enumerate(self.sem_names)]
        self.n_instr = {e: len(v) for e, v in self.ops.items()}
        final_waits = {}
        for eng in self.ENGS:
            for op in self.ops[eng]:
                if op.dma_key is not None and op.dma_key.name.startswith("OUT"):
                    for slot, cnt in op.dma_key.epochs:
                        final_waits[slot] = 16 * cnt
        with nc.Block() as block:
            def run(eng_name, e):
                waited = {}
                cur = [None, None]
                for op in self.ops[eng_name]:
                    if self.scopes and op.phase != cur[0]:
                        if cur[1] is not None:
                            cur[1].__exit__(None, None, None)
                        cur[0] = op.phase
                        cur[1] = nc.named_scope(op.phase)
                        cur[1].__enter__()
                    for d in op.deps:
                        slot, v = d.sig
                        if waited.get(slot, 0) < v:
                            waited[slot] = v
                            e.wait_ge(sems[slot], v)
                    for slot, v in op.waits.items():
                        if waited.get(slot, 0) < v:
                            waited[slot] = v
                            e.wait_ge(sems[slot], v)
                    inst = op.fn(e)
                    if op.dma_key is not None:
                        inst.then_inc(sems[op.sig], 16)
                    elif op.need_inc:
                        inst.then_inc(sems[op.sig[0]], 1)
                if cur[1] is not None:
                    cur[1].__exit__(None, None, None)
                if eng_name == "sp":
                    for slot, v in final_waits.items():
                        e.wait_ge(sems[slot], v)

            @block.tensor
            def _(e):
                run("pe", e)

            @block.scalar
            def _(e):
                run("act", e)

            @block.vector
            def _(e):
                run("dve", e)

            @block.gpsimd
            def _(e):
                run("pool", e)

            @block.sync
            def _(e):
                run("sp", e)
        self.stack.close()


D = 1024
CT = 256
DIN = 7696
DFF = 2816
EPS = 1e-6
NEG = -1.0e30
LN16 = 2.772588722239781


class OpsMixin:
    def mm(self, out, lhsT, rhs, start, stop, r, w):
        self.op("pe", lambda e: e.matmul(out, lhsT, rhs, start=start, stop=stop), r, w)

    def tp(self, out, in_, ident, r, w):
        self.op("pe", lambda e: e.transpose(out, in_, ident), r, w)

    def act(self, out, in_, func, r, w, **kw):
        self.op("act", lambda e: e.activation(out, in_, func, **kw), r, w)

    def tt(self, eng, out, a, b, op, r, w):
        self.op(eng, lambda e: e.tensor_tensor(out, a, b, op), r, w)

    def ts(self, eng, out, a, s1, s2, op0, op1, r, w):
        if op1 is None:
            self.op(eng, lambda e: e.tensor_scalar(out, a, s1, s2, op0), r, w)
        else:
            self.op(eng, lambda e: e.tensor_scalar(out, a, s1, s2, op0, op1), r, w)

    def stt(self, eng, out, a, s, b, op0, op1, r, w):
        self.op(eng, lambda e: e.scalar_tensor_tensor(out, a, s, b, op0, op1), r, w)

    def cp(self, eng, out, in_, r, w):
        if eng == "act":
            self.op("act", lambda e: e.activation(out, in_, AF.Copy), r, w)
        else:
            self.op(eng, lambda e: e.tensor_copy(out, in_), r, w)

    def memset(self, eng, out, val, w):
        self.op(eng, lambda e: e.memset(out, val), (), w)

    def recip(self, out, in_, r, w):
        self.op("dve", lambda e: e.reciprocal(out, in_), r, w)


class KProg(Prog, OpsMixin):
    pass


def conv_blocks(length):
    out = []
    s = 0
    while s < length:
        n = min(508, length - s)
        out.append((s, n))
        s += n
    return out


def build(S, debug=False, stop_after=None, scopes=False):
    NT = S + CT
    NTL = NT // 128
    NL = S // 128
    NCT = CT // 128
    nc = bass.Bass("TRN2", target_bir_lowering=False)
    P = KProg(nc, arena_bytes=206 * 1024)
    P.scopes = scopes
    P.phase = "p0"

    def din(name, shape, dt=F32):
        return nc.dram_tensor(name, list(shape), dt, kind="ExternalInput").ap()

    def dscr(name, shape, dt):
        kind = "ExternalOutput" if debug else "Internal"
        return nc.dram_tensor(name, list(shape), dt, kind=kind).ap()

    x = din("x", [S, D]); c = din("c", [D]); ctx = din("ctx", [CT, D]); c_ctx = din("c_ctx", [D])
    w_mod = din("w_mod", [D, 6 * D]); b_mod = din("b_mod", [6 * D])
    norm1_g = din("norm1_g", [D]); norm2_g = din("norm2_g", [D])
    w_in = din("w_in", [D, DIN]); gate_b = din("gate_b", [16])
    conv_w = din("conv_w", [5, 2 * D]); conv_b = din("conv_b", [2 * D])
    m_norm_g = din("m_norm_g", [D]); q_norm_g = din("q_norm_g", [128]); k_norm_g = din("k_norm_g", [128])
    w_pa = din("w_pa", [D, D]); w_pb = din("w_pb", [D, D]); w_o = din("w_o", [D, D])
    w_g = din("w_ffn_gate", [D, DFF]); w_u = din("w_ffn_up", [D, DFF]); w_d = din("w_ffn_down", [DFF, D])
    final_g = din("final_g", [D])
    cst = din("cst", [128, 512]); rope = din("rope", [S, 128])
    out = nc.dram_tensor("out", [S, D], F32, kind="ExternalOutput").ap()

    mqT = dscr("mqT", [D, NT], BF16); mkT = dscr("mkT", [D, NT], BF16)
    mv = dscr("mv", [NT, D], BF16); osig = dscr("osig", [S, D], BF16)
    qaT = dscr("qaT", [D, S], BF16); kaT = dscr("kaT", [256, NT], BF16); va = dscr("va", [NT, 256], BF16)
    gT = dscr("gT", [2 * D, S], BF16)
    hdir = dscr("hdir", [2, S, D], F32)
    ymT = dscr("ymT", [D, S], BF16); yaT = dscr("yaT", [D, S], BF16)
    x1d = dscr("x1d", [S, D], F32)
    k_mqT = P.T("mqT"); k_mkT = P.T("mkT"); k_mv = P.T("mv"); k_osig = P.T("osig"); k_qaT = P.T("qaT")
    k_kaT = P.T("kaT"); k_va = P.T("va"); k_gT = P.T("gT"); k_hdir = P.T("hdir"); k_ymT = P.T("ymT")
    k_yaT = P.T("yaT"); k_x1 = P.T("x1d"); k_out = P.T("OUT")
    dbg = {}

    identf = P.sb("identf", [128, 128], F32)
    identb = P.sb("identb", [128, 128], BF16)
    trif = P.sb("trif", [128, 2, 128], F32)
    trib = P.sb("trib", [128, 2, 128], BF16)
    e0f = P.sb("e0f", [128, 128], F32)
    onesf = P.sb("onesf", [128, 128], F32)
    onesb = P.sb("onesb", [128, 2], BF16)
    modT = P.sb("modT", [128, 48, 2], F32)
    A1 = P.sb("A1", [128, 8, 2], F32)
    A2 = P.sb("A2", [128, 8], F32)
    G1bc = P.sb("G1bc", [128, D], F32)
    G2bc = P.sb("G2bc", [128, D], F32)
    Gd = [P.sb("Gd%d" % d, [128, NTL, 8], F32) for d in range(2)]
    P.dma("sp", identf.ap(), cst[:, 0:128], w=[identf])
    P.dma("sp", trif.ap(), cst[:, 128:384].rearrange("p (a b) -> p a b", a=2), w=[trif])
    P.dma("sp", e0f.ap(), cst[:, 384:512], w=[e0f])
    P.cp("dve", identb.ap(), identf.ap(), [identf], [identb])
    P.cp("dve", trib.ap(), trif.ap(), [trif], [trib])
    P.memset("dve", onesf.ap(), 1.0, [onesf])
    P.memset("dve", onesb.ap(), 1.0, [onesb])
    P.set_mark()

    def tile_of(d, k):
        if d == 0:
            return k
        if k < NCT:
            return NCT - 1 - k
        return NTL + NCT - 1 - k

    step_of = [{tile_of(d, k): k for k in range(NTL)} for d in range(2)]

    sc = P.sb("sc", [128, 8, 2], F32)
    scs = P.sb("scs", [128, 8, 2], F32)
    bmod = P.sb("bmod", [128, 48], F32)
    n1g = P.sb("n1g", [128, 8], F32)
    n2g = P.sb("n2g", [128, 8], F32)
    wm = [P.sb("wm%d" % i, [128, 8, 512], F32) for i in range(2)]
    P.dma("sp", sc.ap()[:, :, 0], c.rearrange("(k p) -> p k", p=128), w=[sc], slow=True)
    P.dma("sp", sc.ap()[:, :, 1], c_ctx.rearrange("(k p) -> p k", p=128), w=[sc], slow=True)
    P.dma("sp", bmod.ap(), b_mod.rearrange("(k p) -> p k", p=128), w=[bmod], slow=True)
    P.dma("sp", n1g.ap(), norm1_g.rearrange("(k p) -> p k", p=128), w=[n1g], slow=True)
    P.dma("sp", n2g.ap(), norm2_g.rearrange("(k p) -> p k", p=128), w=[n2g], slow=True)
    P.act(scs.ap(), sc.ap(), AF.Silu, [sc], [scs])
    pmod = P.ps("pmod", [128, 48, 2], F32, bank=0)
    for pc in range(12):
        wt = wm[pc % 2]
        P.dma("sp", wt.ap(), w_mod[:, pc * 512:(pc + 1) * 512].rearrange("(k p) f -> p k f", p=128), w=[wt])
        for fl in range(4):
            fc = pc * 4 + fl
            for kc in range(8):
                P.mm(pmod.ap()[:, fc, :], wt.ap()[:, kc, fl * 128:(fl + 1) * 128], scs.ap()[:, kc, :],
                     kc == 0, kc == 7, [wt, scs], [pmod])
    P.tt("dve", modT.ap(), pmod.ap(), bmod.ap().unsqueeze(2).to_broadcast([128, 48, 2]), ALU.add, [pmod, bmod], [modT])
    tmpa = P.sb("tmpa", [128, 8, 2], F32)
    P.ts("dve", tmpa.ap(), modT.ap()[:, 8:16, :], 1.0, None, ALU.add, None, [modT], [tmpa])
    P.tt("dve", A1.ap(), tmpa.ap(), n1g.ap().unsqueeze(2).to_broadcast([128, 8, 2]), ALU.mult, [tmpa, n1g], [A1])
    tmpb = P.sb("tmpb", [128, 8], F32)
    P.ts("dve", tmpb.ap(), modT.ap()[:, 32:40, 0], 1.0, None, ALU.add, None, [modT], [tmpb])
    P.tt("dve", A2.ap(), tmpb.ap(), n2g.ap(), ALU.mult, [tmpb, n2g], [A2])
    dg = [P.sb("dg%d" % i, [128, 128], F32) for i in range(2)]
    for gi, (Gbc, base) in enumerate(((G1bc, 16), (G2bc, 40))):
        pb = [P.ps("pbc%d" % h, [128, 512], F32, bank=1 + h) for h in range(2)]
        for kc in range(8):
            dgt = dg[kc % 2]
            P.ts("dve", dgt.ap(), identf.ap(), modT.ap()[:, base + kc, 0:1], None, ALU.mult, None, [identf, modT], [dgt])
            P.mm(pb[kc // 4].ap()[:, (kc % 4) * 128:(kc % 4 + 1) * 128], onesf.ap(), dgt.ap(), True, True, [onesf, dgt], [pb[kc // 4]])
        for h in range(2):
            P.cp("dve", Gbc.ap()[:, h * 512:(h + 1) * 512], pb[h].ap(), [pb[h]], [Gbc])
    if debug:
        dbg["modT"] = nc.dram_tensor("d_modT", [128, 96], F32, kind="ExternalOutput").ap()
        P.dma("sp", dbg["modT"], modT.ap().rearrange("p a b -> p (a b)"), r=[modT], w=[P.T("OUTd0")])
        dbg["G1bc"] = nc.dram_tensor("d_G1bc", [128, D], F32, kind="ExternalOutput").ap()
        P.dma("sp", dbg["G1bc"], G1bc.ap(), r=[G1bc], w=[P.T("OUTd1")])
    P.barrier()
    P.reset()
    if stop_after == 0:
        P.finish()
        return nc

    P.phase = "p1a"
    hT = P.sb("hT", [128, 8, NT], BF16)
    hTt = [P.T("hT%d" % i) for i in range(NTL)]
    mark2 = P.off
    xt = [P.sb("xt%d" % i, [128, D], F32) for i in range(3)]
    junk = P.sb("junk", [128, D], BF16)
    st = [P.sb("st%d" % i, [128, 4], F32) for i in range(3)]
    xn = [P.sb("xn%d" % i, [128, D], BF16) for i in range(2)]
    tmpf = [P.sb("tmpf%d" % i, [128, 8, 128], F32) for i in range(2)]
    pT = [P.ps("pT%d" % i, [128, 8, 128], BF16, bank=i) for i in range(2)]

    def rstd_ops(stt_, n):
        P.act(stt_.ap()[:, 1:2], stt_.ap()[:, 0:1], AF.Sqrt, [stt_], [stt_], bias=EPS, scale=1.0 / n)
        P.recip(stt_.ap()[:, 2:3], stt_.ap()[:, 1:2], [stt_], [stt_])

    for i in range(NTL):
        xs = xt[i % 3]; ss = st[i % 3]; xb = xn[i % 2]; pt = pT[i % 2]; tf = tmpf[i % 2]
        src = ctx[i * 128:(i + 1) * 128, :] if i < NCT else x[(i - NCT) * 128:(i - NCT + 1) * 128, :]
        P.dma("sp", xs.ap(), src, w=[xs])
        P.act(junk.ap(), xs.ap(), AF.Square, [xs], [junk, ss], accum_out=ss.ap()[:, 0:1])
        rstd_ops(ss, D)
        P.ts("dve", xb.ap(), xs.ap(), ss.ap()[:, 2:3], None, ALU.mult, None, [xs, ss], [xb])
        for kc in range(8):
            P.tp(pt.ap()[:, kc, :], xb.ap()[:, kc * 128:(kc + 1) * 128], identb.ap(), [xb, identb], [pt])
        m = 1 if i < NCT else 0
        P.tt("dve", tf.ap(), pt.ap(), A1.ap()[:, :, m:m + 1].to_broadcast([128, 8, 128]), ALU.mult, [pt, A1], [tf])
        P.tt("pool", hT.ap()[:, :, i * 128:(i + 1) * 128], tf.ap(), modT.ap()[:, 0:8, m:m + 1].to_broadcast([128, 8, 128]), ALU.add,
             [tf, modT], [hTt[i]])
    if debug:
        dbg["hT"] = nc.dram_tensor("d_hT", [128, 8 * NT], BF16, kind="ExternalOutput").ap()
        P.dma("sp", dbg["hT"], hT.ap().rearrange("p a b -> p (a b)"), r=hTt, w=[P.T("OUTd2")])
    if stop_after == 1:
        P.finish()
        return nc
    P.barrier()
    P.off = mark2

    P.phase = "p1b"
    wb = [P.sb("wb%d" % i, [128, 8, 512], BF16) for i in range(2)]
    wcnt = [0]

    wgroups = [(g * 512, 512) for g in range(4)] + [(5648 + g * 512, 512) for g in range(4)] + \
              [(2048, 512), (2560, 512), (3072, 512), (3584, 512), (4096, 16), (4112, 512), (4624, 512), (5136, 512)]
    wtiles = {}

    def issue_w(gi):
        if gi >= len(wgroups) or gi in wtiles:
            return
        c0, ncols = wgroups[gi]
        t = wb[gi % 2]
        P.dma("pool", t.ap()[:, :, 0:ncols], w_in[:, c0:c0 + ncols].rearrange("(k p) f -> p k f", p=128), w=[t])
        wtiles[gi] = t

    def load_w(c0, ncols):
        gi = wcnt[0]
        wcnt[0] += 1
        assert wgroups[gi] == (c0, ncols), (gi, c0, ncols)
        issue_w(gi)
        t = wtiles[gi]
        issue_w(gi + 1)
        return t

    pz = [P.ps("pz%d" % i, [128, 512], F32, bank=2 + i) for i in range(4)]
    pzc = [0]

    def next_pz():
        t = pz[pzc[0] % 4]
        pzc[0] += 1
        return t

    cw = P.sb("cw", [128, 16, 5], F32)
    cb = P.sb("cb", [128, 16], F32)
    gbb = P.sb("gbb", [128, 16], F32)
    qgb = P.sb("qgb", [128, 128], F32)
    kgb = P.sb("kgb", [128, 128], F32)
    for j in range(5):
        P.dma("sp", cw.ap()[:, :, j], conv_w[j].rearrange("(c p) -> p c", p=128), w=[cw], slow=True)
    P.dma("sp", cb.ap(), conv_b.rearrange("(c p) -> p c", p=128), w=[cb], slow=True)
    P.dma("sp", gbb.ap(), gate_b.partition_broadcast(128), w=[gbb])
    P.dma("sp", qgb.ap(), q_norm_g.partition_broadcast(128), w=[qgb])
    P.dma("sp", kgb.ap(), k_norm_g.partition_broadcast(128), w=[kgb])

    def hts(g0, g1):
        return hTt[g0 // 128:(g1 - 1) // 128 + 1]

    Zs = [P.sb("Zs%d" % i, [128, 512], F32) for i in range(2)]
    acc = [P.sb("acc%d" % i, [128, 508], F32) for i in range(2)]
    stg8 = [P.sb("stg8_%d" % i, [128, 512], BF16) for i in range(8)]
    scnt = [0]

    def next_stage():
        t = stg8[scnt[0] % 8]
        scnt[0] += 1
        return t

    sqc = [0]

    def stq(from_act):
        sqc[0] += 1
        m = sqc[0] % 3
        if m == 0:
            return "sp"
        if m == 1:
            return "act" if from_act else "sp"
        return "pool"

    fcnt = [0]
    for grp in range(4):
        wt = load_w(grp * 512, 512)
        for cl in range(4):
            ch = grp * 4 + cl
            is_k = ch >= 8
            dst, kdst = (mkT, k_mkT) if is_k else (mqT, k_mqT)
            seqs = [(CT, S)] + ([(0, CT)] if is_k else [])
            for (s0, ln) in seqs:
                for (bs, n) in conv_blocks(ln):
                    w0 = max(0, bs - 2); w1 = min(ln, bs + n + 2)
                    ncol = w1 - w0
                    off = 2 - (bs - w0)
                    i = fcnt[0]; fcnt[0] += 1
                    z = Zs[i % 2]; a = acc[i % 2]; o = next_stage()
                    p = next_pz()
                    hr = hts(s0 + w0, s0 + w1)
                    for kc in range(8):
                        P.mm(p.ap()[:, 0:ncol], wt.ap()[:, kc, cl * 128:(cl + 1) * 128], hT.ap()[:, kc, s0 + w0:s0 + w1],
                             kc == 0, kc == 7, [wt] + hr, [p])
                    if bs == 0:
                        P.memset("pool", z.ap()[:, 0:2], 0.0, [z])
                    if bs + n == ln:
                        P.memset("pool", z.ap()[:, 2 + n:4 + n], 0.0, [z])
                    P.cp("act", z.ap()[:, off:off + ncol], p.ap()[:, 0:ncol], [p], [z])
                    P.ts("dve", a.ap()[:, 0:n], z.ap()[:, 0:n], cw.ap()[:, ch, 0:1], None, ALU.mult, None, [z, cw], [a])
                    for j in range(1, 5):
                        P.stt("dve", a.ap()[:, 0:n], z.ap()[:, j:j + n], cw.ap()[:, ch, j:j + 1], a.ap()[:, 0:n], ALU.mult, ALU.add, [z, cw, a], [a])
                    P.act(o.ap()[:, 0:n], a.ap()[:, 0:n], AF.Silu, [a, cb], [o], bias=cb.ap()[:, ch:ch + 1])
                    r0 = (ch % 8) * 128
                    P.dma(stq(True), dst[r0:r0 + 128, s0 + bs:s0 + bs + n], o.ap()[:, 0:n], r=[o], w=[kdst], key=o)

    if stop_after == "F":
        P.finish()
        return nc
    P.phase = "p1b_G"
    gcnt = 0
    for grp in range(4):
        wt = load_w(5648 + grp * 512, 512)
        for cl in range(4):
            ch = grp * 4 + cl
            for b0 in range(0, S, 512):
                p = next_pz()
                o = next_stage()
                hr = hts(CT + b0, CT + b0 + 512)
                for kc in range(8):
                    P.mm(p.ap(), wt.ap()[:, kc, cl * 128:(cl + 1) * 128], hT.ap()[:, kc, CT + b0:CT + b0 + 512], kc == 0, kc == 7, [wt] + hr, [p])
                P.act(o.ap(), p.ap(), AF.Sigmoid, [p], [o])
                P.dma(stq(True), gT[ch * 128:(ch + 1) * 128, b0:b0 + 512], o.ap(), r=[o], w=[k_gT], key=o)

    if stop_after == "G":
        P.finish()
        return nc
    P.phase = "p1b_mv"
    tcnt = [0]

    def tok_mm(wt, ncols, i):
        p = next_pz()
        for kc in range(8):
            P.mm(p.ap()[:, 0:ncols], hT.ap()[:, kc, i * 128:(i + 1) * 128], wt.ap()[:, kc, 0:ncols], kc == 0, kc == 7, [wt, hTt[i]], [p])
        return p

    for grp in range(2):
        wt = load_w(2048 + grp * 512, 512)
        for i in range(NTL):
            p = tok_mm(wt, 512, i)
            o = next_stage()
            P.cp("act" if i % 2 else "dve", o.ap(), p.ap(), [p], [o])
            P.dma(stq(i % 2 == 1), mv[i * 128:(i + 1) * 128, grp * 512:(grp + 1) * 512], o.ap(), r=[o], w=[k_mv], key=o)
    if stop_after == "mv":
        P.finish()
        return nc
    P.phase = "p1b_o"
    for grp in range(2):
        wt = load_w(3072 + grp * 512, 512)
        for i in range(NCT, NTL):
            p = tok_mm(wt, 512, i)
            o = next_stage()
            P.act(o.ap(), p.ap(), AF.Sigmoid, [p], [o])
            P.dma(stq(True), osig[(i - NCT) * 128:(i - NCT + 1) * 128, grp * 512:(grp + 1) * 512], o.ap(), r=[o], w=[k_osig], key=o)
    if stop_after == "o":
        P.finish()
        return nc
    P.phase = "p1b_gt"
    wt = load_w(4096, 16)
    Gdt = [P.T("Gdt0"), P.T("Gdt1")]
    for i in range(NTL):
        p = tok_mm(wt, 16, i)
        for d in range(2):
            P.tt("dve", Gd[d].ap()[:, step_of[d][i], :], p.ap()[:, d * 8:(d + 1) * 8], gbb.ap()[:, d * 8:(d + 1) * 8], ALU.add, [p, gbb], [Gdt[d]])

    if stop_after == "gates":
        P.finish()
        return nc
    P.phase = "p1b_q"
    ropet = [P.sb("ropet%d" % i, [128, 128], F32) for i in range(2)]
    sqb = [P.sb("sq%d" % i, [128, 512], BF16) for i in range(2)]
    qst = [P.sb("qst%d" % i, [128, 8], F32) for i in range(2)]
    qn = [P.sb("qn%d" % i, [128, 512], F32) for i in range(2)]
    t1 = [P.sb("rt%d" % i, [128, 256], F32) for i in range(4)]
    qr = [P.sb("qr%d" % i, [128, 512], BF16) for i in range(2)]
    qstage = [P.sb("qstage%d" % i, [128, 4, 512], BF16) for i in range(2)]
    acnt = [0]

    def norm_rope(p, nh, gb, rt, do_rope):
        i = acnt[0]; acnt[0] += 1
        s_ = qst[i % 2]; q_ = qn[i % 2]; o_ = qr[i % 2]; sq = sqb[i % 2]
        W = nh * 128
        P.act(sq.ap()[:, 0:W], p.ap()[:, 0:W], AF.Square, [p], [sq])
        P.op("dve", lambda e: e.reduce_sum(s_.ap()[:, 0:nh], sq.ap()[:, 0:W].rearrange("p (h d) -> p h d", h=nh), AX.X), [sq], [s_])
        P.act(s_.ap()[:, 4:4 + nh], s_.ap()[:, 0:nh], AF.Sqrt, [s_], [s_], bias=EPS, scale=1.0 / 128)
        P.recip(s_.ap()[:, 0:nh], s_.ap()[:, 4:4 + nh], [s_], [s_])
        P.tt("dve", q_.ap()[:, 0:W].rearrange("p (h d) -> p h d", h=nh), p.ap()[:, 0:W].rearrange("p (h d) -> p h d", h=nh),
             s_.ap()[:, 0:nh].unsqueeze(2).to_broadcast([128, nh, 128]), ALU.mult, [p, s_], [q_])
        if not do_rope:
            P.tt("pool", o_.ap()[:, 0:W].rearrange("p (h d) -> p h d", h=nh), q_.ap()[:, 0:W].rearrange("p (h d) -> p h d", h=nh),
                 gb.ap().unsqueeze(1).to_broadcast([128, nh, 128]), ALU.mult, [q_, gb], [o_])
            return o_
        P.tt("pool", q_.ap()[:, 0:W].rearrange("p (h d) -> p h d", h=nh), q_.ap()[:, 0:W].rearrange("p (h d) -> p h d", h=nh),
             gb.ap().unsqueeze(1).to_broadcast([128, nh, 128]), ALU.mult, [q_, gb], [q_])
        qv = q_.ap()[:, 0:W].rearrange("p (h i two) -> p h i two", h=nh, two=2)
        ov = o_.ap()[:, 0:W].rearrange("p (h i two) -> p h i two", h=nh, two=2)
        x1 = qv[:, :, :, 0]; x2 = qv[:, :, :, 1]
        cosb = rt.ap()[:, 0:64].unsqueeze(1).to_broadcast([128, nh, 64])
        sinb = rt.ap()[:, 64:128].unsqueeze(1).to_broadcast([128, nh, 64])
        tv = [t.ap()[:, 0:nh * 64].rearrange("p (h i) -> p h i", h=nh) for t in t1]
        P.tt("dve", tv[0], x1, cosb, ALU.mult, [q_, rt], [t1[0]])
        P.tt("dve", tv[1], x2, sinb, ALU.mult, [q_, rt], [t1[1]])
        P.tt("dve", ov[:, :, :, 0], tv[0], tv[1], ALU.subtract, [t1[0], t1[1]], [o_])
        P.tt("pool", tv[2], x1, sinb, ALU.mult, [q_, rt], [t1[2]])
        P.tt("pool", tv[3], x2, cosb, ALU.mult, [q_, rt], [t1[3]])
        P.tt("pool", ov[:, :, :, 1], tv[2], tv[3], ALU.add, [t1[2], t1[3]], [o_])
        return o_

    def load_rope(i):
        rt = ropet[i % 2]
        P.dma("sp", rt.ap(), rope[(i - NCT) * 128:(i - NCT + 1) * 128, :], w=[rt])
        return rt

    pT2 = [P.ps("pT2_%d" % i, [128, 4, 128], BF16, bank=i) for i in range(2)]
    for grp in range(2):
        wt = load_w(4112 + grp * 512, 512)
        for b0 in range(0, NL, 4):
            stg = qstage[(b0 // 4) % 2]
            for j in range(4):
                i = NCT + b0 + j
                rt = load_rope(i)
                p = tok_mm(wt, 512, i)
                o_ = norm_rope(p, 4, qgb, rt, True)
                ptt = pT2[(b0 + j) % 2]
                for h in range(4):
                    P.tp(ptt.ap()[:, h, :], o_.ap()[:, h * 128:(h + 1) * 128], identb.ap(), [o_, identb], [ptt])
                P.cp("act", stg.ap()[:, :, j * 128:(j + 1) * 128], ptt.ap(), [ptt], [stg])
            for h in range(4):
                P.dma(stq(True), qaT[(grp * 4 + h) * 128:(grp * 4 + h + 1) * 128, b0 * 128:(b0 + 4) * 128], stg.ap()[:, h, :], r=[stg], w=[k_qaT], key=stg)
    if stop_after == "q":
        P.finish()
        return nc
    P.phase = "p1b_kv"
    wt = load_w(5136, 512)
    kstage = [P.sb("kstage%d" % i, [128, 2, 128], BF16) for i in range(2)]
    for i in range(NTL):
        lat = i >= NCT
        rt = load_rope(i) if lat else None
        p = tok_mm(wt, 512, i)
        o_ = norm_rope(p, 2, kgb, rt, lat)
        ptt = pT2[i % 2]
        for h in range(2):
            P.tp(ptt.ap()[:, h, :], o_.ap()[:, h * 128:(h + 1) * 128], identb.ap(), [o_, identb], [ptt])
        ks = kstage[i % 2]
        P.cp("act", ks.ap(), ptt.ap()[:, 0:2, :], [ptt], [ks])
        for h in range(2):
            P.dma(stq(True), kaT[h * 128:(h + 1) * 128, i * 128:(i + 1) * 128], ks.ap()[:, h, :], r=[ks], w=[k_kaT], key=ks)
        o = next_stage()
        P.cp("dve", o.ap()[:, 0:256], p.ap()[:, 256:512], [p], [o])
        P.dma(stq(False), va[i * 128:(i + 1) * 128, :], o.ap()[:, 0:256], r=[o], w=[k_va], key=o)
    if debug:
        for d in range(2):
            dbg["Gd%d" % d] = nc.dram_tensor("d_Gd%d" % d, [128, NTL * 8], F32, kind="ExternalOutput").ap()
            P.dma("sp", dbg["Gd%d" % d], Gd[d].ap().rearrange("p a b -> p (a b)"), r=[Gdt[d]], w=[P.T("OUTg%d" % d)])
    P.barrier()
    P.reset()
    if stop_after == 2:
        P.finish()
        return nc

    P.phase = "p2a"
    NS = NTL
    W4 = NS * 4
    Aa = [P.sb("Aa%d" % d, [128, W4], F32) for d in range(2)]
    Ee = [P.sb("Ee%d" % d, [128, W4], F32) for d in range(2)]
    C0 = [P.sb("C0%d" % d, [128, W4], F32) for d in range(2)]
    mng = P.sb("mng", [128, D], F32)
    P.dma("sp", mng.ap(), m_norm_g.partition_broadcast(128), w=[mng])
    pieces = [(c0_, min(c0_ + 128, W4)) for c0_ in range(0, W4, 128)]
    pTr = P.ps("pTr", [128, 128], F32, bank=4)
    for d in range(2):
        LF = P.sb("LF%d" % d, [128, W4], F32)
        gtmp = P.sb("gtmp%d" % d, [128, W4], F32)
        Bv = P.sb("Bv%d" % d, [128, W4], F32)
        Mb = P.sb("Mb%d" % d, [128, W4], F32)
        mrow = P.sb("mrow%d" % d, [128, W4], F32)
        mprev = P.sb("mprev%d" % d, [128, W4], F32)
        Rr = P.sb("Rr%d" % d, [128, W4], F32)
        FLb = P.sb("FLb%d" % d, [128, W4], F32)
        mcol = P.sb("mcol%d" % d, [128, 4], F32)
        dgm = P.sb("dgm%d" % d, [128, 128], F32)
        v3 = lambda t: t.ap().rearrange("p (s h) -> p s h", h=4)
        pF = P.ps("pF%d" % d, [128, W4], F32, bank=0 + d)
        pFL = P.ps("pFL%d" % d, [128, W4], F32, bank=2 + d)
        pM = P.ps("pM%d" % d, [128, W4], F32, bank=5 + d)
        P.act(v3(gtmp), Gd[d].ap()[:, :, 4:8], AF.Exp, [Gdt[d]], [gtmp], scale=-1.0)
        P.act(gtmp.ap(), gtmp.ap(), AF.Ln, [gtmp], [gtmp], bias=1.0)
        P.ts("dve", LF.ap(), gtmp.ap(), -1.0, None, ALU.mult, None, [gtmp], [LF])
        P.mm(pF.ap(), trif.ap()[:, d, :], LF.ap(), True, True, [trif, LF], [pF])
        P.mm(pFL.ap(), onesf.ap(), LF.ap(), True, True, [onesf, LF], [pFL])
        P.tt("dve", v3(Bv), Gd[d].ap()[:, :, 0:4], pF.ap().rearrange("p (s h) -> p s h", h=4), ALU.subtract, [Gdt[d], pF], [Bv])
        P.cp("dve", FLb.ap(), pFL.ap(), [pFL], [FLb])
        for pi, (a0, a1) in enumerate(pieces):
            w_ = a1 - a0
            P.tp(pTr.ap()[0:w_, :], Bv.ap()[:, a0:a1], identf.ap(), [Bv, identf], [pTr])
            P.memset("dve", mcol.ap()[:, pi:pi + 1], 0.0, [mcol])
            P.op("dve", lambda e, o_=mcol.ap()[0:w_, pi:pi + 1], i_=pTr.ap()[0:w_, :]: e.reduce_max(o_, i_, AX.X), [pTr, mcol], [mcol])
            P.ts("dve", dgm.ap(), identf.ap(), mcol.ap()[:, pi:pi + 1], None, ALU.mult, None, [identf, mcol], [dgm])
            P.mm(pM.ap()[:, a0:a1], onesf.ap(), dgm.ap()[:, 0:w_], True, True, [onesf, dgm], [pM])
        P.cp("dve", Mb.ap(), pM.ap(), [pM], [Mb])
        for h in range(4):
            P.op("dve", lambda e, o_=v3(mrow)[:, :, h], a_=v3(Mb)[:, :, h], b_=v3(FLb)[:, :, h]: e.tensor_tensor_scan(o_, a_, b_, NEG, ALU.max, ALU.add),
                 [Mb, FLb], [mrow])
        P.memset("dve", mprev.ap()[:, 0:4], NEG, [mprev])
        P.cp("dve", mprev.ap()[:, 4:W4], mrow.ap()[:, 0:W4 - 4], [mrow], [mprev])
        P.tt("dve", Rr.ap(), mprev.ap(), Mb.ap(), ALU.max, [mprev, Mb], [Rr])
        P.tt("dve", gtmp.ap(), mprev.ap(), Rr.ap(), ALU.subtract, [mprev, Rr], [gtmp])
        P.ts("dve", gtmp.ap(), gtmp.ap(), -200.0, None, ALU.max, None, [gtmp], [gtmp])
        P.act(C0[d].ap(), gtmp.ap(), AF.Exp, [gtmp], [C0[d]])
        if debug:
            for nm, tl in (("Mb", Mb), ("FLb", FLb), ("mrow", mrow), ("Rr", Rr), ("Bv", Bv)):
                dbg[nm + str(d)] = nc.dram_tensor("d_%s%d" % (nm, d), [128, W4], F32, kind="ExternalOutput").ap()
                P.dma("sp", dbg[nm + str(d)], tl.ap(), r=[tl], w=[P.T("OUT%s%d" % (nm, d))])
        P.tt("dve", Bv.ap(), Bv.ap(), Rr.ap(), ALU.subtract, [Bv, Rr], [Bv])
        P.act(Aa[d].ap(), Bv.ap(), AF.Exp, [Bv], [Aa[d]], bias=-LN16)
        P.tt("dve", LF.ap(), pF.ap(), Rr.ap(), ALU.add, [pF, Rr], [LF])
        P.act(Ee[d].ap(), LF.ap(), AF.Exp, [LF], [Ee[d]], scale=-1.0)
    if debug:
        for nm, tl in (("Aa", Aa), ("Ee", Ee), ("C0", C0)):
            for d in range(2):
                dbg[nm + str(d)] = nc.dram_tensor("d_%s%d" % (nm, d), [128, W4], F32, kind="ExternalOutput").ap()
                P.dma("sp", dbg[nm + str(d)], tl[d].ap(), r=[tl[d]], w=[P.T("OUT%s%d" % (nm, d))])
    P.barrier()
    if stop_after == "2a":
        P.finish()
        return nc

    P.phase = "p2b"
    kTin = [[P.sb("kTin%d%d" % (d, i), [128, 8, 128], BF16) for i in range(2)] for d in range(2)]
    qTin = [[P.sb("qTin%d%d" % (d, i), [128, 8, 128], BF16) for i in range(2)] for d in range(2)]
    vin = [[P.sb("vin%d%d" % (d, i), [128, D], BF16) for i in range(2)] for d in range(2)]
    kp = [[P.sb("kp%d%d" % (d, i), [128, D], BF16) for i in range(2)] for d in range(2)]
    Sm = [[P.sb("Sm%d%d" % (d, i), [128, 4, 128], BF16) for i in range(2)] for d in range(2)]
    qs = [[P.sb("qs%d%d" % (d, i), [128, 8, 128], BF16) for i in range(2)] for d in range(2)]
    Cst = [P.sb("Cst%d" % d, [128, 4, 512], F32) for d in range(2)]
    Cbf = [P.sb("Cbf%d" % d, [128, 4, 512], BF16) for d in range(2)]
    Cbt = [[P.T("Cbt%d%d" % (d, h)) for h in range(4)] for d in range(2)]
    Cft = [[P.T("Cft%d%d" % (d, h)) for h in range(4)] for d in range(2)]
    nst = [P.sb("nst%d" % d, [128, 8], F32) for d in range(2)]
    ntmp = [P.sb("ntmp%d" % d, [128, 8], F32) for d in range(2)]
    nbf = [P.sb("nbf%d" % d, [128, 8], BF16) for d in range(2)]
    hbuf = [[P.sb("hbuf%d%d" % (d, i), [128, D], F32) for i in range(2)] for d in range(2)]
    hoth = [P.sb("hoth%d" % i, [128, D], F32) for i in range(2)]
    osg = [P.sb("osg%d" % i, [128, D], BF16) for i in range(2)]
    ymt = [P.sb("ymt%d" % i, [128, D], BF16) for i in range(2)]
    dsb = [P.sb("dsb%d" % d, [128, 16], F32) for d in range(2)]
    fst = [P.sb("fst%d" % i, [128, 16], F32) for i in range(2)]
    fjunk = P.sb("fjunk", [128, 256], BF16)
    half = NL // 2
    GRP = 4 if half % 4 == 0 else (2 if half % 2 == 0 else 1)
    ymstage = [[P.sb("ymst%d%d" % (d, i), [128, 8, GRP * 128], BF16) for i in range(2)] for d in range(2)]
    pK = P.ps("pK", [128, 8, 128], BF16, bank=0)
    pdc = [P.ps("pdc%d" % i, [128, 512], F32, bank=1 + i) for i in range(2)]
    pb3 = P.T("pbank3")
    pdn = [TV(pb3, P.ps("pdn%d" % d, [128, 8], F32, bank=3, off=d * 32).h) for d in range(2)]
    pden = [TV(pb3, P.ps("pden%d" % d, [128, 4], F32, bank=3, off=64 + d * 32).h) for d in range(2)]
    pS = P.ps("pS", [128, 4, 128], F32, bank=4)
    pnum = [P.ps("pnum%d" % i, [128, 2, 256], F32, bank=5 + i) for i in range(2)]
    pT3 = P.ps("pT3", [128, 8, 128], BF16, bank=7)
    for d in range(2):
        P.memset("dve", Cst[d].ap(), 0.0, Cft[d])
        P.memset("pool", Cbf[d].ap(), 0.0, Cbt[d])
        P.memset("dve", nst[d].ap(), 0.0, [nst[d]])
        P.memset("dve", nbf[d].ap(), 0.0, [nbf[d]])
    fin_cnt = [0, 0]
    fcount = [0]
    for k in range(NS):
        for d in range(2):
            tile = tile_of(d, k)
            g0 = tile * 128
            lat = tile >= NCT
            sl = k % 2
            kt = kTin[d][sl]; vt = vin[d][sl]; qt = qTin[d][sl]; kpt = kp[d][sl]; smt = Sm[d][sl]; qst_ = qs[d][sl]
            P.dma("sp", kt.ap(), mkT[:, g0:g0 + 128].rearrange("(j p) t -> p j t", p=128), r=[k_mkT], w=[kt])
            P.dma("sp", vt.ap(), mv[g0:g0 + 128, :], r=[k_mv], w=[vt])
            if lat:
                P.dma("sp", qt.ap(), mqT[:, g0:g0 + 128].rearrange("(j p) t -> p j t", p=128), r=[k_mqT], w=[qt])
            for j in range(8):
                P.tp(pK.ap()[:, j, :], kt.ap()[:, j, :], identb.ap(), [kt, identb], [pK])
            if lat:
                for h in range(4):
                    for dc in range(2):
                        P.mm(pS.ap()[:, h, :], kt.ap()[:, 2 * h + dc, :], qt.ap()[:, 2 * h + dc, :], dc == 0, dc == 1, [kt, qt], [pS])
            for h in range(4):
                col = k * 4 + h
                dstv = kpt.ap()[:, h * 256:(h + 1) * 256].rearrange("p (a b) -> p a b", a=2)
                if h % 2:
                    P.act(dstv, pK.ap()[:, 2 * h:2 * h + 2, :], AF.Copy, [pK, Aa[d]], [kpt], scale=Aa[d].ap()[:, col:col + 1])
                else:
                    P.ts("dve", dstv, pK.ap()[:, 2 * h:2 * h + 2, :], Aa[d].ap()[:, col:col + 1], None, ALU.mult, None, [pK, Aa[d]], [kpt])
            if lat:
                for h in range(4):
                    col = k * 4 + h
                    P.stt("dve", smt.ap()[:, h, :], pS.ap()[:, h, :], Aa[d].ap()[:, col:col + 1], trib.ap()[:, d, :], ALU.mult, ALU.mult,
                          [pS, Aa[d], trib], [smt])
                    P.act(qst_.ap()[:, 2 * h:2 * h + 2, :], qt.ap()[:, 2 * h:2 * h + 2, :], AF.Copy, [qt, C0[d]], [qst_], scale=C0[d].ap()[:, col:col + 1])
                for h in range(4):
                    pn = pnum[h // 2]
                    P.mm(pn.ap()[:, h % 2, :], smt.ap()[:, h, :], vt.ap()[:, h * 256:(h + 1) * 256], True, False, [smt, vt], [pn])
                    P.mm(pn.ap()[:, h % 2, :], qst_.ap()[:, 2 * h, :], Cbf[d].ap()[:, h, 0:256], False, False, [qst_, Cbt[d][h]], [pn])
                    P.mm(pn.ap()[:, h % 2, :], qst_.ap()[:, 2 * h + 1, :], Cbf[d].ap()[:, h, 256:512], False, True, [qst_, Cbt[d][h]], [pn])
                for h in range(4):
                    P.mm(pden[d].ap()[:, h:h + 1], smt.ap()[:, h, :], onesb.ap()[:, 0:1], True, False, [smt, onesb], [pden[d]])
                    P.mm(pden[d].ap()[:, h:h + 1], qst_.ap()[:, 2 * h, :], nbf[d].ap()[:, 2 * h:2 * h + 1], False, False, [qst_, nbf[d]], [pden[d]])
                    P.mm(pden[d].ap()[:, h:h + 1], qst_.ap()[:, 2 * h + 1, :], nbf[d].ap()[:, 2 * h + 1:2 * h + 2], False, True, [qst_, nbf[d]], [pden[d]])
            for h in range(4):
                col = k * 4 + h
                pd = pdc[h % 2]
                for dc in range(2):
                    P.mm(pd.ap()[:, dc * 256:(dc + 1) * 256], kpt.ap()[:, h * 256 + dc * 128:h * 256 + (dc + 1) * 128], vt.ap()[:, h * 256:(h + 1) * 256],
                         True, True, [kpt, vt], [pd])
                P.stt("dve", Cst[d].ap()[:, h, :], Cst[d].ap()[:, h, :], C0[d].ap()[:, col:col + 1], pd.ap(), ALU.mult, ALU.add,
                      [Cft[d][h], C0[d], pd], [Cft[d][h]])
                P.cp("act", Cbf[d].ap()[:, h, :], Cst[d].ap()[:, h, :], [Cft[d][h]], [Cbt[d][h]])
            for j in range(8):
                P.mm(pdn[d].ap()[:, j:j + 1], kpt.ap()[:, j * 128:(j + 1) * 128], onesb.ap()[:, 0:1], True, True, [kpt, onesb], [pdn[d]])
            P.tt("dve", ntmp[d].ap().rearrange("p (h two) -> p h two", two=2), nst[d].ap().rearrange("p (h two) -> p h two", two=2),
                 C0[d].ap()[:, k * 4:(k + 1) * 4].unsqueeze(2).to_broadcast([128, 4, 2]), ALU.mult, [nst[d], C0[d]], [ntmp[d]])
            P.tt("dve", nst[d].ap(), ntmp[d].ap(), pdn[d].ap(), ALU.add, [ntmp[d], pdn[d]], [nst[d]])
            P.cp("dve", nbf[d].ap(), nst[d].ap(), [nst[d]], [nbf[d]])
            if not lat:
                continue
            ds_ = dsb[d]
            hb = hbuf[d][sl]
            P.cp("dve", ds_.ap()[:, 0:4], pden[d].ap(), [pden[d]], [ds_])
            P.stt("dve", ds_.ap()[:, 4:8], ds_.ap()[:, 0:4], -1.0, ds_.ap()[:, 0:4], ALU.mult, ALU.max, [ds_], [ds_])
            P.tt("dve", ds_.ap()[:, 8:12], ds_.ap()[:, 4:8], Ee[d].ap()[:, k * 4:(k + 1) * 4], ALU.max, [ds_, Ee[d]], [ds_])
            P.recip(ds_.ap()[:, 12:16], ds_.ap()[:, 8:12], [ds_], [ds_])
            for h in range(4):
                pn = pnum[h // 2]
                if h % 2:
                    P.act(hb.ap()[:, h * 256:(h + 1) * 256], pn.ap()[:, h % 2, :], AF.Copy, [pn, ds_], [hb], scale=ds_.ap()[:, 12 + h:13 + h])
                else:
                    P.ts("dve", hb.ap()[:, h * 256:(h + 1) * 256], pn.ap()[:, h % 2, :], ds_.ap()[:, 12 + h:13 + h], None, ALU.mult, None, [pn, ds_], [hb])
            li_ = tile - NCT
            ko = step_of[1 - d][tile]
            if ko > k:
                P.dma("pool", hdir[d, li_ * 128:(li_ + 1) * 128, :], hb.ap(), r=[hb], w=[k_hdir])
                continue
            fi = fcount[0]; fcount[0] += 1
            ho = hoth[fi % 2]; og = osg[fi % 2]; ym_ = ymt[fi % 2]; fs = fst[fi % 2]
            P.dma("sp", ho.ap(), hdir[1 - d, li_ * 128:(li_ + 1) * 128, :], r=[k_hdir], w=[ho])
            P.dma("sp", og.ap(), osig[li_ * 128:(li_ + 1) * 128, :], r=[k_osig], w=[og])
            P.tt("pool", ho.ap(), ho.ap(), hb.ap(), ALU.add, [ho, hb], [ho])
            for h in range(4):
                P.act(fjunk.ap(), ho.ap()[:, h * 256:(h + 1) * 256], AF.Square, [ho], [fjunk, fs], accum_out=fs.ap()[:, h:h + 1])
            P.act(fs.ap()[:, 4:8], fs.ap()[:, 0:4], AF.Sqrt, [fs], [fs], bias=EPS, scale=1.0 / 256)
            P.recip(fs.ap()[:, 8:12], fs.ap()[:, 4:8], [fs], [fs])
            P.tt("dve", ho.ap().rearrange("p (h e) -> p h e", h=4), ho.ap().rearrange("p (h e) -> p h e", h=4),
                 fs.ap()[:, 8:12].unsqueeze(2).to_broadcast([128, 4, 256]), ALU.mult, [ho, fs], [ho])
            P.tt("pool", ho.ap(), ho.ap(), mng.ap(), ALU.mult, [ho, mng], [ho])
            P.tt("dve", ym_.ap(), ho.ap(), og.ap(), ALU.mult, [ho, og], [ym_])
            for j in range(8):
                P.tp(pT3.ap()[:, j, :], ym_.ap()[:, j * 128:(j + 1) * 128], identb.ap(), [ym_, identb], [pT3])
            blk = li_ // GRP
            stg = ymstage[d][(fin_cnt[d] // GRP) % 2]
            P.cp("act", stg.ap()[:, :, (li_ % GRP) * 128:(li_ % GRP + 1) * 128], pT3.ap(), [pT3], [stg])
            fin_cnt[d] += 1
            if fin_cnt[d] % GRP == 0:
                for j in range(8):
                    P.dma("pool", ymT[j * 128:(j + 1) * 128, blk * GRP * 128:(blk + 1) * GRP * 128], stg.ap()[:, j, :], r=[stg], w=[k_ymT], key=stg)
    P.barrier()
    P.reset()
    if stop_after == 3:
        P.finish()
        return nc

    P.phase = "p3"
    KT = P.sb("KT", [128, 2, NT], BF16)
    Vr = P.sb("Vr", [128, NTL, 2, 132], BF16)
    P.memset("dve", Vr.ap()[:, :, :, 128:129], 1.0, [Vr])
    for h in range(2):
        P.dma("sp", KT.ap()[:, h, :], kaT[h * 128:(h + 1) * 128, :], r=[k_kaT], w=[KT])
        P.dma("sp", Vr.ap()[:, :, h, 0:128], va[:, h * 128:(h + 1) * 128].rearrange("(t p) e -> p t e", p=128), r=[k_va], w=[Vr])
    QB = 512
    qin = [P.sb("qin%d" % i, [128, 8, QB], BF16) for i in range(2)]
    Pt = [P.sb("Pt%d" % i, [128, QB], BF16) for i in range(3)]
    yat = [[P.sb("yat%d%d" % (i, j), [128, D], BF16) for j in range(4)] for i in range(2)]
    arec = [P.sb("arec%d" % i, [128, 4], F32) for i in range(2)]
    yastage = [P.sb("yast%d" % i, [128, 8, QB], BF16) for i in range(2)]
    pSa = [P.ps("pSa%d" % i, [128, QB], F32, bank=(0, 1, 7)[i]) for i in range(3)]
    pacc1 = [P.ps("pacc%d" % j, [128, 129], F32, bank=2 + j) for j in range(4)]
    pacc = [pacc1, pacc1]
    pT4 = P.ps("pT4", [128, 8, 128], BF16, bank=6)
    sc_att = 128.0 ** -0.5
    its = [(qb, h, kt_) for qb in range(S // QB) for h in range(8) for kt_ in range(NTL)]
    NI = len(its)
    DEPTH = 2

    def issue_qk(i):
        qb, h, kt_ = its[i]
        qi = qin[qb % 2]
        if h == 0 and kt_ == 0:
            for hh in range(8):
                P.dma("sp", qi.ap()[:, hh, :], qaT[hh * 128:(hh + 1) * 128, qb * QB:(qb + 1) * QB], r=[k_qaT], w=[qi])
        ps_ = pSa[i % 3]; pt_ = Pt[i % 3]
        P.mm(ps_.ap(), KT.ap()[:, h // 4, kt_ * 128:(kt_ + 1) * 128], qi.ap()[:, h, :], True, True, [KT, qi], [ps_])
        P.act(pt_.ap(), ps_.ap(), AF.Exp, [ps_], [pt_], scale=sc_att)

    def issue_pv(i):
        qb, h, kt_ = its[i]
        kvh = h // 4
        pt_ = Pt[i % 3]
        acc_ = pacc[h % 2]
        yt = yat[qb % 2]
        for j in range(4):
            P.mm(acc_[j].ap(), pt_.ap()[:, j * 128:(j + 1) * 128], Vr.ap()[:, kt_, kvh, 0:129], kt_ == 0, kt_ == NTL - 1, [pt_, Vr], [acc_[j]])
        if kt_ != NTL - 1:
            return
        ar = arec[h % 2]
        for j in range(4):
            P.recip(ar.ap()[:, j:j + 1], acc_[j].ap()[:, 128:129], [acc_[j]], [ar])
            P.ts("dve", yt[j].ap()[:, h * 128:(h + 1) * 128], acc_[j].ap()[:, 0:128], ar.ap()[:, j:j + 1], None, ALU.mult, None, [acc_[j], ar], [yt[j]])
        if h != 7:
            return
        stg = yastage[qb % 2]
        for j in range(4):
            for hh in range(8):
                P.tp(pT4.ap()[:, hh, :], yt[j].ap()[:, hh * 128:(hh + 1) * 128], identb.ap(), [yt[j], identb], [pT4])
            P.cp("dve", stg.ap()[:, :, j * 128:(j + 1) * 128], pT4.ap(), [pT4], [stg])
        for hh in range(8):
            P.dma("pool", yaT[hh * 128:(hh + 1) * 128, qb * QB:(qb + 1) * QB], stg.ap()[:, hh, :], r=[stg], w=[k_yaT], key=stg)

    for i in range(-DEPTH, NI):
        if i + DEPTH < NI:
            issue_qk(i + DEPTH)
        if i >= 0:
            issue_pv(i)
    P.barrier()
    P.reset()
    if stop_after == 4:
        P.finish()
        return nc

    P.phase = "p4a"
    wpa = P.sb("wpa", [128, 8, D], BF16); wpb = P.sb("wpb", [128, 8, D], BF16); wo = P.sb("wo", [128, 8, D], BF16)
    for wt_, src in ((wpa, w_pa), (wpb, w_pb), (wo, w_o)):
        for hh in range(2):
            P.dma("pool", wt_.ap()[:, :, hh * 512:(hh + 1) * 512], src[:, hh * 512:(hh + 1) * 512].rearrange("(k p) f -> p k f", p=128), w=[wt_])
    ymin = [P.sb("ymin%d" % i, [128, 8, 512], BF16) for i in range(2)]
    yain = [P.sb("yain%d" % i, [128, 8, 512], BF16) for i in range(2)]
    gin = [P.sb("gin%d" % i, [128, 16, 512], BF16) for i in range(2)]
    uT = [P.sb("uT%d" % i, [128, 8, 512], BF16) for i in range(2)]
    u1 = [P.sb("u1_%d" % i, [128, 512], F32) for i in range(2)]
    u2 = [P.sb("u2_%d" % i, [128, 512], F32) for i in range(2)]
    xin = [P.sb("xin%d" % i, [128, D], F32) for i in range(3)]
    ytmp = [P.sb("ytmp%d" % i, [128, D], F32) for i in range(2)]
    x1t = [P.sb("x1t%d" % i, [128, D], F32) for i in range(2)]
    pA = [P.ps("pA%d" % i, [128, 512], F32, bank=i) for i in range(2)]
    pB = [P.ps("pB%d" % i, [128, 512], F32, bank=2 + i) for i in range(2)]
    pY = [P.ps("pY%d" % i, [128, 512], F32, bank=4 + i) for i in range(4)]
    xc = 0
    for b in range(S // 512):
        ym_ = ymin[b % 2]; ya_ = yain[b % 2]; g_ = gin[b % 2]; u_ = uT[b % 2]
        c0_ = b * 512
        P.dma("sp", ym_.ap(), ymT[:, c0_:c0_ + 512].rearrange("(j p) t -> p j t", p=128), r=[k_ymT], w=[ym_])
        P.dma("sp", ya_.ap(), yaT[:, c0_:c0_ + 512].rearrange("(j p) t -> p j t", p=128), r=[k_yaT], w=[ya_])
        P.dma("sp", g_.ap(), gT[:, c0_:c0_ + 512].rearrange("(j p) t -> p j t", p=128), r=[k_gT], w=[g_])
        for fc in range(8):
            pa = pA[fc % 2]; pb_ = pB[fc % 2]; a1 = u1[fc % 2]; a2 = u2[fc % 2]
            for kc in range(8):
                P.mm(pa.ap(), wpa.ap()[:, kc, fc * 128:(fc + 1) * 128], ym_.ap()[:, kc, :], kc == 0, kc == 7, [wpa, ym_], [pa])
            for kc in range(8):
                P.mm(pb_.ap(), wpb.ap()[:, kc, fc * 128:(fc + 1) * 128], ya_.ap()[:, kc, :], kc == 0, kc == 7, [wpb, ya_], [pb_])
            P.tt("dve", a1.ap(), pa.ap(), g_.ap()[:, fc, :], ALU.mult, [pa, g_], [a1])
            P.tt("dve", a2.ap(), pb_.ap(), g_.ap()[:, 8 + fc, :], ALU.mult, [pb_, g_], [a2])
            P.tt("pool", u_.ap()[:, fc, :], a1.ap(), a2.ap(), ALU.add, [a1, a2], [u_])
        for j in range(4):
            xi = xin[xc % 3]; yt_ = ytmp[xc % 2]; xo = x1t[xc % 2]; xc += 1
            r0 = c0_ + j * 128
            P.dma("sp", xi.ap(), x[r0:r0 + 128, :], w=[xi])
            for hh in range(2):
                py = pY[(j * 2 + hh) % 4]
                for kc in range(8):
                    P.mm(py.ap(), u_.ap()[:, kc, j * 128:(j + 1) * 128], wo.ap()[:, kc, hh * 512:(hh + 1) * 512], kc == 0, kc == 7, [u_, wo], [py])
                P.tt("dve", yt_.ap()[:, hh * 512:(hh + 1) * 512], py.ap(), G1bc.ap()[:, hh * 512:(hh + 1) * 512], ALU.mult, [py, G1bc], [yt_])
            P.tt("pool", xo.ap(), yt_.ap(), xi.ap(), ALU.add, [yt_, xi], [xo])
            P.dma("pool", x1d[r0:r0 + 128, :], xo.ap(), r=[xo], w=[k_x1], key=xo)
    P.barrier()
    P.reset()
    if stop_after == 5:
        P.finish()
        return nc

    P.phase = "p4b"
    wg = P.sb("wg", [128, 8, DFF], BF16); wu = P.sb("wu", [128, 8, DFF], BF16); wd = P.sb("wd", [128, 22, D], BF16)
    for wt_, src in ((wg, w_g), (wu, w_u)):
        for c0_ in range(0, DFF, 704):
            P.dma("pool", wt_.ap()[:, :, c0_:c0_ + 704], src[:, c0_:c0_ + 704].rearrange("(k p) f -> p k f", p=128), w=[wt_])
    for hh in range(2):
        P.dma("pool", wd.ap()[:, :, hh * 512:(hh + 1) * 512], w_d[:, hh * 512:(hh + 1) * 512].rearrange("(k p) f -> p k f", p=128), w=[wd])
    fgb = G1bc
    P.dma("sp", fgb.ap(), final_g.partition_broadcast(128), w=[fgb])
    TB = 256
    NJ = TB // 128
    x1in = [P.sb("x1in%d" % i, [128, D], F32) for i in range(2 * NJ)]
    st2 = [P.sb("st2_%d" % i, [128, 4], F32) for i in range(3)]
    junk2 = P.sb("junk2", [128, D], BF16)
    xn2 = [P.sb("xn2_%d" % i, [128, D], BF16) for i in range(1)] * 2
    tf2 = [P.sb("tf2_%d" % i, [128, 8, 128], F32) for i in range(1)] * 2
    h2T = [P.sb("h2T%d" % i, [128, 8, TB], BF16) for i in range(2)]
    aT = [P.sb("aT%d" % i, [128, 22, TB], BF16) for i in range(1)] * 2
    sg = [P.sb("sg%d" % i, [128, TB], F32) for i in range(2)]
    ftmp = [P.sb("ftmp%d" % i, [128, D], F32) for i in range(1)] * 2
    pT5 = [P.ps("pT5_%d" % i, [128, 8, 128], BF16, bank=i) for i in range(2)]
    pG = [P.ps("pG%d" % i, [128, TB], F32, bank=2 + i) for i in range(2)]
    pU = [P.ps("pU%d" % i, [128, TB], F32, bank=4 + i) for i in range(2)]
    pD = [P.ps("pD%d" % i, [128, 512], F32, bank=6 + i) for i in range(2)] * 2
    tcn = [0]
    NB4 = S // TB
    xsets = {}

    def prologue(b):
        h2 = h2T[b % 2]
        xs_ = []
        for j in range(NJ):
            r0 = b * TB + j * 128
            xi = x1in[(b % 2) * NJ + j]; ss = st2[tcn[0] % 3]; xb = xn2[tcn[0] % 2]; pt = pT5[tcn[0] % 2]; tf = tf2[tcn[0] % 2]; tcn[0] += 1
            xs_.append(xi)
            P.dma("sp", xi.ap(), x1d[r0:r0 + 128, :], r=[k_x1], w=[xi])
            P.act(junk2.ap(), xi.ap(), AF.Square, [xi], [junk2, ss], accum_out=ss.ap()[:, 0:1])
            rstd_ops(ss, D)
            P.ts("dve", xb.ap(), xi.ap(), ss.ap()[:, 2:3], None, ALU.mult, None, [xi, ss], [xb])
            for kc in range(8):
                P.tp(pt.ap()[:, kc, :], xb.ap()[:, kc * 128:(kc + 1) * 128], identb.ap(), [xb, identb], [pt])
            P.tt("dve", tf.ap(), pt.ap(), A2.ap().unsqueeze(2).to_broadcast([128, 8, 128]), ALU.mult, [pt, A2], [tf])
            P.tt("pool", h2.ap()[:, :, j * 128:(j + 1) * 128], tf.ap(), modT.ap()[:, 24:32, 0:1].to_broadcast([128, 8, 128]), ALU.add, [tf, modT], [h2])
        xsets[b] = xs_

    def gateup(b):
        h2 = h2T[b % 2]; a_ = aT[b % 2]
        for fc in range(22):
            pg = pG[fc % 2]; pu = pU[fc % 2]; s_ = sg[fc % 2]
            for kc in range(8):
                P.mm(pg.ap(), wg.ap()[:, kc, fc * 128:(fc + 1) * 128], h2.ap()[:, kc, :], kc == 0, kc == 7, [wg, h2], [pg])
            for kc in range(8):
                P.mm(pu.ap(), wu.ap()[:, kc, fc * 128:(fc + 1) * 128], h2.ap()[:, kc, :], kc == 0, kc == 7, [wu, h2], [pu])
            P.act(s_.ap(), pg.ap(), AF.Silu, [pg], [s_])
            P.tt("dve", a_.ap()[:, fc, :], pu.ap(), s_.ap(), ALU.mult, [pu, s_], [a_])

    def down_tail(b):
        a_ = aT[b % 2]
        xs_ = xsets.pop(b)
        for j in range(NJ):
            r0 = b * TB + j * 128
            xi = xs_[j]; ft = ftmp[j % 2]; x2 = xi; o_ = xi; ss = st2[tcn[0] % 3]; tcn[0] += 1
            for hh in range(2):
                pd_ = pD[(j * 2 + hh) % 4]
                for fc in range(22):
                    P.mm(pd_.ap(), a_.ap()[:, fc, j * 128:(j + 1) * 128], wd.ap()[:, fc, hh * 512:(hh + 1) * 512], fc == 0, fc == 21, [a_, wd], [pd_])
                P.tt("dve", ft.ap()[:, hh * 512:(hh + 1) * 512], pd_.ap(), G2bc.ap()[:, hh * 512:(hh + 1) * 512], ALU.mult, [pd_, G2bc], [ft])
            P.tt("pool", x2.ap(), ft.ap(), xi.ap(), ALU.add, [ft, xi], [x2])
            P.act(junk2.ap(), x2.ap(), AF.Square, [x2], [junk2, ss], accum_out=ss.ap()[:, 0:1])
            rstd_ops(ss, D)
            P.ts("dve", ft.ap(), x2.ap(), ss.ap()[:, 2:3], None, ALU.mult, None, [x2, ss], [ft])
            P.tt("pool", o_.ap(), ft.ap(), fgb.ap(), ALU.mult, [ft, fgb], [o_])
            P.dma("sp", out[r0:r0 + 128, :], o_.ap(), r=[o_], w=[k_out])

    prologue(0)
    for b in range(NB4):
        gateup(b)
        if b + 1 < NB4:
            prologue(b + 1)
        down_tail(b)
    P.finish()
    return nc


def make_consts(S):
    cst = np.zeros((128, 512), np.float32)
    cst[:, 0:128] = np.eye(128, dtype=np.float32)
    s = np.arange(128)[:, None]
    t = np.arange(128)[None, :]
    cst[:, 128:256] = (s <= t).astype(np.float32)
    cst[:, 256:384] = (s >= t).astype(np.float32)
    cst[0, 384:512] = 1.0
    rows = S // 64
    row = np.repeat(np.arange(rows, dtype=np.float32), 64)
    col = np.tile(np.arange(64, dtype=np.float32), rows)
    inv = (np.float32(10000.0) ** (-np.arange(32, dtype=np.float32) / np.float32(32))).astype(np.float32)
    ang = np.concatenate([row[:, None] * inv, col[:, None] * inv], axis=-1).astype(np.float32)
    rope = np.concatenate([np.cos(ang), np.sin(ang)], axis=-1).astype(np.float32)
    return cst, rope


def core_inputs(b, inp, S, cst, rope):
    f = lambda a: np.ascontiguousarray(a, dtype=np.float32)
    return {
        "x": f(inp["x"][b, :S]), "c": f(inp["c"][b]), "ctx": f(inp["ctx"][b]), "c_ctx": f(inp["c_ctx"]),
        "w_mod": f(inp["w_mod"][0]), "b_mod": f(inp["b_mod"][0]), "norm1_g": f(inp["norm1_g"][0]),
        "norm2_g": f(inp["norm2_g"][0]), "w_in": f(inp["w_in"][0]), "gate_b": f(inp["gate_b"][0]),
        "conv_w": f(inp["conv_w"][0]), "conv_b": f(inp["conv_b"][0]), "m_norm_g": f(inp["m_norm_g"][0]),
        "q_norm_g": f(inp["q_norm_g"][0]), "k_norm_g": f(inp["k_norm_g"][0]), "w_pa": f(inp["w_pa"][0]),
        "w_pb": f(inp["w_pb"][0]), "w_o": f(inp["w_o"][0]), "w_ffn_gate": f(inp["w_ffn_gate"][0]),
        "w_ffn_up": f(inp["w_ffn_up"][0]), "w_ffn_down": f(inp["w_ffn_down"][0]), "final_g": f(inp["final_g"]),
        "cst": cst, "rope": rope,
    }


_CACHE = {}


def kernel(**inputs):
    S = inputs["x"].shape[1]
    B = inputs["x"].shape[0]
    if S not in _CACHE:
        _CACHE[S] = build(S)
    nc = _CACHE[S]
    cst, rope = make_consts(S)
    in_maps = [core_inputs(b, inputs, S, cst, rope) for b in range(B)]
    res = run_bass_kernel_spmd(nc, in_maps, core_ids=list(range(B)))
    return np.stack([np.asarray(r["out"], dtype=np.float32) for r in res.results], axis=0)
```

```python
import numpy as np
from contextlib import ExitStack
import concourse.bass as bass
import concourse.mybir as mybir
from concourse.bass_utils import run_bass_kernel_spmd

F32 = mybir.dt.float32
BF16 = mybir.dt.bfloat16
AF = mybir.ActivationFunctionType
ALU = mybir.AluOpType
AX = mybir.AxisListType

SEM_EPOCH = 12000
DMA_EPOCH = 1500


class T:
    def __init__(self, name, h=None):
        self.name = name
        self.h = h
        self.last_w = None
        self.readers = []
        self.epochs = []

    def ap(self):
        return self.h if isinstance(self.h, bass.AP) else self.h[:]


class TV:
    def __init__(self, base, h):
        self.base = base
        self.h = h
        self.name = base.name

    def ap(self):
        return self.h

    last_w = property(lambda self: self.base.last_w, lambda self, v: setattr(self.base, "last_w", v))
    readers = property(lambda self: self.base.readers, lambda self, v: setattr(self.base, "readers", v))
    epochs = property(lambda self: self.base.epochs)


def _shape(v, shape):
    if len(shape) == 2:
        return v
    if len(shape) == 3:
        return v.rearrange("p (a b) -> p a b", a=shape[1])
    if len(shape) == 4:
        return v.rearrange("p (a b c) -> p a b c", a=shape[1], b=shape[2])
    raise ValueError(shape)


class Op:
    __slots__ = ("eng", "fn", "deps", "need_inc", "sig", "dma_key", "waits", "idx", "phase")


class Prog:
    ENGS = ("pe", "act", "dve", "pool", "sp")

    def __init__(self, nc, arena_bytes=0):
        self.nc = nc
        self.stack = ExitStack()
        self.ops = {e: [] for e in self.ENGS}
        self.nsem = 0
        self.sem_names = []
        self.all_ops = 0
        self.keys = []
        self.free_sw = []
        self.free_hw = []
        self.scopes = False
        self.bar = {e: None for e in self.ENGS}
        self.arena = None
        if arena_bytes:
            self.arena = self.stack.enter_context(nc.sbuf_tensor("arena", [128, arena_bytes], mybir.dt.uint8))
            self.arena_bytes = arena_bytes
            self.off = 0
            self.mark = 0
            self.banks = [self.stack.enter_context(nc.psum_tensor("bank%d" % i, [128, 512], F32)) for i in range(8)]

    def sb(self, name, shape, dtype):
        if self.arena is None:
            h = self.stack.enter_context(self.nc.sbuf_tensor(name, list(shape), dtype))
            return T(name, h)
        esz = 4 if dtype == F32 else 2
        n = 1
        for d in shape[1:]:
            n *= d
        nb = (n * esz + 31) // 32 * 32
        assert self.off + nb <= self.arena_bytes, ("SBUF arena overflow", name, self.off, nb)
        v = self.arena[0:shape[0], self.off:self.off + n * esz].bitcast(dtype)
        self.off += nb
        v = _shape(v, shape)
        return T(name, v)

    def ps(self, name, shape, dtype, bank=None, off=0):
        if bank is None:
            h = self.stack.enter_context(self.nc.psum_tensor(name, list(shape), dtype))
            return T(name, h)
        n = 1
        for d in shape[1:]:
            n *= d
        esz = 4 if dtype == F32 else 2
        nf = (n * esz + 3) // 4
        assert off + nf <= 512
        v = self.banks[bank][0:shape[0], off:off + nf]
        if dtype != F32:
            v = v.bitcast(dtype)
        return T(name, _shape(v, shape))

    def set_mark(self):
        self.mark = self.off

    def reset(self):
        self.off = self.mark

    def barrier(self):
        last = [self.ops[e][-1] for e in self.ENGS if self.ops[e]]
        pairs = []
        for k in self.keys:
            for slot, cnt in k.epochs:
                pairs.append((slot, 16 * cnt))
            if k.epochs and not k.name.startswith("OUT"):
                (self.free_sw if k.name.endswith("_sw") else self.free_hw).append(tuple(k.epochs[-1]))
                k.epochs = []
        self.keys = [k for k in self.keys if k.epochs]
        for e in self.ENGS:
            self.bar[e] = (last, pairs)

    def T(self, name):
        return T(name)

    def _new_sem(self, name):
        self.sem_names.append(name)
        self.nsem += 1
        return self.nsem - 1

    def _record(self, eng, fn, r, w, dma_key=None):
        op = Op()
        op.eng = eng
        op.fn = fn
        op.need_inc = False
        op.sig = None
        op.dma_key = dma_key
        op.idx = self.all_ops
        op.phase = getattr(self, "phase", "p")
        self.all_ops += 1
        waits = {}
        deps = {}

        def add_dep(d, raw):
            if d is None:
                return
            if d.dma_key is not None:
                k = d.dma_key
                for slot, cnt in k.epochs:
                    v = 16 * cnt
                    if waits.get(slot, 0) < v:
                        waits[slot] = v
                return
            if d.eng == eng and eng == "pe":
                return
            deps[id(d)] = d

        if self.bar[eng] is not None:
            last, keys = self.bar[eng]
            self.bar[eng] = None
            for d in last:
                if d.dma_key is None:
                    add_dep(d, True)
            for slot, v in keys:
                waits[slot] = max(waits.get(slot, 0), v)
        for t in r:
            add_dep(t.last_w, True)
        for t in w:
            add_dep(t.last_w, False)
            for rd in t.readers:
                add_dep(rd, False)
        for t in r:
            t.readers.append(op)
        for t in w:
            t.last_w = op
            t.readers = []
        for d in deps.values():
            d.need_inc = True
        op.deps = list(deps.values())
        op.waits = waits
        if dma_key is not None:
            if not dma_key.epochs:
                self.keys.append(dma_key)
                free = self.free_sw if dma_key.name.endswith("_sw") else self.free_hw
                if free and free[-1][1] < DMA_EPOCH:
                    slot, base = free.pop()
                    dma_key.epochs.append([slot, base])
            if not dma_key.epochs or dma_key.epochs[-1][1] >= 2 * DMA_EPOCH:
                dma_key.epochs.append([self._new_sem("d_" + dma_key.name), 0])
            dma_key.epochs[-1][1] += 1
            op.sig = dma_key.epochs[-1][0]
        self.ops[eng].append(op)
        return op

    def op(self, eng, fn, r=(), w=()):
        return self._record(eng, fn, r, w)

    def dma(self, eng, out, in_, r=(), w=(), key=None, slow=False):
        if key is None:
            key = w[0]
        if eng == "pool":
            base = key.base if isinstance(key, TV) else key
            if not hasattr(base, "_sw"):
                base._sw = T(base.name + "_sw")
            key = base._sw
        if slow:
            return self._record(eng, lambda e: e.dma_start(out=out, in_=in_, allow_slow_non_contiguous=True), r, w, dma_key=key)
        return self._record(eng, lambda e: e.dma_start(out=out, in_=in_), r, w, dma_key=key)

    def finish(self):
        nc = self.nc
        for eng in ("pe", "act", "dve", "pool"):
            slot = None
            cnt = SEM_EPOCH
            for op in self.ops[eng]:
                if op.dma_key is not None or not op.need_inc:
                    continue
                if cnt >= SEM_EPOCH:
                    slot = self._new_sem("e_%s" % eng)
                    cnt = 0
                cnt += 1
                op.sig = (slot, cnt)
        sems = [self.stack.enter_context(nc.semaphore(n + "_%d" % i)) for i, n in enumerate(self.sem_names)]
        self.n_instr = {e: len(v) for e, v in self.ops.items()}
        final_waits = {}
        for eng in self.ENGS:
            for op in self.ops[eng]:
                if op.dma_key is not None and op.dma_key.name.startswith("OUT"):
                    for slot, cnt in op.dma_key.epochs:
                        final_waits[slot] = 16 * cnt
        with nc.Block() as block:
            def run(eng_name, e):
                waited = {}
                cur = [None, None]
                for op in self.ops[eng_name]:
                    if self.scopes and op.phase != cur[0]:
                        if cur[1] is not None:
                            cur[1].__exit__(None, None, None)
                        cur[0] = op.phase
                        cur[1] = nc.named_scope(op.phase)
                        cur[1].__enter__()
                    for d in op.deps:
                        slot, v = d.sig
                        if waited.get(slot, 0) < v:
                            waited[slot] = v
                            e.wait_ge(sems[slot], v)
                    for slot, v in op.waits.items():
                        if waited.get(slot, 0) < v:
                            waited[slot] = v
                            e.wait_ge(sems[slot], v)
                    inst = op.fn(e)
                    if op.dma_key is not None:
                        inst.then_inc(sems[op.sig], 16)
                    elif op.need_inc:
                        inst.then_inc(sems[op.sig[0]], 1)
                if cur[1] is not None:
                    cur[1].__exit__(None, None, None)
                if eng_name == "sp":
                    for slot, v in final_waits.items():
                        e.wait_ge(sems[slot], v)

            @block.tensor
            def _(e):
                run("pe", e)

            @block.scalar
            def _(e):
                run("act", e)

            @block.vector
            def _(e):
                run("dve", e)

            @block.gpsimd
            def _(e):
                run("pool", e)

            @block.sync
            def _(e):
                run("sp", e)
        self.stack.close()


D = 1024
CT = 256
DIN = 7696
DFF = 2816
EPS = 1e-6
NEG = -1.0e30
LN16 = 2.772588722239781


class OpsMixin:
    def mm(self, out, lhsT, rhs, start, stop, r, w):
        self.op("pe", lambda e: e.matmul(out, lhsT, rhs, start=start, stop=stop), r, w)

    def tp(self, out, in_, ident, r, w):
        self.op("pe", lambda e: e.transpose(out, in_, ident), r, w)

    def act(self, out, in_, func, r, w, **kw):
        self.op("act", lambda e: e.activation(out, in_, func, **kw), r, w)

    def tt(self, eng, out, a, b, op, r, w):
        self.op(eng, lambda e: e.tensor_tensor(out, a, b, op), r, w)

    def ts(self, eng, out, a, s1, s2, op0, op1, r, w):
        if op1 is None:
            self.op(eng, lambda e: e.tensor_scalar(out, a, s1, s2, op0), r, w)
        else:
            self.op(eng, lambda e: e.tensor_scalar(out, a, s1, s2, op0, op1), r, w)

    def stt(self, eng, out, a, s, b, op0, op1, r, w):
        self.op(eng, lambda e: e.scalar_tensor_tensor(out, a, s, b, op0, op1), r, w)

    def cp(self, eng, out, in_, r, w):
        if eng == "act":
            self.op("act", lambda e: e.activation(out, in_, AF.Copy), r, w)
        else:
            self.op(eng, lambda e: e.tensor_copy(out, in_), r, w)

    def memset(self, eng, out, val, w):
        self.op(eng, lambda e: e.memset(out, val), (), w)

    def recip(self, out, in_, r, w):
        self.op("dve", lambda e: e.reciprocal(out, in_), r, w)


class KProg(Prog, OpsMixin):
    pass


def conv_blocks(length):
    out = []
    s = 0
    while s < length:
        n = min(508, length - s)
        out.append((s, n))
        s += n
    return out


def build(S, debug=False, stop_after=None, scopes=False):
    NT = S + CT
    NTL = NT // 128
    NL = S // 128
    NCT = CT // 128
    nc = bass.Bass("TRN2", target_bir_lowering=False)
    P = KProg(nc, arena_bytes=206 * 1024)
    P.scopes = scopes
    P.phase = "p0"

    def din(name, shape, dt=F32):
        return nc.dram_tensor(name, list(shape), dt, kind="ExternalInput").ap()

    def dscr(name, shape, dt):
        kind = "ExternalOutput" if debug else "Internal"
        return nc.dram_tensor(name, list(shape), dt, kind=kind).ap()

    x = din("x", [S, D]); c = din("c", [D]); ctx = din("ctx", [CT, D]); c_ctx = din("c_ctx", [D])
    w_mod = din("w_mod", [D, 6 * D]); b_mod = din("b_mod", [6 * D])
    norm1_g = din("norm1_g", [D]); norm2_g = din("norm2_g", [D])
    w_in = din("w_in", [D, DIN]); gate_b = din("gate_b", [16])
    conv_w = din("conv_w", [5, 2 * D]); conv_b = din("conv_b", [2 * D])
    m_norm_g = din("m_norm_g", [D]); q_norm_g = din("q_norm_g", [128]); k_norm_g = din("k_norm_g", [128])
    w_pa = din("w_pa", [D, D]); w_pb = din("w_pb", [D, D]); w_o = din("w_o", [D, D])
    w_g = din("w_ffn_gate", [D, DFF]); w_u = din("w_ffn_up", [D, DFF]); w_d = din("w_ffn_down", [DFF, D])
    final_g = din("final_g", [D])
    cst = din("cst", [128, 512]); rope = din("rope", [S, 128])
    out = nc.dram_tensor("out", [S, D], F32, kind="ExternalOutput").ap()

    mqT = dscr("mqT", [D, NT], BF16); mkT = dscr("mkT", [D, NT], BF16)
    mv = dscr("mv", [NT, D], BF16); osig = dscr("osig", [S, D], BF16)
    qaT = dscr("qaT", [D, S], BF16); kaT = dscr("kaT", [256, NT], BF16); va = dscr("va", [NT, 256], BF16)
    gT = dscr("gT", [2 * D, S], BF16)
    hdir = dscr("hdir", [2, S, D], F32)
    ymT = dscr("ymT", [D, S], BF16); yaT = dscr("yaT", [D, S], BF16)
    x1d = dscr("x1d", [S, D], F32)
    k_mqT = P.T("mqT"); k_mkT = P.T("mkT"); k_mv = P.T("mv"); k_osig = P.T("osig"); k_qaT = P.T("qaT")
    k_kaT = P.T("kaT"); k_va = P.T("va"); k_gT = P.T("gT"); k_hdir = P.T("hdir"); k_ymT = P.T("ymT")
    k_yaT = P.T("yaT"); k_x1 = P.T("x1d"); k_out = P.T("OUT")
    dbg = {}

    identf = P.sb("identf", [128, 128], F32)
    identb = P.sb("identb", [128, 128], BF16)
    trif = P.sb("trif", [128, 2, 128], F32)
    trib = P.sb("trib", [128, 2, 128], BF16)
    e0f = P.sb("e0f", [128, 128], F32)
    onesf = P.sb("onesf", [128, 128], F32)
    onesb = P.sb("onesb", [128, 2], BF16)
    modT = P.sb("modT", [128, 48, 2], F32)
    A1 = P.sb("A1", [128, 8, 2], F32)
    A2 = P.sb("A2", [128, 8], F32)
    G1bc = P.sb("G1bc", [128, D], F32)
    G2bc = P.sb("G2bc", [128, D], F32)
    Gd = [P.sb("Gd%d" % d, [128, NTL, 8], F32) for d in range(2)]
    P.dma("sp", identf.ap(), cst[:, 0:128], w=[identf])
    P.dma("sp", trif.ap(), cst[:, 128:384].rearrange("p (a b) -> p a b", a=2), w=[trif])
    P.dma("sp", e0f.ap(), cst[:, 384:512], w=[e0f])
    P.cp("dve", identb.ap(), identf.ap(), [identf], [identb])
    P.cp("dve", trib.ap(), trif.ap(), [trif], [trib])
    P.memset("dve", onesf.ap(), 1.0, [onesf])
    P.memset("dve", onesb.ap(), 1.0, [onesb])
    P.set_mark()

    def tile_of(d, k):
        if d == 0:
            return k
        if k < NCT:
            return NCT - 1 - k
        return NTL + NCT - 1 - k

    step_of = [{tile_of(d, k): k for k in range(NTL)} for d in range(2)]

    sc = P.sb("sc", [128, 8, 2], F32)
    scs = P.sb("scs", [128, 8, 2], F32)
    bmod = P.sb("bmod", [128, 48], F32)
    n1g = P.sb("n1g", [128, 8], F32)
    n2g = P.sb("n2g", [128, 8], F32)
    wm = [P.sb("wm%d" % i, [128, 8, 512], F32) for i in range(2)]
    P.dma("sp", sc.ap()[:, :, 0], c.rearrange("(k p) -> p k", p=128), w=[sc], slow=True)
    P.dma("sp", sc.ap()[:, :, 1], c_ctx.rearrange("(k p) -> p k", p=128), w=[sc], slow=True)
    P.dma("sp", bmod.ap(), b_mod.rearrange("(k p) -> p k", p=128), w=[bmod], slow=True)
    P.dma("sp", n1g.ap(), norm1_g.rearrange("(k p) -> p k", p=128), w=[n1g], slow=True)
    P.dma("sp", n2g.ap(), norm2_g.rearrange("(k p) -> p k", p=128), w=[n2g], slow=True)
    P.act(scs.ap(), sc.ap(), AF.Silu, [sc], [scs])
    pmod = P.ps("pmod", [128, 48, 2], F32, bank=0)
    for pc in range(12):
        wt = wm[pc % 2]
        P.dma("sp", wt.ap(), w_mod[:, pc * 512:(pc + 1) * 512].rearrange("(k p) f -> p k f", p=128), w=[wt])
        for fl in range(4):
            fc = pc * 4 + fl
            for kc in range(8):
                P.mm(pmod.ap()[:, fc, :], wt.ap()[:, kc, fl * 128:(fl + 1) * 128], scs.ap()[:, kc, :],
                     kc == 0, kc == 7, [wt, scs], [pmod])
    P.tt("dve", modT.ap(), pmod.ap(), bmod.ap().unsqueeze(2).to_broadcast([128, 48, 2]), ALU.add, [pmod, bmod], [modT])
    tmpa = P.sb("tmpa", [128, 8, 2], F32)
    P.ts("dve", tmpa.ap(), modT.ap()[:, 8:16, :], 1.0, None, ALU.add, None, [modT], [tmpa])
    P.tt("dve", A1.ap(), tmpa.ap(), n1g.ap().unsqueeze(2).to_broadcast([128, 8, 2]), ALU.mult, [tmpa, n1g], [A1])
    tmpb = P.sb("tmpb", [128, 8], F32)
    P.ts("dve", tmpb.ap(), modT.ap()[:, 32:40, 0], 1.0, None, ALU.add, None, [modT], [tmpb])
    P.tt("dve", A2.ap(), tmpb.ap(), n2g.ap(), ALU.mult, [tmpb, n2g], [A2])
    dg = [P.sb("dg%d" % i, [128, 128], F32) for i in range(2)]
    for gi, (Gbc, base) in enumerate(((G1bc, 16), (G2bc, 40))):
        pb = [P.ps("pbc%d" % h, [128, 512], F32, bank=1 + h) for h in range(2)]
        for kc in range(8):
            dgt = dg[kc % 2]
            P.ts("dve", dgt.ap(), identf.ap(), modT.ap()[:, base + kc, 0:1], None, ALU.mult, None, [identf, modT], [dgt])
            P.mm(pb[kc // 4].ap()[:, (kc % 4) * 128:(kc % 4 + 1) * 128], onesf.ap(), dgt.ap(), True, True, [onesf, dgt], [pb[kc // 4]])
        for h in range(2):
            P.cp("dve", Gbc.ap()[:, h * 512:(h + 1) * 512], pb[h].ap(), [pb[h]], [Gbc])
    if debug:
        dbg["modT"] = nc.dram_tensor("d_modT", [128, 96], F32, kind="ExternalOutput").ap()
        P.dma("sp", dbg["modT"], modT.ap().rearrange("p a b -> p (a b)"), r=[modT], w=[P.T("OUTd0")])
        dbg["G1bc"] = nc.dram_tensor("d_G1bc", [128, D], F32, kind="ExternalOutput").ap()
        P.dma("sp", dbg["G1bc"], G1bc.ap(), r=[G1bc], w=[P.T("OUTd1")])
    P.barrier()
    P.reset()
    if stop_after == 0:
        P.finish()
        return nc

    P.phase = "p1a"
    hT = P.sb("hT", [128, 8, NT], BF16)
    hTt = [P.T("hT%d" % i) for i in range(NTL)]
    mark2 = P.off
    xt = [P.sb("xt%d" % i, [128, D], F32) for i in range(3)]
    junk = P.sb("junk", [128, D], BF16)
    st = [P.sb("st%d" % i, [128, 4], F32) for i in range(3)]
    xn = [P.sb("xn%d" % i, [128, D], BF16) for i in range(2)]
    tmpf = [P.sb("tmpf%d" % i, [128, 8, 128], F32) for i in range(2)]
    pT = [P.ps("pT%d" % i, [128, 8, 128], BF16, bank=i) for i in range(2)]

    def rstd_ops(stt_, n):
        P.act(stt_.ap()[:, 1:2], stt_.ap()[:, 0:1], AF.Sqrt, [stt_], [stt_], bias=EPS, scale=1.0 / n)
        P.recip(stt_.ap()[:, 2:3], stt_.ap()[:, 1:2], [stt_], [stt_])

    for i in range(NTL):
        xs = xt[i % 3]; ss = st[i % 3]; xb = xn[i % 2]; pt = pT[i % 2]; tf = tmpf[i % 2]
        src = ctx[i * 128:(i + 1) * 128, :] if i < NCT else x[(i - NCT) * 128:(i - NCT + 1) * 128, :]
        P.dma("sp", xs.ap(), src, w=[xs])
        P.act(junk.ap(), xs.ap(), AF.Square, [xs], [junk, ss], accum_out=ss.ap()[:, 0:1])
        rstd_ops(ss, D)
        P.ts("dve", xb.ap(), xs.ap(), ss.ap()[:, 2:3], None, ALU.mult, None, [xs, ss], [xb])
        for kc in range(8):
            P.tp(pt.ap()[:, kc, :], xb.ap()[:, kc * 128:(kc + 1) * 128], identb.ap(), [xb, identb], [pt])
        m = 1 if i < NCT else 0
        P.tt("dve", tf.ap(), pt.ap(), A1.ap()[:, :, m:m + 1].to_broadcast([128, 8, 128]), ALU.mult, [pt, A1], [tf])
        P.tt("pool", hT.ap()[:, :, i * 128:(i + 1) * 128], tf.ap(), modT.ap()[:, 0:8, m:m + 1].to_broadcast([128, 8, 128]), ALU.add,
             [tf, modT], [hTt[i]])
    if debug:
        dbg["hT"] = nc.dram_tensor("d_hT", [128, 8 * NT], BF16, kind="ExternalOutput").ap()
        P.dma("sp", dbg["hT"], hT.ap().rearrange("p a b -> p (a b)"), r=hTt, w=[P.T("OUTd2")])
    if stop_after == 1:
        P.finish()
        return nc
    P.barrier()
    P.off = mark2

    P.phase = "p1b"
    wb = [P.sb("wb%d" % i, [128, 8, 512], BF16) for i in range(2)]
    wcnt = [0]

    wgroups = [(g * 512, 512) for g in range(4)] + [(5648 + g * 512, 512) for g in range(4)] + \
              [(2048, 512), (2560, 512), (3072, 512), (3584, 512), (4096, 16), (4112, 512), (4624, 512), (5136, 512)]
    wtiles = {}

    def issue_w(gi):
        if gi >= len(wgroups) or gi in wtiles:
            return
        c0, ncols = wgroups[gi]
        t = wb[gi % 2]
        P.dma("pool", t.ap()[:, :, 0:ncols], w_in[:, c0:c0 + ncols].rearrange("(k p) f -> p k f", p=128), w=[t])
        wtiles[gi] = t

    def load_w(c0, ncols, prefetch=True):
        gi = wcnt[0]
        wcnt[0] += 1
        assert wgroups[gi] == (c0, ncols), (gi, c0, ncols)
        issue_w(gi)
        t = wtiles[gi]
        if prefetch:
            issue_w(gi + 1)
        return t

    pz = [P.ps("pz%d" % i, [128, 512], F32, bank=2 + i) for i in range(4)]
    pzc = [0]

    def next_pz():
        t = pz[pzc[0] % 4]
        pzc[0] += 1
        return t

    cw = P.sb("cw", [128, 16, 5], F32)
    cb = P.sb("cb", [128, 16], F32)
    gbb = P.sb("gbb", [128, 16], F32)
    qgb = P.sb("qgb", [128, 128], F32)
    kgb = P.sb("kgb", [128, 128], F32)
    for j in range(5):
        P.dma("sp", cw.ap()[:, :, j], conv_w[j].rearrange("(c p) -> p c", p=128), w=[cw], slow=True)
    P.dma("sp", cb.ap(), conv_b.rearrange("(c p) -> p c", p=128), w=[cb], slow=True)
    P.dma("sp", gbb.ap(), gate_b.partition_broadcast(128), w=[gbb])
    P.dma("sp", qgb.ap(), q_norm_g.partition_broadcast(128), w=[qgb])
    P.dma("sp", kgb.ap(), k_norm_g.partition_broadcast(128), w=[kgb])

    def hts(g0, g1):
        return hTt[g0 // 128:(g1 - 1) // 128 + 1]

    Zs = [P.sb("Zs%d" % i, [128, 512], F32) for i in range(2)]
    acc = [P.sb("acc%d" % i, [128, 508], F32) for i in range(2)]
    stg8 = [P.sb("stg8_%d" % i, [128, 512], BF16) for i in range(8)]
    scnt = [0]

    def next_stage():
        t = stg8[scnt[0] % 8]
        scnt[0] += 1
        return t

    sqc = [0]

    def stq(from_act):
        sqc[0] += 1
        m = sqc[0] % 3
        if m == 0:
            return "sp"
        if m == 1:
            return "act" if from_act else "sp"
        return "pool"

    fcnt = [0]
    for grp in range(4):
        wt = load_w(grp * 512, 512)
        for pair in range(2):
            chs = [grp * 4 + pair * 2, grp * 4 + pair * 2 + 1]
            is_k = chs[0] >= 8
            dst, kdst = (mkT, k_mkT) if is_k else (mqT, k_mqT)
            seqs = [(CT, S)] + ([(0, CT)] if is_k else [])
            for (s0, ln) in seqs:
                for (bs, n) in conv_blocks(ln):
                    w0 = max(0, bs - 2); w1 = min(ln, bs + n + 2)
                    ncol = w1 - w0
                    off = 2 - (bs - w0)
                    hr = hts(s0 + w0, s0 + w1)
                    st_ = []
                    for ch in chs:
                        cl = ch % 4
                        i = fcnt[0]; fcnt[0] += 1
                        z = Zs[i % 2]; a = acc[i % 2]; o = next_stage()
                        p = next_pz()
                        st_.append((ch, z, a, o))
                        for kc in range(8):
                            P.mm(p.ap()[:, 0:ncol], wt.ap()[:, kc, cl * 128:(cl + 1) * 128], hT.ap()[:, kc, s0 + w0:s0 + w1],
                                 kc == 0, kc == 7, [wt] + hr, [p])
                        if bs == 0:
                            P.memset("pool", z.ap()[:, 0:2], 0.0, [z])
                        if bs + n == ln:
                            P.memset("pool", z.ap()[:, 2 + n:4 + n], 0.0, [z])
                        P.cp("act", z.ap()[:, off:off + ncol], p.ap()[:, 0:ncol], [p], [z])
                    for j in range(5):
                        for (ch, z, a, o) in st_:
                            if j == 0:
                                P.ts("dve", a.ap()[:, 0:n], z.ap()[:, 0:n], cw.ap()[:, ch, 0:1], None, ALU.mult, None, [z, cw], [a])
                            else:
                                P.stt("dve", a.ap()[:, 0:n], z.ap()[:, j:j + n], cw.ap()[:, ch, j:j + 1], a.ap()[:, 0:n], ALU.mult, ALU.add, [z, cw, a], [a])
                    for (ch, z, a, o) in st_:
                        P.act(o.ap()[:, 0:n], a.ap()[:, 0:n], AF.Silu, [a, cb], [o], bias=cb.ap()[:, ch:ch + 1])
                        r0 = (ch % 8) * 128
                        P.dma(stq(True), dst[r0:r0 + 128, s0 + bs:s0 + bs + n], o.ap()[:, 0:n], r=[o], w=[kdst], key=o)

    if stop_after == "F":
        P.finish()
        return nc
    P.phase = "p1b_G"
    gcnt = 0
    for grp in range(4):
        wt = load_w(5648 + grp * 512, 512)
        for cl in range(4):
            ch = grp * 4 + cl
            for b0 in range(0, S, 512):
                p = next_pz()
                o = next_stage()
                hr = hts(CT + b0, CT + b0 + 512)
                for kc in range(8):
                    P.mm(p.ap(), wt.ap()[:, kc, cl * 128:(cl + 1) * 128], hT.ap()[:, kc, CT + b0:CT + b0 + 512], kc == 0, kc == 7, [wt] + hr, [p])
                P.act(o.ap(), p.ap(), AF.Sigmoid, [p], [o])
                P.dma(stq(True), gT[ch * 128:(ch + 1) * 128, b0:b0 + 512], o.ap(), r=[o], w=[k_gT], key=o)

    if stop_after == "G":
        P.finish()
        return nc
    P.phase = "p1b_mv"
    tcnt = [0]

    def tok_mm(wt, ncols, i):
        p = next_pz()
        for kc in range(8):
            P.mm(p.ap()[:, 0:ncols], hT.ap()[:, kc, i * 128:(i + 1) * 128], wt.ap()[:, kc, 0:ncols], kc == 0, kc == 7, [wt, hTt[i]], [p])
        return p

    for grp in range(2):
        wt = load_w(2048 + grp * 512, 512)
        for i in range(NTL):
            p = tok_mm(wt, 512, i)
            o = next_stage()
            P.cp("act" if i % 2 else "dve", o.ap(), p.ap(), [p], [o])
            P.dma(stq(i % 2 == 1), mv[i * 128:(i + 1) * 128, grp * 512:(grp + 1) * 512], o.ap(), r=[o], w=[k_mv], key=o)
    if stop_after == "mv":
        P.finish()
        return nc
    P.phase = "p1b_o"
    for grp in range(2):
        wt = load_w(3072 + grp * 512, 512)
        for i in range(NCT, NTL):
            p = tok_mm(wt, 512, i)
            o = next_stage()
            P.act(o.ap(), p.ap(), AF.Sigmoid, [p], [o])
            P.dma(stq(True), osig[(i - NCT) * 128:(i - NCT + 1) * 128, grp * 512:(grp + 1) * 512], o.ap(), r=[o], w=[k_osig], key=o)
    if stop_after == "o":
        P.finish()
        return nc
    P.phase = "p1b_gt"
    wt = load_w(4096, 16)
    Gdt = [P.T("Gdt0"), P.T("Gdt1")]
    for i in range(NTL):
        p = tok_mm(wt, 16, i)
        for d in range(2):
            P.tt("dve", Gd[d].ap()[:, step_of[d][i], :], p.ap()[:, d * 8:(d + 1) * 8], gbb.ap()[:, d * 8:(d + 1) * 8], ALU.add, [p, gbb], [Gdt[d]])

    if stop_after == "gates":
        P.finish()
        return nc
    P.phase = "p1b_q"
    ropet = [P.sb("ropet%d" % i, [128, 128], F32) for i in range(2)]
    sqb = [P.sb("sq%d" % i, [128, 512], BF16) for i in range(2)]
    qst = [P.sb("qst%d" % i, [128, 8], F32) for i in range(2)]
    qn = [P.sb("qn%d" % i, [128, 512], F32) for i in range(2)]
    t1 = [P.sb("rt%d" % i, [128, 256], F32) for i in range(4)]
    qr = [P.sb("qr%d" % i, [128, 512], BF16) for i in range(2)]
    qstage = [P.sb("qstage%d" % i, [128, 4, 512], BF16) for i in range(2)]
    acnt = [0]

    def norm_rope(p, nh, gb, rt, do_rope):
        i = acnt[0]; acnt[0] += 1
        s_ = qst[i % 2]; q_ = qn[i % 2]; o_ = qr[i % 2]; sq = sqb[i % 2]
        W = nh * 128
        P.act(sq.ap()[:, 0:W], p.ap()[:, 0:W], AF.Square, [p], [sq])
        P.op("dve", lambda e: e.reduce_sum(s_.ap()[:, 0:nh], sq.ap()[:, 0:W].rearrange("p (h d) -> p h d", h=nh), AX.X), [sq], [s_])
        P.act(s_.ap()[:, 4:4 + nh], s_.ap()[:, 0:nh], AF.Sqrt, [s_], [s_], bias=EPS, scale=1.0 / 128)
        P.recip(s_.ap()[:, 0:nh], s_.ap()[:, 4:4 + nh], [s_], [s_])
        P.tt("dve", q_.ap()[:, 0:W].rearrange("p (h d) -> p h d", h=nh), p.ap()[:, 0:W].rearrange("p (h d) -> p h d", h=nh),
             s_.ap()[:, 0:nh].unsqueeze(2).to_broadcast([128, nh, 128]), ALU.mult, [p, s_], [q_])
        if not do_rope:
            P.tt("pool", o_.ap()[:, 0:W].rearrange("p (h d) -> p h d", h=nh), q_.ap()[:, 0:W].rearrange("p (h d) -> p h d", h=nh),
                 gb.ap().unsqueeze(1).to_broadcast([128, nh, 128]), ALU.mult, [q_, gb], [o_])
            return o_
        P.tt("pool", q_.ap()[:, 0:W].rearrange("p (h d) -> p h d", h=nh), q_.ap()[:, 0:W].rearrange("p (h d) -> p h d", h=nh),
             gb.ap().unsqueeze(1).to_broadcast([128, nh, 128]), ALU.mult, [q_, gb], [q_])
        qv = q_.ap()[:, 0:W].rearrange("p (h i two) -> p h i two", h=nh, two=2)
        ov = o_.ap()[:, 0:W].rearrange("p (h i two) -> p h i two", h=nh, two=2)
        x1 = qv[:, :, :, 0]; x2 = qv[:, :, :, 1]
        cosb = rt.ap()[:, 0:64].unsqueeze(1).to_broadcast([128, nh, 64])
        sinb = rt.ap()[:, 64:128].unsqueeze(1).to_broadcast([128, nh, 64])
        tv = [t.ap()[:, 0:nh * 64].rearrange("p (h i) -> p h i", h=nh) for t in t1]
        P.tt("dve", tv[0], x1, cosb, ALU.mult, [q_, rt], [t1[0]])
        P.tt("dve", tv[1], x2, sinb, ALU.mult, [q_, rt], [t1[1]])
        P.tt("dve", ov[:, :, :, 0], tv[0], tv[1], ALU.subtract, [t1[0], t1[1]], [o_])
        P.tt("pool", tv[2], x1, sinb, ALU.mult, [q_, rt], [t1[2]])
        P.tt("pool", tv[3], x2, cosb, ALU.mult, [q_, rt], [t1[3]])
        P.tt("pool", ov[:, :, :, 1], tv[2], tv[3], ALU.add, [t1[2], t1[3]], [o_])
        return o_

    def load_rope(i):
        rt = ropet[i % 2]
        P.dma("sp", rt.ap(), rope[(i - NCT) * 128:(i - NCT + 1) * 128, :], w=[rt])
        return rt

    pT2 = [P.ps("pT2_%d" % i, [128, 4, 128], BF16, bank=i) for i in range(2)]
    wq = [load_w(4112, 512, prefetch=False), load_w(4624, 512, prefetch=False)]
    ptc = 0
    for b0 in range(0, NL, 4):
        for j in range(4):
            i = NCT + b0 + j
            rt = load_rope(i)
            for grp in range(2):
                stg = qstage[grp]
                p = tok_mm(wq[grp], 512, i)
                o_ = norm_rope(p, 4, qgb, rt, True)
                ptt = pT2[ptc % 2]; ptc += 1
                for h in range(4):
                    P.tp(ptt.ap()[:, h, :], o_.ap()[:, h * 128:(h + 1) * 128], identb.ap(), [o_, identb], [ptt])
                P.cp("act", stg.ap()[:, :, j * 128:(j + 1) * 128], ptt.ap(), [ptt], [stg])
        for grp in range(2):
            stg = qstage[grp]
            for h in range(4):
                P.dma(stq(True), qaT[(grp * 4 + h) * 128:(grp * 4 + h + 1) * 128, b0 * 128:(b0 + 4) * 128], stg.ap()[:, h, :], r=[stg], w=[k_qaT], key=stg)
    if stop_after == "q":
        P.finish()
        return nc
    P.phase = "p1b_kv"
    wt = load_w(5136, 512)
    kstage = [P.sb("kstage%d" % i, [128, 2, 128], BF16) for i in range(2)]
    for i in range(NTL):
        lat = i >= NCT
        rt = load_rope(i) if lat else None
        p = tok_mm(wt, 512, i)
        o_ = norm_rope(p, 2, kgb, rt, lat)
        ptt = pT2[i % 2]
        for h in range(2):
            P.tp(ptt.ap()[:, h, :], o_.ap()[:, h * 128:(h + 1) * 128], identb.ap(), [o_, identb], [ptt])
        ks = kstage[i % 2]
        P.cp("act", ks.ap(), ptt.ap()[:, 0:2, :], [ptt], [ks])
        for h in range(2):
            P.dma(stq(True), kaT[h * 128:(h + 1) * 128, i * 128:(i + 1) * 128], ks.ap()[:, h, :], r=[ks], w=[k_kaT], key=ks)
        o = next_stage()
        P.cp("dve", o.ap()[:, 0:256], p.ap()[:, 256:512], [p], [o])
        P.dma(stq(False), va[i * 128:(i + 1) * 128, :], o.ap()[:, 0:256], r=[o], w=[k_va], key=o)
    if debug:
        for d in range(2):
            dbg["Gd%d" % d] = nc.dram_tensor("d_Gd%d" % d, [128, NTL * 8], F32, kind="ExternalOutput").ap()
            P.dma("sp", dbg["Gd%d" % d], Gd[d].ap().rearrange("p a b -> p (a b)"), r=[Gdt[d]], w=[P.T("OUTg%d" % d)])
    P.barrier()
    P.reset()
    if stop_after == 2:
        P.finish()
        return nc

    P.phase = "p2a"
    NS = NTL
    W4 = NS * 4
    Aa = [P.sb("Aa%d" % d, [128, W4], F32) for d in range(2)]
    Ee = [P.sb("Ee%d" % d, [128, W4], F32) for d in range(2)]
    C0 = [P.sb("C0%d" % d, [128, W4], F32) for d in range(2)]
    mng = P.sb("mng", [128, D], F32)
    P.dma("sp", mng.ap(), m_norm_g.partition_broadcast(128), w=[mng])
    pieces = [(c0_, min(c0_ + 128, W4)) for c0_ in range(0, W4, 128)]
    pTr = P.ps("pTr", [128, 128], F32, bank=4)
    for d in range(2):
        LF = P.sb("LF%d" % d, [128, W4], F32)
        gtmp = P.sb("gtmp%d" % d, [128, W4], F32)
        Bv = P.sb("Bv%d" % d, [128, W4], F32)
        Mb = P.sb("Mb%d" % d, [128, W4], F32)
        mrow = P.sb("mrow%d" % d, [128, W4], F32)
        mprev = P.sb("mprev%d" % d, [128, W4], F32)
        Rr = P.sb("Rr%d" % d, [128, W4], F32)
        FLb = P.sb("FLb%d" % d, [128, W4], F32)
        mcol = P.sb("mcol%d" % d, [128, 4], F32)
        dgm = P.sb("dgm%d" % d, [128, 128], F32)
        v3 = lambda t: t.ap().rearrange("p (s h) -> p s h", h=4)
        pF = P.ps("pF%d" % d, [128, W4], F32, bank=0 + d)
        pFL = P.ps("pFL%d" % d, [128, W4], F32, bank=2 + d)
        pM = P.ps("pM%d" % d, [128, W4], F32, bank=5 + d)
        P.act(v3(gtmp), Gd[d].ap()[:, :, 4:8], AF.Exp, [Gdt[d]], [gtmp], scale=-1.0)
        P.act(gtmp.ap(), gtmp.ap(), AF.Ln, [gtmp], [gtmp], bias=1.0)
        P.ts("dve", LF.ap(), gtmp.ap(), -1.0, None, ALU.mult, None, [gtmp], [LF])
        P.mm(pF.ap(), trif.ap()[:, d, :], LF.ap(), True, True, [trif, LF], [pF])
        P.mm(pFL.ap(), onesf.ap(), LF.ap(), True, True, [onesf, LF], [pFL])
        P.tt("dve", v3(Bv), Gd[d].ap()[:, :, 0:4], pF.ap().rearrange("p (s h) -> p s h", h=4), ALU.subtract, [Gdt[d], pF], [Bv])
        P.cp("dve", FLb.ap(), pFL.ap(), [pFL], [FLb])
        for pi, (a0, a1) in enumerate(pieces):
            w_ = a1 - a0
            P.tp(pTr.ap()[0:w_, :], Bv.ap()[:, a0:a1], identf.ap(), [Bv, identf], [pTr])
            P.memset("dve", mcol.ap()[:, pi:pi + 1], 0.0, [mcol])
            P.op("dve", lambda e, o_=mcol.ap()[0:w_, pi:pi + 1], i_=pTr.ap()[0:w_, :]: e.reduce_max(o_, i_, AX.X), [pTr, mcol], [mcol])
            P.ts("dve", dgm.ap(), identf.ap(), mcol.ap()[:, pi:pi + 1], None, ALU.mult, None, [identf, mcol], [dgm])
            P.mm(pM.ap()[:, a0:a1], onesf.ap(), dgm.ap()[:, 0:w_], True, True, [onesf, dgm], [pM])
        P.cp("dve", Mb.ap(), pM.ap(), [pM], [Mb])
        for h in range(4):
            P.op("dve", lambda e, o_=v3(mrow)[:, :, h], a_=v3(Mb)[:, :, h], b_=v3(FLb)[:, :, h]: e.tensor_tensor_scan(o_, a_, b_, NEG, ALU.max, ALU.add),
                 [Mb, FLb], [mrow])
        P.memset("dve", mprev.ap()[:, 0:4], NEG, [mprev])
        P.cp("dve", mprev.ap()[:, 4:W4], mrow.ap()[:, 0:W4 - 4], [mrow], [mprev])
        P.tt("dve", Rr.ap(), mprev.ap(), Mb.ap(), ALU.max, [mprev, Mb], [Rr])
        P.tt("dve", gtmp.ap(), mprev.ap(), Rr.ap(), ALU.subtract, [mprev, Rr], [gtmp])
        P.ts("dve", gtmp.ap(), gtmp.ap(), -200.0, None, ALU.max, None, [gtmp], [gtmp])
        P.act(C0[d].ap(), gtmp.ap(), AF.Exp, [gtmp], [C0[d]])
        if debug:
            for nm, tl in (("Mb", Mb), ("FLb", FLb), ("mrow", mrow), ("Rr", Rr), ("Bv", Bv)):
                dbg[nm + str(d)] = nc.dram_tensor("d_%s%d" % (nm, d), [128, W4], F32, kind="ExternalOutput").ap()
                P.dma("sp", dbg[nm + str(d)], tl.ap(), r=[tl], w=[P.T("OUT%s%d" % (nm, d))])
        P.tt("dve", Bv.ap(), Bv.ap(), Rr.ap(), ALU.subtract, [Bv, Rr], [Bv])
        P.act(Aa[d].ap(), Bv.ap(), AF.Exp, [Bv], [Aa[d]], bias=-LN16)
        P.tt("dve", LF.ap(), pF.ap(), Rr.ap(), ALU.add, [pF, Rr], [LF])
        P.act(Ee[d].ap(), LF.ap(), AF.Exp, [LF], [Ee[d]], scale=-1.0)
    if debug:
        for nm, tl in (("Aa", Aa), ("Ee", Ee), ("C0", C0)):
            for d in range(2):
                dbg[nm + str(d)] = nc.dram_tensor("d_%s%d" % (nm, d), [128, W4], F32, kind="ExternalOutput").ap()
                P.dma("sp", dbg[nm + str(d)], tl[d].ap(), r=[tl[d]], w=[P.T("OUT%s%d" % (nm, d))])
    P.barrier()
    if stop_after == "2a":
        P.finish()
        return nc

    P.phase = "p2b"
    kTin = [[P.sb("kTin%d%d" % (d, i), [128, 8, 128], BF16) for i in range(2)] for d in range(2)]
    qTin = [[P.sb("qTin%d%d" % (d, i), [128, 8, 128], BF16) for i in range(2)] for d in range(2)]
    vin = [[P.sb("vin%d%d" % (d, i), [128, D], BF16) for i in range(2)] for d in range(2)]
    kp = [[P.sb("kp%d%d" % (d, i), [128, D], BF16) for i in range(2)] for d in range(2)]
    Sm = [[P.sb("Sm%d%d" % (d, i), [128, 4, 128], BF16) for i in range(2)] for d in range(2)]
    qs = [[P.sb("qs%d%d" % (d, i), [128, 8, 128], BF16) for i in range(2)] for d in range(2)]
    Cst = [P.sb("Cst%d" % d, [128, 4, 512], F32) for d in range(2)]
    Cbf = [P.sb("Cbf%d" % d, [128, 4, 512], BF16) for d in range(2)]
    Cbt = [[P.T("Cbt%d%d" % (d, h)) for h in range(4)] for d in range(2)]
    Cft = [[P.T("Cft%d%d" % (d, h)) for h in range(4)] for d in range(2)]
    nst = [P.sb("nst%d" % d, [128, 8], F32) for d in range(2)]
    ntmp = [P.sb("ntmp%d" % d, [128, 8], F32) for d in range(2)]
    nbf = [P.sb("nbf%d" % d, [128, 8], BF16) for d in range(2)]
    hbuf = [[P.sb("hbuf%d%d" % (d, i), [128, D], F32) for i in range(2)] for d in range(2)]
    hoth = [P.sb("hoth%d" % i, [128, D], F32) for i in range(2)]
    osg = [P.sb("osg%d" % i, [128, D], BF16) for i in range(2)]
    ymt = [P.sb("ymt%d" % i, [128, D], BF16) for i in range(2)]
    dsb = [P.sb("dsb%d" % d, [128, 16], F32) for d in range(2)]
    fst = [P.sb("fst%d" % i, [128, 16], F32) for i in range(2)]
    fjunk = P.sb("fjunk", [128, 256], BF16)
    half = NL // 2
    GRP = 4 if half % 4 == 0 else (2 if half % 2 == 0 else 1)
    ymstage = [[P.sb("ymst%d%d" % (d, i), [128, 8, GRP * 128], BF16) for i in range(2)] for d in range(2)]
    pK = P.ps("pK", [128, 8, 128], BF16, bank=0)
    pdc = [P.ps("pdc%d" % i, [128, 512], F32, bank=1 + i) for i in range(2)]
    pb3 = P.T("pbank3")
    pdn = [TV(pb3, P.ps("pdn%d" % d, [128, 8], F32, bank=3, off=d * 32).h) for d in range(2)]
    pden = [TV(pb3, P.ps("pden%d" % d, [128, 4], F32, bank=3, off=64 + d * 32).h) for d in range(2)]
    pS = P.ps("pS", [128, 4, 128], F32, bank=4)
    pnum = [P.ps("pnum%d" % i, [128, 2, 256], F32, bank=5 + i) for i in range(2)]
    pT3 = P.ps("pT3", [128, 8, 128], BF16, bank=7)
    for d in range(2):
        P.memset("dve", Cst[d].ap(), 0.0, Cft[d])
        P.memset("pool", Cbf[d].ap(), 0.0, Cbt[d])
        P.memset("dve", nst[d].ap(), 0.0, [nst[d]])
        P.memset("dve", nbf[d].ap(), 0.0, [nbf[d]])
    fin_cnt = [0, 0]
    fcount = [0]
    for k in range(NS):
        for d in range(2):
            tile = tile_of(d, k)
            g0 = tile * 128
            lat = tile >= NCT
            sl = k % 2
            kt = kTin[d][sl]; vt = vin[d][sl]; qt = qTin[d][sl]; kpt = kp[d][sl]; smt = Sm[d][sl]; qst_ = qs[d][sl]
            P.dma("sp", kt.ap(), mkT[:, g0:g0 + 128].rearrange("(j p) t -> p j t", p=128), r=[k_mkT], w=[kt])
            P.dma("sp", vt.ap(), mv[g0:g0 + 128, :], r=[k_mv], w=[vt])
            if lat:
                P.dma("sp", qt.ap(), mqT[:, g0:g0 + 128].rearrange("(j p) t -> p j t", p=128), r=[k_mqT], w=[qt])
            for j in range(8):
                P.tp(pK.ap()[:, j, :], kt.ap()[:, j, :], identb.ap(), [kt, identb], [pK])
            if lat:
                for h in range(4):
                    for dc in range(2):
                        P.mm(pS.ap()[:, h, :], kt.ap()[:, 2 * h + dc, :], qt.ap()[:, 2 * h + dc, :], dc == 0, dc == 1, [kt, qt], [pS])
            for h in range(4):
                col = k * 4 + h
                dstv = kpt.ap()[:, h * 256:(h + 1) * 256].rearrange("p (a b) -> p a b", a=2)
                if h % 2:
                    P.act(dstv, pK.ap()[:, 2 * h:2 * h + 2, :], AF.Copy, [pK, Aa[d]], [kpt], scale=Aa[d].ap()[:, col:col + 1])
                else:
                    P.ts("dve", dstv, pK.ap()[:, 2 * h:2 * h + 2, :], Aa[d].ap()[:, col:col + 1], None, ALU.mult, None, [pK, Aa[d]], [kpt])
            if lat:
                for h in range(4):
                    col = k * 4 + h
                    P.stt("dve", smt.ap()[:, h, :], pS.ap()[:, h, :], Aa[d].ap()[:, col:col + 1], trib.ap()[:, d, :], ALU.mult, ALU.mult,
                          [pS, Aa[d], trib], [smt])
                    P.act(qst_.ap()[:, 2 * h:2 * h + 2, :], qt.ap()[:, 2 * h:2 * h + 2, :], AF.Copy, [qt, C0[d]], [qst_], scale=C0[d].ap()[:, col:col + 1])
                for h in range(4):
                    pn = pnum[h // 2]
                    P.mm(pn.ap()[:, h % 2, :], smt.ap()[:, h, :], vt.ap()[:, h * 256:(h + 1) * 256], True, False, [smt, vt], [pn])
                    P.mm(pn.ap()[:, h % 2, :], qst_.ap()[:, 2 * h, :], Cbf[d].ap()[:, h, 0:256], False, False, [qst_, Cbt[d][h]], [pn])
                    P.mm(pn.ap()[:, h % 2, :], qst_.ap()[:, 2 * h + 1, :], Cbf[d].ap()[:, h, 256:512], False, True, [qst_, Cbt[d][h]], [pn])
                for h in range(4):
                    P.mm(pden[d].ap()[:, h:h + 1], smt.ap()[:, h, :], onesb.ap()[:, 0:1], True, False, [smt, onesb], [pden[d]])
                    P.mm(pden[d].ap()[:, h:h + 1], qst_.ap()[:, 2 * h, :], nbf[d].ap()[:, 2 * h:2 * h + 1], False, False, [qst_, nbf[d]], [pden[d]])
                    P.mm(pden[d].ap()[:, h:h + 1], qst_.ap()[:, 2 * h + 1, :], nbf[d].ap()[:, 2 * h + 1:2 * h + 2], False, True, [qst_, nbf[d]], [pden[d]])
            for h in range(4):
                col = k * 4 + h
                pd = pdc[h % 2]
                for dc in range(2):
                    P.mm(pd.ap()[:, dc * 256:(dc + 1) * 256], kpt.ap()[:, h * 256 + dc * 128:h * 256 + (dc + 1) * 128], vt.ap()[:, h * 256:(h + 1) * 256],
                         True, True, [kpt, vt], [pd])
                P.stt("dve", Cst[d].ap()[:, h, :], Cst[d].ap()[:, h, :], C0[d].ap()[:, col:col + 1], pd.ap(), ALU.mult, ALU.add,
                      [Cft[d][h], C0[d], pd], [Cft[d][h]])
                P.cp("act", Cbf[d].ap()[:, h, :], Cst[d].ap()[:, h, :], [Cft[d][h]], [Cbt[d][h]])
            for j in range(8):
                P.mm(pdn[d].ap()[:, j:j + 1], kpt.ap()[:, j * 128:(j + 1) * 128], onesb.ap()[:, 0:1], True, True, [kpt, onesb], [pdn[d]])
            P.tt("dve", ntmp[d].ap().rearrange("p (h two) -> p h two", two=2), nst[d].ap().rearrange("p (h two) -> p h two", two=2),
                 C0[d].ap()[:, k * 4:(k + 1) * 4].unsqueeze(2).to_broadcast([128, 4, 2]), ALU.mult, [nst[d], C0[d]], [ntmp[d]])
            P.tt("dve", nst[d].ap(), ntmp[d].ap(), pdn[d].ap(), ALU.add, [ntmp[d], pdn[d]], [nst[d]])
            P.cp("dve", nbf[d].ap(), nst[d].ap(), [nst[d]], [nbf[d]])
            if not lat:
                continue
            ds_ = dsb[d]
            hb = hbuf[d][sl]
            P.cp("dve", ds_.ap()[:, 0:4], pden[d].ap(), [pden[d]], [ds_])
            P.stt("dve", ds_.ap()[:, 4:8], ds_.ap()[:, 0:4], -1.0, ds_.ap()[:, 0:4], ALU.mult, ALU.max, [ds_], [ds_])
            P.tt("dve", ds_.ap()[:, 8:12], ds_.ap()[:, 4:8], Ee[d].ap()[:, k * 4:(k + 1) * 4], ALU.max, [ds_, Ee[d]], [ds_])
            P.recip(ds_.ap()[:, 12:16], ds_.ap()[:, 8:12], [ds_], [ds_])
            for h in range(4):
                pn = pnum[h // 2]
                if h % 2:
                    P.act(hb.ap()[:, h * 256:(h + 1) * 256], pn.ap()[:, h % 2, :], AF.Copy, [pn, ds_], [hb], scale=ds_.ap()[:, 12 + h:13 + h])
                else:
                    P.ts("dve", hb.ap()[:, h * 256:(h + 1) * 256], pn.ap()[:, h % 2, :], ds_.ap()[:, 12 + h:13 + h], None, ALU.mult, None, [pn, ds_], [hb])
            li_ = tile - NCT
            ko = step_of[1 - d][tile]
            if ko > k:
                P.dma("pool", hdir[d, li_ * 128:(li_ + 1) * 128, :], hb.ap(), r=[hb], w=[k_hdir])
                continue
            fi = fcount[0]; fcount[0] += 1
            ho = hoth[fi % 2]; og = osg[fi % 2]; ym_ = ymt[fi % 2]; fs = fst[fi % 2]
            P.dma("sp", ho.ap(), hdir[1 - d, li_ * 128:(li_ + 1) * 128, :], r=[k_hdir], w=[ho])
            P.dma("sp", og.ap(), osig[li_ * 128:(li_ + 1) * 128, :], r=[k_osig], w=[og])
            P.tt("pool", ho.ap(), ho.ap(), hb.ap(), ALU.add, [ho, hb], [ho])
            for h in range(4):
                P.act(fjunk.ap(), ho.ap()[:, h * 256:(h + 1) * 256], AF.Square, [ho], [fjunk, fs], accum_out=fs.ap()[:, h:h + 1])
            P.act(fs.ap()[:, 4:8], fs.ap()[:, 0:4], AF.Sqrt, [fs], [fs], bias=EPS, scale=1.0 / 256)
            P.recip(fs.ap()[:, 8:12], fs.ap()[:, 4:8], [fs], [fs])
            P.tt("dve", ho.ap().rearrange("p (h e) -> p h e", h=4), ho.ap().rearrange("p (h e) -> p h e", h=4),
                 fs.ap()[:, 8:12].unsqueeze(2).to_broadcast([128, 4, 256]), ALU.mult, [ho, fs], [ho])
            P.tt("pool", ho.ap(), ho.ap(), mng.ap(), ALU.mult, [ho, mng], [ho])
            P.tt("dve", ym_.ap(), ho.ap(), og.ap(), ALU.mult, [ho, og], [ym_])
            for j in range(8):
                P.tp(pT3.ap()[:, j, :], ym_.ap()[:, j * 128:(j + 1) * 128], identb.ap(), [ym_, identb], [pT3])
            blk = li_ // GRP
            stg = ymstage[d][(fin_cnt[d] // GRP) % 2]
            P.cp("act", stg.ap()[:, :, (li_ % GRP) * 128:(li_ % GRP + 1) * 128], pT3.ap(), [pT3], [stg])
            fin_cnt[d] += 1
            if fin_cnt[d] % GRP == 0:
                for j in range(8):
                    P.dma("pool", ymT[j * 128:(j + 1) * 128, blk * GRP * 128:(blk + 1) * GRP * 128], stg.ap()[:, j, :], r=[stg], w=[k_ymT], key=stg)
    P.barrier()
    P.reset()
    if stop_after == 3:
        P.finish()
        return nc

    P.phase = "p3"
    KT = P.sb("KT", [128, 2, NT], BF16)
    Vr = P.sb("Vr", [128, NTL, 2, 132], BF16)
    P.memset("dve", Vr.ap()[:, :, :, 128:129], 1.0, [Vr])
    for h in range(2):
        P.dma("sp", KT.ap()[:, h, :], kaT[h * 128:(h + 1) * 128, :], r=[k_kaT], w=[KT])
        P.dma("sp", Vr.ap()[:, :, h, 0:128], va[:, h * 128:(h + 1) * 128].rearrange("(t p) e -> p t e", p=128), r=[k_va], w=[Vr])
    QB = 512
    qin = [P.sb("qin%d" % i, [128, 8, QB], BF16) for i in range(2)]
    Pt = [P.sb("Pt%d" % i, [128, QB], BF16) for i in range(3)]
    yat = [[P.sb("yat%d%d" % (i, j), [128, D], BF16) for j in range(4)] for i in range(2)]
    arec = [P.sb("arec%d" % i, [128, 4], F32) for i in range(2)]
    yastage = [P.sb("yast%d" % i, [128, 8, QB], BF16) for i in range(2)]
    pSa = [P.ps("pSa%d" % i, [128, QB], F32, bank=(0, 1, 7)[i]) for i in range(3)]
    pacc1 = [P.ps("pacc%d" % j, [128, 129], F32, bank=2 + j) for j in range(4)]
    pacc = [pacc1, pacc1]
    pT4 = P.ps("pT4", [128, 8, 128], BF16, bank=6)
    sc_att = 128.0 ** -0.5
    its = [(qb, h, kt_) for qb in range(S // QB) for h in range(8) for kt_ in range(NTL)]
    NI = len(its)
    DEPTH = 2

    def issue_qk(i):
        qb, h, kt_ = its[i]
        qi = qin[qb % 2]
        if h == 0 and kt_ == 0:
            for hh in range(8):
                P.dma("sp", qi.ap()[:, hh, :], qaT[hh * 128:(hh + 1) * 128, qb * QB:(qb + 1) * QB], r=[k_qaT], w=[qi])
        ps_ = pSa[i % 3]; pt_ = Pt[i % 3]
        P.mm(ps_.ap(), KT.ap()[:, h // 4, kt_ * 128:(kt_ + 1) * 128], qi.ap()[:, h, :], True, True, [KT, qi], [ps_])
        P.act(pt_.ap(), ps_.ap(), AF.Exp, [ps_], [pt_], scale=sc_att)

    def issue_pv(i):
        qb, h, kt_ = its[i]
        kvh = h // 4
        pt_ = Pt[i % 3]
        acc_ = pacc[h % 2]
        yt = yat[qb % 2]
        for j in range(4):
            P.mm(acc_[j].ap(), pt_.ap()[:, j * 128:(j + 1) * 128], Vr.ap()[:, kt_, kvh, 0:129], kt_ == 0, kt_ == NTL - 1, [pt_, Vr], [acc_[j]])
        if kt_ != NTL - 1:
            return
        ar = arec[h % 2]
        for j in range(4):
            P.recip(ar.ap()[:, j:j + 1], acc_[j].ap()[:, 128:129], [acc_[j]], [ar])
            P.ts("dve", yt[j].ap()[:, h * 128:(h + 1) * 128], acc_[j].ap()[:, 0:128], ar.ap()[:, j:j + 1], None, ALU.mult, None, [acc_[j], ar], [yt[j]])
        if h != 7:
            return
        stg = yastage[qb % 2]
        for j in range(4):
            for hh in range(8):
                P.tp(pT4.ap()[:, hh, :], yt[j].ap()[:, hh * 128:(hh + 1) * 128], identb.ap(), [yt[j], identb], [pT4])
            P.cp("dve", stg.ap()[:, :, j * 128:(j + 1) * 128], pT4.ap(), [pT4], [stg])
        for hh in range(8):
            P.dma("pool", yaT[hh * 128:(hh + 1) * 128, qb * QB:(qb + 1) * QB], stg.ap()[:, hh, :], r=[stg], w=[k_yaT], key=stg)

    for i in range(-DEPTH, NI):
        if i + DEPTH < NI:
            issue_qk(i + DEPTH)
        if i >= 0:
            issue_pv(i)
    P.barrier()
    P.reset()
    if stop_after == 4:
        P.finish()
        return nc

    P.phase = "p4a"
    wpa = P.sb("wpa", [128, 8, D], BF16); wpb = P.sb("wpb", [128, 8, D], BF16); wo = P.sb("wo", [128, 8, D], BF16)
    for wt_, src in ((wpa, w_pa), (wpb, w_pb), (wo, w_o)):
        for hh in range(2):
            P.dma("pool", wt_.ap()[:, :, hh * 512:(hh + 1) * 512], src[:, hh * 512:(hh + 1) * 512].rearrange("(k p) f -> p k f", p=128), w=[wt_])
    ymin = [P.sb("ymin%d" % i, [128, 8, 512], BF16) for i in range(2)]
    yain = [P.sb("yain%d" % i, [128, 8, 512], BF16) for i in range(2)]
    gin = [P.sb("gin%d" % i, [128, 16, 512], BF16) for i in range(2)]
    uT = [P.sb("uT%d" % i, [128, 8, 512], BF16) for i in range(2)]
    u1 = [P.sb("u1_%d" % i, [128, 512], F32) for i in range(2)]
    u2 = [P.sb("u2_%d" % i, [128, 512], F32) for i in range(2)]
    xin = [P.sb("xin%d" % i, [128, D], F32) for i in range(3)]
    ytmp = [P.sb("ytmp%d" % i, [128, D], F32) for i in range(2)]
    x1t = [P.sb("x1t%d" % i, [128, D], F32) for i in range(2)]
    pA = [P.ps("pA%d" % i, [128, 512], F32, bank=i) for i in range(2)]
    pB = [P.ps("pB%d" % i, [128, 512], F32, bank=2 + i) for i in range(2)]
    pY = [P.ps("pY%d" % i, [128, 512], F32, bank=4 + i) for i in range(4)]
    xc = 0
    for b in range(S // 512):
        ym_ = ymin[b % 2]; ya_ = yain[b % 2]; g_ = gin[b % 2]; u_ = uT[b % 2]
        c0_ = b * 512
        P.dma("sp", ym_.ap(), ymT[:, c0_:c0_ + 512].rearrange("(j p) t -> p j t", p=128), r=[k_ymT], w=[ym_])
        P.dma("sp", ya_.ap(), yaT[:, c0_:c0_ + 512].rearrange("(j p) t -> p j t", p=128), r=[k_yaT], w=[ya_])
        P.dma("sp", g_.ap(), gT[:, c0_:c0_ + 512].rearrange("(j p) t -> p j t", p=128), r=[k_gT], w=[g_])
        for fc in range(8):
            pa = pA[fc % 2]; pb_ = pB[fc % 2]; a1 = u1[fc % 2]; a2 = u2[fc % 2]
            for kc in range(8):
                P.mm(pa.ap(), wpa.ap()[:, kc, fc * 128:(fc + 1) * 128], ym_.ap()[:, kc, :], kc == 0, kc == 7, [wpa, ym_], [pa])
            for kc in range(8):
                P.mm(pb_.ap(), wpb.ap()[:, kc, fc * 128:(fc + 1) * 128], ya_.ap()[:, kc, :], kc == 0, kc == 7, [wpb, ya_], [pb_])
            P.tt("dve", a1.ap(), pa.ap(), g_.ap()[:, fc, :], ALU.mult, [pa, g_], [a1])
            P.tt("dve", a2.ap(), pb_.ap(), g_.ap()[:, 8 + fc, :], ALU.mult, [pb_, g_], [a2])
            P.tt("pool", u_.ap()[:, fc, :], a1.ap(), a2.ap(), ALU.add, [a1, a2], [u_])
        for j in range(4):
            xi = xin[xc % 3]; yt_ = ytmp[xc % 2]; xo = x1t[xc % 2]; xc += 1
            r0 = c0_ + j * 128
            P.dma("sp", xi.ap(), x[r0:r0 + 128, :], w=[xi])
            for hh in range(2):
                py = pY[(j * 2 + hh) % 4]
                for kc in range(8):
                    P.mm(py.ap(), u_.ap()[:, kc, j * 128:(j + 1) * 128], wo.ap()[:, kc, hh * 512:(hh + 1) * 512], kc == 0, kc == 7, [u_, wo], [py])
                P.tt("dve", yt_.ap()[:, hh * 512:(hh + 1) * 512], py.ap(), G1bc.ap()[:, hh * 512:(hh + 1) * 512], ALU.mult, [py, G1bc], [yt_])
            P.tt("pool", xo.ap(), yt_.ap(), xi.ap(), ALU.add, [yt_, xi], [xo])
            P.dma("pool", x1d[r0:r0 + 128, :], xo.ap(), r=[xo], w=[k_x1], key=xo)
    P.barrier()
    P.reset()
    if stop_after == 5:
        P.finish()
        return nc

    P.phase = "p4b"
    wg = P.sb("wg", [128, 8, DFF], BF16); wu = P.sb("wu", [128, 8, DFF], BF16); wd = P.sb("wd", [128, 22, D], BF16)
    for wt_, src in ((wg, w_g), (wu, w_u)):
        for c0_ in range(0, DFF, 704):
            P.dma("pool", wt_.ap()[:, :, c0_:c0_ + 704], src[:, c0_:c0_ + 704].rearrange("(k p) f -> p k f", p=128), w=[wt_])
    for hh in range(2):
        P.dma("pool", wd.ap()[:, :, hh * 512:(hh + 1) * 512], w_d[:, hh * 512:(hh + 1) * 512].rearrange("(k p) f -> p k f", p=128), w=[wd])
    fgb = G1bc
    P.dma("sp", fgb.ap(), final_g.partition_broadcast(128), w=[fgb])
    TB = 256
    NJ = TB // 128
    x1in = [P.sb("x1in%d" % i, [128, D], F32) for i in range(2 * NJ)]
    st2 = [P.sb("st2_%d" % i, [128, 4], F32) for i in range(3)]
    junk2 = P.sb("junk2", [128, D], BF16)
    xn2 = [P.sb("xn2_%d" % i, [128, D], BF16) for i in range(1)] * 2
    tf2 = [P.sb("tf2_%d" % i, [128, 8, 128], F32) for i in range(1)] * 2
    h2T = [P.sb("h2T%d" % i, [128, 8, TB], BF16) for i in range(2)]
    aT = [P.sb("aT%d" % i, [128, 22, TB], BF16) for i in range(1)] * 2
    sg = [P.sb("sg%d" % i, [128, TB], F32) for i in range(2)]
    ftmp = [P.sb("ftmp%d" % i, [128, D], F32) for i in range(1)] * 2
    pT5 = [P.ps("pT5_%d" % i, [128, 8, 128], BF16, bank=i) for i in range(2)]
    pG = [P.ps("pG%d" % i, [128, TB], F32, bank=2 + i) for i in range(2)]
    pU = [P.ps("pU%d" % i, [128, TB], F32, bank=4 + i) for i in range(2)]
    pD = [P.ps("pD%d" % i, [128, 512], F32, bank=6 + i) for i in range(2)] * 2
    tcn = [0]
    NB4 = S // TB
    xsets = {}

    def prologue(b):
        h2 = h2T[b % 2]
        xs_ = []
        for j in range(NJ):
            r0 = b * TB + j * 128
            xi = x1in[(b % 2) * NJ + j]; ss = st2[tcn[0] % 3]; xb = xn2[tcn[0] % 2]; pt = pT5[tcn[0] % 2]; tf = tf2[tcn[0] % 2]; tcn[0] += 1
            xs_.append(xi)
            P.dma("sp", xi.ap(), x1d[r0:r0 + 128, :], r=[k_x1], w=[xi])
            P.act(junk2.ap(), xi.ap(), AF.Square, [xi], [junk2, ss], accum_out=ss.ap()[:, 0:1])
            rstd_ops(ss, D)
            P.ts("dve", xb.ap(), xi.ap(), ss.ap()[:, 2:3], None, ALU.mult, None, [xi, ss], [xb])
            for kc in range(8):
                P.tp(pt.ap()[:, kc, :], xb.ap()[:, kc * 128:(kc + 1) * 128], identb.ap(), [xb, identb], [pt])
            P.tt("dve", tf.ap(), pt.ap(), A2.ap().unsqueeze(2).to_broadcast([128, 8, 128]), ALU.mult, [pt, A2], [tf])
            P.tt("pool", h2.ap()[:, :, j * 128:(j + 1) * 128], tf.ap(), modT.ap()[:, 24:32, 0:1].to_broadcast([128, 8, 128]), ALU.add, [tf, modT], [h2])
        xsets[b] = xs_

    def gateup(b):
        h2 = h2T[b % 2]; a_ = aT[b % 2]
        for fc in range(22):
            pg = pG[fc % 2]; pu = pU[fc % 2]; s_ = sg[fc % 2]
            for kc in range(8):
                P.mm(pg.ap(), wg.ap()[:, kc, fc * 128:(fc + 1) * 128], h2.ap()[:, kc, :], kc == 0, kc == 7, [wg, h2], [pg])
            for kc in range(8):
                P.mm(pu.ap(), wu.ap()[:, kc, fc * 128:(fc + 1) * 128], h2.ap()[:, kc, :], kc == 0, kc == 7, [wu, h2], [pu])
            P.act(s_.ap(), pg.ap(), AF.Silu, [pg], [s_])
            P.tt("dve", a_.ap()[:, fc, :], pu.ap(), s_.ap(), ALU.mult, [pu, s_], [a_])

    def down_tail(b):
        a_ = aT[b % 2]
        xs_ = xsets.pop(b)
        for j in range(NJ):
            r0 = b * TB + j * 128
            xi = xs_[j]; ft = ftmp[j % 2]; x2 = xi; o_ = xi; ss = st2[tcn[0] % 3]; tcn[0] += 1
            for hh in range(2):
                pd_ = pD[(j * 2 + hh) % 4]
                for fc in range(22):
                    P.mm(pd_.ap(), a_.ap()[:, fc, j * 128:(j + 1) * 128], wd.ap()[:, fc, hh * 512:(hh + 1) * 512], fc == 0, fc == 21, [a_, wd], [pd_])
                P.tt("dve", ft.ap()[:, hh * 512:(hh + 1) * 512], pd_.ap(), G2bc.ap()[:, hh * 512:(hh + 1) * 512], ALU.mult, [pd_, G2bc], [ft])
            P.tt("pool", x2.ap(), ft.ap(), xi.ap(), ALU.add, [ft, xi], [x2])
            P.act(junk2.ap(), x2.ap(), AF.Square, [x2], [junk2, ss], accum_out=ss.ap()[:, 0:1])
            rstd_ops(ss, D)
            P.ts("dve", ft.ap(), x2.ap(), ss.ap()[:, 2:3], None, ALU.mult, None, [x2, ss], [ft])
            P.tt("pool", o_.ap(), ft.ap(), fgb.ap(), ALU.mult, [ft, fgb], [o_])
            P.dma("sp", out[r0:r0 + 128, :], o_.ap(), r=[o_], w=[k_out])

    prologue(0)
    for b in range(NB4):
        gateup(b)
        if b + 1 < NB4:
            prologue(b + 1)
        down_tail(b)
    P.finish()
    return nc


def make_consts(S):
    cst = np.zeros((128, 512), np.float32)
    cst[:, 0:128] = np.eye(128, dtype=np.float32)
    s = np.arange(128)[:, None]
    t = np.arange(128)[None, :]
    cst[:, 128:256] = (s <= t).astype(np.float32)
    cst[:, 256:384] = (s >= t).astype(np.float32)
    cst[0, 384:512] = 1.0
    rows = S // 64
    row = np.repeat(np.arange(rows, dtype=np.float32), 64)
    col = np.tile(np.arange(64, dtype=np.float32), rows)
    inv = (np.float32(10000.0) ** (-np.arange(32, dtype=np.float32) / np.float32(32))).astype(np.float32)
    ang = np.concatenate([row[:, None] * inv, col[:, None] * inv], axis=-1).astype(np.float32)
    rope = np.concatenate([np.cos(ang), np.sin(ang)], axis=-1).astype(np.float32)
    return cst, rope


def core_inputs(b, inp, S, cst, rope):
    f = lambda a: np.ascontiguousarray(a, dtype=np.float32)
    return {
        "x": f(inp["x"][b, :S]), "c": f(inp["c"][b]), "ctx": f(inp["ctx"][b]), "c_ctx": f(inp["c_ctx"]),
        "w_mod": f(inp["w_mod"][0]), "b_mod": f(inp["b_mod"][0]), "norm1_g": f(inp["norm1_g"][0]),
        "norm2_g": f(inp["norm2_g"][0]), "w_in": f(inp["w_in"][0]), "gate_b": f(inp["gate_b"][0]),
        "conv_w": f(inp["conv_w"][0]), "conv_b": f(inp["conv_b"][0]), "m_norm_g": f(inp["m_norm_g"][0]),
        "q_norm_g": f(inp["q_norm_g"][0]), "k_norm_g": f(inp["k_norm_g"][0]), "w_pa": f(inp["w_pa"][0]),
        "w_pb": f(inp["w_pb"][0]), "w_o": f(inp["w_o"][0]), "w_ffn_gate": f(inp["w_ffn_gate"][0]),
        "w_ffn_up": f(inp["w_ffn_up"][0]), "w_ffn_down": f(inp["w_ffn_down"][0]), "final_g": f(inp["final_g"]),
        "cst": cst, "rope": rope,
    }


_CACHE = {}


def kernel(**inputs):
    S = inputs["x"].shape[1]
    B = inputs["x"].shape[0]
    if S not in _CACHE:
        _CACHE[S] = build(S)
    nc = _CACHE[S]
    cst, rope = make_consts(S)
    in_maps = [core_inputs(b, inputs, S, cst, rope) for b in range(B)]
    res = run_bass_kernel_spmd(nc, in_maps, core_ids=list(range(B)))
    return np.stack([np.asarray(r["out"], dtype=np.float32) for r in res.results], axis=0)
```

```python
import numpy as np
from contextlib import ExitStack
import concourse.bass as bass
import concourse.mybir as mybir
from concourse.bass_utils import run_bass_kernel_spmd

F32 = mybir.dt.float32
BF16 = mybir.dt.bfloat16
AF = mybir.ActivationFunctionType
ALU = mybir.AluOpType
AX = mybir.AxisListType

SEM_EPOCH = 12000
DMA_EPOCH = 1500


class T:
    def __init__(self, name, h=None):
        self.name = name
        self.h = h
        self.last_w = None
        self.readers = []
        self.epochs = []

    def ap(self):
        return self.h if isinstance(self.h, bass.AP) else self.h[:]


class TV:
    def __init__(self, base, h):
        self.base = base
        self.h = h
        self.name = base.name

    def ap(self):
        return self.h

    last_w = property(lambda self: self.base.last_w, lambda self, v: setattr(self.base, "last_w", v))
    readers = property(lambda self: self.base.readers, lambda self, v: setattr(self.base, "readers", v))
    epochs = property(lambda self: self.base.epochs)


def _shape(v, shape):
    if len(shape) == 2:
        return v
    if len(shape) == 3:
        return v.rearrange("p (a b) -> p a b", a=shape[1])
    if len(shape) == 4:
        return v.rearrange("p (a b c) -> p a b c", a=shape[1], b=shape[2])
    raise ValueError(shape)


class Op:
    __slots__ = ("eng", "fn", "deps", "need_inc", "sig", "dma_key", "waits", "idx", "phase")


class Prog:
    ENGS = ("pe", "act", "dve", "pool", "sp")

    def __init__(self, nc, arena_bytes=0):
        self.nc = nc
        self.stack = ExitStack()
        self.ops = {e: [] for e in self.ENGS}
        self.nsem = 0
        self.sem_names = []
        self.all_ops = 0
        self.keys = []
        self.free_sw = []
        self.free_hw = []
        self.scopes = False
        self.bar = {e: None for e in self.ENGS}
        self.arena = None
        if arena_bytes:
            self.arena = self.stack.enter_context(nc.sbuf_tensor("arena", [128, arena_bytes], mybir.dt.uint8))
            self.arena_bytes = arena_bytes
            self.off = 0
            self.mark = 0
            self.banks = [self.stack.enter_context(nc.psum_tensor("bank%d" % i, [128, 512], F32)) for i in range(8)]

    def sb(self, name, shape, dtype):
        if self.arena is None:
            h = self.stack.enter_context(self.nc.sbuf_tensor(name, list(shape), dtype))
            return T(name, h)
        esz = 4 if dtype == F32 else 2
        n = 1
        for d in shape[1:]:
            n *= d
        nb = (n * esz + 31) // 32 * 32
        assert self.off + nb <= self.arena_bytes, ("SBUF arena overflow", name, self.off, nb)
        v = self.arena[0:shape[0], self.off:self.off + n * esz].bitcast(dtype)
        self.off += nb
        v = _shape(v, shape)
        return T(name, v)

    def ps(self, name, shape, dtype, bank=None, off=0):
        if bank is None:
            h = self.stack.enter_context(self.nc.psum_tensor(name, list(shape), dtype))
            return T(name, h)
        n = 1
        for d in shape[1:]:
            n *= d
        esz = 4 if dtype == F32 else 2
        nf = (n * esz + 3) // 4
        assert off + nf <= 512
        v = self.banks[bank][0:shape[0], off:off + nf]
        if dtype != F32:
            v = v.bitcast(dtype)
        return T(name, _shape(v, shape))

    def set_mark(self):
        self.mark = self.off

    def reset(self):
        self.off = self.mark

    def barrier(self):
        last = [self.ops[e][-1] for e in self.ENGS if self.ops[e]]
        pairs = []
        for k in self.keys:
            for slot, cnt in k.epochs:
                pairs.append((slot, 16 * cnt))
            if k.epochs and not k.name.startswith("OUT"):
                (self.free_sw if k.name.endswith("_sw") else self.free_hw).append(tuple(k.epochs[-1]))
                k.epochs = []
        self.keys = [k for k in self.keys if k.epochs]
        for e in self.ENGS:
            self.bar[e] = (last, pairs)

    def T(self, name):
        return T(name)

    def _new_sem(self, name):
        self.sem_names.append(name)
        self.nsem += 1
        return self.nsem - 1

    def _record(self, eng, fn, r, w, dma_key=None):
        op = Op()
        op.eng = eng
        op.fn = fn
        op.need_inc = False
        op.sig = None
        op.dma_key = dma_key
        op.idx = self.all_ops
        op.phase = getattr(self, "phase", "p")
        self.all_ops += 1
        waits = {}
        deps = {}

        def add_dep(d, raw):
            if d is None:
                return
            if d.dma_key is not None:
                k = d.dma_key
                for slot, cnt in k.epochs:
                    v = 16 * cnt
                    if waits.get(slot, 0) < v:
                        waits[slot] = v
                return
            if d.eng == eng and eng == "pe":
                return
            deps[id(d)] = d

        if self.bar[eng] is not None:
            last, keys = self.bar[eng]
            self.bar[eng] = None
            for d in last:
                if d.dma_key is None:
                    add_dep(d, True)
            for slot, v in keys:
                waits[slot] = max(waits.get(slot, 0), v)
        for t in r:
            add_dep(t.last_w, True)
        for t in w:
            add_dep(t.last_w, False)
            for rd in t.readers:
                add_dep(rd, False)
        for t in r:
            t.readers.append(op)
        for t in w:
            t.last_w = op
            t.readers = []
        for d in deps.values():
            d.need_inc = True
        op.deps = list(deps.values())
        op.waits = waits
        if dma_key is not None:
            if not dma_key.epochs:
                self.keys.append(dma_key)
                free = self.free_sw if dma_key.name.endswith("_sw") else self.free_hw
                if free and free[-1][1] < DMA_EPOCH:
                    slot, base = free.pop()
                    dma_key.epochs.append([slot, base])
            if not dma_key.epochs or dma_key.epochs[-1][1] >= 2 * DMA_EPOCH:
                dma_key.epochs.append([self._new_sem("d_" + dma_key.name), 0])
            dma_key.epochs[-1][1] += 1
            op.sig = dma_key.epochs[-1][0]
        self.ops[eng].append(op)
        return op

    def op(self, eng, fn, r=(), w=()):
        return self._record(eng, fn, r, w)

    def dma(self, eng, out, in_, r=(), w=(), key=None, slow=False):
        if key is None:
            key = w[0]
        if eng == "pool":
            base = key.base if isinstance(key, TV) else key
            if not hasattr(base, "_sw"):
                base._sw = T(base.name + "_sw")
            key = base._sw
        if slow:
            return self._record(eng, lambda e: e.dma_start(out=out, in_=in_, allow_slow_non_contiguous=True), r, w, dma_key=key)
        return self._record(eng, lambda e: e.dma_start(out=out, in_=in_), r, w, dma_key=key)

    def finish(self):
        nc = self.nc
        for eng in ("pe", "act", "dve", "pool"):
            slot = None
            cnt = SEM_EPOCH
            for op in self.ops[eng]:
                if op.dma_key is not None or not op.need_inc:
                    continue
                if cnt >= SEM_EPOCH:
                    slot = self._new_sem("e_%s" % eng)
                    cnt = 0
                cnt += 1
                op.sig = (slot, cnt)
        sems = [self.stack.enter_context(nc.semaphore(n + "_%d" % i)) for i, n in enumerate(self.sem_names)]
        self.n_instr = {e: len(v) for e, v in self.ops.items()}
        final_waits = {}
        for eng in self.ENGS:
            for op in self.ops[eng]:
                if op.dma_key is not None and op.dma_key.name.startswith("OUT"):
                    for slot, cnt in op.dma_key.epochs:
                        final_waits[slot] = 16 * cnt
        with nc.Block() as block:
            def run(eng_name, e):
                waited = {}
                cur = [None, None]
                for op in self.ops[eng_name]:
                    if self.scopes and op.phase != cur[0]:
                        if cur[1] is not None:
                            cur[1].__exit__(None, None, None)
                        cur[0] = op.phase
                        cur[1] = nc.named_scope(op.phase)
                        cur[1].__enter__()
                    for d in op.deps:
                        slot, v = d.sig
                        if waited.get(slot, 0) < v:
                            waited[slot] = v
                            e.wait_ge(sems[slot], v)
                    for slot, v in op.waits.items():
                        if waited.get(slot, 0) < v:
                            waited[slot] = v
                            e.wait_ge(sems[slot], v)
                    inst = op.fn(e)
                    if op.dma_key is not None:
                        inst.then_inc(sems[op.sig], 16)
                    elif op.need_inc:
                        inst.then_inc(sems[op.sig[0]], 1)
                if cur[1] is not None:
                    cur[1].__exit__(None, None, None)
                if eng_name == "sp":
                    for slot, v in final_waits.items():
                        e.wait_ge(sems[slot], v)

            @block.tensor
            def _(e):
                run("pe", e)

            @block.scalar
            def _(e):
                run("act", e)

            @block.vector
            def _(e):
                run("dve", e)

            @block.gpsimd
            def _(e):
                run("pool", e)

            @block.sync
            def _(e):
                run("sp", e)
        self.stack.close()


D = 1024
CT = 256
DIN = 7696
DFF = 2816
EPS = 1e-6
NEG = -1.0e30
LN16 = 2.772588722239781


class OpsMixin:
    def mm(self, out, lhsT, rhs, start, stop, r, w):
        self.op("pe", lambda e: e.matmul(out, lhsT, rhs, start=start, stop=stop), r, w)

    def tp(self, out, in_, ident, r, w):
        self.op("pe", lambda e: e.transpose(out, in_, ident), r, w)

    def act(self, out, in_, func, r, w, **kw):
        self.op("act", lambda e: e.activation(out, in_, func, **kw), r, w)

    def tt(self, eng, out, a, b, op, r, w):
        self.op(eng, lambda e: e.tensor_tensor(out, a, b, op), r, w)

    def ts(self, eng, out, a, s1, s2, op0, op1, r, w):
        if op1 is None:
            self.op(eng, lambda e: e.tensor_scalar(out, a, s1, s2, op0), r, w)
        else:
            self.op(eng, lambda e: e.tensor_scalar(out, a, s1, s2, op0, op1), r, w)

    def stt(self, eng, out, a, s, b, op0, op1, r, w):
        self.op(eng, lambda e: e.scalar_tensor_tensor(out, a, s, b, op0, op1), r, w)

    def cp(self, eng, out, in_, r, w):
        if eng == "act":
            self.op("act", lambda e: e.activation(out, in_, AF.Copy), r, w)
        else:
            self.op(eng, lambda e: e.tensor_copy(out, in_), r, w)

    def memset(self, eng, out, val, w):
        self.op(eng, lambda e: e.memset(out, val), (), w)

    def recip(self, out, in_, r, w):
        self.op("dve", lambda e: e.reciprocal(out, in_), r, w)


class KProg(Prog, OpsMixin):
    pass


def conv_blocks(length):
    out = []
    s = 0
    while s < length:
        n = min(508, length - s)
        out.append((s, n))
        s += n
    return out


def build(S, debug=False, stop_after=None, scopes=False):
    NT = S + CT
    NTL = NT // 128
    NL = S // 128
    NCT = CT // 128
    nc = bass.Bass("TRN2", target_bir_lowering=False)
    P = KProg(nc, arena_bytes=206 * 1024)
    P.scopes = scopes
    P.phase = "p0"

    def din(name, shape, dt=F32):
        return nc.dram_tensor(name, list(shape), dt, kind="ExternalInput").ap()

    def dscr(name, shape, dt):
        kind = "ExternalOutput" if debug else "Internal"
        return nc.dram_tensor(name, list(shape), dt, kind=kind).ap()

    x = din("x", [S, D]); c = din("c", [D]); ctx = din("ctx", [CT, D]); c_ctx = din("c_ctx", [D])
    w_mod = din("w_mod", [D, 6 * D]); b_mod = din("b_mod", [6 * D])
    norm1_g = din("norm1_g", [D]); norm2_g = din("norm2_g", [D])
    w_in = din("w_in", [D, DIN]); gate_b = din("gate_b", [16])
    conv_w = din("conv_w", [5, 2 * D]); conv_b = din("conv_b", [2 * D])
    m_norm_g = din("m_norm_g", [D]); q_norm_g = din("q_norm_g", [128]); k_norm_g = din("k_norm_g", [128])
    w_pa = din("w_pa", [D, D]); w_pb = din("w_pb", [D, D]); w_o = din("w_o", [D, D])
    w_g = din("w_ffn_gate", [D, DFF]); w_u = din("w_ffn_up", [D, DFF]); w_d = din("w_ffn_down", [DFF, D])
    final_g = din("final_g", [D])
    cst = din("cst", [128, 512]); rope = din("rope", [S, 128])
    out = nc.dram_tensor("out", [S, D], F32, kind="ExternalOutput").ap()

    mqT = dscr("mqT", [D, NT], BF16); mkT = dscr("mkT", [D, NT], BF16)
    mv = dscr("mv", [NT, D], BF16); osig = dscr("osig", [S, D], BF16)
    qaT = dscr("qaT", [D, S], BF16); kaT = dscr("kaT", [256, NT], BF16); va = dscr("va", [NT, 256], BF16)
    gT = dscr("gT", [2 * D, S], BF16)
    hdir = dscr("hdir", [2, S, D], F32)
    ymT = dscr("ymT", [D, S], BF16); yaT = dscr("yaT", [D, S], BF16)
    x1d = dscr("x1d", [S, D], F32)
    k_mqT = P.T("mqT"); k_mkT = P.T("mkT"); k_mv = P.T("mv"); k_osig = P.T("osig"); k_qaT = P.T("qaT")
    k_kaT = P.T("kaT"); k_va = P.T("va"); k_gT = P.T("gT"); k_hdir = P.T("hdir"); k_ymT = P.T("ymT")
    k_yaT = P.T("yaT"); k_x1 = P.T("x1d"); k_out = P.T("OUT")
    dbg = {}

    identf = P.sb("identf", [128, 128], F32)
    identb = P.sb("identb", [128, 128], BF16)
    trif = P.sb("trif", [128, 2, 128], F32)
    trib = P.sb("trib", [128, 2, 128], BF16)
    e0f = P.sb("e0f", [128, 128], F32)
    onesf = P.sb("onesf", [128, 128], F32)
    onesb = P.sb("onesb", [128, 2], BF16)
    modT = P.sb("modT", [128, 48, 2], F32)
    A1 = P.sb("A1", [128, 8, 2], F32)
    A2 = P.sb("A2", [128, 8], F32)
    G1bc = P.sb("G1bc", [128, D], F32)
    G2bc = P.sb("G2bc", [128, D], F32)
    Gd = [P.sb("Gd%d" % d, [128, NTL, 8], F32) for d in range(2)]
    P.dma("sp", identf.ap(), cst[:, 0:128], w=[identf])
    P.dma("sp", trif.ap(), cst[:, 128:384].rearrange("p (a b) -> p a b", a=2), w=[trif])
    P.dma("sp", e0f.ap(), cst[:, 384:512], w=[e0f])
    P.cp("dve", identb.ap(), identf.ap(), [identf], [identb])
    P.cp("dve", trib.ap(), trif.ap(), [trif], [trib])
    P.memset("dve", onesf.ap(), 1.0, [onesf])
    P.memset("dve", onesb.ap(), 1.0, [onesb])
    P.set_mark()

    def tile_of(d, k):
        if d == 0:
            return k
        if k < NCT:
            return NCT - 1 - k
        return NTL + NCT - 1 - k

    step_of = [{tile_of(d, k): k for k in range(NTL)} for d in range(2)]

    sc = P.sb("sc", [128, 8, 2], F32)
    scs = P.sb("scs", [128, 8, 2], F32)
    bmod = P.sb("bmod", [128, 48], F32)
    n1g = P.sb("n1g", [128, 8], F32)
    n2g = P.sb("n2g", [128, 8], F32)
    wm = [P.sb("wm%d" % i, [128, 8, 512], F32) for i in range(2)]
    P.dma("sp", sc.ap()[:, :, 0], c.rearrange("(k p) -> p k", p=128), w=[sc], slow=True)
    P.dma("sp", sc.ap()[:, :, 1], c_ctx.rearrange("(k p) -> p k", p=128), w=[sc], slow=True)
    P.dma("sp", bmod.ap(), b_mod.rearrange("(k p) -> p k", p=128), w=[bmod], slow=True)
    P.dma("sp", n1g.ap(), norm1_g.rearrange("(k p) -> p k", p=128), w=[n1g], slow=True)
    P.dma("sp", n2g.ap(), norm2_g.rearrange("(k p) -> p k", p=128), w=[n2g], slow=True)
    P.act(scs.ap(), sc.ap(), AF.Silu, [sc], [scs])
    pmod = P.ps("pmod", [128, 48, 2], F32, bank=0)
    for pc in range(12):
        wt = wm[pc % 2]
        P.dma("sp", wt.ap(), w_mod[:, pc * 512:(pc + 1) * 512].rearrange("(k p) f -> p k f", p=128), w=[wt])
        for fl in range(4):
            fc = pc * 4 + fl
            for kc in range(8):
                P.mm(pmod.ap()[:, fc, :], wt.ap()[:, kc, fl * 128:(fl + 1) * 128], scs.ap()[:, kc, :],
                     kc == 0, kc == 7, [wt, scs], [pmod])
    P.tt("dve", modT.ap(), pmod.ap(), bmod.ap().unsqueeze(2).to_broadcast([128, 48, 2]), ALU.add, [pmod, bmod], [modT])
    tmpa = P.sb("tmpa", [128, 8, 2], F32)
    P.ts("dve", tmpa.ap(), modT.ap()[:, 8:16, :], 1.0, None, ALU.add, None, [modT], [tmpa])
    P.tt("dve", A1.ap(), tmpa.ap(), n1g.ap().unsqueeze(2).to_broadcast([128, 8, 2]), ALU.mult, [tmpa, n1g], [A1])
    tmpb = P.sb("tmpb", [128, 8], F32)
    P.ts("dve", tmpb.ap(), modT.ap()[:, 32:40, 0], 1.0, None, ALU.add, None, [modT], [tmpb])
    P.tt("dve", A2.ap(), tmpb.ap(), n2g.ap(), ALU.mult, [tmpb, n2g], [A2])
    dg = [P.sb("dg%d" % i, [128, 128], F32) for i in range(2)]
    for gi, (Gbc, base) in enumerate(((G1bc, 16), (G2bc, 40))):
        pb = [P.ps("pbc%d" % h, [128, 512], F32, bank=1 + h) for h in range(2)]
        for kc in range(8):
            dgt = dg[kc % 2]
            P.ts("dve", dgt.ap(), identf.ap(), modT.ap()[:, base + kc, 0:1], None, ALU.mult, None, [identf, modT], [dgt])
            P.mm(pb[kc // 4].ap()[:, (kc % 4) * 128:(kc % 4 + 1) * 128], onesf.ap(), dgt.ap(), True, True, [onesf, dgt], [pb[kc // 4]])
        for h in range(2):
            P.cp("dve", Gbc.ap()[:, h * 512:(h + 1) * 512], pb[h].ap(), [pb[h]], [Gbc])
    if debug:
        dbg["modT"] = nc.dram_tensor("d_modT", [128, 96], F32, kind="ExternalOutput").ap()
        P.dma("sp", dbg["modT"], modT.ap().rearrange("p a b -> p (a b)"), r=[modT], w=[P.T("OUTd0")])
        dbg["G1bc"] = nc.dram_tensor("d_G1bc", [128, D], F32, kind="ExternalOutput").ap()
        P.dma("sp", dbg["G1bc"], G1bc.ap(), r=[G1bc], w=[P.T("OUTd1")])
    P.barrier()
    P.reset()
    if stop_after == 0:
        P.finish()
        return nc

    P.phase = "p1a"
    hT = P.sb("hT", [128, 8, NT], BF16)
    hTt = [P.T("hT%d" % i) for i in range(NTL)]
    mark2 = P.off
    xt = [P.sb("xt%d" % i, [128, D], F32) for i in range(3)]
    junk = P.sb("junk", [128, D], BF16)
    st = [P.sb("st%d" % i, [128, 4], F32) for i in range(3)]
    xn = [P.sb("xn%d" % i, [128, D], BF16) for i in range(2)]
    tmpf = [P.sb("tmpf%d" % i, [128, 8, 128], F32) for i in range(2)]
    pT = [P.ps("pT%d" % i, [128, 8, 128], BF16, bank=i) for i in range(2)]

    def rstd_ops(stt_, n):
        P.act(stt_.ap()[:, 1:2], stt_.ap()[:, 0:1], AF.Sqrt, [stt_], [stt_], bias=EPS, scale=1.0 / n)
        P.recip(stt_.ap()[:, 2:3], stt_.ap()[:, 1:2], [stt_], [stt_])

    for i in range(NTL):
        xs = xt[i % 3]; ss = st[i % 3]; xb = xn[i % 2]; pt = pT[i % 2]; tf = tmpf[i % 2]
        src = ctx[i * 128:(i + 1) * 128, :] if i < NCT else x[(i - NCT) * 128:(i - NCT + 1) * 128, :]
        P.dma("sp", xs.ap(), src, w=[xs])
        P.act(junk.ap(), xs.ap(), AF.Square, [xs], [junk, ss], accum_out=ss.ap()[:, 0:1])
        rstd_ops(ss, D)
        P.ts("dve", xb.ap(), xs.ap(), ss.ap()[:, 2:3], None, ALU.mult, None, [xs, ss], [xb])
        for kc in range(8):
            P.tp(pt.ap()[:, kc, :], xb.ap()[:, kc * 128:(kc + 1) * 128], identb.ap(), [xb, identb], [pt])
        m = 1 if i < NCT else 0
        P.tt("dve", tf.ap(), pt.ap(), A1.ap()[:, :, m:m + 1].to_broadcast([128, 8, 128]), ALU.mult, [pt, A1], [tf])
        P.tt("pool", hT.ap()[:, :, i * 128:(i + 1) * 128], tf.ap(), modT.ap()[:, 0:8, m:m + 1].to_broadcast([128, 8, 128]), ALU.add,
             [tf, modT], [hTt[i]])
    if debug:
        dbg["hT"] = nc.dram_tensor("d_hT", [128, 8 * NT], BF16, kind="ExternalOutput").ap()
        P.dma("sp", dbg["hT"], hT.ap().rearrange("p a b -> p (a b)"), r=hTt, w=[P.T("OUTd2")])
    if stop_after == 1:
        P.finish()
        return nc
    P.barrier()
    P.off = mark2

    P.phase = "p1b"
    wb = [P.sb("wb%d" % i, [128, 8, 512], BF16) for i in range(2)]
    wcnt = [0]

    wgroups = [(g * 512, 512) for g in range(4)] + [(5648 + g * 512, 512) for g in range(4)] + \
              [(2048, 512), (2560, 512), (3072, 512), (3584, 512), (4096, 16), (4112, 512), (4624, 512), (5136, 512)]
    wtiles = {}

    def issue_w(gi):
        if gi >= len(wgroups) or gi in wtiles:
            return
        c0, ncols = wgroups[gi]
        t = wb[gi % 2]
        P.dma("pool", t.ap()[:, :, 0:ncols], w_in[:, c0:c0 + ncols].rearrange("(k p) f -> p k f", p=128), w=[t])
        wtiles[gi] = t

    def load_w(c0, ncols, prefetch=True):
        gi = wcnt[0]
        wcnt[0] += 1
        assert wgroups[gi] == (c0, ncols), (gi, c0, ncols)
        issue_w(gi)
        t = wtiles[gi]
        if prefetch:
            issue_w(gi + 1)
        return t

    pz = [P.ps("pz%d" % i, [128, 512], F32, bank=2 + i) for i in range(4)]
    pzc = [0]

    def next_pz():
        t = pz[pzc[0] % 4]
        pzc[0] += 1
        return t

    cw = P.sb("cw", [128, 16, 5], F32)
    cb = P.sb("cb", [128, 16], F32)
    gbb = P.sb("gbb", [128, 16], F32)
    qgb = P.sb("qgb", [128, 128], F32)
    kgb = P.sb("kgb", [128, 128], F32)
    for j in range(5):
        P.dma("sp", cw.ap()[:, :, j], conv_w[j].rearrange("(c p) -> p c", p=128), w=[cw], slow=True)
    P.dma("sp", cb.ap(), conv_b.rearrange("(c p) -> p c", p=128), w=[cb], slow=True)
    P.dma("sp", gbb.ap(), gate_b.partition_broadcast(128), w=[gbb])
    P.dma("sp", qgb.ap(), q_norm_g.partition_broadcast(128), w=[qgb])
    P.dma("sp", kgb.ap(), k_norm_g.partition_broadcast(128), w=[kgb])

    def hts(g0, g1):
        return hTt[g0 // 128:(g1 - 1) // 128 + 1]

    stg8 = [P.sb("stg8_%d" % i, [128, 512], BF16) for i in range(8)]
    scnt = [0]

    def next_stage():
        t = stg8[scnt[0] % 8]
        scnt[0] += 1
        return t

    sqc = [0]

    def stq(from_act):
        sqc[0] += 1
        m = sqc[0] % 3
        if m == 0:
            return "sp"
        if m == 1:
            return "act" if from_act else "sp"
        return "pool"

    mark3 = P.off
    Zs = [P.sb("Zs%d" % i, [128, 512], F32) for i in range(4)]
    acc = [P.sb("acc%d" % i, [128, 508], F32) for i in range(4)]
    fcnt = [0]
    items = []
    for grp in range(4):
        for pair in range(2):
            chs = [grp * 4 + pair * 2, grp * 4 + pair * 2 + 1]
            is_k = chs[0] >= 8
            seqs = [(CT, S)] + ([(0, CT)] if is_k else [])
            for (s0, ln) in seqs:
                for (bs, n) in conv_blocks(ln):
                    items.append((grp, chs, is_k, s0, ln, bs, n))
    fw = {}

    def f_stage1(it):
        grp, chs, is_k, s0, ln, bs, n = it
        if grp not in fw:
            fw[grp] = load_w(grp * 512, 512)
        wt = fw[grp]
        w0 = max(0, bs - 2); w1 = min(ln, bs + n + 2)
        ncol = w1 - w0
        off = 2 - (bs - w0)
        hr = hts(s0 + w0, s0 + w1)
        st_ = []
        for ch in chs:
            cl = ch % 4
            i = fcnt[0]; fcnt[0] += 1
            z = Zs[i % 4]; a = acc[i % 4]
            p = next_pz()
            st_.append((ch, z, a))
            for kc in range(8):
                P.mm(p.ap()[:, 0:ncol], wt.ap()[:, kc, cl * 128:(cl + 1) * 128], hT.ap()[:, kc, s0 + w0:s0 + w1],
                     kc == 0, kc == 7, [wt] + hr, [p])
            if bs == 0:
                P.memset("dve", z.ap()[:, 0:2], 0.0, [z])
            if bs + n == ln:
                P.memset("dve", z.ap()[:, 2 + n:4 + n], 0.0, [z])
            P.cp("act", z.ap()[:, off:off + ncol], p.ap()[:, 0:ncol], [p], [z])
        return (it, st_)

    def f_stage2(rec):
        (grp, chs, is_k, s0, ln, bs, n), st_ = rec
        dst, kdst = (mkT, k_mkT) if is_k else (mqT, k_mqT)
        for j in range(5):
            for (ch, z, a) in st_:
                if j == 0:
                    P.ts("dve", a.ap()[:, 0:n], z.ap()[:, 0:n], cw.ap()[:, ch, 0:1], None, ALU.mult, None, [z, cw], [a])
                else:
                    P.stt("dve", a.ap()[:, 0:n], z.ap()[:, j:j + n], cw.ap()[:, ch, j:j + 1], a.ap()[:, 0:n], ALU.mult, ALU.add, [z, cw, a], [a])
        for (ch, z, a) in st_:
            o = next_stage()
            P.act(o.ap()[:, 0:n], a.ap()[:, 0:n], AF.Silu, [a, cb], [o], bias=cb.ap()[:, ch:ch + 1])
            r0 = (ch % 8) * 128
            P.dma(stq(True), dst[r0:r0 + 128, s0 + bs:s0 + bs + n], o.ap()[:, 0:n], r=[o], w=[kdst], key=o)

    prev = None
    for it in items:
        cur = f_stage1(it)
        if prev is not None:
            f_stage2(prev)
        prev = cur
    f_stage2(prev)
    P.barrier()
    P.off = mark3

    if stop_after == "F":
        P.finish()
        return nc
    P.phase = "p1b_G"
    gcnt = 0
    for grp in range(4):
        wt = load_w(5648 + grp * 512, 512)
        for cl in range(4):
            ch = grp * 4 + cl
            for b0 in range(0, S, 512):
                p = next_pz()
                o = next_stage()
                hr = hts(CT + b0, CT + b0 + 512)
                for kc in range(8):
                    P.mm(p.ap(), wt.ap()[:, kc, cl * 128:(cl + 1) * 128], hT.ap()[:, kc, CT + b0:CT + b0 + 512], kc == 0, kc == 7, [wt] + hr, [p])
                P.act(o.ap(), p.ap(), AF.Sigmoid, [p], [o])
                P.dma(stq(True), gT[ch * 128:(ch + 1) * 128, b0:b0 + 512], o.ap(), r=[o], w=[k_gT], key=o)

    if stop_after == "G":
        P.finish()
        return nc
    P.phase = "p1b_mv"
    tcnt = [0]

    def tok_mm(wt, ncols, i):
        p = next_pz()
        for kc in range(8):
            P.mm(p.ap()[:, 0:ncols], hT.ap()[:, kc, i * 128:(i + 1) * 128], wt.ap()[:, kc, 0:ncols], kc == 0, kc == 7, [wt, hTt[i]], [p])
        return p

    for grp in range(2):
        wt = load_w(2048 + grp * 512, 512)
        for i in range(NTL):
            p = tok_mm(wt, 512, i)
            o = next_stage()
            P.cp("act" if i % 2 else "dve", o.ap(), p.ap(), [p], [o])
            P.dma(stq(i % 2 == 1), mv[i * 128:(i + 1) * 128, grp * 512:(grp + 1) * 512], o.ap(), r=[o], w=[k_mv], key=o)
    if stop_after == "mv":
        P.finish()
        return nc
    P.phase = "p1b_o"
    for grp in range(2):
        wt = load_w(3072 + grp * 512, 512)
        for i in range(NCT, NTL):
            p = tok_mm(wt, 512, i)
            o = next_stage()
            P.act(o.ap(), p.ap(), AF.Sigmoid, [p], [o])
            P.dma(stq(True), osig[(i - NCT) * 128:(i - NCT + 1) * 128, grp * 512:(grp + 1) * 512], o.ap(), r=[o], w=[k_osig], key=o)
    if stop_after == "o":
        P.finish()
        return nc
    P.phase = "p1b_gt"
    wt = load_w(4096, 16)
    Gdt = [P.T("Gdt0"), P.T("Gdt1")]
    for i in range(NTL):
        p = tok_mm(wt, 16, i)
        for d in range(2):
            P.tt("dve", Gd[d].ap()[:, step_of[d][i], :], p.ap()[:, d * 8:(d + 1) * 8], gbb.ap()[:, d * 8:(d + 1) * 8], ALU.add, [p, gbb], [Gdt[d]])

    if stop_after == "gates":
        P.finish()
        return nc
    P.phase = "p1b_q"
    ropet = [P.sb("ropet%d" % i, [128, 128], F32) for i in range(2)]
    sqb = [P.sb("sq%d" % i, [128, 512], BF16) for i in range(2)]
    qst = [P.sb("qst%d" % i, [128, 8], F32) for i in range(2)]
    qn = [P.sb("qn%d" % i, [128, 512], F32) for i in range(2)]
    t1 = [P.sb("rt%d" % i, [128, 256], F32) for i in range(4)]
    qr = [P.sb("qr%d" % i, [128, 512], BF16) for i in range(2)]
    qstage = [P.sb("qstage%d" % i, [128, 4, 512], BF16) for i in range(2)]
    acnt = [0]

    def norm_rope(p, nh, gb, rt, do_rope):
        i = acnt[0]; acnt[0] += 1
        s_ = qst[i % 2]; q_ = qn[i % 2]; o_ = qr[i % 2]; sq = sqb[i % 2]
        W = nh * 128
        P.act(sq.ap()[:, 0:W], p.ap()[:, 0:W], AF.Square, [p], [sq])
        P.op("dve", lambda e: e.reduce_sum(s_.ap()[:, 0:nh], sq.ap()[:, 0:W].rearrange("p (h d) -> p h d", h=nh), AX.X), [sq], [s_])
        P.act(s_.ap()[:, 4:4 + nh], s_.ap()[:, 0:nh], AF.Sqrt, [s_], [s_], bias=EPS, scale=1.0 / 128)
        P.recip(s_.ap()[:, 0:nh], s_.ap()[:, 4:4 + nh], [s_], [s_])
        P.tt("dve", q_.ap()[:, 0:W].rearrange("p (h d) -> p h d", h=nh), p.ap()[:, 0:W].rearrange("p (h d) -> p h d", h=nh),
             s_.ap()[:, 0:nh].unsqueeze(2).to_broadcast([128, nh, 128]), ALU.mult, [p, s_], [q_])
        if not do_rope:
            P.tt("pool", o_.ap()[:, 0:W].rearrange("p (h d) -> p h d", h=nh), q_.ap()[:, 0:W].rearrange("p (h d) -> p h d", h=nh),
                 gb.ap().unsqueeze(1).to_broadcast([128, nh, 128]), ALU.mult, [q_, gb], [o_])
            return o_
        P.tt("pool", q_.ap()[:, 0:W].rearrange("p (h d) -> p h d", h=nh), q_.ap()[:, 0:W].rearrange("p (h d) -> p h d", h=nh),
             gb.ap().unsqueeze(1).to_broadcast([128, nh, 128]), ALU.mult, [q_, gb], [q_])
        qv = q_.ap()[:, 0:W].rearrange("p (h i two) -> p h i two", h=nh, two=2)
        ov = o_.ap()[:, 0:W].rearrange("p (h i two) -> p h i two", h=nh, two=2)
        x1 = qv[:, :, :, 0]; x2 = qv[:, :, :, 1]
        cosb = rt.ap()[:, 0:64].unsqueeze(1).to_broadcast([128, nh, 64])
        sinb = rt.ap()[:, 64:128].unsqueeze(1).to_broadcast([128, nh, 64])
        tv = [t.ap()[:, 0:nh * 64].rearrange("p (h i) -> p h i", h=nh) for t in t1]
        P.tt("dve", tv[0], x1, cosb, ALU.mult, [q_, rt], [t1[0]])
        P.tt("dve", tv[1], x2, sinb, ALU.mult, [q_, rt], [t1[1]])
        P.tt("dve", ov[:, :, :, 0], tv[0], tv[1], ALU.subtract, [t1[0], t1[1]], [o_])
        P.tt("pool", tv[2], x1, sinb, ALU.mult, [q_, rt], [t1[2]])
        P.tt("pool", tv[3], x2, cosb, ALU.mult, [q_, rt], [t1[3]])
        P.tt("pool", ov[:, :, :, 1], tv[2], tv[3], ALU.add, [t1[2], t1[3]], [o_])
        return o_

    def load_rope(i):
        rt = ropet[i % 2]
        P.dma("sp", rt.ap(), rope[(i - NCT) * 128:(i - NCT + 1) * 128, :], w=[rt])
        return rt

    pT2 = [P.ps("pT2_%d" % i, [128, 4, 128], BF16, bank=i) for i in range(2)]
    wq = [load_w(4112, 512, prefetch=False), load_w(4624, 512, prefetch=False)]
    ptc = 0
    for b0 in range(0, NL, 4):
        for j in range(4):
            i = NCT + b0 + j
            rt = load_rope(i)
            for grp in range(2):
                stg = qstage[grp]
                p = tok_mm(wq[grp], 512, i)
                o_ = norm_rope(p, 4, qgb, rt, True)
                ptt = pT2[ptc % 2]; ptc += 1
                for h in range(4):
                    P.tp(ptt.ap()[:, h, :], o_.ap()[:, h * 128:(h + 1) * 128], identb.ap(), [o_, identb], [ptt])
                P.cp("act", stg.ap()[:, :, j * 128:(j + 1) * 128], ptt.ap(), [ptt], [stg])
        for grp in range(2):
            stg = qstage[grp]
            for h in range(4):
                P.dma(stq(True), qaT[(grp * 4 + h) * 128:(grp * 4 + h + 1) * 128, b0 * 128:(b0 + 4) * 128], stg.ap()[:, h, :], r=[stg], w=[k_qaT], key=stg)
    if stop_after == "q":
        P.finish()
        return nc
    P.phase = "p1b_kv"
    wt = load_w(5136, 512)
    kstage = [P.sb("kstage%d" % i, [128, 2, 128], BF16) for i in range(2)]
    for i in range(NTL):
        lat = i >= NCT
        rt = load_rope(i) if lat else None
        p = tok_mm(wt, 512, i)
        o_ = norm_rope(p, 2, kgb, rt, lat)
        ptt = pT2[i % 2]
        for h in range(2):
            P.tp(ptt.ap()[:, h, :], o_.ap()[:, h * 128:(h + 1) * 128], identb.ap(), [o_, identb], [ptt])
        ks = kstage[i % 2]
        P.cp("act", ks.ap(), ptt.ap()[:, 0:2, :], [ptt], [ks])
        for h in range(2):
            P.dma(stq(True), kaT[h * 128:(h + 1) * 128, i * 128:(i + 1) * 128], ks.ap()[:, h, :], r=[ks], w=[k_kaT], key=ks)
        o = next_stage()
        P.cp("dve", o.ap()[:, 0:256], p.ap()[:, 256:512], [p], [o])
        P.dma(stq(False), va[i * 128:(i + 1) * 128, :], o.ap()[:, 0:256], r=[o], w=[k_va], key=o)
    if debug:
        for d in range(2):
            dbg["Gd%d" % d] = nc.dram_tensor("d_Gd%d" % d, [128, NTL * 8], F32, kind="ExternalOutput").ap()
            P.dma("sp", dbg["Gd%d" % d], Gd[d].ap().rearrange("p a b -> p (a b)"), r=[Gdt[d]], w=[P.T("OUTg%d" % d)])
    P.barrier()
    P.reset()
    if stop_after == 2:
        P.finish()
        return nc

    P.phase = "p2a"
    NS = NTL
    W4 = NS * 4
    Aa = [P.sb("Aa%d" % d, [128, W4], F32) for d in range(2)]
    Ee = [P.sb("Ee%d" % d, [128, W4], F32) for d in range(2)]
    C0 = [P.sb("C0%d" % d, [128, W4], F32) for d in range(2)]
    mng = P.sb("mng", [128, D], F32)
    P.dma("sp", mng.ap(), m_norm_g.partition_broadcast(128), w=[mng])
    pieces = [(c0_, min(c0_ + 128, W4)) for c0_ in range(0, W4, 128)]
    pTr = P.ps("pTr", [128, 128], F32, bank=4)
    for d in range(2):
        LF = P.sb("LF%d" % d, [128, W4], F32)
        gtmp = P.sb("gtmp%d" % d, [128, W4], F32)
        Bv = P.sb("Bv%d" % d, [128, W4], F32)
        Mb = P.sb("Mb%d" % d, [128, W4], F32)
        mrow = P.sb("mrow%d" % d, [128, W4], F32)
        mprev = P.sb("mprev%d" % d, [128, W4], F32)
        Rr = P.sb("Rr%d" % d, [128, W4], F32)
        FLb = P.sb("FLb%d" % d, [128, W4], F32)
        mcol = P.sb("mcol%d" % d, [128, 4], F32)
        dgm = P.sb("dgm%d" % d, [128, 128], F32)
        v3 = lambda t: t.ap().rearrange("p (s h) -> p s h", h=4)
        pF = P.ps("pF%d" % d, [128, W4], F32, bank=0 + d)
        pFL = P.ps("pFL%d" % d, [128, W4], F32, bank=2 + d)
        pM = P.ps("pM%d" % d, [128, W4], F32, bank=5 + d)
        P.act(v3(gtmp), Gd[d].ap()[:, :, 4:8], AF.Exp, [Gdt[d]], [gtmp], scale=-1.0)
        P.act(gtmp.ap(), gtmp.ap(), AF.Ln, [gtmp], [gtmp], bias=1.0)
        P.ts("dve", LF.ap(), gtmp.ap(), -1.0, None, ALU.mult, None, [gtmp], [LF])
        P.mm(pF.ap(), trif.ap()[:, d, :], LF.ap(), True, True, [trif, LF], [pF])
        P.mm(pFL.ap(), onesf.ap(), LF.ap(), True, True, [onesf, LF], [pFL])
        P.tt("dve", v3(Bv), Gd[d].ap()[:, :, 0:4], pF.ap().rearrange("p (s h) -> p s h", h=4), ALU.subtract, [Gdt[d], pF], [Bv])
        P.cp("dve", FLb.ap(), pFL.ap(), [pFL], [FLb])
        for pi, (a0, a1) in enumerate(pieces):
            w_ = a1 - a0
            P.tp(pTr.ap()[0:w_, :], Bv.ap()[:, a0:a1], identf.ap(), [Bv, identf], [pTr])
            P.memset("dve", mcol.ap()[:, pi:pi + 1], 0.0, [mcol])
            P.op("dve", lambda e, o_=mcol.ap()[0:w_, pi:pi + 1], i_=pTr.ap()[0:w_, :]: e.reduce_max(o_, i_, AX.X), [pTr, mcol], [mcol])
            P.ts("dve", dgm.ap(), identf.ap(), mcol.ap()[:, pi:pi + 1], None, ALU.mult, None, [identf, mcol], [dgm])
            P.mm(pM.ap()[:, a0:a1], onesf.ap(), dgm.ap()[:, 0:w_], True, True, [onesf, dgm], [pM])
        P.cp("dve", Mb.ap(), pM.ap(), [pM], [Mb])
        for h in range(4):
            P.op("dve", lambda e, o_=v3(mrow)[:, :, h], a_=v3(Mb)[:, :, h], b_=v3(FLb)[:, :, h]: e.tensor_tensor_scan(o_, a_, b_, NEG, ALU.max, ALU.add),
                 [Mb, FLb], [mrow])
        P.memset("dve", mprev.ap()[:, 0:4], NEG, [mprev])
        P.cp("dve", mprev.ap()[:, 4:W4], mrow.ap()[:, 0:W4 - 4], [mrow], [mprev])
        P.tt("dve", Rr.ap(), mprev.ap(), Mb.ap(), ALU.max, [mprev, Mb], [Rr])
        P.tt("dve", gtmp.ap(), mprev.ap(), Rr.ap(), ALU.subtract, [mprev, Rr], [gtmp])
        P.ts("dve", gtmp.ap(), gtmp.ap(), -200.0, None, ALU.max, None, [gtmp], [gtmp])
        P.act(C0[d].ap(), gtmp.ap(), AF.Exp, [gtmp], [C0[d]])
        if debug:
            for nm, tl in (("Mb", Mb), ("FLb", FLb), ("mrow", mrow), ("Rr", Rr), ("Bv", Bv)):
                dbg[nm + str(d)] = nc.dram_tensor("d_%s%d" % (nm, d), [128, W4], F32, kind="ExternalOutput").ap()
                P.dma("sp", dbg[nm + str(d)], tl.ap(), r=[tl], w=[P.T("OUT%s%d" % (nm, d))])
        P.tt("dve", Bv.ap(), Bv.ap(), Rr.ap(), ALU.subtract, [Bv, Rr], [Bv])
        P.act(Aa[d].ap(), Bv.ap(), AF.Exp, [Bv], [Aa[d]], bias=-LN16)
        P.tt("dve", LF.ap(), pF.ap(), Rr.ap(), ALU.add, [pF, Rr], [LF])
        P.act(Ee[d].ap(), LF.ap(), AF.Exp, [LF], [Ee[d]], scale=-1.0)
    if debug:
        for nm, tl in (("Aa", Aa), ("Ee", Ee), ("C0", C0)):
            for d in range(2):
                dbg[nm + str(d)] = nc.dram_tensor("d_%s%d" % (nm, d), [128, W4], F32, kind="ExternalOutput").ap()
                P.dma("sp", dbg[nm + str(d)], tl[d].ap(), r=[tl[d]], w=[P.T("OUT%s%d" % (nm, d))])
    P.barrier()
    if stop_after == "2a":
        P.finish()
        return nc

    P.phase = "p2b"
    kTin = [[P.sb("kTin%d%d" % (d, i), [128, 8, 128], BF16) for i in range(2)] for d in range(2)]
    qTin = [[P.sb("qTin%d%d" % (d, i), [128, 8, 128], BF16) for i in range(2)] for d in range(2)]
    vin = [[P.sb("vin%d%d" % (d, i), [128, D], BF16) for i in range(2)] for d in range(2)]
    kp = [[P.sb("kp%d%d" % (d, i), [128, D], BF16) for i in range(2)] for d in range(2)]
    Sm = [[P.sb("Sm%d%d" % (d, i), [128, 4, 128], BF16) for i in range(2)] for d in range(2)]
    qs = [[P.sb("qs%d%d" % (d, i), [128, 8, 128], BF16) for i in range(2)] for d in range(2)]
    Cst = [P.sb("Cst%d" % d, [128, 4, 512], F32) for d in range(2)]
    Cbf = [P.sb("Cbf%d" % d, [128, 4, 512], BF16) for d in range(2)]
    Cbt = [[P.T("Cbt%d%d" % (d, h)) for h in range(4)] for d in range(2)]
    Cft = [[P.T("Cft%d%d" % (d, h)) for h in range(4)] for d in range(2)]
    nst = [P.sb("nst%d" % d, [128, 8], F32) for d in range(2)]
    ntmp = [P.sb("ntmp%d" % d, [128, 8], F32) for d in range(2)]
    nbf = [P.sb("nbf%d" % d, [128, 8], BF16) for d in range(2)]
    hbuf = [[P.sb("hbuf%d%d" % (d, i), [128, D], F32) for i in range(2)] for d in range(2)]
    hoth = [P.sb("hoth%d" % i, [128, D], F32) for i in range(2)]
    osg = [P.sb("osg%d" % i, [128, D], BF16) for i in range(2)]
    ymt = [P.sb("ymt%d" % i, [128, D], BF16) for i in range(2)]
    dsb = [P.sb("dsb%d" % d, [128, 16], F32) for d in range(2)]
    fst = [P.sb("fst%d" % i, [128, 16], F32) for i in range(2)]
    fjunk = P.sb("fjunk", [128, 256], BF16)
    half = NL // 2
    GRP = 4 if half % 4 == 0 else (2 if half % 2 == 0 else 1)
    ymstage = [[P.sb("ymst%d%d" % (d, i), [128, 8, GRP * 128], BF16) for i in range(2)] for d in range(2)]
    pK = P.ps("pK", [128, 8, 128], BF16, bank=0)
    pdc = [P.ps("pdc%d" % i, [128, 512], F32, bank=1 + i) for i in range(2)]
    pb3 = P.T("pbank3")
    pdn = [TV(pb3, P.ps("pdn%d" % d, [128, 8], F32, bank=3, off=d * 32).h) for d in range(2)]
    pden = [TV(pb3, P.ps("pden%d" % d, [128, 4], F32, bank=3, off=64 + d * 32).h) for d in range(2)]
    pS = P.ps("pS", [128, 4, 128], F32, bank=4)
    pnum = [P.ps("pnum%d" % i, [128, 2, 256], F32, bank=5 + i) for i in range(2)]
    pT3 = P.ps("pT3", [128, 8, 128], BF16, bank=7)
    for d in range(2):
        P.memset("dve", Cst[d].ap(), 0.0, Cft[d])
        P.memset("pool", Cbf[d].ap(), 0.0, Cbt[d])
        P.memset("dve", nst[d].ap(), 0.0, [nst[d]])
        P.memset("dve", nbf[d].ap(), 0.0, [nbf[d]])
    fin_cnt = [0, 0]
    fcount = [0]
    for k in range(NS):
        for d in range(2):
            tile = tile_of(d, k)
            g0 = tile * 128
            lat = tile >= NCT
            sl = k % 2
            kt = kTin[d][sl]; vt = vin[d][sl]; qt = qTin[d][sl]; kpt = kp[d][sl]; smt = Sm[d][sl]; qst_ = qs[d][sl]
            P.dma("sp", kt.ap(), mkT[:, g0:g0 + 128].rearrange("(j p) t -> p j t", p=128), r=[k_mkT], w=[kt])
            P.dma("sp", vt.ap(), mv[g0:g0 + 128, :], r=[k_mv], w=[vt])
            if lat:
                P.dma("sp", qt.ap(), mqT[:, g0:g0 + 128].rearrange("(j p) t -> p j t", p=128), r=[k_mqT], w=[qt])
            for j in range(8):
                P.tp(pK.ap()[:, j, :], kt.ap()[:, j, :], identb.ap(), [kt, identb], [pK])
            if lat:
                for h in range(4):
                    for dc in range(2):
                        P.mm(pS.ap()[:, h, :], kt.ap()[:, 2 * h + dc, :], qt.ap()[:, 2 * h + dc, :], dc == 0, dc == 1, [kt, qt], [pS])
            for h in range(4):
                col = k * 4 + h
                dstv = kpt.ap()[:, h * 256:(h + 1) * 256].rearrange("p (a b) -> p a b", a=2)
                if h % 2:
                    P.act(dstv, pK.ap()[:, 2 * h:2 * h + 2, :], AF.Copy, [pK, Aa[d]], [kpt], scale=Aa[d].ap()[:, col:col + 1])
                else:
                    P.ts("dve", dstv, pK.ap()[:, 2 * h:2 * h + 2, :], Aa[d].ap()[:, col:col + 1], None, ALU.mult, None, [pK, Aa[d]], [kpt])
            if lat:
                for h in range(4):
                    col = k * 4 + h
                    P.stt("dve", smt.ap()[:, h, :], pS.ap()[:, h, :], Aa[d].ap()[:, col:col + 1], trib.ap()[:, d, :], ALU.mult, ALU.mult,
                          [pS, Aa[d], trib], [smt])
                    P.act(qst_.ap()[:, 2 * h:2 * h + 2, :], qt.ap()[:, 2 * h:2 * h + 2, :], AF.Copy, [qt, C0[d]], [qst_], scale=C0[d].ap()[:, col:col + 1])
                for h in range(4):
                    pn = pnum[h // 2]
                    P.mm(pn.ap()[:, h % 2, :], smt.ap()[:, h, :], vt.ap()[:, h * 256:(h + 1) * 256], True, False, [smt, vt], [pn])
                    P.mm(pn.ap()[:, h % 2, :], qst_.ap()[:, 2 * h, :], Cbf[d].ap()[:, h, 0:256], False, False, [qst_, Cbt[d][h]], [pn])
                    P.mm(pn.ap()[:, h % 2, :], qst_.ap()[:, 2 * h + 1, :], Cbf[d].ap()[:, h, 256:512], False, True, [qst_, Cbt[d][h]], [pn])
                for h in range(4):
                    P.mm(pden[d].ap()[:, h:h + 1], smt.ap()[:, h, :], onesb.ap()[:, 0:1], True, False, [smt, onesb], [pden[d]])
                    P.mm(pden[d].ap()[:, h:h + 1], qst_.ap()[:, 2 * h, :], nbf[d].ap()[:, 2 * h:2 * h + 1], False, False, [qst_, nbf[d]], [pden[d]])
                    P.mm(pden[d].ap()[:, h:h + 1], qst_.ap()[:, 2 * h + 1, :], nbf[d].ap()[:, 2 * h + 1:2 * h + 2], False, True, [qst_, nbf[d]], [pden[d]])
            for h in range(4):
                col = k * 4 + h
                pd = pdc[h % 2]
                for dc in range(2):
                    P.mm(pd.ap()[:, dc * 256:(dc + 1) * 256], kpt.ap()[:, h * 256 + dc * 128:h * 256 + (dc + 1) * 128], vt.ap()[:, h * 256:(h + 1) * 256],
                         True, True, [kpt, vt], [pd])
                P.stt("dve", Cst[d].ap()[:, h, :], Cst[d].ap()[:, h, :], C0[d].ap()[:, col:col + 1], pd.ap(), ALU.mult, ALU.add,
                      [Cft[d][h], C0[d], pd], [Cft[d][h]])
                P.cp("act", Cbf[d].ap()[:, h, :], Cst[d].ap()[:, h, :], [Cft[d][h]], [Cbt[d][h]])
            for j in range(8):
                P.mm(pdn[d].ap()[:, j:j + 1], kpt.ap()[:, j * 128:(j + 1) * 128], onesb.ap()[:, 0:1], True, True, [kpt, onesb], [pdn[d]])
            P.tt("dve", ntmp[d].ap().rearrange("p (h two) -> p h two", two=2), nst[d].ap().rearrange("p (h two) -> p h two", two=2),
                 C0[d].ap()[:, k * 4:(k + 1) * 4].unsqueeze(2).to_broadcast([128, 4, 2]), ALU.mult, [nst[d], C0[d]], [ntmp[d]])
            P.tt("dve", nst[d].ap(), ntmp[d].ap(), pdn[d].ap(), ALU.add, [ntmp[d], pdn[d]], [nst[d]])
            P.cp("dve", nbf[d].ap(), nst[d].ap(), [nst[d]], [nbf[d]])
            if not lat:
                continue
            ds_ = dsb[d]
            hb = hbuf[d][sl]
            P.cp("dve", ds_.ap()[:, 0:4], pden[d].ap(), [pden[d]], [ds_])
            P.stt("dve", ds_.ap()[:, 4:8], ds_.ap()[:, 0:4], -1.0, ds_.ap()[:, 0:4], ALU.mult, ALU.max, [ds_], [ds_])
            P.tt("dve", ds_.ap()[:, 8:12], ds_.ap()[:, 4:8], Ee[d].ap()[:, k * 4:(k + 1) * 4], ALU.max, [ds_, Ee[d]], [ds_])
            P.recip(ds_.ap()[:, 12:16], ds_.ap()[:, 8:12], [ds_], [ds_])
            for h in range(4):
                pn = pnum[h // 2]
                if h % 2:
                    P.act(hb.ap()[:, h * 256:(h + 1) * 256], pn.ap()[:, h % 2, :], AF.Copy, [pn, ds_], [hb], scale=ds_.ap()[:, 12 + h:13 + h])
                else:
                    P.ts("dve", hb.ap()[:, h * 256:(h + 1) * 256], pn.ap()[:, h % 2, :], ds_.ap()[:, 12 + h:13 + h], None, ALU.mult, None, [pn, ds_], [hb])
            li_ = tile - NCT
            ko = step_of[1 - d][tile]
            if ko > k:
                P.dma("pool", hdir[d, li_ * 128:(li_ + 1) * 128, :], hb.ap(), r=[hb], w=[k_hdir])
                continue
            fi = fcount[0]; fcount[0] += 1
            ho = hoth[fi % 2]; og = osg[fi % 2]; ym_ = ymt[fi % 2]; fs = fst[fi % 2]
            P.dma("sp", ho.ap(), hdir[1 - d, li_ * 128:(li_ + 1) * 128, :], r=[k_hdir], w=[ho])
            P.dma("sp", og.ap(), osig[li_ * 128:(li_ + 1) * 128, :], r=[k_osig], w=[og])
            P.tt("pool", ho.ap(), ho.ap(), hb.ap(), ALU.add, [ho, hb], [ho])
            for h in range(4):
                P.act(fjunk.ap(), ho.ap()[:, h * 256:(h + 1) * 256], AF.Square, [ho], [fjunk, fs], accum_out=fs.ap()[:, h:h + 1])
            P.act(fs.ap()[:, 4:8], fs.ap()[:, 0:4], AF.Sqrt, [fs], [fs], bias=EPS, scale=1.0 / 256)
            P.recip(fs.ap()[:, 8:12], fs.ap()[:, 4:8], [fs], [fs])
            P.tt("dve", ho.ap().rearrange("p (h e) -> p h e", h=4), ho.ap().rearrange("p (h e) -> p h e", h=4),
                 fs.ap()[:, 8:12].unsqueeze(2).to_broadcast([128, 4, 256]), ALU.mult, [ho, fs], [ho])
            P.tt("pool", ho.ap(), ho.ap(), mng.ap(), ALU.mult, [ho, mng], [ho])
            P.tt("dve", ym_.ap(), ho.ap(), og.ap(), ALU.mult, [ho, og], [ym_])
            for j in range(8):
                P.tp(pT3.ap()[:, j, :], ym_.ap()[:, j * 128:(j + 1) * 128], identb.ap(), [ym_, identb], [pT3])
            blk = li_ // GRP
            stg = ymstage[d][(fin_cnt[d] // GRP) % 2]
            P.cp("act", stg.ap()[:, :, (li_ % GRP) * 128:(li_ % GRP + 1) * 128], pT3.ap(), [pT3], [stg])
            fin_cnt[d] += 1
            if fin_cnt[d] % GRP == 0:
                for j in range(8):
                    P.dma("pool", ymT[j * 128:(j + 1) * 128, blk * GRP * 128:(blk + 1) * GRP * 128], stg.ap()[:, j, :], r=[stg], w=[k_ymT], key=stg)
    P.barrier()
    P.reset()
    if stop_after == 3:
        P.finish()
        return nc

    P.phase = "p3"
    KT = P.sb("KT", [128, 2, NT], BF16)
    Vr = P.sb("Vr", [128, NTL, 2, 132], BF16)
    P.memset("dve", Vr.ap()[:, :, :, 128:129], 1.0, [Vr])
    for h in range(2):
        P.dma("sp", KT.ap()[:, h, :], kaT[h * 128:(h + 1) * 128, :], r=[k_kaT], w=[KT])
        P.dma("sp", Vr.ap()[:, :, h, 0:128], va[:, h * 128:(h + 1) * 128].rearrange("(t p) e -> p t e", p=128), r=[k_va], w=[Vr])
    QB = 512
    qin = [P.sb("qin%d" % i, [128, 8, QB], BF16) for i in range(2)]
    Pt = [P.sb("Pt%d" % i, [128, QB], BF16) for i in range(3)]
    yat = [[P.sb("yat%d%d" % (i, j), [128, D], BF16) for j in range(4)] for i in range(2)]
    arec = [P.sb("arec%d" % i, [128, 4], F32) for i in range(2)]
    yastage = [P.sb("yast%d" % i, [128, 8, QB], BF16) for i in range(2)]
    pSa = [P.ps("pSa%d" % i, [128, QB], F32, bank=(0, 1, 7)[i]) for i in range(3)]
    pacc1 = [P.ps("pacc%d" % j, [128, 129], F32, bank=2 + j) for j in range(4)]
    pacc = [pacc1, pacc1]
    pT4 = P.ps("pT4", [128, 8, 128], BF16, bank=6)
    sc_att = 128.0 ** -0.5
    its = [(qb, h, kt_) for qb in range(S // QB) for h in range(8) for kt_ in range(NTL)]
    NI = len(its)
    DEPTH = 2

    def issue_qk(i):
        qb, h, kt_ = its[i]
        qi = qin[qb % 2]
        if h == 0 and kt_ == 0:
            for hh in range(8):
                P.dma("sp", qi.ap()[:, hh, :], qaT[hh * 128:(hh + 1) * 128, qb * QB:(qb + 1) * QB], r=[k_qaT], w=[qi])
        ps_ = pSa[i % 3]; pt_ = Pt[i % 3]
        P.mm(ps_.ap(), KT.ap()[:, h // 4, kt_ * 128:(kt_ + 1) * 128], qi.ap()[:, h, :], True, True, [KT, qi], [ps_])
        P.act(pt_.ap(), ps_.ap(), AF.Exp, [ps_], [pt_], scale=sc_att)

    def issue_pv(i):
        qb, h, kt_ = its[i]
        kvh = h // 4
        pt_ = Pt[i % 3]
        acc_ = pacc[h % 2]
        yt = yat[qb % 2]
        for j in range(4):
            P.mm(acc_[j].ap(), pt_.ap()[:, j * 128:(j + 1) * 128], Vr.ap()[:, kt_, kvh, 0:129], kt_ == 0, kt_ == NTL - 1, [pt_, Vr], [acc_[j]])
        if kt_ != NTL - 1:
            return
        ar = arec[h % 2]
        for j in range(4):
            P.recip(ar.ap()[:, j:j + 1], acc_[j].ap()[:, 128:129], [acc_[j]], [ar])
            P.ts("dve", yt[j].ap()[:, h * 128:(h + 1) * 128], acc_[j].ap()[:, 0:128], ar.ap()[:, j:j + 1], None, ALU.mult, None, [acc_[j], ar], [yt[j]])
        if h != 7:
            return
        stg = yastage[qb % 2]
        for j in range(4):
            for hh in range(8):
                P.tp(pT4.ap()[:, hh, :], yt[j].ap()[:, hh * 128:(hh + 1) * 128], identb.ap(), [yt[j], identb], [pT4])
            P.cp("dve", stg.ap()[:, :, j * 128:(j + 1) * 128], pT4.ap(), [pT4], [stg])
        for hh in range(8):
            P.dma("pool", yaT[hh * 128:(hh + 1) * 128, qb * QB:(qb + 1) * QB], stg.ap()[:, hh, :], r=[stg], w=[k_yaT], key=stg)

    for i in range(-DEPTH, NI):
        if i + DEPTH < NI:
            issue_qk(i + DEPTH)
        if i >= 0:
            issue_pv(i)
    P.barrier()
    P.reset()
    if stop_after == 4:
        P.finish()
        return nc

    P.phase = "p4a"
    wpa = P.sb("wpa", [128, 8, D], BF16); wpb = P.sb("wpb", [128, 8, D], BF16); wo = P.sb("wo", [128, 8, D], BF16)
    for wt_, src in ((wpa, w_pa), (wpb, w_pb), (wo, w_o)):
        for hh in range(2):
            P.dma("pool", wt_.ap()[:, :, hh * 512:(hh + 1) * 512], src[:, hh * 512:(hh + 1) * 512].rearrange("(k p) f -> p k f", p=128), w=[wt_])
    ymin = [P.sb("ymin%d" % i, [128, 8, 512], BF16) for i in range(2)]
    yain = [P.sb("yain%d" % i, [128, 8, 512], BF16) for i in range(2)]
    gin = [P.sb("gin%d" % i, [128, 16, 512], BF16) for i in range(2)]
    uT = [P.sb("uT%d" % i, [128, 8, 512], BF16) for i in range(2)]
    u1 = [P.sb("u1_%d" % i, [128, 512], F32) for i in range(2)]
    u2 = [P.sb("u2_%d" % i, [128, 512], F32) for i in range(2)]
    xin = [P.sb("xin%d" % i, [128, D], F32) for i in range(3)]
    ytmp = [P.sb("ytmp%d" % i, [128, D], F32) for i in range(2)]
    x1t = [P.sb("x1t%d" % i, [128, D], F32) for i in range(2)]
    pA = [P.ps("pA%d" % i, [128, 512], F32, bank=i) for i in range(2)]
    pB = [P.ps("pB%d" % i, [128, 512], F32, bank=2 + i) for i in range(2)]
    pY = [P.ps("pY%d" % i, [128, 512], F32, bank=4 + i) for i in range(4)]
    xc = 0
    for b in range(S // 512):
        ym_ = ymin[b % 2]; ya_ = yain[b % 2]; g_ = gin[b % 2]; u_ = uT[b % 2]
        c0_ = b * 512
        P.dma("sp", ym_.ap(), ymT[:, c0_:c0_ + 512].rearrange("(j p) t -> p j t", p=128), r=[k_ymT], w=[ym_])
        P.dma("sp", ya_.ap(), yaT[:, c0_:c0_ + 512].rearrange("(j p) t -> p j t", p=128), r=[k_yaT], w=[ya_])
        P.dma("sp", g_.ap(), gT[:, c0_:c0_ + 512].rearrange("(j p) t -> p j t", p=128), r=[k_gT], w=[g_])
        for fc in range(8):
            pa = pA[fc % 2]; pb_ = pB[fc % 2]; a1 = u1[fc % 2]; a2 = u2[fc % 2]
            for kc in range(8):
                P.mm(pa.ap(), wpa.ap()[:, kc, fc * 128:(fc + 1) * 128], ym_.ap()[:, kc, :], kc == 0, kc == 7, [wpa, ym_], [pa])
            for kc in range(8):
                P.mm(pb_.ap(), wpb.ap()[:, kc, fc * 128:(fc + 1) * 128], ya_.ap()[:, kc, :], kc == 0, kc == 7, [wpb, ya_], [pb_])
            P.tt("dve", a1.ap(), pa.ap(), g_.ap()[:, fc, :], ALU.mult, [pa, g_], [a1])
            P.tt("dve", a2.ap(), pb_.ap(), g_.ap()[:, 8 + fc, :], ALU.mult, [pb_, g_], [a2])
            P.tt("pool", u_.ap()[:, fc, :], a1.ap(), a2.ap(), ALU.add, [a1, a2], [u_])
        for j in range(4):
            xi = xin[xc % 3]; yt_ = ytmp[xc % 2]; xo = x1t[xc % 2]; xc += 1
            r0 = c0_ + j * 128
            P.dma("sp", xi.ap(), x[r0:r0 + 128, :], w=[xi])
            for hh in range(2):
                py = pY[(j * 2 + hh) % 4]
                for kc in range(8):
                    P.mm(py.ap(), u_.ap()[:, kc, j * 128:(j + 1) * 128], wo.ap()[:, kc, hh * 512:(hh + 1) * 512], kc == 0, kc == 7, [u_, wo], [py])
                P.tt("dve", yt_.ap()[:, hh * 512:(hh + 1) * 512], py.ap(), G1bc.ap()[:, hh * 512:(hh + 1) * 512], ALU.mult, [py, G1bc], [yt_])
            P.tt("pool", xo.ap(), yt_.ap(), xi.ap(), ALU.add, [yt_, xi], [xo])
            P.dma("pool", x1d[r0:r0 + 128, :], xo.ap(), r=[xo], w=[k_x1], key=xo)
    P.barrier()
    P.reset()
    if stop_after == 5:
        P.finish()
        return nc

    P.phase = "p4b"
    wg = P.sb("wg", [128, 8, DFF], BF16); wu = P.sb("wu", [128, 8, DFF], BF16); wd = P.sb("wd", [128, 22, D], BF16)
    for wt_, src in ((wg, w_g), (wu, w_u)):
        for c0_ in range(0, DFF, 704):
            P.dma("pool", wt_.ap()[:, :, c0_:c0_ + 704], src[:, c0_:c0_ + 704].rearrange("(k p) f -> p k f", p=128), w=[wt_])
    for hh in range(2):
        P.dma("pool", wd.ap()[:, :, hh * 512:(hh + 1) * 512], w_d[:, hh * 512:(hh + 1) * 512].rearrange("(k p) f -> p k f", p=128), w=[wd])
    fgb = G1bc
    P.dma("sp", fgb.ap(), final_g.partition_broadcast(128), w=[fgb])
    TB = 256
    NJ = TB // 128
    x1in = [P.sb("x1in%d" % i, [128, D], F32) for i in range(2 * NJ)]
    st2 = [P.sb("st2_%d" % i, [128, 4], F32) for i in range(3)]
    junk2 = P.sb("junk2", [128, D], BF16)
    xn2 = [P.sb("xn2_%d" % i, [128, D], BF16) for i in range(1)] * 2
    tf2 = [P.sb("tf2_%d" % i, [128, 8, 128], F32) for i in range(1)] * 2
    h2T = [P.sb("h2T%d" % i, [128, 8, TB], BF16) for i in range(2)]
    aT = [P.sb("aT%d" % i, [128, 22, TB], BF16) for i in range(1)] * 2
    sg = [P.sb("sg%d" % i, [128, TB], F32) for i in range(2)]
    ftmp = [P.sb("ftmp%d" % i, [128, D], F32) for i in range(1)] * 2
    pT5 = [P.ps("pT5_%d" % i, [128, 8, 128], BF16, bank=i) for i in range(2)]
    pG = [P.ps("pG%d" % i, [128, TB], F32, bank=2 + i) for i in range(2)]
    pU = [P.ps("pU%d" % i, [128, TB], F32, bank=4 + i) for i in range(2)]
    pD = [P.ps("pD%d" % i, [128, 512], F32, bank=6 + i) for i in range(2)] * 2
    tcn = [0]
    NB4 = S // TB
    xsets = {}

    def prologue(b):
        h2 = h2T[b % 2]
        xs_ = []
        for j in range(NJ):
            r0 = b * TB + j * 128
            xi = x1in[(b % 2) * NJ + j]; ss = st2[tcn[0] % 3]; xb = xn2[tcn[0] % 2]; pt = pT5[tcn[0] % 2]; tf = tf2[tcn[0] % 2]; tcn[0] += 1
            xs_.append(xi)
            P.dma("sp", xi.ap(), x1d[r0:r0 + 128, :], r=[k_x1], w=[xi])
            P.act(junk2.ap(), xi.ap(), AF.Square, [xi], [junk2, ss], accum_out=ss.ap()[:, 0:1])
            rstd_ops(ss, D)
            P.ts("dve", xb.ap(), xi.ap(), ss.ap()[:, 2:3], None, ALU.mult, None, [xi, ss], [xb])
            for kc in range(8):
                P.tp(pt.ap()[:, kc, :], xb.ap()[:, kc * 128:(kc + 1) * 128], identb.ap(), [xb, identb], [pt])
            P.tt("dve", tf.ap(), pt.ap(), A2.ap().unsqueeze(2).to_broadcast([128, 8, 128]), ALU.mult, [pt, A2], [tf])
            P.tt("pool", h2.ap()[:, :, j * 128:(j + 1) * 128], tf.ap(), modT.ap()[:, 24:32, 0:1].to_broadcast([128, 8, 128]), ALU.add, [tf, modT], [h2])
        xsets[b] = xs_

    def gateup(b):
        h2 = h2T[b % 2]; a_ = aT[b % 2]
        for fc in range(22):
            pg = pG[fc % 2]; pu = pU[fc % 2]; s_ = sg[fc % 2]
            for kc in range(8):
                P.mm(pg.ap(), wg.ap()[:, kc, fc * 128:(fc + 1) * 128], h2.ap()[:, kc, :], kc == 0, kc == 7, [wg, h2], [pg])
            for kc in range(8):
                P.mm(pu.ap(), wu.ap()[:, kc, fc * 128:(fc + 1) * 128], h2.ap()[:, kc, :], kc == 0, kc == 7, [wu, h2], [pu])
            P.act(s_.ap(), pg.ap(), AF.Silu, [pg], [s_])
            P.tt("dve", a_.ap()[:, fc, :], pu.ap(), s_.ap(), ALU.mult, [pu, s_], [a_])

    def down_tail(b):
        a_ = aT[b % 2]
        xs_ = xsets.pop(b)
        for j in range(NJ):
            r0 = b * TB + j * 128
            xi = xs_[j]; ft = ftmp[j % 2]; x2 = xi; o_ = xi; ss = st2[tcn[0] % 3]; tcn[0] += 1
            for hh in range(2):
                pd_ = pD[(j * 2 + hh) % 4]
                for fc in range(22):
                    P.mm(pd_.ap(), a_.ap()[:, fc, j * 128:(j + 1) * 128], wd.ap()[:, fc, hh * 512:(hh + 1) * 512], fc == 0, fc == 21, [a_, wd], [pd_])
                P.tt("dve", ft.ap()[:, hh * 512:(hh + 1) * 512], pd_.ap(), G2bc.ap()[:, hh * 512:(hh + 1) * 512], ALU.mult, [pd_, G2bc], [ft])
            P.tt("pool", x2.ap(), ft.ap(), xi.ap(), ALU.add, [ft, xi], [x2])
            P.act(junk2.ap(), x2.ap(), AF.Square, [x2], [junk2, ss], accum_out=ss.ap()[:, 0:1])
            rstd_ops(ss, D)
            P.ts("dve", ft.ap(), x2.ap(), ss.ap()[:, 2:3], None, ALU.mult, None, [x2, ss], [ft])
            P.tt("pool", o_.ap(), ft.ap(), fgb.ap(), ALU.mult, [ft, fgb], [o_])
            P.dma("sp", out[r0:r0 + 128, :], o_.ap(), r=[o_], w=[k_out])

    prologue(0)
    for b in range(NB4):
        gateup(b)
        if b + 1 < NB4:
            prologue(b + 1)
        down_tail(b)
    P.finish()
    return nc


def make_consts(S):
    cst = np.zeros((128, 512), np.float32)
    cst[:, 0:128] = np.eye(128, dtype=np.float32)
    s = np.arange(128)[:, None]
    t = np.arange(128)[None, :]
    cst[:, 128:256] = (s <= t).astype(np.float32)
    cst[:, 256:384] = (s >= t).astype(np.float32)
    cst[0, 384:512] = 1.0
    rows = S // 64
    row = np.repeat(np.arange(rows, dtype=np.float32), 64)
    col = np.tile(np.arange(64, dtype=np.float32), rows)
    inv = (np.float32(10000.0) ** (-np.arange(32, dtype=np.float32) / np.float32(32))).astype(np.float32)
    ang = np.concatenate([row[:, None] * inv, col[:, None] * inv], axis=-1).astype(np.float32)
    rope = np.concatenate([np.cos(ang), np.sin(ang)], axis=-1).astype(np.float32)
    return cst, rope


def core_inputs(b, inp, S, cst, rope):
    f = lambda a: np.ascontiguousarray(a, dtype=np.float32)
    return {
        "x": f(inp["x"][b, :S]), "c": f(inp["c"][b]), "ctx": f(inp["ctx"][b]), "c_ctx": f(inp["c_ctx"]),
        "w_mod": f(inp["w_mod"][0]), "b_mod": f(inp["b_mod"][0]), "norm1_g": f(inp["norm1_g"][0]),
        "norm2_g": f(inp["norm2_g"][0]), "w_in": f(inp["w_in"][0]), "gate_b": f(inp["gate_b"][0]),
        "conv_w": f(inp["conv_w"][0]), "conv_b": f(inp["conv_b"][0]), "m_norm_g": f(inp["m_norm_g"][0]),
        "q_norm_g": f(inp["q_norm_g"][0]), "k_norm_g": f(inp["k_norm_g"][0]), "w_pa": f(inp["w_pa"][0]),
        "w_pb": f(inp["w_pb"][0]), "w_o": f(inp["w_o"][0]), "w_ffn_gate": f(inp["w_ffn_gate"][0]),
        "w_ffn_up": f(inp["w_ffn_up"][0]), "w_ffn_down": f(inp["w_ffn_down"][0]), "final_g": f(inp["final_g"]),
        "cst": cst, "rope": rope,
    }


_CACHE = {}


def kernel(**inputs):
    S = inputs["x"].shape[1]
    B = inputs["x"].shape[0]
    if S not in _CACHE:
        _CACHE[S] = build(S)
    nc = _CACHE[S]
    cst, rope = make_consts(S)
    in_maps = [core_inputs(b, inputs, S, cst, rope) for b in range(B)]
    res = run_bass_kernel_spmd(nc, in_maps, core_ids=list(range(B)))
    return np.stack([np.asarray(r["out"], dtype=np.float32) for r in res.results], axis=0)
```

```python
import numpy as np
from contextlib import ExitStack
import concourse.bass as bass
import concourse.mybir as mybir
from concourse.bass_utils import run_bass_kernel_spmd

F32 = mybir.dt.float32
BF16 = mybir.dt.bfloat16
AF = mybir.ActivationFunctionType
ALU = mybir.AluOpType
AX = mybir.AxisListType

SEM_EPOCH = 12000
DMA_EPOCH = 1500


class T:
    def __init__(self, name, h=None):
        self.name = name
        self.h = h
        self.last_w = None
        self.readers = []
        self.epochs = []

    def ap(self):
        return self.h if isinstance(self.h, bass.AP) else self.h[:]


class TV:
    def __init__(self, base, h):
        self.base = base
        self.h = h
        self.name = base.name

    def ap(self):
        return self.h

    last_w = property(lambda self: self.base.last_w, lambda self, v: setattr(self.base, "last_w", v))
    readers = property(lambda self: self.base.readers, lambda self, v: setattr(self.base, "readers", v))
    epochs = property(lambda self: self.base.epochs)


def _shape(v, shape):
    if len(shape) == 2:
        return v
    if len(shape) == 3:
        return v.rearrange("p (a b) -> p a b", a=shape[1])
    if len(shape) == 4:
        return v.rearrange("p (a b c) -> p a b c", a=shape[1], b=shape[2])
    raise ValueError(shape)


class Op:
    __slots__ = ("eng", "fn", "deps", "need_inc", "sig", "dma_key", "waits", "idx", "phase")


class Prog:
    ENGS = ("pe", "act", "dve", "pool", "sp")

    def __init__(self, nc, arena_bytes=0):
        self.nc = nc
        self.stack = ExitStack()
        self.ops = {e: [] for e in self.ENGS}
        self.nsem = 0
        self.sem_names = []
        self.all_ops = 0
        self.keys = []
        self.free_sw = []
        self.free_hw = []
        self.scopes = False
        self.bar = {e: None for e in self.ENGS}
        self.arena = None
        if arena_bytes:
            self.arena = self.stack.enter_context(nc.sbuf_tensor("arena", [128, arena_bytes], mybir.dt.uint8))
            self.arena_bytes = arena_bytes
            self.off = 0
            self.mark = 0
            self.banks = [self.stack.enter_context(nc.psum_tensor("bank%d" % i, [128, 512], F32)) for i in range(8)]

    def sb(self, name, shape, dtype):
        if self.arena is None:
            h = self.stack.enter_context(self.nc.sbuf_tensor(name, list(shape), dtype))
            return T(name, h)
        esz = 4 if dtype == F32 else 2
        n = 1
        for d in shape[1:]:
            n *= d
        nb = (n * esz + 31) // 32 * 32
        assert self.off + nb <= self.arena_bytes, ("SBUF arena overflow", name, self.off, nb)
        v = self.arena[0:shape[0], self.off:self.off + n * esz].bitcast(dtype)
        self.off += nb
        v = _shape(v, shape)
        return T(name, v)

    def ps(self, name, shape, dtype, bank=None, off=0):
        if bank is None:
            h = self.stack.enter_context(self.nc.psum_tensor(name, list(shape), dtype))
            return T(name, h)
        n = 1
        for d in shape[1:]:
            n *= d
        esz = 4 if dtype == F32 else 2
        nf = (n * esz + 3) // 4
        assert off + nf <= 512
        v = self.banks[bank][0:shape[0], off:off + nf]
        if dtype != F32:
            v = v.bitcast(dtype)
        return T(name, _shape(v, shape))

    def set_mark(self):
        self.mark = self.off

    def reset(self):
        self.off = self.mark

    def barrier(self):
        last = [self.ops[e][-1] for e in self.ENGS if self.ops[e]]
        pairs = []
        for k in self.keys:
            for slot, cnt in k.epochs:
                pairs.append((slot, 16 * cnt))
            if k.epochs and not k.name.startswith("OUT"):
                (self.free_sw if k.name.endswith("_sw") else self.free_hw).append(tuple(k.epochs[-1]))
                k.epochs = []
        self.keys = [k for k in self.keys if k.epochs]
        for e in self.ENGS:
            self.bar[e] = (last, pairs)

    def T(self, name):
        return T(name)

    def _new_sem(self, name):
        self.sem_names.append(name)
        self.nsem += 1
        return self.nsem - 1

    def _record(self, eng, fn, r, w, dma_key=None):
        op = Op()
        op.eng = eng
        op.fn = fn
        op.need_inc = False
        op.sig = None
        op.dma_key = dma_key
        op.idx = self.all_ops
        op.phase = getattr(self, "phase", "p")
        self.all_ops += 1
        waits = {}
        deps = {}

        def add_dep(d, raw):
            if d is None:
                return
            if d.dma_key is not None:
                k = d.dma_key
                for slot, cnt in k.epochs:
                    v = 16 * cnt
                    if waits.get(slot, 0) < v:
                        waits[slot] = v
                return
            if d.eng == eng and eng == "pe":
                return
            deps[id(d)] = d

        if self.bar[eng] is not None:
            last, keys = self.bar[eng]
            self.bar[eng] = None
            for d in last:
                if d.dma_key is None:
                    add_dep(d, True)
            for slot, v in keys:
                waits[slot] = max(waits.get(slot, 0), v)
        for t in r:
            add_dep(t.last_w, True)
        for t in w:
            add_dep(t.last_w, False)
            for rd in t.readers:
                add_dep(rd, False)
        for t in r:
            t.readers.append(op)
        for t in w:
            t.last_w = op
            t.readers = []
        for d in deps.values():
            d.need_inc = True
        op.deps = list(deps.values())
        op.waits = waits
        if dma_key is not None:
            if not dma_key.epochs:
                self.keys.append(dma_key)
                free = self.free_sw if dma_key.name.endswith("_sw") else self.free_hw
                if free and free[-1][1] < DMA_EPOCH:
                    slot, base = free.pop()
                    dma_key.epochs.append([slot, base])
            if not dma_key.epochs or dma_key.epochs[-1][1] >= 2 * DMA_EPOCH:
                dma_key.epochs.append([self._new_sem("d_" + dma_key.name), 0])
            dma_key.epochs[-1][1] += 1
            op.sig = dma_key.epochs[-1][0]
        self.ops[eng].append(op)
        return op

    def op(self, eng, fn, r=(), w=()):
        return self._record(eng, fn, r, w)

    def dma(self, eng, out, in_, r=(), w=(), key=None, slow=False):
        if key is None:
            key = w[0]
        if eng == "pool":
            base = key.base if isinstance(key, TV) else key
            if not hasattr(base, "_sw"):
                base._sw = T(base.name + "_sw")
            key = base._sw
        if slow:
            return self._record(eng, lambda e: e.dma_start(out=out, in_=in_, allow_slow_non_contiguous=True), r, w, dma_key=key)
        return self._record(eng, lambda e: e.dma_start(out=out, in_=in_), r, w, dma_key=key)

    def finish(self):
        nc = self.nc
        for eng in ("pe", "act", "dve", "pool"):
            slot = None
            cnt = SEM_EPOCH
            for op in self.ops[eng]:
                if op.dma_key is not None or not op.need_inc:
                    continue
                if cnt >= SEM_EPOCH:
                    slot = self._new_sem("e_%s" % eng)
                    cnt = 0
                cnt += 1
                op.sig = (slot, cnt)
        sems = [self.stack.enter_context(nc.semaphore(n + "_%d" % i)) for i, n in enumerate(self.sem_names)]
        self.n_instr = {e: len(v) for e, v in self.ops.items()}
        final_waits = {}
        for eng in self.ENGS:
            for op in self.ops[eng]:
                if op.dma_key is not None and op.dma_key.name.startswith("OUT"):
                    for slot, cnt in op.dma_key.epochs:
                        final_waits[slot] = 16 * cnt
        with nc.Block() as block:
            def run(eng_name, e):
                waited = {}
                cur = [None, None]
                for op in self.ops[eng_name]:
                    if self.scopes and op.phase != cur[0]:
                        if cur[1] is not None:
                            cur[1].__exit__(None, None, None)
                        cur[0] = op.phase
                        cur[1] = nc.named_scope(op.phase)
                        cur[1].__enter__()
                    for d in op.deps:
                        slot, v = d.sig
                        if waited.get(slot, 0) < v:
                            waited[slot] = v
                            e.wait_ge(sems[slot], v)
                    for slot, v in op.waits.items():
                        if waited.get(slot, 0) < v:
                            waited[slot] = v
                            e.wait_ge(sems[slot], v)
                    inst = op.fn(e)
                    if op.dma_key is not None:
                        inst.then_inc(sems[op.sig], 16)
                    elif op.need_inc:
                        inst.then_inc(sems[op.sig[0]], 1)
                if cur[1] is not None:
                    cur[1].__exit__(None, None, None)
                if eng_name == "sp":
                    for slot, v in final_waits.items():
                        e.wait_ge(sems[slot], v)

            @block.tensor
            def _(e):
                run("pe", e)

            @block.scalar
            def _(e):
                run("act", e)

            @block.vector
            def _(e):
                run("dve", e)

            @block.gpsimd
            def _(e):
                run("pool", e)

            @block.sync
            def _(e):
                run("sp", e)
        self.stack.close()


D = 1024
CT = 256
DIN = 7696
DFF = 2816
EPS = 1e-6
NEG = -1.0e30
LN16 = 2.772588722239781


class OpsMixin:
    def mm(self, out, lhsT, rhs, start, stop, r, w):
        self.op("pe", lambda e: e.matmul(out, lhsT, rhs, start=start, stop=stop), r, w)

    def tp(self, out, in_, ident, r, w):
        self.op("pe", lambda e: e.transpose(out, in_, ident), r, w)

    def act(self, out, in_, func, r, w, **kw):
        self.op("act", lambda e: e.activation(out, in_, func, **kw), r, w)

    def tt(self, eng, out, a, b, op, r, w):
        self.op(eng, lambda e: e.tensor_tensor(out, a, b, op), r, w)

    def ts(self, eng, out, a, s1, s2, op0, op1, r, w):
        if op1 is None:
            self.op(eng, lambda e: e.tensor_scalar(out, a, s1, s2, op0), r, w)
        else:
            self.op(eng, lambda e: e.tensor_scalar(out, a, s1, s2, op0, op1), r, w)

    def stt(self, eng, out, a, s, b, op0, op1, r, w):
        self.op(eng, lambda e: e.scalar_tensor_tensor(out, a, s, b, op0, op1), r, w)

    def cp(self, eng, out, in_, r, w):
        if eng == "act":
            self.op("act", lambda e: e.activation(out, in_, AF.Copy), r, w)
        else:
            self.op(eng, lambda e: e.tensor_copy(out, in_), r, w)

    def memset(self, eng, out, val, w):
        self.op(eng, lambda e: e.memset(out, val), (), w)

    def recip(self, out, in_, r, w):
        self.op("dve", lambda e: e.reciprocal(out, in_), r, w)


class KProg(Prog, OpsMixin):
    pass


def conv_blocks(length):
    out = []
    s = 0
    while s < length:
        n = min(508, length - s)
        out.append((s, n))
        s += n
    return out


def build(S, debug=False, stop_after=None, scopes=False):
    NT = S + CT
    NTL = NT // 128
    NL = S // 128
    NCT = CT // 128
    nc = bass.Bass("TRN2", target_bir_lowering=False)
    P = KProg(nc, arena_bytes=206 * 1024)
    P.scopes = scopes
    P.phase = "p0"

    def din(name, shape, dt=F32):
        return nc.dram_tensor(name, list(shape), dt, kind="ExternalInput").ap()

    def dscr(name, shape, dt):
        kind = "ExternalOutput" if debug else "Internal"
        return nc.dram_tensor(name, list(shape), dt, kind=kind).ap()

    x = din("x", [S, D]); c = din("c", [D]); ctx = din("ctx", [CT, D]); c_ctx = din("c_ctx", [D])
    w_mod = din("w_mod", [D, 6 * D]); b_mod = din("b_mod", [6 * D])
    norm1_g = din("norm1_g", [D]); norm2_g = din("norm2_g", [D])
    w_in = din("w_in", [D, DIN]); gate_b = din("gate_b", [16])
    conv_w = din("conv_w", [5, 2 * D]); conv_b = din("conv_b", [2 * D])
    m_norm_g = din("m_norm_g", [D]); q_norm_g = din("q_norm_g", [128]); k_norm_g = din("k_norm_g", [128])
    w_pa = din("w_pa", [D, D]); w_pb = din("w_pb", [D, D]); w_o = din("w_o", [D, D])
    w_g = din("w_ffn_gate", [D, DFF]); w_u = din("w_ffn_up", [D, DFF]); w_d = din("w_ffn_down", [DFF, D])
    final_g = din("final_g", [D])
    cst = din("cst", [128, 512]); rope = din("rope", [S, 128])
    out = nc.dram_tensor("out", [S, D], F32, kind="ExternalOutput").ap()

    mqT = dscr("mqT", [D, NT], BF16); mkT = dscr("mkT", [D, NT], BF16)
    mv = dscr("mv", [NT, D], BF16); osig = dscr("osig", [S, D], BF16)
    qaT = dscr("qaT", [D, S], BF16); kaT = dscr("kaT", [256, NT], BF16); va = dscr("va", [NT, 256], BF16)
    gT = dscr("gT", [2 * D, S], BF16)
    hdir = dscr("hdir", [2, S, D], F32)
    ymT = dscr("ymT", [D, S], BF16); yaT = dscr("yaT", [D, S], BF16)
    x1d = dscr("x1d", [S, D], F32)
    k_mqT = P.T("mqT"); k_mkT = P.T("mkT"); k_mv = P.T("mv"); k_osig = P.T("osig"); k_qaT = P.T("qaT")
    k_kaT = P.T("kaT"); k_va = P.T("va"); k_gT = P.T("gT"); k_hdir = P.T("hdir"); k_ymT = P.T("ymT")
    k_yaT = P.T("yaT"); k_x1 = P.T("x1d"); k_out = P.T("OUT")
    dbg = {}

    identf = P.sb("identf", [128, 128], F32)
    identb = P.sb("identb", [128, 128], BF16)
    trif = P.sb("trif", [128, 2, 128], F32)
    trib = P.sb("trib", [128, 2, 128], BF16)
    e0f = P.sb("e0f", [128, 128], F32)
    onesf = P.sb("onesf", [128, 128], F32)
    onesb = P.sb("onesb", [128, 2], BF16)
    modT = P.sb("modT", [128, 48, 2], F32)
    A1 = P.sb("A1", [128, 8, 2], F32)
    A2 = P.sb("A2", [128, 8], F32)
    G1bc = P.sb("G1bc", [128, D], F32)
    G2bc = P.sb("G2bc", [128, D], F32)
    Gd = [P.sb("Gd%d" % d, [128, NTL, 8], F32) for d in range(2)]
    P.dma("sp", identf.ap(), cst[:, 0:128], w=[identf])
    P.dma("sp", trif.ap(), cst[:, 128:384].rearrange("p (a b) -> p a b", a=2), w=[trif])
    P.dma("sp", e0f.ap(), cst[:, 384:512], w=[e0f])
    P.cp("dve", identb.ap(), identf.ap(), [identf], [identb])
    P.cp("dve", trib.ap(), trif.ap(), [trif], [trib])
    P.memset("dve", onesf.ap(), 1.0, [onesf])
    P.memset("dve", onesb.ap(), 1.0, [onesb])
    P.set_mark()

    def tile_of(d, k):
        if d == 0:
            return k
        if k < NCT:
            return NCT - 1 - k
        return NTL + NCT - 1 - k

    step_of = [{tile_of(d, k): k for k in range(NTL)} for d in range(2)]

    sc = P.sb("sc", [128, 8, 2], F32)
    scs = P.sb("scs", [128, 8, 2], F32)
    bmod = P.sb("bmod", [128, 48], F32)
    n1g = P.sb("n1g", [128, 8], F32)
    n2g = P.sb("n2g", [128, 8], F32)
    wm = [P.sb("wm%d" % i, [128, 8, 512], F32) for i in range(2)]
    P.dma("sp", sc.ap()[:, :, 0], c.rearrange("(k p) -> p k", p=128), w=[sc], slow=True)
    P.dma("sp", sc.ap()[:, :, 1], c_ctx.rearrange("(k p) -> p k", p=128), w=[sc], slow=True)
    P.dma("sp", bmod.ap(), b_mod.rearrange("(k p) -> p k", p=128), w=[bmod], slow=True)
    P.dma("sp", n1g.ap(), norm1_g.rearrange("(k p) -> p k", p=128), w=[n1g], slow=True)
    P.dma("sp", n2g.ap(), norm2_g.rearrange("(k p) -> p k", p=128), w=[n2g], slow=True)
    P.act(scs.ap(), sc.ap(), AF.Silu, [sc], [scs])
    pmod = P.ps("pmod", [128, 48, 2], F32, bank=0)
    for pc in range(12):
        wt = wm[pc % 2]
        P.dma("sp", wt.ap(), w_mod[:, pc * 512:(pc + 1) * 512].rearrange("(k p) f -> p k f", p=128), w=[wt])
        for fl in range(4):
            fc = pc * 4 + fl
            for kc in range(8):
                P.mm(pmod.ap()[:, fc, :], wt.ap()[:, kc, fl * 128:(fl + 1) * 128], scs.ap()[:, kc, :],
                     kc == 0, kc == 7, [wt, scs], [pmod])
    P.tt("dve", modT.ap(), pmod.ap(), bmod.ap().unsqueeze(2).to_broadcast([128, 48, 2]), ALU.add, [pmod, bmod], [modT])
    tmpa = P.sb("tmpa", [128, 8, 2], F32)
    P.ts("dve", tmpa.ap(), modT.ap()[:, 8:16, :], 1.0, None, ALU.add, None, [modT], [tmpa])
    P.tt("dve", A1.ap(), tmpa.ap(), n1g.ap().unsqueeze(2).to_broadcast([128, 8, 2]), ALU.mult, [tmpa, n1g], [A1])
    tmpb = P.sb("tmpb", [128, 8], F32)
    P.ts("dve", tmpb.ap(), modT.ap()[:, 32:40, 0], 1.0, None, ALU.add, None, [modT], [tmpb])
    P.tt("dve", A2.ap(), tmpb.ap(), n2g.ap(), ALU.mult, [tmpb, n2g], [A2])
    dg = [P.sb("dg%d" % i, [128, 128], F32) for i in range(2)]
    for gi, (Gbc, base) in enumerate(((G1bc, 16), (G2bc, 40))):
        pb = [P.ps("pbc%d" % h, [128, 512], F32, bank=1 + h) for h in range(2)]
        for kc in range(8):
            dgt = dg[kc % 2]
            P.ts("dve", dgt.ap(), identf.ap(), modT.ap()[:, base + kc, 0:1], None, ALU.mult, None, [identf, modT], [dgt])
            P.mm(pb[kc // 4].ap()[:, (kc % 4) * 128:(kc % 4 + 1) * 128], onesf.ap(), dgt.ap(), True, True, [onesf, dgt], [pb[kc // 4]])
        for h in range(2):
            P.cp("dve", Gbc.ap()[:, h * 512:(h + 1) * 512], pb[h].ap(), [pb[h]], [Gbc])
    if debug:
        dbg["modT"] = nc.dram_tensor("d_modT", [128, 96], F32, kind="ExternalOutput").ap()
        P.dma("sp", dbg["modT"], modT.ap().rearrange("p a b -> p (a b)"), r=[modT], w=[P.T("OUTd0")])
        dbg["G1bc"] = nc.dram_tensor("d_G1bc", [128, D], F32, kind="ExternalOutput").ap()
        P.dma("sp", dbg["G1bc"], G1bc.ap(), r=[G1bc], w=[P.T("OUTd1")])
    P.barrier()
    P.reset()
    if stop_after == 0:
        P.finish()
        return nc

    P.phase = "p1a"
    hT = P.sb("hT", [128, 8, NT], BF16)
    hTt = [P.T("hT%d" % i) for i in range(NTL)]
    mark2 = P.off
    xt = [P.sb("xt%d" % i, [128, D], F32) for i in range(3)]
    junk = P.sb("junk", [128, D], BF16)
    st = [P.sb("st%d" % i, [128, 4], F32) for i in range(3)]
    xn = [P.sb("xn%d" % i, [128, D], BF16) for i in range(2)]
    tmpf = [P.sb("tmpf%d" % i, [128, 8, 128], F32) for i in range(2)]
    pT = [P.ps("pT%d" % i, [128, 8, 128], BF16, bank=i) for i in range(2)]

    def rstd_ops(stt_, n):
        P.act(stt_.ap()[:, 1:2], stt_.ap()[:, 0:1], AF.Sqrt, [stt_], [stt_], bias=EPS, scale=1.0 / n)
        P.recip(stt_.ap()[:, 2:3], stt_.ap()[:, 1:2], [stt_], [stt_])

    for i in range(NTL):
        xs = xt[i % 3]; ss = st[i % 3]; xb = xn[i % 2]; pt = pT[i % 2]; tf = tmpf[i % 2]
        src = ctx[i * 128:(i + 1) * 128, :] if i < NCT else x[(i - NCT) * 128:(i - NCT + 1) * 128, :]
        P.dma("sp", xs.ap(), src, w=[xs])
        P.act(junk.ap(), xs.ap(), AF.Square, [xs], [junk, ss], accum_out=ss.ap()[:, 0:1])
        rstd_ops(ss, D)
        P.ts("dve", xb.ap(), xs.ap(), ss.ap()[:, 2:3], None, ALU.mult, None, [xs, ss], [xb])
        for kc in range(8):
            P.tp(pt.ap()[:, kc, :], xb.ap()[:, kc * 128:(kc + 1) * 128], identb.ap(), [xb, identb], [pt])
        m = 1 if i < NCT else 0
        P.tt("dve", tf.ap(), pt.ap(), A1.ap()[:, :, m:m + 1].to_broadcast([128, 8, 128]), ALU.mult, [pt, A1], [tf])
        P.tt("pool", hT.ap()[:, :, i * 128:(i + 1) * 128], tf.ap(), modT.ap()[:, 0:8, m:m + 1].to_broadcast([128, 8, 128]), ALU.add,
             [tf, modT], [hTt[i]])
    if debug:
        dbg["hT"] = nc.dram_tensor("d_hT", [128, 8 * NT], BF16, kind="ExternalOutput").ap()
        P.dma("sp", dbg["hT"], hT.ap().rearrange("p a b -> p (a b)"), r=hTt, w=[P.T("OUTd2")])
    if stop_after == 1:
        P.finish()
        return nc
    P.barrier()
    P.off = mark2

    P.phase = "p1b"
    wb = [P.sb("wb%d" % i, [128, 8, 512], BF16) for i in range(2)]
    wcnt = [0]

    wgroups = [(g * 512, 512) for g in range(4)] + [(5648 + g * 512, 512) for g in range(4)] + \
              [(2048, 512), (2560, 512), (3072, 512), (3584, 512), (4096, 16), (4112, 512), (4624, 512), (5136, 512)]
    wtiles = {}

    def issue_w(gi):
        if gi >= len(wgroups) or gi in wtiles:
            return
        c0, ncols = wgroups[gi]
        t = wb[gi % 2]
        P.dma("pool", t.ap()[:, :, 0:ncols], w_in[:, c0:c0 + ncols].rearrange("(k p) f -> p k f", p=128), w=[t])
        wtiles[gi] = t

    def load_w(c0, ncols, prefetch=True):
        gi = wcnt[0]
        wcnt[0] += 1
        assert wgroups[gi] == (c0, ncols), (gi, c0, ncols)
        issue_w(gi)
        t = wtiles[gi]
        if prefetch:
            issue_w(gi + 1)
        return t

    pz = [P.ps("pz%d" % i, [128, 512], F32, bank=2 + i) for i in range(4)]
    pzc = [0]

    def next_pz():
        t = pz[pzc[0] % 4]
        pzc[0] += 1
        return t

    cw = P.sb("cw", [128, 16, 5], F32)
    cb = P.sb("cb", [128, 16], F32)
    gbb = P.sb("gbb", [128, 16], F32)
    qgb = P.sb("qgb", [128, 128], F32)
    kgb = P.sb("kgb", [128, 128], F32)
    for j in range(5):
        P.dma("sp", cw.ap()[:, :, j], conv_w[j].rearrange("(c p) -> p c", p=128), w=[cw], slow=True)
    P.dma("sp", cb.ap(), conv_b.rearrange("(c p) -> p c", p=128), w=[cb], slow=True)
    P.dma("sp", gbb.ap(), gate_b.partition_broadcast(128), w=[gbb])
    P.dma("sp", qgb.ap(), q_norm_g.partition_broadcast(128), w=[qgb])
    P.dma("sp", kgb.ap(), k_norm_g.partition_broadcast(128), w=[kgb])

    def hts(g0, g1):
        return hTt[g0 // 128:(g1 - 1) // 128 + 1]

    stg8 = [P.sb("stg8_%d" % i, [128, 512], BF16) for i in range(8)]
    scnt = [0]

    def next_stage():
        t = stg8[scnt[0] % 8]
        scnt[0] += 1
        return t

    sqc = [0]

    def stq(from_act):
        sqc[0] += 1
        m = sqc[0] % 3
        if m == 0:
            return "sp"
        if m == 1:
            return "act" if from_act else "sp"
        return "pool"

    mark3 = P.off
    Zs = [P.sb("Zs%d" % i, [128, 512], F32) for i in range(4)]
    acc = [P.sb("acc%d" % i, [128, 508], F32) for i in range(4)]
    fcnt = [0]
    items = []
    for grp in range(4):
        for pair in range(2):
            chs = [grp * 4 + pair * 2, grp * 4 + pair * 2 + 1]
            is_k = chs[0] >= 8
            seqs = [(CT, S)] + ([(0, CT)] if is_k else [])
            for (s0, ln) in seqs:
                for (bs, n) in conv_blocks(ln):
                    items.append((grp, chs, is_k, s0, ln, bs, n))
    fw = {}

    def f_stage1(it):
        grp, chs, is_k, s0, ln, bs, n = it
        if grp not in fw:
            fw[grp] = load_w(grp * 512, 512)
        wt = fw[grp]
        w0 = max(0, bs - 2); w1 = min(ln, bs + n + 2)
        ncol = w1 - w0
        off = 2 - (bs - w0)
        hr = hts(s0 + w0, s0 + w1)
        st_ = []
        for ch in chs:
            cl = ch % 4
            i = fcnt[0]; fcnt[0] += 1
            z = Zs[i % 4]; a = acc[i % 4]
            p = next_pz()
            st_.append((ch, z, a))
            for kc in range(8):
                P.mm(p.ap()[:, 0:ncol], wt.ap()[:, kc, cl * 128:(cl + 1) * 128], hT.ap()[:, kc, s0 + w0:s0 + w1],
                     kc == 0, kc == 7, [wt] + hr, [p])
            if bs == 0:
                P.memset("dve", z.ap()[:, 0:2], 0.0, [z])
            if bs + n == ln:
                P.memset("dve", z.ap()[:, 2 + n:4 + n], 0.0, [z])
            P.cp("act", z.ap()[:, off:off + ncol], p.ap()[:, 0:ncol], [p], [z])
        return (it, st_)

    def f_stage2(rec):
        (grp, chs, is_k, s0, ln, bs, n), st_ = rec
        dst, kdst = (mkT, k_mkT) if is_k else (mqT, k_mqT)
        for j in range(5):
            for (ch, z, a) in st_:
                if j == 0:
                    P.ts("dve", a.ap()[:, 0:n], z.ap()[:, 0:n], cw.ap()[:, ch, 0:1], None, ALU.mult, None, [z, cw], [a])
                else:
                    P.stt("dve", a.ap()[:, 0:n], z.ap()[:, j:j + n], cw.ap()[:, ch, j:j + 1], a.ap()[:, 0:n], ALU.mult, ALU.add, [z, cw, a], [a])
        for (ch, z, a) in st_:
            o = next_stage()
            P.act(o.ap()[:, 0:n], a.ap()[:, 0:n], AF.Silu, [a, cb], [o], bias=cb.ap()[:, ch:ch + 1])
            r0 = (ch % 8) * 128
            P.dma(stq(True), dst[r0:r0 + 128, s0 + bs:s0 + bs + n], o.ap()[:, 0:n], r=[o], w=[kdst], key=o)

    prev = None
    for it in items:
        cur = f_stage1(it)
        if prev is not None:
            f_stage2(prev)
        prev = cur
    f_stage2(prev)
    P.barrier()
    P.off = mark3

    if stop_after == "F":
        P.finish()
        return nc
    P.phase = "p1b_G"
    gcnt = 0
    for grp in range(4):
        wt = load_w(5648 + grp * 512, 512)
        for cl in range(4):
            ch = grp * 4 + cl
            for b0 in range(0, S, 512):
                p = next_pz()
                o = next_stage()
                hr = hts(CT + b0, CT + b0 + 512)
                for kc in range(8):
                    P.mm(p.ap(), wt.ap()[:, kc, cl * 128:(cl + 1) * 128], hT.ap()[:, kc, CT + b0:CT + b0 + 512], kc == 0, kc == 7, [wt] + hr, [p])
                P.act(o.ap(), p.ap(), AF.Sigmoid, [p], [o])
                P.dma(stq(True), gT[ch * 128:(ch + 1) * 128, b0:b0 + 512], o.ap(), r=[o], w=[k_gT], key=o)

    if stop_after == "G":
        P.finish()
        return nc
    P.phase = "p1b_mv"
    tcnt = [0]

    def tok_mm(wt, ncols, i):
        p = next_pz()
        for kc in range(8):
            P.mm(p.ap()[:, 0:ncols], hT.ap()[:, kc, i * 128:(i + 1) * 128], wt.ap()[:, kc, 0:ncols], kc == 0, kc == 7, [wt, hTt[i]], [p])
        return p

    for grp in range(2):
        wt = load_w(2048 + grp * 512, 512)
        for i in range(NTL):
            p = tok_mm(wt, 512, i)
            o = next_stage()
            P.cp("act" if i % 2 else "dve", o.ap(), p.ap(), [p], [o])
            P.dma(stq(i % 2 == 1), mv[i * 128:(i + 1) * 128, grp * 512:(grp + 1) * 512], o.ap(), r=[o], w=[k_mv], key=o)
    if stop_after == "mv":
        P.finish()
        return nc
    P.phase = "p1b_o"
    for grp in range(2):
        wt = load_w(3072 + grp * 512, 512)
        for i in range(NCT, NTL):
            p = tok_mm(wt, 512, i)
            o = next_stage()
            P.act(o.ap(), p.ap(), AF.Sigmoid, [p], [o])
            P.dma(stq(True), osig[(i - NCT) * 128:(i - NCT + 1) * 128, grp * 512:(grp + 1) * 512], o.ap(), r=[o], w=[k_osig], key=o)
    if stop_after == "o":
        P.finish()
        return nc
    P.phase = "p1b_gt"
    wt = load_w(4096, 16)
    Gdt = [P.T("Gdt0"), P.T("Gdt1")]
    for i in range(NTL):
        p = tok_mm(wt, 16, i)
        for d in range(2):
            P.tt("dve", Gd[d].ap()[:, step_of[d][i], :], p.ap()[:, d * 8:(d + 1) * 8], gbb.ap()[:, d * 8:(d + 1) * 8], ALU.add, [p, gbb], [Gdt[d]])

    if stop_after == "gates":
        P.finish()
        return nc
    P.phase = "p1b_q"
    ropet = [P.sb("ropet%d" % i, [128, 128], F32) for i in range(2)]
    sqb = [P.sb("sq%d" % i, [128, 512], BF16) for i in range(2)]
    qst = [P.sb("qst%d" % i, [128, 8], F32) for i in range(2)]
    qn = [P.sb("qn%d" % i, [128, 512], F32) for i in range(2)]
    t1 = [P.sb("rt%d" % i, [128, 256], F32) for i in range(4)]
    qr = [P.sb("qr%d" % i, [128, 512], BF16) for i in range(2)]
    qstage = [P.sb("qstage%d" % i, [128, 4, 512], BF16) for i in range(2)]
    acnt = [0]

    def norm_rope(p, nh, gb, rt, do_rope):
        i = acnt[0]; acnt[0] += 1
        s_ = qst[i % 2]; q_ = qn[i % 2]; o_ = qr[i % 2]; sq = sqb[i % 2]
        W = nh * 128
        P.act(sq.ap()[:, 0:W], p.ap()[:, 0:W], AF.Square, [p], [sq])
        P.op("dve", lambda e: e.reduce_sum(s_.ap()[:, 0:nh], sq.ap()[:, 0:W].rearrange("p (h d) -> p h d", h=nh), AX.X), [sq], [s_])
        P.act(s_.ap()[:, 4:4 + nh], s_.ap()[:, 0:nh], AF.Sqrt, [s_], [s_], bias=EPS, scale=1.0 / 128)
        P.recip(s_.ap()[:, 0:nh], s_.ap()[:, 4:4 + nh], [s_], [s_])
        P.tt("dve", q_.ap()[:, 0:W].rearrange("p (h d) -> p h d", h=nh), p.ap()[:, 0:W].rearrange("p (h d) -> p h d", h=nh),
             s_.ap()[:, 0:nh].unsqueeze(2).to_broadcast([128, nh, 128]), ALU.mult, [p, s_], [q_])
        if not do_rope:
            P.tt("pool", o_.ap()[:, 0:W].rearrange("p (h d) -> p h d", h=nh), q_.ap()[:, 0:W].rearrange("p (h d) -> p h d", h=nh),
                 gb.ap().unsqueeze(1).to_broadcast([128, nh, 128]), ALU.mult, [q_, gb], [o_])
            return o_
        P.tt("pool", q_.ap()[:, 0:W].rearrange("p (h d) -> p h d", h=nh), q_.ap()[:, 0:W].rearrange("p (h d) -> p h d", h=nh),
             gb.ap().unsqueeze(1).to_broadcast([128, nh, 128]), ALU.mult, [q_, gb], [q_])
        qv = q_.ap()[:, 0:W].rearrange("p (h i two) -> p h i two", h=nh, two=2)
        ov = o_.ap()[:, 0:W].rearrange("p (h i two) -> p h i two", h=nh, two=2)
        x1 = qv[:, :, :, 0]; x2 = qv[:, :, :, 1]
        cosb = rt.ap()[:, 0:64].unsqueeze(1).to_broadcast([128, nh, 64])
        sinb = rt.ap()[:, 64:128].unsqueeze(1).to_broadcast([128, nh, 64])
        tv = [t.ap()[:, 0:nh * 64].rearrange("p (h i) -> p h i", h=nh) for t in t1]
        P.tt("dve", tv[0], x1, cosb, ALU.mult, [q_, rt], [t1[0]])
        P.tt("dve", tv[1], x2, sinb, ALU.mult, [q_, rt], [t1[1]])
        P.tt("dve", ov[:, :, :, 0], tv[0], tv[1], ALU.subtract, [t1[0], t1[1]], [o_])
        P.tt("pool", tv[2], x1, sinb, ALU.mult, [q_, rt], [t1[2]])
        P.tt("pool", tv[3], x2, cosb, ALU.mult, [q_, rt], [t1[3]])
        P.tt("pool", ov[:, :, :, 1], tv[2], tv[3], ALU.add, [t1[2], t1[3]], [o_])
        return o_

    def load_rope(i):
        rt = ropet[i % 2]
        P.dma("sp", rt.ap(), rope[(i - NCT) * 128:(i - NCT + 1) * 128, :], w=[rt])
        return rt

    pT2 = [P.ps("pT2_%d" % i, [128, 4, 128], BF16, bank=i) for i in range(2)]
    wq = [load_w(4112, 512, prefetch=False), load_w(4624, 512, prefetch=False)]
    ptc = [0]
    rts = {}
    qunits = [(b0, j, grp) for b0 in range(0, NL, 4) for j in range(4) for grp in range(2)]

    def q_stage1(u):
        b0, j, grp = u
        i = NCT + b0 + j
        if grp == 0:
            rts[i] = load_rope(i)
        return tok_mm(wq[grp], 512, i)

    def q_stage2(u, p):
        b0, j, grp = u
        i = NCT + b0 + j
        stg = qstage[grp]
        o_ = norm_rope(p, 4, qgb, rts[i], True)
        ptt = pT2[ptc[0] % 2]; ptc[0] += 1
        for h in range(4):
            P.tp(ptt.ap()[:, h, :], o_.ap()[:, h * 128:(h + 1) * 128], identb.ap(), [o_, identb], [ptt])
        P.cp("act", stg.ap()[:, :, j * 128:(j + 1) * 128], ptt.ap(), [ptt], [stg])
        if j == 3 and grp == 1:
            for g2 in range(2):
                for h in range(4):
                    P.dma(stq(True), qaT[(g2 * 4 + h) * 128:(g2 * 4 + h + 1) * 128, b0 * 128:(b0 + 4) * 128], qstage[g2].ap()[:, h, :],
                          r=[qstage[g2]], w=[k_qaT], key=qstage[g2])

    prevq = None
    for u in qunits:
        pcur = q_stage1(u)
        if prevq is not None:
            q_stage2(*prevq)
        prevq = (u, pcur)
    q_stage2(*prevq)
    if stop_after == "q":
        P.finish()
        return nc
    P.phase = "p1b_kv"
    wt = load_w(5136, 512)
    kstage = [P.sb("kstage%d" % i, [128, 2, 128], BF16) for i in range(2)]
    krt = {}

    def kv_stage1(i):
        if i >= NCT:
            krt[i] = load_rope(i)
        return tok_mm(wt, 512, i)

    def kv_stage2(i, p):
        lat = i >= NCT
        o_ = norm_rope(p, 2, kgb, krt.get(i), lat)
        ptt = pT2[i % 2]
        for h in range(2):
            P.tp(ptt.ap()[:, h, :], o_.ap()[:, h * 128:(h + 1) * 128], identb.ap(), [o_, identb], [ptt])
        ks = kstage[i % 2]
        P.cp("act", ks.ap(), ptt.ap()[:, 0:2, :], [ptt], [ks])
        for h in range(2):
            P.dma(stq(True), kaT[h * 128:(h + 1) * 128, i * 128:(i + 1) * 128], ks.ap()[:, h, :], r=[ks], w=[k_kaT], key=ks)
        o = next_stage()
        P.cp("dve", o.ap()[:, 0:256], p.ap()[:, 256:512], [p], [o])
        P.dma(stq(False), va[i * 128:(i + 1) * 128, :], o.ap()[:, 0:256], r=[o], w=[k_va], key=o)

    prevk = None
    for i in range(NTL):
        pcur = kv_stage1(i)
        if prevk is not None:
            kv_stage2(*prevk)
        prevk = (i, pcur)
    kv_stage2(*prevk)
    if debug:
        for d in range(2):
            dbg["Gd%d" % d] = nc.dram_tensor("d_Gd%d" % d, [128, NTL * 8], F32, kind="ExternalOutput").ap()
            P.dma("sp", dbg["Gd%d" % d], Gd[d].ap().rearrange("p a b -> p (a b)"), r=[Gdt[d]], w=[P.T("OUTg%d" % d)])
    P.barrier()
    P.reset()
    if stop_after == 2:
        P.finish()
        return nc

    P.phase = "p2a"
    NS = NTL
    W4 = NS * 4
    Aa = [P.sb("Aa%d" % d, [128, W4], F32) for d in range(2)]
    Ee = [P.sb("Ee%d" % d, [128, W4], F32) for d in range(2)]
    C0 = [P.sb("C0%d" % d, [128, W4], F32) for d in range(2)]
    mng = P.sb("mng", [128, D], F32)
    P.dma("sp", mng.ap(), m_norm_g.partition_broadcast(128), w=[mng])
    pieces = [(c0_, min(c0_ + 128, W4)) for c0_ in range(0, W4, 128)]
    pTr = P.ps("pTr", [128, 128], F32, bank=4)
    for d in range(2):
        LF = P.sb("LF%d" % d, [128, W4], F32)
        gtmp = P.sb("gtmp%d" % d, [128, W4], F32)
        Bv = P.sb("Bv%d" % d, [128, W4], F32)
        Mb = P.sb("Mb%d" % d, [128, W4], F32)
        mrow = P.sb("mrow%d" % d, [128, W4], F32)
        mprev = P.sb("mprev%d" % d, [128, W4], F32)
        Rr = P.sb("Rr%d" % d, [128, W4], F32)
        FLb = P.sb("FLb%d" % d, [128, W4], F32)
        mcol = P.sb("mcol%d" % d, [128, 4], F32)
        dgm = P.sb("dgm%d" % d, [128, 128], F32)
        v3 = lambda t: t.ap().rearrange("p (s h) -> p s h", h=4)
        pF = P.ps("pF%d" % d, [128, W4], F32, bank=0 + d)
        pFL = P.ps("pFL%d" % d, [128, W4], F32, bank=2 + d)
        pM = P.ps("pM%d" % d, [128, W4], F32, bank=5 + d)
        P.act(v3(gtmp), Gd[d].ap()[:, :, 4:8], AF.Exp, [Gdt[d]], [gtmp], scale=-1.0)
        P.act(gtmp.ap(), gtmp.ap(), AF.Ln, [gtmp], [gtmp], bias=1.0)
        P.ts("dve", LF.ap(), gtmp.ap(), -1.0, None, ALU.mult, None, [gtmp], [LF])
        P.mm(pF.ap(), trif.ap()[:, d, :], LF.ap(), True, True, [trif, LF], [pF])
        P.mm(pFL.ap(), onesf.ap(), LF.ap(), True, True, [onesf, LF], [pFL])
        P.tt("dve", v3(Bv), Gd[d].ap()[:, :, 0:4], pF.ap().rearrange("p (s h) -> p s h", h=4), ALU.subtract, [Gdt[d], pF], [Bv])
        P.cp("dve", FLb.ap(), pFL.ap(), [pFL], [FLb])
        for pi, (a0, a1) in enumerate(pieces):
            w_ = a1 - a0
            P.tp(pTr.ap()[0:w_, :], Bv.ap()[:, a0:a1], identf.ap(), [Bv, identf], [pTr])
            P.memset("dve", mcol.ap()[:, pi:pi + 1], 0.0, [mcol])
            P.op("dve", lambda e, o_=mcol.ap()[0:w_, pi:pi + 1], i_=pTr.ap()[0:w_, :]: e.reduce_max(o_, i_, AX.X), [pTr, mcol], [mcol])
            P.ts("dve", dgm.ap(), identf.ap(), mcol.ap()[:, pi:pi + 1], None, ALU.mult, None, [identf, mcol], [dgm])
            P.mm(pM.ap()[:, a0:a1], onesf.ap(), dgm.ap()[:, 0:w_], True, True, [onesf, dgm], [pM])
        P.cp("dve", Mb.ap(), pM.ap(), [pM], [Mb])
        for h in range(4):
            P.op("dve", lambda e, o_=v3(mrow)[:, :, h], a_=v3(Mb)[:, :, h], b_=v3(FLb)[:, :, h]: e.tensor_tensor_scan(o_, a_, b_, NEG, ALU.max, ALU.add),
                 [Mb, FLb], [mrow])
        P.memset("dve", mprev.ap()[:, 0:4], NEG, [mprev])
        P.cp("dve", mprev.ap()[:, 4:W4], mrow.ap()[:, 0:W4 - 4], [mrow], [mprev])
        P.tt("dve", Rr.ap(), mprev.ap(), Mb.ap(), ALU.max, [mprev, Mb], [Rr])
        P.tt("dve", gtmp.ap(), mprev.ap(), Rr.ap(), ALU.subtract, [mprev, Rr], [gtmp])
        P.ts("dve", gtmp.ap(), gtmp.ap(), -200.0, None, ALU.max, None, [gtmp], [gtmp])
        P.act(C0[d].ap(), gtmp.ap(), AF.Exp, [gtmp], [C0[d]])
        if debug:
            for nm, tl in (("Mb", Mb), ("FLb", FLb), ("mrow", mrow), ("Rr", Rr), ("Bv", Bv)):
                dbg[nm + str(d)] = nc.dram_tensor("d_%s%d" % (nm, d), [128, W4], F32, kind="ExternalOutput").ap()
                P.dma("sp", dbg[nm + str(d)], tl.ap(), r=[tl], w=[P.T("OUT%s%d" % (nm, d))])
        P.tt("dve", Bv.ap(), Bv.ap(), Rr.ap(), ALU.subtract, [Bv, Rr], [Bv])
        P.act(Aa[d].ap(), Bv.ap(), AF.Exp, [Bv], [Aa[d]], bias=-LN16)
        P.tt("dve", LF.ap(), pF.ap(), Rr.ap(), ALU.add, [pF, Rr], [LF])
        P.act(Ee[d].ap(), LF.ap(), AF.Exp, [LF], [Ee[d]], scale=-1.0)
    if debug:
        for nm, tl in (("Aa", Aa), ("Ee", Ee), ("C0", C0)):
            for d in range(2):
                dbg[nm + str(d)] = nc.dram_tensor("d_%s%d" % (nm, d), [128, W4], F32, kind="ExternalOutput").ap()
                P.dma("sp", dbg[nm + str(d)], tl[d].ap(), r=[tl[d]], w=[P.T("OUT%s%d" % (nm, d))])
    P.barrier()
    if stop_after == "2a":
        P.finish()
        return nc

    P.phase = "p2b"
    kTin = [[P.sb("kTin%d%d" % (d, i), [128, 8, 128], BF16) for i in range(2)] for d in range(2)]
    qTin = [[P.sb("qTin%d%d" % (d, i), [128, 8, 128], BF16) for i in range(2)] for d in range(2)]
    vin = [[P.sb("vin%d%d" % (d, i), [128, D], BF16) for i in range(2)] for d in range(2)]
    kp = [[P.sb("kp%d%d" % (d, i), [128, D], BF16) for i in range(2)] for d in range(2)]
    Sm = [[P.sb("Sm%d%d" % (d, i), [128, 4, 128], BF16) for i in range(2)] for d in range(2)]
    qs = [[P.sb("qs%d%d" % (d, i), [128, 8, 128], BF16) for i in range(2)] for d in range(2)]
    Cst = [P.sb("Cst%d" % d, [128, 4, 512], F32) for d in range(2)]
    Cbf = [P.sb("Cbf%d" % d, [128, 4, 512], BF16) for d in range(2)]
    Cbt = [[P.T("Cbt%d%d" % (d, h)) for h in range(4)] for d in range(2)]
    Cft = [[P.T("Cft%d%d" % (d, h)) for h in range(4)] for d in range(2)]
    nst = [P.sb("nst%d" % d, [128, 8], F32) for d in range(2)]
    ntmp = [P.sb("ntmp%d" % d, [128, 8], F32) for d in range(2)]
    nbf = [P.sb("nbf%d" % d, [128, 8], BF16) for d in range(2)]
    hbuf = [[P.sb("hbuf%d%d" % (d, i), [128, D], F32) for i in range(2)] for d in range(2)]
    hoth = [P.sb("hoth%d" % i, [128, D], F32) for i in range(2)]
    osg = [P.sb("osg%d" % i, [128, D], BF16) for i in range(2)]
    ymt = [P.sb("ymt%d" % i, [128, D], BF16) for i in range(2)]
    dsb = [P.sb("dsb%d" % d, [128, 16], F32) for d in range(2)]
    fst = [P.sb("fst%d" % i, [128, 16], F32) for i in range(2)]
    fjunk = P.sb("fjunk", [128, 256], BF16)
    half = NL // 2
    GRP = 4 if half % 4 == 0 else (2 if half % 2 == 0 else 1)
    ymstage = [[P.sb("ymst%d%d" % (d, i), [128, 8, GRP * 128], BF16) for i in range(2)] for d in range(2)]
    pK = P.ps("pK", [128, 8, 128], BF16, bank=0)
    pdc = [P.ps("pdc%d" % i, [128, 512], F32, bank=1 + i) for i in range(2)]
    pb3 = P.T("pbank3")
    pdn = [TV(pb3, P.ps("pdn%d" % d, [128, 8], F32, bank=3, off=d * 32).h) for d in range(2)]
    pden = [TV(pb3, P.ps("pden%d" % d, [128, 4], F32, bank=3, off=64 + d * 32).h) for d in range(2)]
    pS = P.ps("pS", [128, 4, 128], F32, bank=4)
    pnum = [P.ps("pnum%d" % i, [128, 2, 256], F32, bank=5 + i) for i in range(2)]
    pT3 = P.ps("pT3", [128, 8, 128], BF16, bank=7)
    for d in range(2):
        P.memset("dve", Cst[d].ap(), 0.0, Cft[d])
        P.memset("pool", Cbf[d].ap(), 0.0, Cbt[d])
        P.memset("dve", nst[d].ap(), 0.0, [nst[d]])
        P.memset("dve", nbf[d].ap(), 0.0, [nbf[d]])
    fin_cnt = [0, 0]
    fcount = [0]
    for k in range(NS):
        for d in range(2):
            tile = tile_of(d, k)
            g0 = tile * 128
            lat = tile >= NCT
            sl = k % 2
            kt = kTin[d][sl]; vt = vin[d][sl]; qt = qTin[d][sl]; kpt = kp[d][sl]; smt = Sm[d][sl]; qst_ = qs[d][sl]
            P.dma("sp", kt.ap(), mkT[:, g0:g0 + 128].rearrange("(j p) t -> p j t", p=128), r=[k_mkT], w=[kt])
            P.dma("sp", vt.ap(), mv[g0:g0 + 128, :], r=[k_mv], w=[vt])
            if lat:
                P.dma("sp", qt.ap(), mqT[:, g0:g0 + 128].rearrange("(j p) t -> p j t", p=128), r=[k_mqT], w=[qt])
            for j in range(8):
                P.tp(pK.ap()[:, j, :], kt.ap()[:, j, :], identb.ap(), [kt, identb], [pK])
            if lat:
                for h in range(4):
                    for dc in range(2):
                        P.mm(pS.ap()[:, h, :], kt.ap()[:, 2 * h + dc, :], qt.ap()[:, 2 * h + dc, :], dc == 0, dc == 1, [kt, qt], [pS])
            for h in range(4):
                col = k * 4 + h
                dstv = kpt.ap()[:, h * 256:(h + 1) * 256].rearrange("p (a b) -> p a b", a=2)
                if h % 2:
                    P.act(dstv, pK.ap()[:, 2 * h:2 * h + 2, :], AF.Copy, [pK, Aa[d]], [kpt], scale=Aa[d].ap()[:, col:col + 1])
                else:
                    P.ts("dve", dstv, pK.ap()[:, 2 * h:2 * h + 2, :], Aa[d].ap()[:, col:col + 1], None, ALU.mult, None, [pK, Aa[d]], [kpt])
            if lat:
                for h in range(4):
                    col = k * 4 + h
                    P.stt("dve", smt.ap()[:, h, :], pS.ap()[:, h, :], Aa[d].ap()[:, col:col + 1], trib.ap()[:, d, :], ALU.mult, ALU.mult,
                          [pS, Aa[d], trib], [smt])
                    P.act(qst_.ap()[:, 2 * h:2 * h + 2, :], qt.ap()[:, 2 * h:2 * h + 2, :], AF.Copy, [qt, C0[d]], [qst_], scale=C0[d].ap()[:, col:col + 1])
                for h in range(4):
                    pn = pnum[h // 2]
                    P.mm(pn.ap()[:, h % 2, :], smt.ap()[:, h, :], vt.ap()[:, h * 256:(h + 1) * 256], True, False, [smt, vt], [pn])
                    P.mm(pn.ap()[:, h % 2, :], qst_.ap()[:, 2 * h, :], Cbf[d].ap()[:, h, 0:256], False, False, [qst_, Cbt[d][h]], [pn])
                    P.mm(pn.ap()[:, h % 2, :], qst_.ap()[:, 2 * h + 1, :], Cbf[d].ap()[:, h, 256:512], False, True, [qst_, Cbt[d][h]], [pn])
                for h in range(4):
                    P.mm(pden[d].ap()[:, h:h + 1], smt.ap()[:, h, :], onesb.ap()[:, 0:1], True, False, [smt, onesb], [pden[d]])
                    P.mm(pden[d].ap()[:, h:h + 1], qst_.ap()[:, 2 * h, :], nbf[d].ap()[:, 2 * h:2 * h + 1], False, False, [qst_, nbf[d]], [pden[d]])
                    P.mm(pden[d].ap()[:, h:h + 1], qst_.ap()[:, 2 * h + 1, :], nbf[d].ap()[:, 2 * h + 1:2 * h + 2], False, True, [qst_, nbf[d]], [pden[d]])
            for h in range(4):
                col = k * 4 + h
                pd = pdc[h % 2]
                for dc in range(2):
                    P.mm(pd.ap()[:, dc * 256:(dc + 1) * 256], kpt.ap()[:, h * 256 + dc * 128:h * 256 + (dc + 1) * 128], vt.ap()[:, h * 256:(h + 1) * 256],
                         True, True, [kpt, vt], [pd])
                P.stt("dve", Cst[d].ap()[:, h, :], Cst[d].ap()[:, h, :], C0[d].ap()[:, col:col + 1], pd.ap(), ALU.mult, ALU.add,
                      [Cft[d][h], C0[d], pd], [Cft[d][h]])
                P.cp("act", Cbf[d].ap()[:, h, :], Cst[d].ap()[:, h, :], [Cft[d][h]], [Cbt[d][h]])
            for j in range(8):
                P.mm(pdn[d].ap()[:, j:j + 1], kpt.ap()[:, j * 128:(j + 1) * 128], onesb.ap()[:, 0:1], True, True, [kpt, onesb], [pdn[d]])
            P.tt("dve", ntmp[d].ap().rearrange("p (h two) -> p h two", two=2), nst[d].ap().rearrange("p (h two) -> p h two", two=2),
                 C0[d].ap()[:, k * 4:(k + 1) * 4].unsqueeze(2).to_broadcast([128, 4, 2]), ALU.mult, [nst[d], C0[d]], [ntmp[d]])
            P.tt("dve", nst[d].ap(), ntmp[d].ap(), pdn[d].ap(), ALU.add, [ntmp[d], pdn[d]], [nst[d]])
            P.cp("dve", nbf[d].ap(), nst[d].ap(), [nst[d]], [nbf[d]])
            if not lat:
                continue
            ds_ = dsb[d]
            hb = hbuf[d][sl]
            P.cp("dve", ds_.ap()[:, 0:4], pden[d].ap(), [pden[d]], [ds_])
            P.stt("dve", ds_.ap()[:, 4:8], ds_.ap()[:, 0:4], -1.0, ds_.ap()[:, 0:4], ALU.mult, ALU.max, [ds_], [ds_])
            P.tt("dve", ds_.ap()[:, 8:12], ds_.ap()[:, 4:8], Ee[d].ap()[:, k * 4:(k + 1) * 4], ALU.max, [ds_, Ee[d]], [ds_])
            P.recip(ds_.ap()[:, 12:16], ds_.ap()[:, 8:12], [ds_], [ds_])
            for h in range(4):
                pn = pnum[h // 2]
                if h % 2:
                    P.act(hb.ap()[:, h * 256:(h + 1) * 256], pn.ap()[:, h % 2, :], AF.Copy, [pn, ds_], [hb], scale=ds_.ap()[:, 12 + h:13 + h])
                else:
                    P.ts("dve", hb.ap()[:, h * 256:(h + 1) * 256], pn.ap()[:, h % 2, :], ds_.ap()[:, 12 + h:13 + h], None, ALU.mult, None, [pn, ds_], [hb])
            li_ = tile - NCT
            ko = step_of[1 - d][tile]
            if ko > k:
                P.dma("pool", hdir[d, li_ * 128:(li_ + 1) * 128, :], hb.ap(), r=[hb], w=[k_hdir])
                continue
            fi = fcount[0]; fcount[0] += 1
            ho = hoth[fi % 2]; og = osg[fi % 2]; ym_ = ymt[fi % 2]; fs = fst[fi % 2]
            P.dma("sp", ho.ap(), hdir[1 - d, li_ * 128:(li_ + 1) * 128, :], r=[k_hdir], w=[ho])
            P.dma("sp", og.ap(), osig[li_ * 128:(li_ + 1) * 128, :], r=[k_osig], w=[og])
            P.tt("pool", ho.ap(), ho.ap(), hb.ap(), ALU.add, [ho, hb], [ho])
            for h in range(4):
                P.act(fjunk.ap(), ho.ap()[:, h * 256:(h + 1) * 256], AF.Square, [ho], [fjunk, fs], accum_out=fs.ap()[:, h:h + 1])
            P.act(fs.ap()[:, 4:8], fs.ap()[:, 0:4], AF.Sqrt, [fs], [fs], bias=EPS, scale=1.0 / 256)
            P.recip(fs.ap()[:, 8:12], fs.ap()[:, 4:8], [fs], [fs])
            P.tt("dve", ho.ap().rearrange("p (h e) -> p h e", h=4), ho.ap().rearrange("p (h e) -> p h e", h=4),
                 fs.ap()[:, 8:12].unsqueeze(2).to_broadcast([128, 4, 256]), ALU.mult, [ho, fs], [ho])
            P.tt("pool", ho.ap(), ho.ap(), mng.ap(), ALU.mult, [ho, mng], [ho])
            P.tt("dve", ym_.ap(), ho.ap(), og.ap(), ALU.mult, [ho, og], [ym_])
            for j in range(8):
                P.tp(pT3.ap()[:, j, :], ym_.ap()[:, j * 128:(j + 1) * 128], identb.ap(), [ym_, identb], [pT3])
            blk = li_ // GRP
            stg = ymstage[d][(fin_cnt[d] // GRP) % 2]
            P.cp("act", stg.ap()[:, :, (li_ % GRP) * 128:(li_ % GRP + 1) * 128], pT3.ap(), [pT3], [stg])
            fin_cnt[d] += 1
            if fin_cnt[d] % GRP == 0:
                for j in range(8):
                    P.dma("pool", ymT[j * 128:(j + 1) * 128, blk * GRP * 128:(blk + 1) * GRP * 128], stg.ap()[:, j, :], r=[stg], w=[k_ymT], key=stg)
    P.barrier()
    P.reset()
    if stop_after == 3:
        P.finish()
        return nc

    P.phase = "p3"
    KT = P.sb("KT", [128, 2, NT], BF16)
    Vr = P.sb("Vr", [128, NTL, 2, 132], BF16)
    P.memset("dve", Vr.ap()[:, :, :, 128:129], 1.0, [Vr])
    for h in range(2):
        P.dma("sp", KT.ap()[:, h, :], kaT[h * 128:(h + 1) * 128, :], r=[k_kaT], w=[KT])
        P.dma("sp", Vr.ap()[:, :, h, 0:128], va[:, h * 128:(h + 1) * 128].rearrange("(t p) e -> p t e", p=128), r=[k_va], w=[Vr])
    QB = 512
    qin = [P.sb("qin%d" % i, [128, 8, QB], BF16) for i in range(2)]
    Pt = [P.sb("Pt%d" % i, [128, QB], BF16) for i in range(3)]
    yat = [[P.sb("yat%d%d" % (i, j), [128, D], BF16) for j in range(4)] for i in range(2)]
    arec = [P.sb("arec%d" % i, [128, 4], F32) for i in range(2)]
    yastage = [P.sb("yast%d" % i, [128, 8, QB], BF16) for i in range(2)]
    pSa = [P.ps("pSa%d" % i, [128, QB], F32, bank=(0, 1, 7)[i]) for i in range(3)]
    pacc1 = [P.ps("pacc%d" % j, [128, 129], F32, bank=2 + j) for j in range(4)]
    pacc = [pacc1, pacc1]
    pT4 = P.ps("pT4", [128, 8, 128], BF16, bank=6)
    sc_att = 128.0 ** -0.5
    its = [(qb, h, kt_) for qb in range(S // QB) for h in range(8) for kt_ in range(NTL)]
    NI = len(its)
    DEPTH = 2

    def issue_qk(i):
        qb, h, kt_ = its[i]
        qi = qin[qb % 2]
        if h == 0 and kt_ == 0:
            for hh in range(8):
                P.dma("sp", qi.ap()[:, hh, :], qaT[hh * 128:(hh + 1) * 128, qb * QB:(qb + 1) * QB], r=[k_qaT], w=[qi])
        ps_ = pSa[i % 3]; pt_ = Pt[i % 3]
        P.mm(ps_.ap(), KT.ap()[:, h // 4, kt_ * 128:(kt_ + 1) * 128], qi.ap()[:, h, :], True, True, [KT, qi], [ps_])
        P.act(pt_.ap(), ps_.ap(), AF.Exp, [ps_], [pt_], scale=sc_att)

    def issue_pv(i):
        qb, h, kt_ = its[i]
        kvh = h // 4
        pt_ = Pt[i % 3]
        acc_ = pacc[h % 2]
        yt = yat[qb % 2]
        for j in range(4):
            P.mm(acc_[j].ap(), pt_.ap()[:, j * 128:(j + 1) * 128], Vr.ap()[:, kt_, kvh, 0:129], kt_ == 0, kt_ == NTL - 1, [pt_, Vr], [acc_[j]])
        if kt_ != NTL - 1:
            return
        ar = arec[h % 2]
        for j in range(4):
            P.recip(ar.ap()[:, j:j + 1], acc_[j].ap()[:, 128:129], [acc_[j]], [ar])
            P.ts("dve", yt[j].ap()[:, h * 128:(h + 1) * 128], acc_[j].ap()[:, 0:128], ar.ap()[:, j:j + 1], None, ALU.mult, None, [acc_[j], ar], [yt[j]])
        if h != 7:
            return
        stg = yastage[qb % 2]
        for j in range(4):
            for hh in range(8):
                P.tp(pT4.ap()[:, hh, :], yt[j].ap()[:, hh * 128:(hh + 1) * 128], identb.ap(), [yt[j], identb], [pT4])
            P.cp("dve", stg.ap()[:, :, j * 128:(j + 1) * 128], pT4.ap(), [pT4], [stg])
        for hh in range(8):
            P.dma("pool", yaT[hh * 128:(hh + 1) * 128, qb * QB:(qb + 1) * QB], stg.ap()[:, hh, :], r=[stg], w=[k_yaT], key=stg)

    for i in range(-DEPTH, NI):
        if i + DEPTH < NI:
            issue_qk(i + DEPTH)
        if i >= 0:
            issue_pv(i)
    P.barrier()
    P.reset()
    if stop_after == 4:
        P.finish()
        return nc

    P.phase = "p4a"
    wpa = P.sb("wpa", [128, 8, D], BF16); wpb = P.sb("wpb", [128, 8, D], BF16); wo = P.sb("wo", [128, 8, D], BF16)
    for wt_, src in ((wpa, w_pa), (wpb, w_pb), (wo, w_o)):
        for hh in range(2):
            P.dma("pool", wt_.ap()[:, :, hh * 512:(hh + 1) * 512], src[:, hh * 512:(hh + 1) * 512].rearrange("(k p) f -> p k f", p=128), w=[wt_])
    ymin = [P.sb("ymin%d" % i, [128, 8, 512], BF16) for i in range(2)]
    yain = [P.sb("yain%d" % i, [128, 8, 512], BF16) for i in range(2)]
    gin = [P.sb("gin%d" % i, [128, 16, 512], BF16) for i in range(2)]
    uT = [P.sb("uT%d" % i, [128, 8, 512], BF16) for i in range(2)]
    u1 = [P.sb("u1_%d" % i, [128, 512], F32) for i in range(2)]
    u2 = [P.sb("u2_%d" % i, [128, 512], F32) for i in range(2)]
    xin = [P.sb("xin%d" % i, [128, D], F32) for i in range(3)]
    ytmp = [P.sb("ytmp%d" % i, [128, D], F32) for i in range(2)]
    x1t = [P.sb("x1t%d" % i, [128, D], F32) for i in range(2)]
    pA = [P.ps("pA%d" % i, [128, 512], F32, bank=i) for i in range(2)]
    pB = [P.ps("pB%d" % i, [128, 512], F32, bank=2 + i) for i in range(2)]
    pY = [P.ps("pY%d" % i, [128, 512], F32, bank=4 + i) for i in range(4)]
    xc = 0
    for b in range(S // 512):
        ym_ = ymin[b % 2]; ya_ = yain[b % 2]; g_ = gin[b % 2]; u_ = uT[b % 2]
        c0_ = b * 512
        P.dma("sp", ym_.ap(), ymT[:, c0_:c0_ + 512].rearrange("(j p) t -> p j t", p=128), r=[k_ymT], w=[ym_])
        P.dma("sp", ya_.ap(), yaT[:, c0_:c0_ + 512].rearrange("(j p) t -> p j t", p=128), r=[k_yaT], w=[ya_])
        P.dma("sp", g_.ap(), gT[:, c0_:c0_ + 512].rearrange("(j p) t -> p j t", p=128), r=[k_gT], w=[g_])
        for fc in range(8):
            pa = pA[fc % 2]; pb_ = pB[fc % 2]; a1 = u1[fc % 2]; a2 = u2[fc % 2]
            for kc in range(8):
                P.mm(pa.ap(), wpa.ap()[:, kc, fc * 128:(fc + 1) * 128], ym_.ap()[:, kc, :], kc == 0, kc == 7, [wpa, ym_], [pa])
            for kc in range(8):
                P.mm(pb_.ap(), wpb.ap()[:, kc, fc * 128:(fc + 1) * 128], ya_.ap()[:, kc, :], kc == 0, kc == 7, [wpb, ya_], [pb_])
            P.tt("dve", a1.ap(), pa.ap(), g_.ap()[:, fc, :], ALU.mult, [pa, g_], [a1])
            P.tt("dve", a2.ap(), pb_.ap(), g_.ap()[:, 8 + fc, :], ALU.mult, [pb_, g_], [a2])
            P.tt("pool", u_.ap()[:, fc, :], a1.ap(), a2.ap(), ALU.add, [a1, a2], [u_])
        for j in range(4):
            xi = xin[xc % 3]; yt_ = ytmp[xc % 2]; xo = x1t[xc % 2]; xc += 1
            r0 = c0_ + j * 128
            P.dma("sp", xi.ap(), x[r0:r0 + 128, :], w=[xi])
            for hh in range(2):
                py = pY[(j * 2 + hh) % 4]
                for kc in range(8):
                    P.mm(py.ap(), u_.ap()[:, kc, j * 128:(j + 1) * 128], wo.ap()[:, kc, hh * 512:(hh + 1) * 512], kc == 0, kc == 7, [u_, wo], [py])
                P.tt("dve", yt_.ap()[:, hh * 512:(hh + 1) * 512], py.ap(), G1bc.ap()[:, hh * 512:(hh + 1) * 512], ALU.mult, [py, G1bc], [yt_])
            P.tt("pool", xo.ap(), yt_.ap(), xi.ap(), ALU.add, [yt_, xi], [xo])
            P.dma("pool", x1d[r0:r0 + 128, :], xo.ap(), r=[xo], w=[k_x1], key=xo)
    P.barrier()
    P.reset()
    if stop_after == 5:
        P.finish()
        return nc

    P.phase = "p4b"
    wg = P.sb("wg", [128, 8, DFF], BF16); wu = P.sb("wu", [128, 8, DFF], BF16); wd = P.sb("wd", [128, 22, D], BF16)
    for wt_, src in ((wg, w_g), (wu, w_u)):
        for c0_ in range(0, DFF, 704):
            P.dma("pool", wt_.ap()[:, :, c0_:c0_ + 704], src[:, c0_:c0_ + 704].rearrange("(k p) f -> p k f", p=128), w=[wt_])
    for hh in range(2):
        P.dma("pool", wd.ap()[:, :, hh * 512:(hh + 1) * 512], w_d[:, hh * 512:(hh + 1) * 512].rearrange("(k p) f -> p k f", p=128), w=[wd])
    fgb = G1bc
    P.dma("sp", fgb.ap(), final_g.partition_broadcast(128), w=[fgb])
    TB = 256
    NJ = TB // 128
    x1in = [P.sb("x1in%d" % i, [128, D], F32) for i in range(2 * NJ)]
    st2 = [P.sb("st2_%d" % i, [128, 4], F32) for i in range(3)]
    junk2 = P.sb("junk2", [128, D], BF16)
    xn2 = [P.sb("xn2_%d" % i, [128, D], BF16) for i in range(2)]
    tf2 = [P.sb("tf2_%d" % i, [128, 8, 128], F32) for i in range(1)] * 2
    h2T = [P.sb("h2T%d" % i, [128, 8, TB], BF16) for i in range(2)]
    aT = [P.sb("aT%d" % i, [128, 22, TB], BF16) for i in range(1)] * 2
    sg = [P.sb("sg%d" % i, [128, TB], F32) for i in range(2)]
    ftmp = [P.sb("ftmp%d" % i, [128, D], F32) for i in range(1)] * 2
    pT5 = [P.ps("pT5_%d" % i, [128, 8, 128], BF16, bank=i) for i in range(2)]
    pG = [P.ps("pG%d" % i, [128, TB], F32, bank=2 + i) for i in range(2)]
    pU = [P.ps("pU%d" % i, [128, TB], F32, bank=4 + i) for i in range(2)]
    pD = [P.ps("pD%d" % i, [128, 512], F32, bank=6 + i) for i in range(2)] * 2
    tcn = [0]
    NB4 = S // TB
    xsets = {}

    def prologue_a(b):
        xs_ = []
        for j in range(NJ):
            r0 = b * TB + j * 128
            xi = x1in[(b % 2) * NJ + j]; ss = st2[tcn[0] % 3]; xb = xn2[j]; tcn[0] += 1
            xs_.append(xi)
            P.dma("sp", xi.ap(), x1d[r0:r0 + 128, :], r=[k_x1], w=[xi])
            P.act(junk2.ap(), xi.ap(), AF.Square, [xi], [junk2, ss], accum_out=ss.ap()[:, 0:1])
            rstd_ops(ss, D)
            P.ts("dve", xb.ap(), xi.ap(), ss.ap()[:, 2:3], None, ALU.mult, None, [xi, ss], [xb])
        xsets[b] = xs_

    def prologue_b(b):
        h2 = h2T[b % 2]
        for j in range(NJ):
            xb = xn2[j]; pt = pT5[j % 2]; tf = tf2[j % 2]
            for kc in range(8):
                P.tp(pt.ap()[:, kc, :], xb.ap()[:, kc * 128:(kc + 1) * 128], identb.ap(), [xb, identb], [pt])
            P.tt("dve", tf.ap(), pt.ap(), A2.ap().unsqueeze(2).to_broadcast([128, 8, 128]), ALU.mult, [pt, A2], [tf])
            P.tt("pool", h2.ap()[:, :, j * 128:(j + 1) * 128], tf.ap(), modT.ap()[:, 24:32, 0:1].to_broadcast([128, 8, 128]), ALU.add, [tf, modT], [h2])

    def gateup(b):
        h2 = h2T[b % 2]; a_ = aT[b % 2]
        for fc in range(22):
            pg = pG[fc % 2]; pu = pU[fc % 2]; s_ = sg[fc % 2]
            for kc in range(8):
                P.mm(pg.ap(), wg.ap()[:, kc, fc * 128:(fc + 1) * 128], h2.ap()[:, kc, :], kc == 0, kc == 7, [wg, h2], [pg])
            for kc in range(8):
                P.mm(pu.ap(), wu.ap()[:, kc, fc * 128:(fc + 1) * 128], h2.ap()[:, kc, :], kc == 0, kc == 7, [wu, h2], [pu])
            P.act(s_.ap(), pg.ap(), AF.Silu, [pg], [s_])
            P.tt("dve", a_.ap()[:, fc, :], pu.ap(), s_.ap(), ALU.mult, [pu, s_], [a_])

    def down_tail(b):
        a_ = aT[b % 2]
        xs_ = xsets.pop(b)
        for j in range(NJ):
            r0 = b * TB + j * 128
            xi = xs_[j]; ft = ftmp[j % 2]; x2 = xi; o_ = xi; ss = st2[tcn[0] % 3]; tcn[0] += 1
            for hh in range(2):
                pd_ = pD[(j * 2 + hh) % 4]
                for fc in range(22):
                    P.mm(pd_.ap(), a_.ap()[:, fc, j * 128:(j + 1) * 128], wd.ap()[:, fc, hh * 512:(hh + 1) * 512], fc == 0, fc == 21, [a_, wd], [pd_])
                P.tt("dve", ft.ap()[:, hh * 512:(hh + 1) * 512], pd_.ap(), G2bc.ap()[:, hh * 512:(hh + 1) * 512], ALU.mult, [pd_, G2bc], [ft])
            P.tt("pool", x2.ap(), ft.ap(), xi.ap(), ALU.add, [ft, xi], [x2])
            P.act(junk2.ap(), x2.ap(), AF.Square, [x2], [junk2, ss], accum_out=ss.ap()[:, 0:1])
            rstd_ops(ss, D)
            P.ts("dve", ft.ap(), x2.ap(), ss.ap()[:, 2:3], None, ALU.mult, None, [x2, ss], [ft])
            P.tt("pool", o_.ap(), ft.ap(), fgb.ap(), ALU.mult, [ft, fgb], [o_])
            P.dma("sp", out[r0:r0 + 128, :], o_.ap(), r=[o_], w=[k_out])

    prologue_a(0)
    prologue_b(0)
    for b in range(NB4):
        if b + 1 < NB4:
            prologue_a(b + 1)
        gateup(b)
        if b + 1 < NB4:
            prologue_b(b + 1)
        down_tail(b)
    P.finish()
    return nc


def make_consts(S):
    cst = np.zeros((128, 512), np.float32)
    cst[:, 0:128] = np.eye(128, dtype=np.float32)
    s = np.arange(128)[:, None]
    t = np.arange(128)[None, :]
    cst[:, 128:256] = (s <= t).astype(np.float32)
    cst[:, 256:384] = (s >= t).astype(np.float32)
    cst[0, 384:512] = 1.0
    rows = S // 64
    row = np.repeat(np.arange(rows, dtype=np.float32), 64)
    col = np.tile(np.arange(64, dtype=np.float32), rows)
    inv = (np.float32(10000.0) ** (-np.arange(32, dtype=np.float32) / np.float32(32))).astype(np.float32)
    ang = np.concatenate([row[:, None] * inv, col[:, None] * inv], axis=-1).astype(np.float32)
    rope = np.concatenate([np.cos(ang), np.sin(ang)], axis=-1).astype(np.float32)
    return cst, rope


def core_inputs(b, inp, S, cst, rope):
    f = lambda a: np.ascontiguousarray(a, dtype=np.float32)
    return {
        "x": f(inp["x"][b, :S]), "c": f(inp["c"][b]), "ctx": f(inp["ctx"][b]), "c_ctx": f(inp["c_ctx"]),
        "w_mod": f(inp["w_mod"][0]), "b_mod": f(inp["b_mod"][0]), "norm1_g": f(inp["norm1_g"][0]),
        "norm2_g": f(inp["norm2_g"][0]), "w_in": f(inp["w_in"][0]), "gate_b": f(inp["gate_b"][0]),
        "conv_w": f(inp["conv_w"][0]), "conv_b": f(inp["conv_b"][0]), "m_norm_g": f(inp["m_norm_g"][0]),
        "q_norm_g": f(inp["q_norm_g"][0]), "k_norm_g": f(inp["k_norm_g"][0]), "w_pa": f(inp["w_pa"][0]),
        "w_pb": f(inp["w_pb"][0]), "w_o": f(inp["w_o"][0]), "w_ffn_gate": f(inp["w_ffn_gate"][0]),
        "w_ffn_up": f(inp["w_ffn_up"][0]), "w_ffn_down": f(inp["w_ffn_down"][0]), "final_g": f(inp["final_g"]),
        "cst": cst, "rope": rope,
    }


_CACHE = {}


def kernel(**inputs):
    S = inputs["x"].shape[1]
    B = inputs["x"].shape[0]
    if S not in _CACHE:
        _CACHE[S] = build(S)
    nc = _CACHE[S]
    cst, rope = make_consts(S)
    in_maps = [core_inputs(b, inputs, S, cst, rope) for b in range(B)]
    res = run_bass_kernel_spmd(nc, in_maps, core_ids=list(range(B)))
    return np.stack([np.asarray(r["out"], dtype=np.float32) for r in res.results], axis=0)
```

```python
import numpy as np
from contextlib import ExitStack
import concourse.bass as bass
import concourse.mybir as mybir
from concourse.bass_utils import run_bass_kernel_spmd

F32 = mybir.dt.float32
BF16 = mybir.dt.bfloat16
AF = mybir.ActivationFunctionType
ALU = mybir.AluOpType
AX = mybir.AxisListType

SEM_EPOCH = 12000
DMA_EPOCH = 1500


class T:
    def __init__(self, name, h=None):
        self.name = name
        self.h = h
        self.last_w = None
        self.readers = []
        self.epochs = []

    def ap(self):
        return self.h if isinstance(self.h, bass.AP) else self.h[:]


class TV:
    def __init__(self, base, h):
        self.base = base
        self.h = h
        self.name = base.name

    def ap(self):
        return self.h

    last_w = property(lambda self: self.base.last_w, lambda self, v: setattr(self.base, "last_w", v))
    readers = property(lambda self: self.base.readers, lambda self, v: setattr(self.base, "readers", v))
    epochs = property(lambda self: self.base.epochs)


def _shape(v, shape):
    if len(shape) == 2:
        return v
    if len(shape) == 3:
        return v.rearrange("p (a b) -> p a b", a=shape[1])
    if len(shape) == 4:
        return v.rearrange("p (a b c) -> p a b c", a=shape[1], b=shape[2])
    raise ValueError(shape)


class Op:
    __slots__ = ("eng", "fn", "deps", "need_inc", "sig", "dma_key", "waits", "idx", "phase")


class Prog:
    ENGS = ("pe", "act", "dve", "pool", "sp")

    def __init__(self, nc, arena_bytes=0):
        self.nc = nc
        self.stack = ExitStack()
        self.ops = {e: [] for e in self.ENGS}
        self.nsem = 0
        self.sem_names = []
        self.all_ops = 0
        self.keys = []
        self.free_sw = []
        self.free_hw = []
        self.scopes = False
        self.bar = {e: None for e in self.ENGS}
        self.arena = None
        if arena_bytes:
            self.arena = self.stack.enter_context(nc.sbuf_tensor("arena", [128, arena_bytes], mybir.dt.uint8))
            self.arena_bytes = arena_bytes
            self.off = 0
            self.mark = 0
            self.banks = [self.stack.enter_context(nc.psum_tensor("bank%d" % i, [128, 512], F32)) for i in range(8)]

    def sb(self, name, shape, dtype):
        if self.arena is None:
            h = self.stack.enter_context(self.nc.sbuf_tensor(name, list(shape), dtype))
            return T(name, h)
        esz = 4 if dtype == F32 else 2
        n = 1
        for d in shape[1:]:
            n *= d
        nb = (n * esz + 31) // 32 * 32
        assert self.off + nb <= self.arena_bytes, ("SBUF arena overflow", name, self.off, nb)
        v = self.arena[0:shape[0], self.off:self.off + n * esz].bitcast(dtype)
        self.off += nb
        v = _shape(v, shape)
        return T(name, v)

    def ps(self, name, shape, dtype, bank=None, off=0):
        if bank is None:
            h = self.stack.enter_context(self.nc.psum_tensor(name, list(shape), dtype))
            return T(name, h)
        n = 1
        for d in shape[1:]:
            n *= d
        esz = 4 if dtype == F32 else 2
        nf = (n * esz + 3) // 4
        assert off + nf <= 512
        v = self.banks[bank][0:shape[0], off:off + nf]
        if dtype != F32:
            v = v.bitcast(dtype)
        return T(name, _shape(v, shape))

    def set_mark(self):
        self.mark = self.off

    def reset(self):
        self.off = self.mark

    def barrier(self):
        last = [self.ops[e][-1] for e in self.ENGS if self.ops[e]]
        pairs = []
        for k in self.keys:
            for slot, cnt in k.epochs:
                pairs.append((slot, 16 * cnt))
            if k.epochs and not k.name.startswith("OUT"):
                (self.free_sw if k.name.endswith("_sw") else self.free_hw).append(tuple(k.epochs[-1]))
                k.epochs = []
        self.keys = [k for k in self.keys if k.epochs]
        for e in self.ENGS:
            self.bar[e] = (last, pairs)

    def T(self, name):
        return T(name)

    def _new_sem(self, name):
        self.sem_names.append(name)
        self.nsem += 1
        return self.nsem - 1

    def _record(self, eng, fn, r, w, dma_key=None):
        op = Op()
        op.eng = eng
        op.fn = fn
        op.need_inc = False
        op.sig = None
        op.dma_key = dma_key
        op.idx = self.all_ops
        op.phase = getattr(self, "phase", "p")
        self.all_ops += 1
        waits = {}
        deps = {}

        def add_dep(d, raw):
            if d is None:
                return
            if d.dma_key is not None:
                k = d.dma_key
                for slot, cnt in k.epochs:
                    v = 16 * cnt
                    if waits.get(slot, 0) < v:
                        waits[slot] = v
                return
            if d.eng == eng and eng == "pe":
                return
            deps[id(d)] = d

        if self.bar[eng] is not None:
            last, keys = self.bar[eng]
            self.bar[eng] = None
            for d in last:
                if d.dma_key is None:
                    add_dep(d, True)
            for slot, v in keys:
                waits[slot] = max(waits.get(slot, 0), v)
        for t in r:
            add_dep(t.last_w, True)
        for t in w:
            add_dep(t.last_w, False)
            for rd in t.readers:
                add_dep(rd, False)
        for t in r:
            t.readers.append(op)
        for t in w:
            t.last_w = op
            t.readers = []
        for d in deps.values():
            d.need_inc = True
        op.deps = list(deps.values())
        op.waits = waits
        if dma_key is not None:
            if not dma_key.epochs:
                self.keys.append(dma_key)
                free = self.free_sw if dma_key.name.endswith("_sw") else self.free_hw
                if free and free[-1][1] < DMA_EPOCH:
                    slot, base = free.pop()
                    dma_key.epochs.append([slot, base])
            if not dma_key.epochs or dma_key.epochs[-1][1] >= 2 * DMA_EPOCH:
                dma_key.epochs.append([self._new_sem("d_" + dma_key.name), 0])
            dma_key.epochs[-1][1] += 1
            op.sig = dma_key.epochs[-1][0]
        self.ops[eng].append(op)
        return op

    def op(self, eng, fn, r=(), w=()):
        return self._record(eng, fn, r, w)

    def dma(self, eng, out, in_, r=(), w=(), key=None, slow=False):
        if key is None:
            key = w[0]
        if eng == "pool":
            base = key.base if isinstance(key, TV) else key
            if not hasattr(base, "_sw"):
                base._sw = T(base.name + "_sw")
            key = base._sw
        if slow:
            return self._record(eng, lambda e: e.dma_start(out=out, in_=in_, allow_slow_non_contiguous=True), r, w, dma_key=key)
        return self._record(eng, lambda e: e.dma_start(out=out, in_=in_), r, w, dma_key=key)

    def finish(self):
        nc = self.nc
        for eng in ("pe", "act", "dve", "pool"):
            slot = None
            cnt = SEM_EPOCH
            for op in self.ops[eng]:
                if op.dma_key is not None or not op.need_inc:
                    continue
                if cnt >= SEM_EPOCH:
                    slot = self._new_sem("e_%s" % eng)
                    cnt = 0
                cnt += 1
                op.sig = (slot, cnt)
        sems = [self.stack.enter_context(nc.semaphore(n + "_%d" % i)) for i, n in enumerate(self.sem_names)]
        self.n_instr = {e: len(v) for e, v in self.ops.items()}
        final_waits = {}
        for eng in self.ENGS:
            for op in self.ops[eng]:
                if op.dma_key is not None and op.dma_key.name.startswith("OUT"):
                    for slot, cnt in op.dma_key.epochs:
                        final_waits[slot] = 16 * cnt
        with nc.Block() as block:
            def run(eng_name, e):
                waited = {}
                cur = [None, None]
                for op in self.ops[eng_name]:
                    if self.scopes and op.phase != cur[0]:
                        if cur[1] is not None:
                            cur[1].__exit__(None, None, None)
                        cur[0] = op.phase
                        cur[1] = nc.named_scope(op.phase)
                        cur[1].__enter__()
                    for d in op.deps:
                        slot, v = d.sig
                        if waited.get(slot, 0) < v:
                            waited[slot] = v
                            e.wait_ge(sems[slot], v)
                    for slot, v in op.waits.items():
                        if waited.get(slot, 0) < v:
                            waited[slot] = v
                            e.wait_ge(sems[slot], v)
                    inst = op.fn(e)
                    if op.dma_key is not None:
                        inst.then_inc(sems[op.sig], 16)
                    elif op.need_inc:
                        inst.then_inc(sems[op.sig[0]], 1)
                if cur[1] is not None:
                    cur[1].__exit__(None, None, None)
                if eng_name == "sp":
                    for slot, v in final_waits.items():
                        e.wait_ge(sems[slot], v)

            @block.tensor
            def _(e):
                run("pe", e)

            @block.scalar
            def _(e):
                run("act", e)

            @block.vector
            def _(e):
                run("dve", e)

            @block.gpsimd
            def _(e):
                run("pool", e)

            @block.sync
            def _(e):
                run("sp", e)
        self.stack.close()


D = 1024
CT = 256
DIN = 7696
DFF = 2816
EPS = 1e-6
NEG = -1.0e30
LN16 = 2.772588722239781


class OpsMixin:
    def mm(self, out, lhsT, rhs, start, stop, r, w):
        self.op("pe", lambda e: e.matmul(out, lhsT, rhs, start=start, stop=stop), r, w)

    def tp(self, out, in_, ident, r, w):
        self.op("pe", lambda e: e.transpose(out, in_, ident), r, w)

    def act(self, out, in_, func, r, w, **kw):
        self.op("act", lambda e: e.activation(out, in_, func, **kw), r, w)

    def tt(self, eng, out, a, b, op, r, w):
        self.op(eng, lambda e: e.tensor_tensor(out, a, b, op), r, w)

    def ts(self, eng, out, a, s1, s2, op0, op1, r, w):
        if op1 is None:
            self.op(eng, lambda e: e.tensor_scalar(out, a, s1, s2, op0), r, w)
        else:
            self.op(eng, lambda e: e.tensor_scalar(out, a, s1, s2, op0, op1), r, w)

    def stt(self, eng, out, a, s, b, op0, op1, r, w):
        self.op(eng, lambda e: e.scalar_tensor_tensor(out, a, s, b, op0, op1), r, w)

    def cp(self, eng, out, in_, r, w):
        if eng == "act":
            self.op("act", lambda e: e.activation(out, in_, AF.Copy), r, w)
        else:
            self.op(eng, lambda e: e.tensor_copy(out, in_), r, w)

    def memset(self, eng, out, val, w):
        self.op(eng, lambda e: e.memset(out, val), (), w)

    def recip(self, out, in_, r, w):
        self.op("dve", lambda e: e.reciprocal(out, in_), r, w)


class KProg(Prog, OpsMixin):
    pass


def conv_blocks(length):
    out = []
    s = 0
    while s < length:
        n = min(508, length - s)
        out.append((s, n))
        s += n
    return out


def build(S, debug=False, stop_after=None, scopes=False):
    NT = S + CT
    NTL = NT // 128
    NL = S // 128
    NCT = CT // 128
    nc = bass.Bass("TRN2", target_bir_lowering=False)
    P = KProg(nc, arena_bytes=206 * 1024)
    P.scopes = scopes
    P.phase = "p0"

    def din(name, shape, dt=F32):
        return nc.dram_tensor(name, list(shape), dt, kind="ExternalInput").ap()

    def dscr(name, shape, dt):
        kind = "ExternalOutput" if debug else "Internal"
        return nc.dram_tensor(name, list(shape), dt, kind=kind).ap()

    x = din("x", [S, D]); c = din("c", [D]); ctx = din("ctx", [CT, D]); c_ctx = din("c_ctx", [D])
    w_mod = din("w_mod", [D, 6 * D]); b_mod = din("b_mod", [6 * D])
    norm1_g = din("norm1_g", [D]); norm2_g = din("norm2_g", [D])
    w_in = din("w_in", [D, DIN]); gate_b = din("gate_b", [16])
    conv_w = din("conv_w", [5, 2 * D]); conv_b = din("conv_b", [2 * D])
    m_norm_g = din("m_norm_g", [D]); q_norm_g = din("q_norm_g", [128]); k_norm_g = din("k_norm_g", [128])
    w_pa = din("w_pa", [D, D]); w_pb = din("w_pb", [D, D]); w_o = din("w_o", [D, D])
    w_g = din("w_ffn_gate", [D, DFF]); w_u = din("w_ffn_up", [D, DFF]); w_d = din("w_ffn_down", [DFF, D])
    final_g = din("final_g", [D])
    cst = din("cst", [128, 512]); rope = din("rope", [S, 128])
    out = nc.dram_tensor("out", [S, D], F32, kind="ExternalOutput").ap()

    mqT = dscr("mqT", [D, NT], BF16); mkT = dscr("mkT", [D, NT], BF16)
    mv = dscr("mv", [NT, D], BF16); osig = dscr("osig", [S, D], BF16)
    qaT = dscr("qaT", [D, S], BF16); kaT = dscr("kaT", [256, NT], BF16); va = dscr("va", [NT, 256], BF16)
    gT = dscr("gT", [2 * D, S], BF16)
    hdir = dscr("hdir", [2, S, D], F32)
    ymT = dscr("ymT", [D, S], BF16); yaT = dscr("yaT", [D, S], BF16)
    x1d = dscr("x1d", [S, D], F32)
    k_mqT = P.T("mqT"); k_mkT = P.T("mkT"); k_mv = P.T("mv"); k_osig = P.T("osig"); k_qaT = P.T("qaT")
    k_kaT = P.T("kaT"); k_va = P.T("va"); k_gT = P.T("gT"); k_hdir = P.T("hdir"); k_ymT = P.T("ymT")
    k_yaT = P.T("yaT"); k_x1 = P.T("x1d"); k_out = P.T("OUT")
    dbg = {}

    identf = P.sb("identf", [128, 128], F32)
    identb = P.sb("identb", [128, 128], BF16)
    trif = P.sb("trif", [128, 2, 128], F32)
    trib = P.sb("trib", [128, 2, 128], BF16)
    e0f = P.sb("e0f", [128, 128], F32)
    onesf = P.sb("onesf", [128, 128], F32)
    onesb = P.sb("onesb", [128, 2], BF16)
    modT = P.sb("modT", [128, 48, 2], F32)
    A1 = P.sb("A1", [128, 8, 2], F32)
    A2 = P.sb("A2", [128, 8], F32)
    G1bc = P.sb("G1bc", [128, D], F32)
    G2bc = P.sb("G2bc", [128, D], F32)
    Gd = [P.sb("Gd%d" % d, [128, NTL, 8], F32) for d in range(2)]
    P.dma("sp", identf.ap(), cst[:, 0:128], w=[identf])
    P.dma("sp", trif.ap(), cst[:, 128:384].rearrange("p (a b) -> p a b", a=2), w=[trif])
    P.dma("sp", e0f.ap(), cst[:, 384:512], w=[e0f])
    P.cp("dve", identb.ap(), identf.ap(), [identf], [identb])
    P.cp("dve", trib.ap(), trif.ap(), [trif], [trib])
    P.memset("dve", onesf.ap(), 1.0, [onesf])
    P.memset("dve", onesb.ap(), 1.0, [onesb])
    P.set_mark()

    def tile_of(d, k):
        if d == 0:
            return k
        if k < NCT:
            return NCT - 1 - k
        return NTL + NCT - 1 - k

    step_of = [{tile_of(d, k): k for k in range(NTL)} for d in range(2)]

    sc = P.sb("sc", [128, 8, 2], F32)
    scs = P.sb("scs", [128, 8, 2], F32)
    bmod = P.sb("bmod", [128, 48], F32)
    n1g = P.sb("n1g", [128, 8], F32)
    n2g = P.sb("n2g", [128, 8], F32)
    wm = [P.sb("wm%d" % i, [128, 8, 512], F32) for i in range(2)]
    P.dma("sp", sc.ap()[:, :, 0], c.rearrange("(k p) -> p k", p=128), w=[sc], slow=True)
    P.dma("sp", sc.ap()[:, :, 1], c_ctx.rearrange("(k p) -> p k", p=128), w=[sc], slow=True)
    P.dma("sp", bmod.ap(), b_mod.rearrange("(k p) -> p k", p=128), w=[bmod], slow=True)
    P.dma("sp", n1g.ap(), norm1_g.rearrange("(k p) -> p k", p=128), w=[n1g], slow=True)
    P.dma("sp", n2g.ap(), norm2_g.rearrange("(k p) -> p k", p=128), w=[n2g], slow=True)
    P.act(scs.ap(), sc.ap(), AF.Silu, [sc], [scs])
    pmod = P.ps("pmod", [128, 48, 2], F32, bank=0)
    for pc in range(12):
        wt = wm[pc % 2]
        P.dma("sp", wt.ap(), w_mod[:, pc * 512:(pc + 1) * 512].rearrange("(k p) f -> p k f", p=128), w=[wt])
        for fl in range(4):
            fc = pc * 4 + fl
            for kc in range(8):
                P.mm(pmod.ap()[:, fc, :], wt.ap()[:, kc, fl * 128:(fl + 1) * 128], scs.ap()[:, kc, :],
                     kc == 0, kc == 7, [wt, scs], [pmod])
    P.tt("dve", modT.ap(), pmod.ap(), bmod.ap().unsqueeze(2).to_broadcast([128, 48, 2]), ALU.add, [pmod, bmod], [modT])
    tmpa = P.sb("tmpa", [128, 8, 2], F32)
    P.ts("dve", tmpa.ap(), modT.ap()[:, 8:16, :], 1.0, None, ALU.add, None, [modT], [tmpa])
    P.tt("dve", A1.ap(), tmpa.ap(), n1g.ap().unsqueeze(2).to_broadcast([128, 8, 2]), ALU.mult, [tmpa, n1g], [A1])
    tmpb = P.sb("tmpb", [128, 8], F32)
    P.ts("dve", tmpb.ap(), modT.ap()[:, 32:40, 0], 1.0, None, ALU.add, None, [modT], [tmpb])
    P.tt("dve", A2.ap(), tmpb.ap(), n2g.ap(), ALU.mult, [tmpb, n2g], [A2])
    dg = [P.sb("dg%d" % i, [128, 128], F32) for i in range(2)]
    for gi, (Gbc, base) in enumerate(((G1bc, 16), (G2bc, 40))):
        pb = [P.ps("pbc%d" % h, [128, 512], F32, bank=1 + h) for h in range(2)]
        for kc in range(8):
            dgt = dg[kc % 2]
            P.ts("dve", dgt.ap(), identf.ap(), modT.ap()[:, base + kc, 0:1], None, ALU.mult, None, [identf, modT], [dgt])
            P.mm(pb[kc // 4].ap()[:, (kc % 4) * 128:(kc % 4 + 1) * 128], onesf.ap(), dgt.ap(), True, True, [onesf, dgt], [pb[kc // 4]])
        for h in range(2):
            P.cp("dve", Gbc.ap()[:, h * 512:(h + 1) * 512], pb[h].ap(), [pb[h]], [Gbc])
    if debug:
        dbg["modT"] = nc.dram_tensor("d_modT", [128, 96], F32, kind="ExternalOutput").ap()
        P.dma("sp", dbg["modT"], modT.ap().rearrange("p a b -> p (a b)"), r=[modT], w=[P.T("OUTd0")])
        dbg["G1bc"] = nc.dram_tensor("d_G1bc", [128, D], F32, kind="ExternalOutput").ap()
        P.dma("sp", dbg["G1bc"], G1bc.ap(), r=[G1bc], w=[P.T("OUTd1")])
    P.barrier()
    P.reset()
    if stop_after == 0:
        P.finish()
        return nc

    P.phase = "p1a"
    hT = P.sb("hT", [128, 8, NT], BF16)
    hTt = [P.T("hT%d" % i) for i in range(NTL)]
    mark2 = P.off
    xt = [P.sb("xt%d" % i, [128, D], F32) for i in range(3)]
    junk = P.sb("junk", [128, D], BF16)
    st = [P.sb("st%d" % i, [128, 4], F32) for i in range(3)]
    xn = [P.sb("xn%d" % i, [128, D], BF16) for i in range(2)]
    tmpf = [P.sb("tmpf%d" % i, [128, 8, 128], F32) for i in range(2)]
    pT = [P.ps("pT%d" % i, [128, 8, 128], BF16, bank=i) for i in range(2)]

    def rstd_ops(stt_, n):
        P.act(stt_.ap()[:, 1:2], stt_.ap()[:, 0:1], AF.Sqrt, [stt_], [stt_], bias=EPS, scale=1.0 / n)
        P.recip(stt_.ap()[:, 2:3], stt_.ap()[:, 1:2], [stt_], [stt_])

    for i in range(NTL):
        xs = xt[i % 3]; ss = st[i % 3]; xb = xn[i % 2]; pt = pT[i % 2]; tf = tmpf[i % 2]
        src = ctx[i * 128:(i + 1) * 128, :] if i < NCT else x[(i - NCT) * 128:(i - NCT + 1) * 128, :]
        P.dma("sp", xs.ap(), src, w=[xs])
        P.act(junk.ap(), xs.ap(), AF.Square, [xs], [junk, ss], accum_out=ss.ap()[:, 0:1])
        rstd_ops(ss, D)
        P.ts("dve", xb.ap(), xs.ap(), ss.ap()[:, 2:3], None, ALU.mult, None, [xs, ss], [xb])
        for kc in range(8):
            P.tp(pt.ap()[:, kc, :], xb.ap()[:, kc * 128:(kc + 1) * 128], identb.ap(), [xb, identb], [pt])
        m = 1 if i < NCT else 0
        P.tt("dve", tf.ap(), pt.ap(), A1.ap()[:, :, m:m + 1].to_broadcast([128, 8, 128]), ALU.mult, [pt, A1], [tf])
        P.tt("pool", hT.ap()[:, :, i * 128:(i + 1) * 128], tf.ap(), modT.ap()[:, 0:8, m:m + 1].to_broadcast([128, 8, 128]), ALU.add,
             [tf, modT], [hTt[i]])
    if debug:
        dbg["hT"] = nc.dram_tensor("d_hT", [128, 8 * NT], BF16, kind="ExternalOutput").ap()
        P.dma("sp", dbg["hT"], hT.ap().rearrange("p a b -> p (a b)"), r=hTt, w=[P.T("OUTd2")])
    if stop_after == 1:
        P.finish()
        return nc
    P.barrier()
    P.off = mark2

    P.phase = "p1b"
    wb = [P.sb("wb%d" % i, [128, 8, 512], BF16) for i in range(2)]
    wcnt = [0]

    wgroups = [(g * 512, 512) for g in range(4)] + [(5648 + g * 512, 512) for g in range(4)] + \
              [(2048, 512), (2560, 512), (3072, 512), (3584, 512), (4096, 16), (4112, 512), (4624, 512), (5136, 512)]
    wtiles = {}

    def issue_w(gi):
        if gi >= len(wgroups) or gi in wtiles:
            return
        c0, ncols = wgroups[gi]
        t = wb[gi % 2]
        P.dma("pool", t.ap()[:, :, 0:ncols], w_in[:, c0:c0 + ncols].rearrange("(k p) f -> p k f", p=128), w=[t])
        wtiles[gi] = t

    def load_w(c0, ncols, prefetch=True):
        gi = wcnt[0]
        wcnt[0] += 1
        assert wgroups[gi] == (c0, ncols), (gi, c0, ncols)
        issue_w(gi)
        t = wtiles[gi]
        if prefetch:
            issue_w(gi + 1)
        return t

    pz = [P.ps("pz%d" % i, [128, 512], F32, bank=2 + i) for i in range(4)]
    pzc = [0]

    def next_pz():
        t = pz[pzc[0] % 4]
        pzc[0] += 1
        return t

    cw = P.sb("cw", [128, 16, 5], F32)
    cb = P.sb("cb", [128, 16], F32)
    gbb = P.sb("gbb", [128, 16], F32)
    qgb = P.sb("qgb", [128, 128], F32)
    kgb = P.sb("kgb", [128, 128], F32)
    for j in range(5):
        P.dma("sp", cw.ap()[:, :, j], conv_w[j].rearrange("(c p) -> p c", p=128), w=[cw], slow=True)
    P.dma("sp", cb.ap(), conv_b.rearrange("(c p) -> p c", p=128), w=[cb], slow=True)
    P.dma("sp", gbb.ap(), gate_b.partition_broadcast(128), w=[gbb])
    P.dma("sp", qgb.ap(), q_norm_g.partition_broadcast(128), w=[qgb])
    P.dma("sp", kgb.ap(), k_norm_g.partition_broadcast(128), w=[kgb])

    def hts(g0, g1):
        return hTt[g0 // 128:(g1 - 1) // 128 + 1]

    stg8 = [P.sb("stg8_%d" % i, [128, 512], BF16) for i in range(8)]
    scnt = [0]

    def next_stage():
        t = stg8[scnt[0] % 8]
        scnt[0] += 1
        return t

    sqc = [0]

    def stq(from_act):
        sqc[0] += 1
        m = sqc[0] % 3
        if m == 0:
            return "sp"
        if m == 1:
            return "act" if from_act else "sp"
        return "pool"

    mark3 = P.off
    Zs = [P.sb("Zs%d" % i, [128, 512], F32) for i in range(4)]
    acc = [P.sb("acc%d" % i, [128, 508], F32) for i in range(4)]
    fcnt = [0]
    items = []
    for grp in range(4):
        for pair in range(2):
            chs = [grp * 4 + pair * 2, grp * 4 + pair * 2 + 1]
            is_k = chs[0] >= 8
            seqs = [(CT, S)] + ([(0, CT)] if is_k else [])
            for (s0, ln) in seqs:
                for (bs, n) in conv_blocks(ln):
                    items.append((grp, chs, is_k, s0, ln, bs, n))
    fw = {}

    def f_stage1(it):
        grp, chs, is_k, s0, ln, bs, n = it
        if grp not in fw:
            fw[grp] = load_w(grp * 512, 512)
        wt = fw[grp]
        w0 = max(0, bs - 2); w1 = min(ln, bs + n + 2)
        ncol = w1 - w0
        off = 2 - (bs - w0)
        hr = hts(s0 + w0, s0 + w1)
        st_ = []
        for ch in chs:
            cl = ch % 4
            i = fcnt[0]; fcnt[0] += 1
            z = Zs[i % 4]; a = acc[i % 4]
            p = next_pz()
            st_.append((ch, z, a))
            for kc in range(8):
                P.mm(p.ap()[:, 0:ncol], wt.ap()[:, kc, cl * 128:(cl + 1) * 128], hT.ap()[:, kc, s0 + w0:s0 + w1],
                     kc == 0, kc == 7, [wt] + hr, [p])
            if bs == 0:
                P.memset("dve", z.ap()[:, 0:2], 0.0, [z])
            if bs + n == ln:
                P.memset("dve", z.ap()[:, 2 + n:4 + n], 0.0, [z])
            P.cp("act", z.ap()[:, off:off + ncol], p.ap()[:, 0:ncol], [p], [z])
        return (it, st_)

    def f_stage2(rec):
        (grp, chs, is_k, s0, ln, bs, n), st_ = rec
        dst, kdst = (mkT, k_mkT) if is_k else (mqT, k_mqT)
        for j in range(5):
            for (ch, z, a) in st_:
                if j == 0:
                    P.ts("dve", a.ap()[:, 0:n], z.ap()[:, 0:n], cw.ap()[:, ch, 0:1], None, ALU.mult, None, [z, cw], [a])
                else:
                    P.stt("dve", a.ap()[:, 0:n], z.ap()[:, j:j + n], cw.ap()[:, ch, j:j + 1], a.ap()[:, 0:n], ALU.mult, ALU.add, [z, cw, a], [a])
        for (ch, z, a) in st_:
            o = next_stage()
            P.act(o.ap()[:, 0:n], a.ap()[:, 0:n], AF.Silu, [a, cb], [o], bias=cb.ap()[:, ch:ch + 1])
            r0 = (ch % 8) * 128
            P.dma(stq(True), dst[r0:r0 + 128, s0 + bs:s0 + bs + n], o.ap()[:, 0:n], r=[o], w=[kdst], key=o)

    prev = None
    for it in items:
        cur = f_stage1(it)
        if prev is not None:
            f_stage2(prev)
        prev = cur
    f_stage2(prev)
    P.barrier()
    P.off = mark3

    if stop_after == "F":
        P.finish()
        return nc
    P.phase = "p1b_G"
    gcnt = 0
    for grp in range(4):
        wt = load_w(5648 + grp * 512, 512)
        for cl in range(4):
            ch = grp * 4 + cl
            for b0 in range(0, S, 512):
                p = next_pz()
                o = next_stage()
                hr = hts(CT + b0, CT + b0 + 512)
                for kc in range(8):
                    P.mm(p.ap(), wt.ap()[:, kc, cl * 128:(cl + 1) * 128], hT.ap()[:, kc, CT + b0:CT + b0 + 512], kc == 0, kc == 7, [wt] + hr, [p])
                P.act(o.ap(), p.ap(), AF.Sigmoid, [p], [o])
                P.dma(stq(True), gT[ch * 128:(ch + 1) * 128, b0:b0 + 512], o.ap(), r=[o], w=[k_gT], key=o)

    if stop_after == "G":
        P.finish()
        return nc
    P.phase = "p1b_mv"
    tcnt = [0]

    def tok_mm(wt, ncols, i):
        p = next_pz()
        for kc in range(8):
            P.mm(p.ap()[:, 0:ncols], hT.ap()[:, kc, i * 128:(i + 1) * 128], wt.ap()[:, kc, 0:ncols], kc == 0, kc == 7, [wt, hTt[i]], [p])
        return p

    for grp in range(2):
        wt = load_w(2048 + grp * 512, 512)
        for i in range(NTL):
            p = tok_mm(wt, 512, i)
            o = next_stage()
            P.cp("act" if i % 2 else "dve", o.ap(), p.ap(), [p], [o])
            P.dma(stq(i % 2 == 1), mv[i * 128:(i + 1) * 128, grp * 512:(grp + 1) * 512], o.ap(), r=[o], w=[k_mv], key=o)
    if stop_after == "mv":
        P.finish()
        return nc
    P.phase = "p1b_o"
    for grp in range(2):
        wt = load_w(3072 + grp * 512, 512)
        for i in range(NCT, NTL):
            p = tok_mm(wt, 512, i)
            o = next_stage()
            P.act(o.ap(), p.ap(), AF.Sigmoid, [p], [o])
            P.dma(stq(True), osig[(i - NCT) * 128:(i - NCT + 1) * 128, grp * 512:(grp + 1) * 512], o.ap(), r=[o], w=[k_osig], key=o)
    if stop_after == "o":
        P.finish()
        return nc
    P.phase = "p1b_gt"
    wt = load_w(4096, 16)
    Gdt = [P.T("Gdt0"), P.T("Gdt1")]
    for i in range(NTL):
        p = tok_mm(wt, 16, i)
        for d in range(2):
            P.tt("dve", Gd[d].ap()[:, step_of[d][i], :], p.ap()[:, d * 8:(d + 1) * 8], gbb.ap()[:, d * 8:(d + 1) * 8], ALU.add, [p, gbb], [Gdt[d]])

    if stop_after == "gates":
        P.finish()
        return nc
    P.phase = "p1b_q"
    ropet = [P.sb("ropet%d" % i, [128, 128], F32) for i in range(2)]
    sqb = [P.sb("sq%d" % i, [128, 512], BF16) for i in range(2)]
    qst = [P.sb("qst%d" % i, [128, 8], F32) for i in range(2)]
    qn = [P.sb("qn%d" % i, [128, 512], F32) for i in range(2)]
    t1 = [P.sb("rt%d" % i, [128, 256], F32) for i in range(4)]
    qr = [P.sb("qr%d" % i, [128, 512], BF16) for i in range(2)]
    qstage = [P.sb("qstage%d" % i, [128, 4, 512], BF16) for i in range(2)]
    acnt = [0]

    def norm_rope(p, nh, gb, rt, do_rope):
        i = acnt[0]; acnt[0] += 1
        s_ = qst[i % 2]; q_ = qn[i % 2]; o_ = qr[i % 2]; sq = sqb[i % 2]
        W = nh * 128
        P.act(sq.ap()[:, 0:W], p.ap()[:, 0:W], AF.Square, [p], [sq])
        P.op("dve", lambda e: e.reduce_sum(s_.ap()[:, 0:nh], sq.ap()[:, 0:W].rearrange("p (h d) -> p h d", h=nh), AX.X), [sq], [s_])
        P.act(s_.ap()[:, 4:4 + nh], s_.ap()[:, 0:nh], AF.Sqrt, [s_], [s_], bias=EPS, scale=1.0 / 128)
        P.recip(s_.ap()[:, 0:nh], s_.ap()[:, 4:4 + nh], [s_], [s_])
        P.tt("dve", q_.ap()[:, 0:W].rearrange("p (h d) -> p h d", h=nh), p.ap()[:, 0:W].rearrange("p (h d) -> p h d", h=nh),
             s_.ap()[:, 0:nh].unsqueeze(2).to_broadcast([128, nh, 128]), ALU.mult, [p, s_], [q_])
        if not do_rope:
            P.tt("pool", o_.ap()[:, 0:W].rearrange("p (h d) -> p h d", h=nh), q_.ap()[:, 0:W].rearrange("p (h d) -> p h d", h=nh),
                 gb.ap().unsqueeze(1).to_broadcast([128, nh, 128]), ALU.mult, [q_, gb], [o_])
            return o_
        P.tt("pool", q_.ap()[:, 0:W].rearrange("p (h d) -> p h d", h=nh), q_.ap()[:, 0:W].rearrange("p (h d) -> p h d", h=nh),
             gb.ap().unsqueeze(1).to_broadcast([128, nh, 128]), ALU.mult, [q_, gb], [q_])
        qv = q_.ap()[:, 0:W].rearrange("p (h i two) -> p h i two", h=nh, two=2)
        ov = o_.ap()[:, 0:W].rearrange("p (h i two) -> p h i two", h=nh, two=2)
        x1 = qv[:, :, :, 0]; x2 = qv[:, :, :, 1]
        cosb = rt.ap()[:, 0:64].unsqueeze(1).to_broadcast([128, nh, 64])
        sinb = rt.ap()[:, 64:128].unsqueeze(1).to_broadcast([128, nh, 64])
        tv = [t.ap()[:, 0:nh * 64].rearrange("p (h i) -> p h i", h=nh) for t in t1]
        P.tt("dve", tv[0], x1, cosb, ALU.mult, [q_, rt], [t1[0]])
        P.tt("dve", tv[1], x2, sinb, ALU.mult, [q_, rt], [t1[1]])
        P.tt("dve", ov[:, :, :, 0], tv[0], tv[1], ALU.subtract, [t1[0], t1[1]], [o_])
        P.tt("pool", tv[2], x1, sinb, ALU.mult, [q_, rt], [t1[2]])
        P.tt("pool", tv[3], x2, cosb, ALU.mult, [q_, rt], [t1[3]])
        P.tt("pool", ov[:, :, :, 1], tv[2], tv[3], ALU.add, [t1[2], t1[3]], [o_])
        return o_

    def load_rope(i):
        rt = ropet[i % 2]
        P.dma("sp", rt.ap(), rope[(i - NCT) * 128:(i - NCT + 1) * 128, :], w=[rt])
        return rt

    pT2 = [P.ps("pT2_%d" % i, [128, 4, 128], BF16, bank=i) for i in range(2)]
    wq = [load_w(4112, 512, prefetch=False), load_w(4624, 512, prefetch=False)]
    ptc = [0]
    rts = {}
    qunits = [(b0, j, grp) for b0 in range(0, NL, 4) for j in range(4) for grp in range(2)]

    def q_stage1(u):
        b0, j, grp = u
        i = NCT + b0 + j
        if grp == 0:
            rts[i] = load_rope(i)
        return tok_mm(wq[grp], 512, i)

    def q_stage2(u, p):
        b0, j, grp = u
        i = NCT + b0 + j
        stg = qstage[grp]
        o_ = norm_rope(p, 4, qgb, rts[i], True)
        ptt = pT2[ptc[0] % 2]; ptc[0] += 1
        for h in range(4):
            P.tp(ptt.ap()[:, h, :], o_.ap()[:, h * 128:(h + 1) * 128], identb.ap(), [o_, identb], [ptt])
        P.cp("act", stg.ap()[:, :, j * 128:(j + 1) * 128], ptt.ap(), [ptt], [stg])
        if j == 3 and grp == 1:
            for g2 in range(2):
                for h in range(4):
                    P.dma(stq(True), qaT[(g2 * 4 + h) * 128:(g2 * 4 + h + 1) * 128, b0 * 128:(b0 + 4) * 128], qstage[g2].ap()[:, h, :],
                          r=[qstage[g2]], w=[k_qaT], key=qstage[g2])

    prevq = None
    for u in qunits:
        pcur = q_stage1(u)
        if prevq is not None:
            q_stage2(*prevq)
        prevq = (u, pcur)
    q_stage2(*prevq)
    if stop_after == "q":
        P.finish()
        return nc
    P.phase = "p1b_kv"
    wt = load_w(5136, 512)
    kstage = [P.sb("kstage%d" % i, [128, 2, 128], BF16) for i in range(2)]
    krt = {}

    def kv_stage1(i):
        if i >= NCT:
            krt[i] = load_rope(i)
        return tok_mm(wt, 512, i)

    def kv_stage2(i, p):
        lat = i >= NCT
        o_ = norm_rope(p, 2, kgb, krt.get(i), lat)
        ptt = pT2[i % 2]
        for h in range(2):
            P.tp(ptt.ap()[:, h, :], o_.ap()[:, h * 128:(h + 1) * 128], identb.ap(), [o_, identb], [ptt])
        ks = kstage[i % 2]
        P.cp("act", ks.ap(), ptt.ap()[:, 0:2, :], [ptt], [ks])
        for h in range(2):
            P.dma(stq(True), kaT[h * 128:(h + 1) * 128, i * 128:(i + 1) * 128], ks.ap()[:, h, :], r=[ks], w=[k_kaT], key=ks)
        o = next_stage()
        P.cp("dve", o.ap()[:, 0:256], p.ap()[:, 256:512], [p], [o])
        P.dma(stq(False), va[i * 128:(i + 1) * 128, :], o.ap()[:, 0:256], r=[o], w=[k_va], key=o)

    prevk = None
    for i in range(NTL):
        pcur = kv_stage1(i)
        if prevk is not None:
            kv_stage2(*prevk)
        prevk = (i, pcur)
    kv_stage2(*prevk)
    if debug:
        for d in range(2):
            dbg["Gd%d" % d] = nc.dram_tensor("d_Gd%d" % d, [128, NTL * 8], F32, kind="ExternalOutput").ap()
            P.dma("sp", dbg["Gd%d" % d], Gd[d].ap().rearrange("p a b -> p (a b)"), r=[Gdt[d]], w=[P.T("OUTg%d" % d)])
    P.barrier()
    P.reset()
    if stop_after == 2:
        P.finish()
        return nc

    P.phase = "p2a"
    NS = NTL
    W4 = NS * 4
    Aa = [P.sb("Aa%d" % d, [128, W4], F32) for d in range(2)]
    Ee = [P.sb("Ee%d" % d, [128, W4], F32) for d in range(2)]
    C0 = [P.sb("C0%d" % d, [128, W4], F32) for d in range(2)]
    mng = P.sb("mng", [128, D], F32)
    P.dma("sp", mng.ap(), m_norm_g.partition_broadcast(128), w=[mng])
    pieces = [(c0_, min(c0_ + 128, W4)) for c0_ in range(0, W4, 128)]
    pTr = P.ps("pTr", [128, 128], F32, bank=4)
    for d in range(2):
        LF = P.sb("LF%d" % d, [128, W4], F32)
        gtmp = P.sb("gtmp%d" % d, [128, W4], F32)
        Bv = P.sb("Bv%d" % d, [128, W4], F32)
        Mb = P.sb("Mb%d" % d, [128, W4], F32)
        mrow = P.sb("mrow%d" % d, [128, W4], F32)
        mprev = P.sb("mprev%d" % d, [128, W4], F32)
        Rr = P.sb("Rr%d" % d, [128, W4], F32)
        FLb = P.sb("FLb%d" % d, [128, W4], F32)
        mcol = P.sb("mcol%d" % d, [128, 4], F32)
        dgm = P.sb("dgm%d" % d, [128, 128], F32)
        v3 = lambda t: t.ap().rearrange("p (s h) -> p s h", h=4)
        pF = P.ps("pF%d" % d, [128, W4], F32, bank=0 + d)
        pFL = P.ps("pFL%d" % d, [128, W4], F32, bank=2 + d)
        pM = P.ps("pM%d" % d, [128, W4], F32, bank=5 + d)
        P.act(v3(gtmp), Gd[d].ap()[:, :, 4:8], AF.Exp, [Gdt[d]], [gtmp], scale=-1.0)
        P.act(gtmp.ap(), gtmp.ap(), AF.Ln, [gtmp], [gtmp], bias=1.0)
        P.ts("dve", LF.ap(), gtmp.ap(), -1.0, None, ALU.mult, None, [gtmp], [LF])
        P.mm(pF.ap(), trif.ap()[:, d, :], LF.ap(), True, True, [trif, LF], [pF])
        P.mm(pFL.ap(), onesf.ap(), LF.ap(), True, True, [onesf, LF], [pFL])
        P.tt("dve", v3(Bv), Gd[d].ap()[:, :, 0:4], pF.ap().rearrange("p (s h) -> p s h", h=4), ALU.subtract, [Gdt[d], pF], [Bv])
        P.cp("dve", FLb.ap(), pFL.ap(), [pFL], [FLb])
        for pi, (a0, a1) in enumerate(pieces):
            w_ = a1 - a0
            P.tp(pTr.ap()[0:w_, :], Bv.ap()[:, a0:a1], identf.ap(), [Bv, identf], [pTr])
            P.memset("dve", mcol.ap()[:, pi:pi + 1], 0.0, [mcol])
            P.op("dve", lambda e, o_=mcol.ap()[0:w_, pi:pi + 1], i_=pTr.ap()[0:w_, :]: e.reduce_max(o_, i_, AX.X), [pTr, mcol], [mcol])
            P.ts("dve", dgm.ap(), identf.ap(), mcol.ap()[:, pi:pi + 1], None, ALU.mult, None, [identf, mcol], [dgm])
            P.mm(pM.ap()[:, a0:a1], onesf.ap(), dgm.ap()[:, 0:w_], True, True, [onesf, dgm], [pM])
        P.cp("dve", Mb.ap(), pM.ap(), [pM], [Mb])
        for h in range(4):
            P.op("dve", lambda e, o_=v3(mrow)[:, :, h], a_=v3(Mb)[:, :, h], b_=v3(FLb)[:, :, h]: e.tensor_tensor_scan(o_, a_, b_, NEG, ALU.max, ALU.add),
                 [Mb, FLb], [mrow])
        P.memset("dve", mprev.ap()[:, 0:4], NEG, [mprev])
        P.cp("dve", mprev.ap()[:, 4:W4], mrow.ap()[:, 0:W4 - 4], [mrow], [mprev])
        P.tt("dve", Rr.ap(), mprev.ap(), Mb.ap(), ALU.max, [mprev, Mb], [Rr])
        P.tt("dve", gtmp.ap(), mprev.ap(), Rr.ap(), ALU.subtract, [mprev, Rr], [gtmp])
        P.ts("dve", gtmp.ap(), gtmp.ap(), -200.0, None, ALU.max, None, [gtmp], [gtmp])
        P.act(C0[d].ap(), gtmp.ap(), AF.Exp, [gtmp], [C0[d]])
        if debug:
            for nm, tl in (("Mb", Mb), ("FLb", FLb), ("mrow", mrow), ("Rr", Rr), ("Bv", Bv)):
                dbg[nm + str(d)] = nc.dram_tensor("d_%s%d" % (nm, d), [128, W4], F32, kind="ExternalOutput").ap()
                P.dma("sp", dbg[nm + str(d)], tl.ap(), r=[tl], w=[P.T("OUT%s%d" % (nm, d))])
        P.tt("dve", Bv.ap(), Bv.ap(), Rr.ap(), ALU.subtract, [Bv, Rr], [Bv])
        P.act(Aa[d].ap(), Bv.ap(), AF.Exp, [Bv], [Aa[d]], bias=-LN16)
        P.tt("dve", LF.ap(), pF.ap(), Rr.ap(), ALU.add, [pF, Rr], [LF])
        P.act(Ee[d].ap(), LF.ap(), AF.Exp, [LF], [Ee[d]], scale=-1.0)
    if debug:
        for nm, tl in (("Aa", Aa), ("Ee", Ee), ("C0", C0)):
            for d in range(2):
                dbg[nm + str(d)] = nc.dram_tensor("d_%s%d" % (nm, d), [128, W4], F32, kind="ExternalOutput").ap()
                P.dma("sp", dbg[nm + str(d)], tl[d].ap(), r=[tl[d]], w=[P.T("OUT%s%d" % (nm, d))])
    P.barrier()
    if stop_after == "2a":
        P.finish()
        return nc

    P.phase = "p2b"
    kTin = [[P.sb("kTin%d%d" % (d, i), [128, 8, 128], BF16) for i in range(2)] for d in range(2)]
    qTin = [[P.sb("qTin%d%d" % (d, i), [128, 8, 128], BF16) for i in range(2)] for d in range(2)]
    vin = [[P.sb("vin%d%d" % (d, i), [128, D], BF16) for i in range(2)] for d in range(2)]
    kp = [[P.sb("kp%d%d" % (d, i), [128, D], BF16) for i in range(2)] for d in range(2)]
    Sm = [[P.sb("Sm%d%d" % (d, i), [128, 4, 128], BF16) for i in range(2)] for d in range(2)]
    qs = [[P.sb("qs%d%d" % (d, i), [128, 8, 128], BF16) for i in range(2)] for d in range(2)]
    Cst = [P.sb("Cst%d" % d, [128, 4, 512], F32) for d in range(2)]
    Cbf = [P.sb("Cbf%d" % d, [128, 4, 512], BF16) for d in range(2)]
    Cbt = [[P.T("Cbt%d%d" % (d, h)) for h in range(4)] for d in range(2)]
    Cft = [[P.T("Cft%d%d" % (d, h)) for h in range(4)] for d in range(2)]
    nst = [P.sb("nst%d" % d, [128, 8], F32) for d in range(2)]
    ntmp = [P.sb("ntmp%d" % d, [128, 8], F32) for d in range(2)]
    nbf = [P.sb("nbf%d" % d, [128, 8], BF16) for d in range(2)]
    hbuf = [[P.sb("hbuf%d%d" % (d, i), [128, D], F32) for i in range(2)] for d in range(2)]
    hoth = [P.sb("hoth%d" % i, [128, D], F32) for i in range(2)]
    osg = [P.sb("osg%d" % i, [128, D], BF16) for i in range(2)]
    ymt = [P.sb("ymt%d" % i, [128, D], BF16) for i in range(2)]
    dsb = [P.sb("dsb%d" % d, [128, 16], F32) for d in range(2)]
    fst = [P.sb("fst%d" % i, [128, 16], F32) for i in range(2)]
    fjunk = P.sb("fjunk", [128, 256], BF16)
    half = NL // 2
    GRP = 4 if half % 4 == 0 else (2 if half % 2 == 0 else 1)
    ymstage = [[P.sb("ymst%d%d" % (d, i), [128, 8, GRP * 128], BF16) for i in range(2)] for d in range(2)]
    pK = P.ps("pK", [128, 8, 128], BF16, bank=0)
    pdc = [P.ps("pdc%d" % i, [128, 512], F32, bank=1 + i) for i in range(2)]
    pb3 = P.T("pbank3")
    pdn = [TV(pb3, P.ps("pdn%d" % d, [128, 8], F32, bank=3, off=d * 32).h) for d in range(2)]
    pden = [TV(pb3, P.ps("pden%d" % d, [128, 4], F32, bank=3, off=64 + d * 32).h) for d in range(2)]
    pS = P.ps("pS", [128, 4, 128], F32, bank=4)
    pnum = [P.ps("pnum%d" % i, [128, 2, 256], F32, bank=5 + i) for i in range(2)]
    pT3 = P.ps("pT3", [128, 8, 128], BF16, bank=7)
    for d in range(2):
        P.memset("dve", Cst[d].ap(), 0.0, Cft[d])
        P.memset("pool", Cbf[d].ap(), 0.0, Cbt[d])
        P.memset("dve", nst[d].ap(), 0.0, [nst[d]])
        P.memset("dve", nbf[d].ap(), 0.0, [nbf[d]])
    fin_cnt = [0, 0]
    fcount = [0]
    pend_fin = []
    for k in range(NS):
        for d in range(2):
            tile = tile_of(d, k)
            g0 = tile * 128
            lat = tile >= NCT
            sl = k % 2
            kt = kTin[d][sl]; vt = vin[d][sl]; qt = qTin[d][sl]; kpt = kp[d][sl]; smt = Sm[d][sl]; qst_ = qs[d][sl]
            P.dma("sp", kt.ap(), mkT[:, g0:g0 + 128].rearrange("(j p) t -> p j t", p=128), r=[k_mkT], w=[kt])
            P.dma("sp", vt.ap(), mv[g0:g0 + 128, :], r=[k_mv], w=[vt])
            if lat:
                P.dma("sp", qt.ap(), mqT[:, g0:g0 + 128].rearrange("(j p) t -> p j t", p=128), r=[k_mqT], w=[qt])
            for j in range(8):
                P.tp(pK.ap()[:, j, :], kt.ap()[:, j, :], identb.ap(), [kt, identb], [pK])
            if lat:
                for h in range(4):
                    for dc in range(2):
                        P.mm(pS.ap()[:, h, :], kt.ap()[:, 2 * h + dc, :], qt.ap()[:, 2 * h + dc, :], dc == 0, dc == 1, [kt, qt], [pS])
            for h in range(4):
                col = k * 4 + h
                dstv = kpt.ap()[:, h * 256:(h + 1) * 256].rearrange("p (a b) -> p a b", a=2)
                if h % 2:
                    P.act(dstv, pK.ap()[:, 2 * h:2 * h + 2, :], AF.Copy, [pK, Aa[d]], [kpt], scale=Aa[d].ap()[:, col:col + 1])
                else:
                    P.ts("dve", dstv, pK.ap()[:, 2 * h:2 * h + 2, :], Aa[d].ap()[:, col:col + 1], None, ALU.mult, None, [pK, Aa[d]], [kpt])
            if lat:
                for h in range(4):
                    col = k * 4 + h
                    P.stt("dve", smt.ap()[:, h, :], pS.ap()[:, h, :], Aa[d].ap()[:, col:col + 1], trib.ap()[:, d, :], ALU.mult, ALU.mult,
                          [pS, Aa[d], trib], [smt])
                    P.act(qst_.ap()[:, 2 * h:2 * h + 2, :], qt.ap()[:, 2 * h:2 * h + 2, :], AF.Copy, [qt, C0[d]], [qst_], scale=C0[d].ap()[:, col:col + 1])
                for h in range(4):
                    pn = pnum[h // 2]
                    P.mm(pn.ap()[:, h % 2, :], smt.ap()[:, h, :], vt.ap()[:, h * 256:(h + 1) * 256], True, False, [smt, vt], [pn])
                    P.mm(pn.ap()[:, h % 2, :], qst_.ap()[:, 2 * h, :], Cbf[d].ap()[:, h, 0:256], False, False, [qst_, Cbt[d][h]], [pn])
                    P.mm(pn.ap()[:, h % 2, :], qst_.ap()[:, 2 * h + 1, :], Cbf[d].ap()[:, h, 256:512], False, True, [qst_, Cbt[d][h]], [pn])
                for h in range(4):
                    P.mm(pden[d].ap()[:, h:h + 1], smt.ap()[:, h, :], onesb.ap()[:, 0:1], True, False, [smt, onesb], [pden[d]])
                    P.mm(pden[d].ap()[:, h:h + 1], qst_.ap()[:, 2 * h, :], nbf[d].ap()[:, 2 * h:2 * h + 1], False, False, [qst_, nbf[d]], [pden[d]])
                    P.mm(pden[d].ap()[:, h:h + 1], qst_.ap()[:, 2 * h + 1, :], nbf[d].ap()[:, 2 * h + 1:2 * h + 2], False, True, [qst_, nbf[d]], [pden[d]])
            for h in range(4):
                col = k * 4 + h
                pd = pdc[h % 2]
                for dc in range(2):
                    P.mm(pd.ap()[:, dc * 256:(dc + 1) * 256], kpt.ap()[:, h * 256 + dc * 128:h * 256 + (dc + 1) * 128], vt.ap()[:, h * 256:(h + 1) * 256],
                         True, True, [kpt, vt], [pd])
                P.stt("dve", Cst[d].ap()[:, h, :], Cst[d].ap()[:, h, :], C0[d].ap()[:, col:col + 1], pd.ap(), ALU.mult, ALU.add,
                      [Cft[d][h], C0[d], pd], [Cft[d][h]])
                P.cp("act", Cbf[d].ap()[:, h, :], Cst[d].ap()[:, h, :], [Cft[d][h]], [Cbt[d][h]])
            for j in range(8):
                P.mm(pdn[d].ap()[:, j:j + 1], kpt.ap()[:, j * 128:(j + 1) * 128], onesb.ap()[:, 0:1], True, True, [kpt, onesb], [pdn[d]])
            P.tt("dve", ntmp[d].ap().rearrange("p (h two) -> p h two", two=2), nst[d].ap().rearrange("p (h two) -> p h two", two=2),
                 C0[d].ap()[:, k * 4:(k + 1) * 4].unsqueeze(2).to_broadcast([128, 4, 2]), ALU.mult, [nst[d], C0[d]], [ntmp[d]])
            P.tt("dve", nst[d].ap(), ntmp[d].ap(), pdn[d].ap(), ALU.add, [ntmp[d], pdn[d]], [nst[d]])
            P.cp("dve", nbf[d].ap(), nst[d].ap(), [nst[d]], [nbf[d]])
            while pend_fin:
                pend_fin.pop(0)()
            if not lat:
                continue
            ds_ = dsb[d]
            hb = hbuf[d][sl]
            P.cp("dve", ds_.ap()[:, 0:4], pden[d].ap(), [pden[d]], [ds_])
            P.stt("dve", ds_.ap()[:, 4:8], ds_.ap()[:, 0:4], -1.0, ds_.ap()[:, 0:4], ALU.mult, ALU.max, [ds_], [ds_])
            P.tt("dve", ds_.ap()[:, 8:12], ds_.ap()[:, 4:8], Ee[d].ap()[:, k * 4:(k + 1) * 4], ALU.max, [ds_, Ee[d]], [ds_])
            P.recip(ds_.ap()[:, 12:16], ds_.ap()[:, 8:12], [ds_], [ds_])
            for h in range(4):
                pn = pnum[h // 2]
                if h % 2:
                    P.act(hb.ap()[:, h * 256:(h + 1) * 256], pn.ap()[:, h % 2, :], AF.Copy, [pn, ds_], [hb], scale=ds_.ap()[:, 12 + h:13 + h])
                else:
                    P.ts("dve", hb.ap()[:, h * 256:(h + 1) * 256], pn.ap()[:, h % 2, :], ds_.ap()[:, 12 + h:13 + h], None, ALU.mult, None, [pn, ds_], [hb])
            li_ = tile - NCT
            ko = step_of[1 - d][tile]
            if ko > k:
                P.dma("pool", hdir[d, li_ * 128:(li_ + 1) * 128, :], hb.ap(), r=[hb], w=[k_hdir])
                continue
            fi = fcount[0]; fcount[0] += 1
            ho = hoth[fi % 2]; og = osg[fi % 2]; ym_ = ymt[fi % 2]; fs = fst[fi % 2]
            P.dma("sp", ho.ap(), hdir[1 - d, li_ * 128:(li_ + 1) * 128, :], r=[k_hdir], w=[ho])
            P.dma("sp", og.ap(), osig[li_ * 128:(li_ + 1) * 128, :], r=[k_osig], w=[og])
            P.tt("pool", ho.ap(), ho.ap(), hb.ap(), ALU.add, [ho, hb], [ho])
            for h in range(4):
                P.act(fjunk.ap(), ho.ap()[:, h * 256:(h + 1) * 256], AF.Square, [ho], [fjunk, fs], accum_out=fs.ap()[:, h:h + 1])
            P.act(fs.ap()[:, 4:8], fs.ap()[:, 0:4], AF.Sqrt, [fs], [fs], bias=EPS, scale=1.0 / 256)
            P.recip(fs.ap()[:, 8:12], fs.ap()[:, 4:8], [fs], [fs])
            P.tt("dve", ho.ap().rearrange("p (h e) -> p h e", h=4), ho.ap().rearrange("p (h e) -> p h e", h=4),
                 fs.ap()[:, 8:12].unsqueeze(2).to_broadcast([128, 4, 256]), ALU.mult, [ho, fs], [ho])
            P.tt("pool", ho.ap(), ho.ap(), mng.ap(), ALU.mult, [ho, mng], [ho])
            P.tt("dve", ym_.ap(), ho.ap(), og.ap(), ALU.mult, [ho, og], [ym_])
            def fin_tail(ym_=ym_, li_=li_, d=d):
                for j in range(8):
                    P.tp(pT3.ap()[:, j, :], ym_.ap()[:, j * 128:(j + 1) * 128], identb.ap(), [ym_, identb], [pT3])
                blk = li_ // GRP
                stg = ymstage[d][(fin_cnt[d] // GRP) % 2]
                P.cp("act", stg.ap()[:, :, (li_ % GRP) * 128:(li_ % GRP + 1) * 128], pT3.ap(), [pT3], [stg])
                fin_cnt[d] += 1
                if fin_cnt[d] % GRP == 0:
                    for j in range(8):
                        P.dma("pool", ymT[j * 128:(j + 1) * 128, blk * GRP * 128:(blk + 1) * GRP * 128], stg.ap()[:, j, :], r=[stg], w=[k_ymT], key=stg)
            pend_fin.append(fin_tail)
    while pend_fin:
        pend_fin.pop(0)()
    P.barrier()
    P.reset()
    if stop_after == 3:
        P.finish()
        return nc

    P.phase = "p3"
    KT = P.sb("KT", [128, 2, NT], BF16)
    Vr = P.sb("Vr", [128, NTL, 2, 132], BF16)
    P.memset("dve", Vr.ap()[:, :, :, 128:129], 1.0, [Vr])
    for h in range(2):
        P.dma("sp", KT.ap()[:, h, :], kaT[h * 128:(h + 1) * 128, :], r=[k_kaT], w=[KT])
        P.dma("sp", Vr.ap()[:, :, h, 0:128], va[:, h * 128:(h + 1) * 128].rearrange("(t p) e -> p t e", p=128), r=[k_va], w=[Vr])
    QB = 512
    qin = [P.sb("qin%d" % i, [128, 8, QB], BF16) for i in range(2)]
    Pt = [P.sb("Pt%d" % i, [128, QB], BF16) for i in range(3)]
    yat = [[P.sb("yat%d%d" % (i, j), [128, D], BF16) for j in range(4)] for i in range(2)]
    arec = [P.sb("arec%d" % i, [128, 4], F32) for i in range(2)]
    yastage = [P.sb("yast%d" % i, [128, 8, QB], BF16) for i in range(2)]
    pSa = [P.ps("pSa%d" % i, [128, QB], F32, bank=(0, 1, 7)[i]) for i in range(3)]
    pacc1 = [P.ps("pacc%d" % j, [128, 129], F32, bank=2 + j) for j in range(4)]
    pacc = [pacc1, pacc1]
    pT4 = P.ps("pT4", [128, 8, 128], BF16, bank=6)
    sc_att = 128.0 ** -0.5
    its = [(qb, h, kt_) for qb in range(S // QB) for h in range(8) for kt_ in range(NTL)]
    NI = len(its)
    DEPTH = 2

    def issue_qk(i):
        qb, h, kt_ = its[i]
        qi = qin[qb % 2]
        if h == 0 and kt_ == 0:
            for hh in range(8):
                P.dma("sp", qi.ap()[:, hh, :], qaT[hh * 128:(hh + 1) * 128, qb * QB:(qb + 1) * QB], r=[k_qaT], w=[qi])
        ps_ = pSa[i % 3]; pt_ = Pt[i % 3]
        P.mm(ps_.ap(), KT.ap()[:, h // 4, kt_ * 128:(kt_ + 1) * 128], qi.ap()[:, h, :], True, True, [KT, qi], [ps_])
        P.act(pt_.ap(), ps_.ap(), AF.Exp, [ps_], [pt_], scale=sc_att)

    def issue_pv(i):
        qb, h, kt_ = its[i]
        kvh = h // 4
        pt_ = Pt[i % 3]
        acc_ = pacc[h % 2]
        yt = yat[qb % 2]
        for j in range(4):
            P.mm(acc_[j].ap(), pt_.ap()[:, j * 128:(j + 1) * 128], Vr.ap()[:, kt_, kvh, 0:129], kt_ == 0, kt_ == NTL - 1, [pt_, Vr], [acc_[j]])
        if kt_ != NTL - 1:
            return
        ar = arec[h % 2]
        for j in range(4):
            P.recip(ar.ap()[:, j:j + 1], acc_[j].ap()[:, 128:129], [acc_[j]], [ar])
            P.ts("dve", yt[j].ap()[:, h * 128:(h + 1) * 128], acc_[j].ap()[:, 0:128], ar.ap()[:, j:j + 1], None, ALU.mult, None, [acc_[j], ar], [yt[j]])
        if h != 7:
            return
        stg = yastage[qb % 2]
        for j in range(4):
            for hh in range(8):
                P.tp(pT4.ap()[:, hh, :], yt[j].ap()[:, hh * 128:(hh + 1) * 128], identb.ap(), [yt[j], identb], [pT4])
            P.cp("dve", stg.ap()[:, :, j * 128:(j + 1) * 128], pT4.ap(), [pT4], [stg])
        for hh in range(8):
            P.dma("pool", yaT[hh * 128:(hh + 1) * 128, qb * QB:(qb + 1) * QB], stg.ap()[:, hh, :], r=[stg], w=[k_yaT], key=stg)

    for i in range(-DEPTH, NI):
        if i + DEPTH < NI:
            issue_qk(i + DEPTH)
        if i >= 0:
            issue_pv(i)
    P.barrier()
    P.reset()
    if stop_after == 4:
        P.finish()
        return nc

    P.phase = "p4a"
    wpa = P.sb("wpa", [128, 8, D], BF16); wpb = P.sb("wpb", [128, 8, D], BF16); wo = P.sb("wo", [128, 8, D], BF16)
    for wt_, src in ((wpa, w_pa), (wpb, w_pb), (wo, w_o)):
        for hh in range(2):
            P.dma("pool", wt_.ap()[:, :, hh * 512:(hh + 1) * 512], src[:, hh * 512:(hh + 1) * 512].rearrange("(k p) f -> p k f", p=128), w=[wt_])
    ymin = [P.sb("ymin%d" % i, [128, 8, 512], BF16) for i in range(2)]
    yain = [P.sb("yain%d" % i, [128, 8, 512], BF16) for i in range(2)]
    gin = [P.sb("gin%d" % i, [128, 16, 512], BF16) for i in range(2)]
    uT = [P.sb("uT%d" % i, [128, 8, 512], BF16) for i in range(2)]
    u1 = [P.sb("u1_%d" % i, [128, 512], F32) for i in range(2)]
    u2 = [P.sb("u2_%d" % i, [128, 512], F32) for i in range(2)]
    xin = [P.sb("xin%d" % i, [128, D], F32) for i in range(3)]
    ytmp = [P.sb("ytmp%d" % i, [128, D], F32) for i in range(2)]
    x1t = [P.sb("x1t%d" % i, [128, D], F32) for i in range(2)]
    pA = [P.ps("pA%d" % i, [128, 512], F32, bank=i) for i in range(2)]
    pB = [P.ps("pB%d" % i, [128, 512], F32, bank=2 + i) for i in range(2)]
    pY = [P.ps("pY%d" % i, [128, 512], F32, bank=4 + i) for i in range(4)]
    xc = 0
    for b in range(S // 512):
        ym_ = ymin[b % 2]; ya_ = yain[b % 2]; g_ = gin[b % 2]; u_ = uT[b % 2]
        c0_ = b * 512
        P.dma("sp", ym_.ap(), ymT[:, c0_:c0_ + 512].rearrange("(j p) t -> p j t", p=128), r=[k_ymT], w=[ym_])
        P.dma("sp", ya_.ap(), yaT[:, c0_:c0_ + 512].rearrange("(j p) t -> p j t", p=128), r=[k_yaT], w=[ya_])
        P.dma("sp", g_.ap(), gT[:, c0_:c0_ + 512].rearrange("(j p) t -> p j t", p=128), r=[k_gT], w=[g_])
        for fc in range(8):
            pa = pA[fc % 2]; pb_ = pB[fc % 2]; a1 = u1[fc % 2]; a2 = u2[fc % 2]
            for kc in range(8):
                P.mm(pa.ap(), wpa.ap()[:, kc, fc * 128:(fc + 1) * 128], ym_.ap()[:, kc, :], kc == 0, kc == 7, [wpa, ym_], [pa])
            for kc in range(8):
                P.mm(pb_.ap(), wpb.ap()[:, kc, fc * 128:(fc + 1) * 128], ya_.ap()[:, kc, :], kc == 0, kc == 7, [wpb, ya_], [pb_])
            P.tt("dve", a1.ap(), pa.ap(), g_.ap()[:, fc, :], ALU.mult, [pa, g_], [a1])
            P.tt("dve", a2.ap(), pb_.ap(), g_.ap()[:, 8 + fc, :], ALU.mult, [pb_, g_], [a2])
            P.tt("pool", u_.ap()[:, fc, :], a1.ap(), a2.ap(), ALU.add, [a1, a2], [u_])
        for j in range(4):
            xi = xin[xc % 3]; yt_ = ytmp[xc % 2]; xo = x1t[xc % 2]; xc += 1
            r0 = c0_ + j * 128
            P.dma("sp", xi.ap(), x[r0:r0 + 128, :], w=[xi])
            for hh in range(2):
                py = pY[(j * 2 + hh) % 4]
                for kc in range(8):
                    P.mm(py.ap(), u_.ap()[:, kc, j * 128:(j + 1) * 128], wo.ap()[:, kc, hh * 512:(hh + 1) * 512], kc == 0, kc == 7, [u_, wo], [py])
                P.tt("dve", yt_.ap()[:, hh * 512:(hh + 1) * 512], py.ap(), G1bc.ap()[:, hh * 512:(hh + 1) * 512], ALU.mult, [py, G1bc], [yt_])
            P.tt("pool", xo.ap(), yt_.ap(), xi.ap(), ALU.add, [yt_, xi], [xo])
            P.dma("pool", x1d[r0:r0 + 128, :], xo.ap(), r=[xo], w=[k_x1], key=xo)
    P.barrier()
    P.reset()
    if stop_after == 5:
        P.finish()
        return nc

    P.phase = "p4b"
    wg = P.sb("wg", [128, 8, DFF], BF16); wu = P.sb("wu", [128, 8, DFF], BF16); wd = P.sb("wd", [128, 22, D], BF16)
    for wt_, src in ((wg, w_g), (wu, w_u)):
        for c0_ in range(0, DFF, 704):
            P.dma("pool", wt_.ap()[:, :, c0_:c0_ + 704], src[:, c0_:c0_ + 704].rearrange("(k p) f -> p k f", p=128), w=[wt_])
    for hh in range(2):
        P.dma("pool", wd.ap()[:, :, hh * 512:(hh + 1) * 512], w_d[:, hh * 512:(hh + 1) * 512].rearrange("(k p) f -> p k f", p=128), w=[wd])
    fgb = G1bc
    P.dma("sp", fgb.ap(), final_g.partition_broadcast(128), w=[fgb])
    TB = 256
    NJ = TB // 128
    x1in = [P.sb("x1in%d" % i, [128, D], F32) for i in range(2 * NJ)]
    st2 = [P.sb("st2_%d" % i, [128, 4], F32) for i in range(3)]
    junk2 = P.sb("junk2", [128, D], BF16)
    xn2 = [P.sb("xn2_%d" % i, [128, D], BF16) for i in range(2)]
    tf2 = [P.sb("tf2_%d" % i, [128, 8, 128], F32) for i in range(1)] * 2
    h2T = [P.sb("h2T%d" % i, [128, 8, TB], BF16) for i in range(2)]
    aT = [P.sb("aT%d" % i, [128, 22, TB], BF16) for i in range(1)] * 2
    sg = [P.sb("sg%d" % i, [128, TB], F32) for i in range(2)]
    ftmp = [P.sb("ftmp%d" % i, [128, D], F32) for i in range(1)] * 2
    pT5 = [P.ps("pT5_%d" % i, [128, 8, 128], BF16, bank=i) for i in range(2)]
    pG = [P.ps("pG%d" % i, [128, TB], F32, bank=2 + i) for i in range(2)]
    pU = [P.ps("pU%d" % i, [128, TB], F32, bank=4 + i) for i in range(2)]
    pD = [P.ps("pD%d" % i, [128, 512], F32, bank=6 + i) for i in range(2)] * 2
    tcn = [0]
    NB4 = S // TB
    xsets = {}

    def prologue_a(b):
        xs_ = []
        for j in range(NJ):
            r0 = b * TB + j * 128
            xi = x1in[(b % 2) * NJ + j]; ss = st2[tcn[0] % 3]; xb = xn2[j]; tcn[0] += 1
            xs_.append(xi)
            P.dma("sp", xi.ap(), x1d[r0:r0 + 128, :], r=[k_x1], w=[xi])
            P.act(junk2.ap(), xi.ap(), AF.Square, [xi], [junk2, ss], accum_out=ss.ap()[:, 0:1])
            rstd_ops(ss, D)
            P.ts("dve", xb.ap(), xi.ap(), ss.ap()[:, 2:3], None, ALU.mult, None, [xi, ss], [xb])
        xsets[b] = xs_

    def prologue_b(b):
        h2 = h2T[b % 2]
        for j in range(NJ):
            xb = xn2[j]; pt = pT5[j % 2]; tf = tf2[j % 2]
            for kc in range(8):
                P.tp(pt.ap()[:, kc, :], xb.ap()[:, kc * 128:(kc + 1) * 128], identb.ap(), [xb, identb], [pt])
            P.tt("dve", tf.ap(), pt.ap(), A2.ap().unsqueeze(2).to_broadcast([128, 8, 128]), ALU.mult, [pt, A2], [tf])
            P.tt("pool", h2.ap()[:, :, j * 128:(j + 1) * 128], tf.ap(), modT.ap()[:, 24:32, 0:1].to_broadcast([128, 8, 128]), ALU.add, [tf, modT], [h2])

    def gateup(b):
        h2 = h2T[b % 2]; a_ = aT[b % 2]
        for fc in range(22):
            pg = pG[fc % 2]; pu = pU[fc % 2]; s_ = sg[fc % 2]
            for kc in range(8):
                P.mm(pg.ap(), wg.ap()[:, kc, fc * 128:(fc + 1) * 128], h2.ap()[:, kc, :], kc == 0, kc == 7, [wg, h2], [pg])
            for kc in range(8):
                P.mm(pu.ap(), wu.ap()[:, kc, fc * 128:(fc + 1) * 128], h2.ap()[:, kc, :], kc == 0, kc == 7, [wu, h2], [pu])
            P.act(s_.ap(), pg.ap(), AF.Silu, [pg], [s_])
            P.tt("dve", a_.ap()[:, fc, :], pu.ap(), s_.ap(), ALU.mult, [pu, s_], [a_])

    def down_tail(b):
        a_ = aT[b % 2]
        xs_ = xsets.pop(b)
        for j in range(NJ):
            r0 = b * TB + j * 128
            xi = xs_[j]; ft = ftmp[j % 2]; x2 = xi; o_ = xi; ss = st2[tcn[0] % 3]; tcn[0] += 1
            for hh in range(2):
                pd_ = pD[(j * 2 + hh) % 4]
                for fc in range(22):
                    P.mm(pd_.ap(), a_.ap()[:, fc, j * 128:(j + 1) * 128], wd.ap()[:, fc, hh * 512:(hh + 1) * 512], fc == 0, fc == 21, [a_, wd], [pd_])
                P.tt("dve", ft.ap()[:, hh * 512:(hh + 1) * 512], pd_.ap(), G2bc.ap()[:, hh * 512:(hh + 1) * 512], ALU.mult, [pd_, G2bc], [ft])
            P.tt("pool", x2.ap(), ft.ap(), xi.ap(), ALU.add, [ft, xi], [x2])
            P.act(junk2.ap(), x2.ap(), AF.Square, [x2], [junk2, ss], accum_out=ss.ap()[:, 0:1])
            rstd_ops(ss, D)
            P.ts("dve", ft.ap(), x2.ap(), ss.ap()[:, 2:3], None, ALU.mult, None, [x2, ss], [ft])
            P.tt("pool", o_.ap(), ft.ap(), fgb.ap(), ALU.mult, [ft, fgb], [o_])
            P.dma("sp", out[r0:r0 + 128, :], o_.ap(), r=[o_], w=[k_out])

    prologue_a(0)
    prologue_b(0)
    for b in range(NB4):
        gateup(b)
        if b + 1 < NB4:
            prologue_a(b + 1)
            prologue_b(b + 1)
        down_tail(b)
    P.finish()
    return nc


def make_consts(S):
    cst = np.zeros((128, 512), np.float32)
    cst[:, 0:128] = np.eye(128, dtype=np.float32)
    s = np.arange(128)[:, None]
    t = np.arange(128)[None, :]
    cst[:, 128:256] = (s <= t).astype(np.float32)
    cst[:, 256:384] = (s >= t).astype(np.float32)
    cst[0, 384:512] = 1.0
    rows = S // 64
    row = np.repeat(np.arange(rows, dtype=np.float32), 64)
    col = np.tile(np.arange(64, dtype=np.float32), rows)
    inv = (np.float32(10000.0) ** (-np.arange(32, dtype=np.float32) / np.float32(32))).astype(np.float32)
    ang = np.concatenate([row[:, None] * inv, col[:, None] * inv], axis=-1).astype(np.float32)
    rope = np.concatenate([np.cos(ang), np.sin(ang)], axis=-1).astype(np.float32)
    return cst, rope


def core_inputs(b, inp, S, cst, rope):
    f = lambda a: np.ascontiguousarray(a, dtype=np.float32)
    return {
        "x": f(inp["x"][b, :S]), "c": f(inp["c"][b]), "ctx": f(inp["ctx"][b]), "c_ctx": f(inp["c_ctx"]),
        "w_mod": f(inp["w_mod"][0]), "b_mod": f(inp["b_mod"][0]), "norm1_g": f(inp["norm1_g"][0]),
        "norm2_g": f(inp["norm2_g"][0]), "w_in": f(inp["w_in"][0]), "gate_b": f(inp["gate_b"][0]),
        "conv_w": f(inp["conv_w"][0]), "conv_b": f(inp["conv_b"][0]), "m_norm_g": f(inp["m_norm_g"][0]),
        "q_norm_g": f(inp["q_norm_g"][0]), "k_norm_g": f(inp["k_norm_g"][0]), "w_pa": f(inp["w_pa"][0]),
        "w_pb": f(inp["w_pb"][0]), "w_o": f(inp["w_o"][0]), "w_ffn_gate": f(inp["w_ffn_gate"][0]),
        "w_ffn_up": f(inp["w_ffn_up"][0]), "w_ffn_down": f(inp["w_ffn_down"][0]), "final_g": f(inp["final_g"]),
        "cst": cst, "rope": rope,
    }


_CACHE = {}


def kernel(**inputs):
    S = inputs["x"].shape[1]
    B = inputs["x"].shape[0]
    if S not in _CACHE:
        _CACHE[S] = build(S)
    nc = _CACHE[S]
    cst, rope = make_consts(S)
    in_maps = [core_inputs(b, inputs, S, cst, rope) for b in range(B)]
    res = run_bass_kernel_spmd(nc, in_maps, core_ids=list(range(B)))
    return np.stack([np.asarray(r["out"], dtype=np.float32) for r in res.results], axis=0)
```

```python
import numpy as np
from contextlib import ExitStack
import concourse.bass as bass
import concourse.mybir as mybir
from concourse.bass_utils import run_bass_kernel_spmd

F32 = mybir.dt.float32
BF16 = mybir.dt.bfloat16
AF = mybir.ActivationFunctionType
ALU = mybir.AluOpType
AX = mybir.AxisListType

SEM_EPOCH = 12000
DMA_EPOCH = 1500


class T:
    def __init__(self, name, h=None):
        self.name = name
        self.h = h
        self.last_w = None
        self.readers = []
        self.epochs = []

    def ap(self):
        return self.h if isinstance(self.h, bass.AP) else self.h[:]


class TV:
    def __init__(self, base, h):
        self.base = base
        self.h = h
        self.name = base.name

    def ap(self):
        return self.h

    last_w = property(lambda self: self.base.last_w, lambda self, v: setattr(self.base, "last_w", v))
    readers = property(lambda self: self.base.readers, lambda self, v: setattr(self.base, "readers", v))
    epochs = property(lambda self: self.base.epochs)


def _shape(v, shape):
    if len(shape) == 2:
        return v
    if len(shape) == 3:
        return v.rearrange("p (a b) -> p a b", a=shape[1])
    if len(shape) == 4:
        return v.rearrange("p (a b c) -> p a b c", a=shape[1], b=shape[2])
    raise ValueError(shape)


class Op:
    __slots__ = ("eng", "fn", "deps", "need_inc", "sig", "dma_key", "waits", "idx", "phase")


class Prog:
    ENGS = ("pe", "act", "dve", "pool", "sp")

    def __init__(self, nc, arena_bytes=0):
        self.nc = nc
        self.stack = ExitStack()
        self.ops = {e: [] for e in self.ENGS}
        self.nsem = 0
        self.sem_names = []
        self.all_ops = 0
        self.keys = []
        self.free_sw = []
        self.free_hw = []
        self.scopes = False
        self.bar = {e: None for e in self.ENGS}
        self.arena = None
        if arena_bytes:
            self.arena = self.stack.enter_context(nc.sbuf_tensor("arena", [128, arena_bytes], mybir.dt.uint8))
            self.arena_bytes = arena_bytes
            self.off = 0
            self.mark = 0
            self.banks = [self.stack.enter_context(nc.psum_tensor("bank%d" % i, [128, 512], F32)) for i in range(8)]

    def sb(self, name, shape, dtype):
        if self.arena is None:
            h = self.stack.enter_context(self.nc.sbuf_tensor(name, list(shape), dtype))
            return T(name, h)
        esz = 4 if dtype == F32 else 2
        n = 1
        for d in shape[1:]:
            n *= d
        nb = (n * esz + 31) // 32 * 32
        assert self.off + nb <= self.arena_bytes, ("SBUF arena overflow", name, self.off, nb)
        v = self.arena[0:shape[0], self.off:self.off + n * esz].bitcast(dtype)
        self.off += nb
        v = _shape(v, shape)
        return T(name, v)

    def ps(self, name, shape, dtype, bank=None, off=0):
        if bank is None:
            h = self.stack.enter_context(self.nc.psum_tensor(name, list(shape), dtype))
            return T(name, h)
        n = 1
        for d in shape[1:]:
            n *= d
        esz = 4 if dtype == F32 else 2
        nf = (n * esz + 3) // 4
        assert off + nf <= 512
        v = self.banks[bank][0:shape[0], off:off + nf]
        if dtype != F32:
            v = v.bitcast(dtype)
        return T(name, _shape(v, shape))

    def set_mark(self):
        self.mark = self.off

    def reset(self):
        self.off = self.mark

    def barrier(self):
        last = [self.ops[e][-1] for e in self.ENGS if self.ops[e]]
        pairs = []
        for k in self.keys:
            for slot, cnt in k.epochs:
                pairs.append((slot, 16 * cnt))
            if k.epochs and not k.name.startswith("OUT"):
                (self.free_sw if k.name.endswith("_sw") else self.free_hw).append(tuple(k.epochs[-1]))
                k.epochs = []
        self.keys = [k for k in self.keys if k.epochs]
        for e in self.ENGS:
            self.bar[e] = (last, pairs)

    def T(self, name):
        return T(name)

    def _new_sem(self, name):
        self.sem_names.append(name)
        self.nsem += 1
        return self.nsem - 1

    def _record(self, eng, fn, r, w, dma_key=None):
        op = Op()
        op.eng = eng
        op.fn = fn
        op.need_inc = False
        op.sig = None
        op.dma_key = dma_key
        op.idx = self.all_ops
        op.phase = getattr(self, "phase", "p")
        self.all_ops += 1
        waits = {}
        deps = {}

        def add_dep(d, raw):
            if d is None:
                return
            if d.dma_key is not None:
                k = d.dma_key
                for slot, cnt in k.epochs:
                    v = 16 * cnt
                    if waits.get(slot, 0) < v:
                        waits[slot] = v
                return
            if d.eng == eng and eng == "pe":
                return
            deps[id(d)] = d

        if self.bar[eng] is not None:
            last, keys = self.bar[eng]
            self.bar[eng] = None
            for d in last:
                if d.dma_key is None:
                    add_dep(d, True)
            for slot, v in keys:
                waits[slot] = max(waits.get(slot, 0), v)
        for t in r:
            add_dep(t.last_w, True)
        for t in w:
            add_dep(t.last_w, False)
            for rd in t.readers:
                add_dep(rd, False)
        for t in r:
            t.readers.append(op)
        for t in w:
            t.last_w = op
            t.readers = []
        for d in deps.values():
            d.need_inc = True
        op.deps = list(deps.values())
        op.waits = waits
        if dma_key is not None:
            if not dma_key.epochs:
                self.keys.append(dma_key)
                free = self.free_sw if dma_key.name.endswith("_sw") else self.free_hw
                if free and free[-1][1] < DMA_EPOCH:
                    slot, base = free.pop()
                    dma_key.epochs.append([slot, base])
            if not dma_key.epochs or dma_key.epochs[-1][1] >= 2 * DMA_EPOCH:
                dma_key.epochs.append([self._new_sem("d_" + dma_key.name), 0])
            dma_key.epochs[-1][1] += 1
            op.sig = dma_key.epochs[-1][0]
        self.ops[eng].append(op)
        return op

    def op(self, eng, fn, r=(), w=()):
        return self._record(eng, fn, r, w)

    def dma(self, eng, out, in_, r=(), w=(), key=None, slow=False):
        if key is None:
            key = w[0]
        if eng == "pool":
            base = key.base if isinstance(key, TV) else key
            if not hasattr(base, "_sw"):
                base._sw = T(base.name + "_sw")
            key = base._sw
        if slow:
            return self._record(eng, lambda e: e.dma_start(out=out, in_=in_, allow_slow_non_contiguous=True), r, w, dma_key=key)
        return self._record(eng, lambda e: e.dma_start(out=out, in_=in_), r, w, dma_key=key)

    def finish(self):
        nc = self.nc
        for eng in ("pe", "act", "dve", "pool"):
            slot = None
            cnt = SEM_EPOCH
            for op in self.ops[eng]:
                if op.dma_key is not None or not op.need_inc:
                    continue
                if cnt >= SEM_EPOCH:
                    slot = self._new_sem("e_%s" % eng)
                    cnt = 0
                cnt += 1
                op.sig = (slot, cnt)
        sems = [self.stack.enter_context(nc.semaphore(n + "_%d" % i)) for i, n in enumerate(self.sem_names)]
        self.n_instr = {e: len(v) for e, v in self.ops.items()}
        final_waits = {}
        for eng in self.ENGS:
            for op in self.ops[eng]:
                if op.dma_key is not None and op.dma_key.name.startswith("OUT"):
                    for slot, cnt in op.dma_key.epochs:
                        final_waits[slot] = 16 * cnt
        with nc.Block() as block:
            def run(eng_name, e):
                waited = {}
                cur = [None, None]
                for op in self.ops[eng_name]:
                    if self.scopes and op.phase != cur[0]:
                        if cur[1] is not None:
                            cur[1].__exit__(None, None, None)
                        cur[0] = op.phase
                        cur[1] = nc.named_scope(op.phase)
                        cur[1].__enter__()
                    for d in op.deps:
                        slot, v = d.sig
                        if waited.get(slot, 0) < v:
                            waited[slot] = v
                            e.wait_ge(sems[slot], v)
                    for slot, v in op.waits.items():
                        if waited.get(slot, 0) < v:
                            waited[slot] = v
                            e.wait_ge(sems[slot], v)
                    inst = op.fn(e)
                    if op.dma_key is not None:
                        inst.then_inc(sems[op.sig], 16)
                    elif op.need_inc:
                        inst.then_inc(sems[op.sig[0]], 1)
                if cur[1] is not None:
                    cur[1].__exit__(None, None, None)
                if eng_name == "sp":
                    for slot, v in final_waits.items():
                        e.wait_ge(sems[slot], v)

            @block.tensor
            def _(e):
                run("pe", e)

            @block.scalar
            def _(e):
                run("act", e)

            @block.vector
            def _(e):
                run("dve", e)

            @block.gpsimd
            def _(e):
                run("pool", e)

            @block.sync
            def _(e):
                run("sp", e)
        self.stack.close()


D = 1024
CT = 256
DIN = 7696
DFF = 2816
EPS = 1e-6
NEG = -1.0e30
LN16 = 2.772588722239781


class OpsMixin:
    def mm(self, out, lhsT, rhs, start, stop, r, w):
        self.op("pe", lambda e: e.matmul(out, lhsT, rhs, start=start, stop=stop), r, w)

    def tp(self, out, in_, ident, r, w):
        self.op("pe", lambda e: e.transpose(out, in_, ident), r, w)

    def act(self, out, in_, func, r, w, **kw):
        self.op("act", lambda e: e.activation(out, in_, func, **kw), r, w)

    def tt(self, eng, out, a, b, op, r, w):
        self.op(eng, lambda e: e.tensor_tensor(out, a, b, op), r, w)

    def ts(self, eng, out, a, s1, s2, op0, op1, r, w):
        if op1 is None:
            self.op(eng, lambda e: e.tensor_scalar(out, a, s1, s2, op0), r, w)
        else:
            self.op(eng, lambda e: e.tensor_scalar(out, a, s1, s2, op0, op1), r, w)

    def stt(self, eng, out, a, s, b, op0, op1, r, w):
        self.op(eng, lambda e: e.scalar_tensor_tensor(out, a, s, b, op0, op1), r, w)

    def cp(self, eng, out, in_, r, w):
        if eng == "act":
            self.op("act", lambda e: e.activation(out, in_, AF.Copy), r, w)
        else:
            self.op(eng, lambda e: e.tensor_copy(out, in_), r, w)

    def memset(self, eng, out, val, w):
        self.op(eng, lambda e: e.memset(out, val), (), w)

    def recip(self, out, in_, r, w):
        self.op("dve", lambda e: e.reciprocal(out, in_), r, w)


class KProg(Prog, OpsMixin):
    pass


def conv_blocks(length):
    out = []
    s = 0
    while s < length:
        n = min(508, length - s)
        out.append((s, n))
        s += n
    return out


def build(S, debug=False, stop_after=None, scopes=False):
    NT = S + CT
    NTL = NT // 128
    NL = S // 128
    NCT = CT // 128
    nc = bass.Bass("TRN2", target_bir_lowering=False)
    P = KProg(nc, arena_bytes=206 * 1024)
    P.scopes = scopes
    P.phase = "p0"

    def din(name, shape, dt=F32):
        return nc.dram_tensor(name, list(shape), dt, kind="ExternalInput").ap()

    def dscr(name, shape, dt):
        kind = "ExternalOutput" if debug else "Internal"
        return nc.dram_tensor(name, list(shape), dt, kind=kind).ap()

    x = din("x", [S, D]); c = din("c", [D]); ctx = din("ctx", [CT, D]); c_ctx = din("c_ctx", [D])
    w_mod = din("w_mod", [D, 6 * D]); b_mod = din("b_mod", [6 * D])
    norm1_g = din("norm1_g", [D]); norm2_g = din("norm2_g", [D])
    w_in = din("w_in", [D, DIN]); gate_b = din("gate_b", [16])
    conv_w = din("conv_w", [5, 2 * D]); conv_b = din("conv_b", [2 * D])
    m_norm_g = din("m_norm_g", [D]); q_norm_g = din("q_norm_g", [128]); k_norm_g = din("k_norm_g", [128])
    w_pa = din("w_pa", [D, D]); w_pb = din("w_pb", [D, D]); w_o = din("w_o", [D, D])
    w_g = din("w_ffn_gate", [D, DFF]); w_u = din("w_ffn_up", [D, DFF]); w_d = din("w_ffn_down", [DFF, D])
    final_g = din("final_g", [D])
    cst = din("cst", [128, 512]); rope = din("rope", [S, 128])
    out = nc.dram_tensor("out", [S, D], F32, kind="ExternalOutput").ap()

    mqT = dscr("mqT", [D, NT], BF16); mkT = dscr("mkT", [D, NT], BF16)
    mv = dscr("mv", [NT, D], BF16); osig = dscr("osig", [S, D], BF16)
    qaT = dscr("qaT", [D, S], BF16); kaT = dscr("kaT", [256, NT], BF16); va = dscr("va", [NT, 256], BF16)
    gT = dscr("gT", [2 * D, S], BF16)
    hdir = dscr("hdir", [2, S, D], F32)
    ymT = dscr("ymT", [D, S], BF16); yaT = dscr("yaT", [D, S], BF16)
    x1d = dscr("x1d", [S, D], F32)
    k_mqT = P.T("mqT"); k_mkT = P.T("mkT"); k_mv = P.T("mv"); k_osig = P.T("osig"); k_qaT = P.T("qaT")
    k_kaT = P.T("kaT"); k_va = P.T("va"); k_gT = P.T("gT"); k_hdir = P.T("hdir"); k_ymT = P.T("ymT")
    k_yaT = P.T("yaT"); k_x1 = P.T("x1d"); k_out = P.T("OUT")
    dbg = {}

    identf = P.sb("identf", [128, 128], F32)
    identb = P.sb("identb", [128, 128], BF16)
    trif = P.sb("trif", [128, 2, 128], F32)
    trib = P.sb("trib", [128, 2, 128], BF16)
    e0f = P.sb("e0f", [128, 128], F32)
    onesf = P.sb("onesf", [128, 128], F32)
    onesb = P.sb("onesb", [128, 2], BF16)
    modT = P.sb("modT", [128, 48, 2], F32)
    A1 = P.sb("A1", [128, 8, 2], F32)
    A2 = P.sb("A2", [128, 8], F32)
    G1bc = P.sb("G1bc", [128, D], F32)
    G2bc = P.sb("G2bc", [128, D], F32)
    Gd = [P.sb("Gd%d" % d, [128, NTL, 8], F32) for d in range(2)]
    P.dma("sp", identf.ap(), cst[:, 0:128], w=[identf])
    P.dma("sp", trif.ap(), cst[:, 128:384].rearrange("p (a b) -> p a b", a=2), w=[trif])
    P.dma("sp", e0f.ap(), cst[:, 384:512], w=[e0f])
    P.cp("dve", identb.ap(), identf.ap(), [identf], [identb])
    P.cp("dve", trib.ap(), trif.ap(), [trif], [trib])
    P.memset("dve", onesf.ap(), 1.0, [onesf])
    P.memset("dve", onesb.ap(), 1.0, [onesb])
    P.set_mark()

    def tile_of(d, k):
        if d == 0:
            return k
        if k < NCT:
            return NCT - 1 - k
        return NTL + NCT - 1 - k

    step_of = [{tile_of(d, k): k for k in range(NTL)} for d in range(2)]

    sc = P.sb("sc", [128, 8, 2], F32)
    scs = P.sb("scs", [128, 8, 2], F32)
    bmod = P.sb("bmod", [128, 48], F32)
    n1g = P.sb("n1g", [128, 8], F32)
    n2g = P.sb("n2g", [128, 8], F32)
    wm = [P.sb("wm%d" % i, [128, 8, 512], F32) for i in range(2)]
    P.dma("sp", sc.ap()[:, :, 0], c.rearrange("(k p) -> p k", p=128), w=[sc], slow=True)
    P.dma("sp", sc.ap()[:, :, 1], c_ctx.rearrange("(k p) -> p k", p=128), w=[sc], slow=True)
    P.dma("sp", bmod.ap(), b_mod.rearrange("(k p) -> p k", p=128), w=[bmod], slow=True)
    P.dma("sp", n1g.ap(), norm1_g.rearrange("(k p) -> p k", p=128), w=[n1g], slow=True)
    P.dma("sp", n2g.ap(), norm2_g.rearrange("(k p) -> p k", p=128), w=[n2g], slow=True)
    P.act(scs.ap(), sc.ap(), AF.Silu, [sc], [scs])
    pmod = P.ps("pmod", [128, 48, 2], F32, bank=0)
    for pc in range(12):
        wt = wm[pc % 2]
        P.dma("sp", wt.ap(), w_mod[:, pc * 512:(pc + 1) * 512].rearrange("(k p) f -> p k f", p=128), w=[wt])
        for fl in range(4):
            fc = pc * 4 + fl
            for kc in range(8):
                P.mm(pmod.ap()[:, fc, :], wt.ap()[:, kc, fl * 128:(fl + 1) * 128], scs.ap()[:, kc, :],
                     kc == 0, kc == 7, [wt, scs], [pmod])
    P.tt("dve", modT.ap(), pmod.ap(), bmod.ap().unsqueeze(2).to_broadcast([128, 48, 2]), ALU.add, [pmod, bmod], [modT])
    tmpa = P.sb("tmpa", [128, 8, 2], F32)
    P.ts("dve", tmpa.ap(), modT.ap()[:, 8:16, :], 1.0, None, ALU.add, None, [modT], [tmpa])
    P.tt("dve", A1.ap(), tmpa.ap(), n1g.ap().unsqueeze(2).to_broadcast([128, 8, 2]), ALU.mult, [tmpa, n1g], [A1])
    tmpb = P.sb("tmpb", [128, 8], F32)
    P.ts("dve", tmpb.ap(), modT.ap()[:, 32:40, 0], 1.0, None, ALU.add, None, [modT], [tmpb])
    P.tt("dve", A2.ap(), tmpb.ap(), n2g.ap(), ALU.mult, [tmpb, n2g], [A2])
    dg = [P.sb("dg%d" % i, [128, 128], F32) for i in range(2)]
    for gi, (Gbc, base) in enumerate(((G1bc, 16), (G2bc, 40))):
        pb = [P.ps("pbc%d" % h, [128, 512], F32, bank=1 + h) for h in range(2)]
        for kc in range(8):
            dgt = dg[kc % 2]
            P.ts("dve", dgt.ap(), identf.ap(), modT.ap()[:, base + kc, 0:1], None, ALU.mult, None, [identf, modT], [dgt])
            P.mm(pb[kc // 4].ap()[:, (kc % 4) * 128:(kc % 4 + 1) * 128], onesf.ap(), dgt.ap(), True, True, [onesf, dgt], [pb[kc // 4]])
        for h in range(2):
            P.cp("dve", Gbc.ap()[:, h * 512:(h + 1) * 512], pb[h].ap(), [pb[h]], [Gbc])
    if debug:
        dbg["modT"] = nc.dram_tensor("d_modT", [128, 96], F32, kind="ExternalOutput").ap()
        P.dma("sp", dbg["modT"], modT.ap().rearrange("p a b -> p (a b)"), r=[modT], w=[P.T("OUTd0")])
        dbg["G1bc"] = nc.dram_tensor("d_G1bc", [128, D], F32, kind="ExternalOutput").ap()
        P.dma("sp", dbg["G1bc"], G1bc.ap(), r=[G1bc], w=[P.T("OUTd1")])
    P.barrier()
    P.reset()
    if stop_after == 0:
        P.finish()
        return nc

    P.phase = "p1a"
    hT = P.sb("hT", [128, 8, NT], BF16)
    hTt = [P.T("hT%d" % i) for i in range(NTL)]
    mark2 = P.off
    xt = [P.sb("xt%d" % i, [128, D], F32) for i in range(3)]
    junk = P.sb("junk", [128, D], BF16)
    st = [P.sb("st%d" % i, [128, 4], F32) for i in range(3)]
    xn = [P.sb("xn%d" % i, [128, D], BF16) for i in range(2)]
    tmpf = [P.sb("tmpf%d" % i, [128, 8, 128], F32) for i in range(2)]
    pT = [P.ps("pT%d" % i, [128, 8, 128], BF16, bank=i) for i in range(2)]

    def rstd_ops(stt_, n):
        P.act(stt_.ap()[:, 1:2], stt_.ap()[:, 0:1], AF.Sqrt, [stt_], [stt_], bias=EPS, scale=1.0 / n)
        P.recip(stt_.ap()[:, 2:3], stt_.ap()[:, 1:2], [stt_], [stt_])

    for i in range(NTL):
        xs = xt[i % 3]; ss = st[i % 3]; xb = xn[i % 2]; pt = pT[i % 2]; tf = tmpf[i % 2]
        src = ctx[i * 128:(i + 1) * 128, :] if i < NCT else x[(i - NCT) * 128:(i - NCT + 1) * 128, :]
        P.dma("sp", xs.ap(), src, w=[xs])
        P.act(junk.ap(), xs.ap(), AF.Square, [xs], [junk, ss], accum_out=ss.ap()[:, 0:1])
        rstd_ops(ss, D)
        P.ts("dve", xb.ap(), xs.ap(), ss.ap()[:, 2:3], None, ALU.mult, None, [xs, ss], [xb])
        for kc in range(8):
            P.tp(pt.ap()[:, kc, :], xb.ap()[:, kc * 128:(kc + 1) * 128], identb.ap(), [xb, identb], [pt])
        m = 1 if i < NCT else 0
        P.tt("dve", tf.ap(), pt.ap(), A1.ap()[:, :, m:m + 1].to_broadcast([128, 8, 128]), ALU.mult, [pt, A1], [tf])
        P.tt("pool", hT.ap()[:, :, i * 128:(i + 1) * 128], tf.ap(), modT.ap()[:, 0:8, m:m + 1].to_broadcast([128, 8, 128]), ALU.add,
             [tf, modT], [hTt[i]])
    if debug:
        dbg["hT"] = nc.dram_tensor("d_hT", [128, 8 * NT], BF16, kind="ExternalOutput").ap()
        P.dma("sp", dbg["hT"], hT.ap().rearrange("p a b -> p (a b)"), r=hTt, w=[P.T("OUTd2")])
    if stop_after == 1:
        P.finish()
        return nc
    P.barrier()
    P.off = mark2

    P.phase = "p1b"
    wb = [P.sb("wb%d" % i, [128, 8, 512], BF16) for i in range(2)]
    wcnt = [0]

    wgroups = [(g * 512, 512) for g in range(4)] + [(5648 + g * 512, 512) for g in range(4)] + \
              [(2048, 512), (2560, 512), (3072, 512), (3584, 512), (4096, 16), (4112, 512), (4624, 512), (5136, 512)]
    wtiles = {}

    def issue_w(gi):
        if gi >= len(wgroups) or gi in wtiles:
            return
        c0, ncols = wgroups[gi]
        t = wb[gi % 2]
        P.dma("pool", t.ap()[:, :, 0:ncols], w_in[:, c0:c0 + ncols].rearrange("(k p) f -> p k f", p=128), w=[t])
        wtiles[gi] = t

    def load_w(c0, ncols, prefetch=True):
        gi = wcnt[0]
        wcnt[0] += 1
        assert wgroups[gi] == (c0, ncols), (gi, c0, ncols)
        issue_w(gi)
        t = wtiles[gi]
        if prefetch:
            issue_w(gi + 1)
        return t

    pz = [P.ps("pz%d" % i, [128, 512], F32, bank=2 + i) for i in range(4)]
    pzc = [0]

    def next_pz():
        t = pz[pzc[0] % 4]
        pzc[0] += 1
        return t

    cw = P.sb("cw", [128, 16, 5], F32)
    cb = P.sb("cb", [128, 16], F32)
    gbb = P.sb("gbb", [128, 16], F32)
    qgb = P.sb("qgb", [128, 128], F32)
    kgb = P.sb("kgb", [128, 128], F32)
    for j in range(5):
        P.dma("sp", cw.ap()[:, :, j], conv_w[j].rearrange("(c p) -> p c", p=128), w=[cw], slow=True)
    P.dma("sp", cb.ap(), conv_b.rearrange("(c p) -> p c", p=128), w=[cb], slow=True)
    P.dma("sp", gbb.ap(), gate_b.partition_broadcast(128), w=[gbb])
    P.dma("sp", qgb.ap(), q_norm_g.partition_broadcast(128), w=[qgb])
    P.dma("sp", kgb.ap(), k_norm_g.partition_broadcast(128), w=[kgb])

    def hts(g0, g1):
        return hTt[g0 // 128:(g1 - 1) // 128 + 1]

    stg8 = [P.sb("stg8_%d" % i, [128, 512], BF16) for i in range(8)]
    scnt = [0]

    def next_stage():
        t = stg8[scnt[0] % 8]
        scnt[0] += 1
        return t

    sqc = [0]

    def stq(from_act):
        sqc[0] += 1
        m = sqc[0] % 3
        if m == 0:
            return "sp"
        if m == 1:
            return "act" if from_act else "sp"
        return "pool"

    mark3 = P.off
    Zs = [P.sb("Zs%d" % i, [128, 512], F32) for i in range(4)]
    acc = [P.sb("acc%d" % i, [128, 508], F32) for i in range(4)]
    fcnt = [0]
    items = []
    for grp in range(4):
        for pair in range(2):
            chs = [grp * 4 + pair * 2, grp * 4 + pair * 2 + 1]
            is_k = chs[0] >= 8
            seqs = [(CT, S)] + ([(0, CT)] if is_k else [])
            for (s0, ln) in seqs:
                for (bs, n) in conv_blocks(ln):
                    items.append((grp, chs, is_k, s0, ln, bs, n))
    fw = {}

    def f_stage1(it):
        grp, chs, is_k, s0, ln, bs, n = it
        if grp not in fw:
            fw[grp] = load_w(grp * 512, 512)
        wt = fw[grp]
        w0 = max(0, bs - 2); w1 = min(ln, bs + n + 2)
        ncol = w1 - w0
        off = 2 - (bs - w0)
        hr = hts(s0 + w0, s0 + w1)
        st_ = []
        for ch in chs:
            cl = ch % 4
            i = fcnt[0]; fcnt[0] += 1
            z = Zs[i % 4]; a = acc[i % 4]
            p = next_pz()
            st_.append((ch, z, a))
            for kc in range(8):
                P.mm(p.ap()[:, 0:ncol], wt.ap()[:, kc, cl * 128:(cl + 1) * 128], hT.ap()[:, kc, s0 + w0:s0 + w1],
                     kc == 0, kc == 7, [wt] + hr, [p])
            if bs == 0:
                P.memset("dve", z.ap()[:, 0:2], 0.0, [z])
            if bs + n == ln:
                P.memset("dve", z.ap()[:, 2 + n:4 + n], 0.0, [z])
            P.cp("act", z.ap()[:, off:off + ncol], p.ap()[:, 0:ncol], [p], [z])
        return (it, st_)

    def f_stage2(rec):
        (grp, chs, is_k, s0, ln, bs, n), st_ = rec
        dst, kdst = (mkT, k_mkT) if is_k else (mqT, k_mqT)
        for j in range(5):
            for (ch, z, a) in st_:
                if j == 0:
                    P.ts("dve", a.ap()[:, 0:n], z.ap()[:, 0:n], cw.ap()[:, ch, 0:1], None, ALU.mult, None, [z, cw], [a])
                else:
                    P.stt("dve", a.ap()[:, 0:n], z.ap()[:, j:j + n], cw.ap()[:, ch, j:j + 1], a.ap()[:, 0:n], ALU.mult, ALU.add, [z, cw, a], [a])
        for (ch, z, a) in st_:
            o = next_stage()
            P.act(o.ap()[:, 0:n], a.ap()[:, 0:n], AF.Silu, [a, cb], [o], bias=cb.ap()[:, ch:ch + 1])
            r0 = (ch % 8) * 128
            P.dma(stq(True), dst[r0:r0 + 128, s0 + bs:s0 + bs + n], o.ap()[:, 0:n], r=[o], w=[kdst], key=o)

    prev = None
    for it in items:
        cur = f_stage1(it)
        if prev is not None:
            f_stage2(prev)
        prev = cur
    f_stage2(prev)
    P.barrier()
    P.off = mark3

    if stop_after == "F":
        P.finish()
        return nc
    P.phase = "p1b_G"
    gcnt = 0
    for grp in range(4):
        wt = load_w(5648 + grp * 512, 512)
        for cl in range(4):
            ch = grp * 4 + cl
            for b0 in range(0, S, 512):
                p = next_pz()
                o = next_stage()
                hr = hts(CT + b0, CT + b0 + 512)
                for kc in range(8):
                    P.mm(p.ap(), wt.ap()[:, kc, cl * 128:(cl + 1) * 128], hT.ap()[:, kc, CT + b0:CT + b0 + 512], kc == 0, kc == 7, [wt] + hr, [p])
                P.act(o.ap(), p.ap(), AF.Sigmoid, [p], [o])
                P.dma(stq(True), gT[ch * 128:(ch + 1) * 128, b0:b0 + 512], o.ap(), r=[o], w=[k_gT], key=o)

    if stop_after == "G":
        P.finish()
        return nc
    P.phase = "p1b_mv"
    tcnt = [0]

    def tok_mm(wt, ncols, i):
        p = next_pz()
        for kc in range(8):
            P.mm(p.ap()[:, 0:ncols], hT.ap()[:, kc, i * 128:(i + 1) * 128], wt.ap()[:, kc, 0:ncols], kc == 0, kc == 7, [wt, hTt[i]], [p])
        return p

    for grp in range(2):
        wt = load_w(2048 + grp * 512, 512)
        for i in range(NTL):
            p = tok_mm(wt, 512, i)
            o = next_stage()
            P.cp("act" if i % 2 else "dve", o.ap(), p.ap(), [p], [o])
            P.dma(stq(i % 2 == 1), mv[i * 128:(i + 1) * 128, grp * 512:(grp + 1) * 512], o.ap(), r=[o], w=[k_mv], key=o)
    if stop_after == "mv":
        P.finish()
        return nc
    P.phase = "p1b_o"
    for grp in range(2):
        wt = load_w(3072 + grp * 512, 512)
        for i in range(NCT, NTL):
            p = tok_mm(wt, 512, i)
            o = next_stage()
            P.act(o.ap(), p.ap(), AF.Sigmoid, [p], [o])
            P.dma(stq(True), osig[(i - NCT) * 128:(i - NCT + 1) * 128, grp * 512:(grp + 1) * 512], o.ap(), r=[o], w=[k_osig], key=o)
    if stop_after == "o":
        P.finish()
        return nc
    P.phase = "p1b_gt"
    wt = load_w(4096, 16)
    Gdt = [P.T("Gdt0"), P.T("Gdt1")]
    for i in range(NTL):
        p = tok_mm(wt, 16, i)
        for d in range(2):
            P.tt("dve", Gd[d].ap()[:, step_of[d][i], :], p.ap()[:, d * 8:(d + 1) * 8], gbb.ap()[:, d * 8:(d + 1) * 8], ALU.add, [p, gbb], [Gdt[d]])

    if stop_after == "gates":
        P.finish()
        return nc
    P.phase = "p1b_q"
    ropet = [P.sb("ropet%d" % i, [128, 128], F32) for i in range(2)]
    sqb = [P.sb("sq%d" % i, [128, 512], BF16) for i in range(2)]
    qst = [P.sb("qst%d" % i, [128, 8], F32) for i in range(2)]
    qn = [P.sb("qn%d" % i, [128, 512], F32) for i in range(2)]
    t1 = [P.sb("rt%d" % i, [128, 256], F32) for i in range(4)]
    qr = [P.sb("qr%d" % i, [128, 512], BF16) for i in range(2)]
    qstage = [P.sb("qstage%d" % i, [128, 4, 512], BF16) for i in range(2)]
    acnt = [0]

    def norm_rope(p, nh, gb, rt, do_rope):
        i = acnt[0]; acnt[0] += 1
        s_ = qst[i % 2]; q_ = qn[i % 2]; o_ = qr[i % 2]; sq = sqb[i % 2]
        W = nh * 128
        P.act(sq.ap()[:, 0:W], p.ap()[:, 0:W], AF.Square, [p], [sq])
        P.op("dve", lambda e: e.reduce_sum(s_.ap()[:, 0:nh], sq.ap()[:, 0:W].rearrange("p (h d) -> p h d", h=nh), AX.X), [sq], [s_])
        P.act(s_.ap()[:, 4:4 + nh], s_.ap()[:, 0:nh], AF.Sqrt, [s_], [s_], bias=EPS, scale=1.0 / 128)
        P.recip(s_.ap()[:, 0:nh], s_.ap()[:, 4:4 + nh], [s_], [s_])
        P.tt("dve", q_.ap()[:, 0:W].rearrange("p (h d) -> p h d", h=nh), p.ap()[:, 0:W].rearrange("p (h d) -> p h d", h=nh),
             s_.ap()[:, 0:nh].unsqueeze(2).to_broadcast([128, nh, 128]), ALU.mult, [p, s_], [q_])
        if not do_rope:
            P.tt("dve", o_.ap()[:, 0:W].rearrange("p (h d) -> p h d", h=nh), q_.ap()[:, 0:W].rearrange("p (h d) -> p h d", h=nh),
                 gb.ap().unsqueeze(1).to_broadcast([128, nh, 128]), ALU.mult, [q_, gb], [o_])
            return o_
        P.tt("dve", q_.ap()[:, 0:W].rearrange("p (h d) -> p h d", h=nh), q_.ap()[:, 0:W].rearrange("p (h d) -> p h d", h=nh),
             gb.ap().unsqueeze(1).to_broadcast([128, nh, 128]), ALU.mult, [q_, gb], [q_])
        qv = q_.ap()[:, 0:W].rearrange("p (h i two) -> p h i two", h=nh, two=2)
        ov = o_.ap()[:, 0:W].rearrange("p (h i two) -> p h i two", h=nh, two=2)
        x1 = qv[:, :, :, 0]; x2 = qv[:, :, :, 1]
        cosb = rt.ap()[:, 0:64].unsqueeze(1).to_broadcast([128, nh, 64])
        sinb = rt.ap()[:, 64:128].unsqueeze(1).to_broadcast([128, nh, 64])
        tv = [t.ap()[:, 0:nh * 64].rearrange("p (h i) -> p h i", h=nh) for t in t1]
        P.tt("dve", tv[0], x1, cosb, ALU.mult, [q_, rt], [t1[0]])
        P.tt("dve", tv[1], x2, sinb, ALU.mult, [q_, rt], [t1[1]])
        P.tt("dve", ov[:, :, :, 0], tv[0], tv[1], ALU.subtract, [t1[0], t1[1]], [o_])
        P.tt("dve", tv[2], x1, sinb, ALU.mult, [q_, rt], [t1[2]])
        P.tt("dve", tv[3], x2, cosb, ALU.mult, [q_, rt], [t1[3]])
        P.tt("dve", ov[:, :, :, 1], tv[2], tv[3], ALU.add, [t1[2], t1[3]], [o_])
        return o_

    def load_rope(i):
        rt = ropet[i % 2]
        P.dma("sp", rt.ap(), rope[(i - NCT) * 128:(i - NCT + 1) * 128, :], w=[rt])
        return rt

    pT2 = [P.ps("pT2_%d" % i, [128, 4, 128], BF16, bank=i) for i in range(2)]
    wq = [load_w(4112, 512, prefetch=False), load_w(4624, 512, prefetch=False)]
    ptc = [0]
    rts = {}
    qunits = [(b0, j, grp) for b0 in range(0, NL, 4) for j in range(4) for grp in range(2)]

    def q_stage1(u):
        b0, j, grp = u
        i = NCT + b0 + j
        if grp == 0:
            rts[i] = load_rope(i)
        return tok_mm(wq[grp], 512, i)

    def q_stage2(u, p):
        b0, j, grp = u
        i = NCT + b0 + j
        stg = qstage[grp]
        o_ = norm_rope(p, 4, qgb, rts[i], True)
        ptt = pT2[ptc[0] % 2]; ptc[0] += 1
        for h in range(4):
            P.tp(ptt.ap()[:, h, :], o_.ap()[:, h * 128:(h + 1) * 128], identb.ap(), [o_, identb], [ptt])
        P.cp("act", stg.ap()[:, :, j * 128:(j + 1) * 128], ptt.ap(), [ptt], [stg])
        if j == 3 and grp == 1:
            for g2 in range(2):
                for h in range(4):
                    P.dma(stq(True), qaT[(g2 * 4 + h) * 128:(g2 * 4 + h + 1) * 128, b0 * 128:(b0 + 4) * 128], qstage[g2].ap()[:, h, :],
                          r=[qstage[g2]], w=[k_qaT], key=qstage[g2])

    prevq = None
    for u in qunits:
        pcur = q_stage1(u)
        if prevq is not None:
            q_stage2(*prevq)
        prevq = (u, pcur)
    q_stage2(*prevq)
    if stop_after == "q":
        P.finish()
        return nc
    P.phase = "p1b_kv"
    wt = load_w(5136, 512)
    kstage = [P.sb("kstage%d" % i, [128, 2, 128], BF16) for i in range(2)]
    krt = {}

    def kv_stage1(i):
        if i >= NCT:
            krt[i] = load_rope(i)
        return tok_mm(wt, 512, i)

    def kv_stage2(i, p):
        lat = i >= NCT
        o_ = norm_rope(p, 2, kgb, krt.get(i), lat)
        ptt = pT2[i % 2]
        for h in range(2):
            P.tp(ptt.ap()[:, h, :], o_.ap()[:, h * 128:(h + 1) * 128], identb.ap(), [o_, identb], [ptt])
        ks = kstage[i % 2]
        P.cp("act", ks.ap(), ptt.ap()[:, 0:2, :], [ptt], [ks])
        for h in range(2):
            P.dma(stq(True), kaT[h * 128:(h + 1) * 128, i * 128:(i + 1) * 128], ks.ap()[:, h, :], r=[ks], w=[k_kaT], key=ks)
        o = next_stage()
        P.cp("dve", o.ap()[:, 0:256], p.ap()[:, 256:512], [p], [o])
        P.dma(stq(False), va[i * 128:(i + 1) * 128, :], o.ap()[:, 0:256], r=[o], w=[k_va], key=o)

    prevk = None
    for i in range(NTL):
        pcur = kv_stage1(i)
        if prevk is not None:
            kv_stage2(*prevk)
        prevk = (i, pcur)
    kv_stage2(*prevk)
    if debug:
        for d in range(2):
            dbg["Gd%d" % d] = nc.dram_tensor("d_Gd%d" % d, [128, NTL * 8], F32, kind="ExternalOutput").ap()
            P.dma("sp", dbg["Gd%d" % d], Gd[d].ap().rearrange("p a b -> p (a b)"), r=[Gdt[d]], w=[P.T("OUTg%d" % d)])
    P.barrier()
    P.reset()
    if stop_after == 2:
        P.finish()
        return nc

    P.phase = "p2a"
    NS = NTL
    W4 = NS * 4
    Aa = [P.sb("Aa%d" % d, [128, W4], F32) for d in range(2)]
    Ee = [P.sb("Ee%d" % d, [128, W4], F32) for d in range(2)]
    C0 = [P.sb("C0%d" % d, [128, W4], F32) for d in range(2)]
    mng = P.sb("mng", [128, D], F32)
    P.dma("sp", mng.ap(), m_norm_g.partition_broadcast(128), w=[mng])
    pieces = [(c0_, min(c0_ + 128, W4)) for c0_ in range(0, W4, 128)]
    pTr = P.ps("pTr", [128, 128], F32, bank=4)
    for d in range(2):
        LF = P.sb("LF%d" % d, [128, W4], F32)
        gtmp = P.sb("gtmp%d" % d, [128, W4], F32)
        Bv = P.sb("Bv%d" % d, [128, W4], F32)
        Mb = P.sb("Mb%d" % d, [128, W4], F32)
        mrow = P.sb("mrow%d" % d, [128, W4], F32)
        mprev = P.sb("mprev%d" % d, [128, W4], F32)
        Rr = P.sb("Rr%d" % d, [128, W4], F32)
        FLb = P.sb("FLb%d" % d, [128, W4], F32)
        mcol = P.sb("mcol%d" % d, [128, 4], F32)
        dgm = P.sb("dgm%d" % d, [128, 128], F32)
        v3 = lambda t: t.ap().rearrange("p (s h) -> p s h", h=4)
        pF = P.ps("pF%d" % d, [128, W4], F32, bank=0 + d)
        pFL = P.ps("pFL%d" % d, [128, W4], F32, bank=2 + d)
        pM = P.ps("pM%d" % d, [128, W4], F32, bank=5 + d)
        P.act(v3(gtmp), Gd[d].ap()[:, :, 4:8], AF.Exp, [Gdt[d]], [gtmp], scale=-1.0)
        P.act(gtmp.ap(), gtmp.ap(), AF.Ln, [gtmp], [gtmp], bias=1.0)
        P.ts("dve", LF.ap(), gtmp.ap(), -1.0, None, ALU.mult, None, [gtmp], [LF])
        P.mm(pF.ap(), trif.ap()[:, d, :], LF.ap(), True, True, [trif, LF], [pF])
        P.mm(pFL.ap(), onesf.ap(), LF.ap(), True, True, [onesf, LF], [pFL])
        P.tt("dve", v3(Bv), Gd[d].ap()[:, :, 0:4], pF.ap().rearrange("p (s h) -> p s h", h=4), ALU.subtract, [Gdt[d], pF], [Bv])
        P.cp("dve", FLb.ap(), pFL.ap(), [pFL], [FLb])
        for pi, (a0, a1) in enumerate(pieces):
            w_ = a1 - a0
            P.tp(pTr.ap()[0:w_, :], Bv.ap()[:, a0:a1], identf.ap(), [Bv, identf], [pTr])
            P.memset("dve", mcol.ap()[:, pi:pi + 1], 0.0, [mcol])
            P.op("dve", lambda e, o_=mcol.ap()[0:w_, pi:pi + 1], i_=pTr.ap()[0:w_, :]: e.reduce_max(o_, i_, AX.X), [pTr, mcol], [mcol])
            P.ts("dve", dgm.ap(), identf.ap(), mcol.ap()[:, pi:pi + 1], None, ALU.mult, None, [identf, mcol], [dgm])
            P.mm(pM.ap()[:, a0:a1], onesf.ap(), dgm.ap()[:, 0:w_], True, True, [onesf, dgm], [pM])
        P.cp("dve", Mb.ap(), pM.ap(), [pM], [Mb])
        for h in range(4):
            P.op("dve", lambda e, o_=v3(mrow)[:, :, h], a_=v3(Mb)[:, :, h], b_=v3(FLb)[:, :, h]: e.tensor_tensor_scan(o_, a_, b_, NEG, ALU.max, ALU.add),
                 [Mb, FLb], [mrow])
        P.memset("dve", mprev.ap()[:, 0:4], NEG, [mprev])
        P.cp("dve", mprev.ap()[:, 4:W4], mrow.ap()[:, 0:W4 - 4], [mrow], [mprev])
        P.tt("dve", Rr.ap(), mprev.ap(), Mb.ap(), ALU.max, [mprev, Mb], [Rr])
        P.tt("dve", gtmp.ap(), mprev.ap(), Rr.ap(), ALU.subtract, [mprev, Rr], [gtmp])
        P.ts("dve", gtmp.ap(), gtmp.ap(), -200.0, None, ALU.max, None, [gtmp], [gtmp])
        P.act(C0[d].ap(), gtmp.ap(), AF.Exp, [gtmp], [C0[d]])
        if debug:
            for nm, tl in (("Mb", Mb), ("FLb", FLb), ("mrow", mrow), ("Rr", Rr), ("Bv", Bv)):
                dbg[nm + str(d)] = nc.dram_tensor("d_%s%d" % (nm, d), [128, W4], F32, kind="ExternalOutput").ap()
                P.dma("sp", dbg[nm + str(d)], tl.ap(), r=[tl], w=[P.T("OUT%s%d" % (nm, d))])
        P.tt("dve", Bv.ap(), Bv.ap(), Rr.ap(), ALU.subtract, [Bv, Rr], [Bv])
        P.act(Aa[d].ap(), Bv.ap(), AF.Exp, [Bv], [Aa[d]], bias=-LN16)
        P.tt("dve", LF.ap(), pF.ap(), Rr.ap(), ALU.add, [pF, Rr], [LF])
        P.act(Ee[d].ap(), LF.ap(), AF.Exp, [LF], [Ee[d]], scale=-1.0)
    if debug:
        for nm, tl in (("Aa", Aa), ("Ee", Ee), ("C0", C0)):
            for d in range(2):
                dbg[nm + str(d)] = nc.dram_tensor("d_%s%d" % (nm, d), [128, W4], F32, kind="ExternalOutput").ap()
                P.dma("sp", dbg[nm + str(d)], tl[d].ap(), r=[tl[d]], w=[P.T("OUT%s%d" % (nm, d))])
    P.barrier()
    if stop_after == "2a":
        P.finish()
        return nc

    P.phase = "p2b"
    kTin = [[P.sb("kTin%d%d" % (d, i), [128, 8, 128], BF16) for i in range(2)] for d in range(2)]
    qTin = [[P.sb("qTin%d%d" % (d, i), [128, 8, 128], BF16) for i in range(2)] for d in range(2)]
    vin = [[P.sb("vin%d%d" % (d, i), [128, D], BF16) for i in range(2)] for d in range(2)]
    kp = [[P.sb("kp%d%d" % (d, i), [128, D], BF16) for i in range(2)] for d in range(2)]
    Sm = [[P.sb("Sm%d%d" % (d, i), [128, 4, 128], BF16) for i in range(2)] for d in range(2)]
    qs = [[P.sb("qs%d%d" % (d, i), [128, 8, 128], BF16) for i in range(2)] for d in range(2)]
    Cst = [P.sb("Cst%d" % d, [128, 4, 512], F32) for d in range(2)]
    Cbf = [P.sb("Cbf%d" % d, [128, 4, 512], BF16) for d in range(2)]
    Cbt = [[P.T("Cbt%d%d" % (d, h)) for h in range(4)] for d in range(2)]
    Cft = [[P.T("Cft%d%d" % (d, h)) for h in range(4)] for d in range(2)]
    nst = [P.sb("nst%d" % d, [128, 8], F32) for d in range(2)]
    ntmp = [P.sb("ntmp%d" % d, [128, 8], F32) for d in range(2)]
    nbf = [P.sb("nbf%d" % d, [128, 8], BF16) for d in range(2)]
    hbuf = [[P.sb("hbuf%d%d" % (d, i), [128, D], F32) for i in range(2)] for d in range(2)]
    hoth = [P.sb("hoth%d" % i, [128, D], F32) for i in range(2)]
    osg = [P.sb("osg%d" % i, [128, D], BF16) for i in range(2)]
    ymt = [P.sb("ymt%d" % i, [128, D], BF16) for i in range(2)]
    dsb = [P.sb("dsb%d" % d, [128, 16], F32) for d in range(2)]
    fst = [P.sb("fst%d" % i, [128, 16], F32) for i in range(2)]
    fjunk = P.sb("fjunk", [128, 256], BF16)
    half = NL // 2
    GRP = 4 if half % 4 == 0 else (2 if half % 2 == 0 else 1)
    ymstage = [[P.sb("ymst%d%d" % (d, i), [128, 8, GRP * 128], BF16) for i in range(2)] for d in range(2)]
    pK = P.ps("pK", [128, 8, 128], BF16, bank=0)
    pdc = [P.ps("pdc%d" % i, [128, 512], F32, bank=1 + i) for i in range(2)]
    pb3 = P.T("pbank3")
    pdn = [TV(pb3, P.ps("pdn%d" % d, [128, 8], F32, bank=3, off=d * 32).h) for d in range(2)]
    pden = [TV(pb3, P.ps("pden%d" % d, [128, 4], F32, bank=3, off=64 + d * 32).h) for d in range(2)]
    pS = P.ps("pS", [128, 4, 128], F32, bank=4)
    pnum = [P.ps("pnum%d" % i, [128, 2, 256], F32, bank=5 + i) for i in range(2)]
    pT3 = P.ps("pT3", [128, 8, 128], BF16, bank=7)
    for d in range(2):
        P.memset("dve", Cst[d].ap(), 0.0, Cft[d])
        P.memset("pool", Cbf[d].ap(), 0.0, Cbt[d])
        P.memset("dve", nst[d].ap(), 0.0, [nst[d]])
        P.memset("dve", nbf[d].ap(), 0.0, [nbf[d]])
    fin_cnt = [0, 0]
    fcount = [0]
    for k in range(NS):
        for d in range(2):
            tile = tile_of(d, k)
            g0 = tile * 128
            lat = tile >= NCT
            sl = k % 2
            kt = kTin[d][sl]; vt = vin[d][sl]; qt = qTin[d][sl]; kpt = kp[d][sl]; smt = Sm[d][sl]; qst_ = qs[d][sl]
            P.dma("sp", kt.ap(), mkT[:, g0:g0 + 128].rearrange("(j p) t -> p j t", p=128), r=[k_mkT], w=[kt])
            P.dma("sp", vt.ap(), mv[g0:g0 + 128, :], r=[k_mv], w=[vt])
            if lat:
                P.dma("sp", qt.ap(), mqT[:, g0:g0 + 128].rearrange("(j p) t -> p j t", p=128), r=[k_mqT], w=[qt])
            for j in range(8):
                P.tp(pK.ap()[:, j, :], kt.ap()[:, j, :], identb.ap(), [kt, identb], [pK])
            if lat:
                for h in range(4):
                    for dc in range(2):
                        P.mm(pS.ap()[:, h, :], kt.ap()[:, 2 * h + dc, :], qt.ap()[:, 2 * h + dc, :], dc == 0, dc == 1, [kt, qt], [pS])
            for h in range(4):
                col = k * 4 + h
                dstv = kpt.ap()[:, h * 256:(h + 1) * 256].rearrange("p (a b) -> p a b", a=2)
                if h % 2:
                    P.act(dstv, pK.ap()[:, 2 * h:2 * h + 2, :], AF.Copy, [pK, Aa[d]], [kpt], scale=Aa[d].ap()[:, col:col + 1])
                else:
                    P.ts("dve", dstv, pK.ap()[:, 2 * h:2 * h + 2, :], Aa[d].ap()[:, col:col + 1], None, ALU.mult, None, [pK, Aa[d]], [kpt])
            if lat:
                for h in range(4):
                    col = k * 4 + h
                    P.stt("dve", smt.ap()[:, h, :], pS.ap()[:, h, :], Aa[d].ap()[:, col:col + 1], trib.ap()[:, d, :], ALU.mult, ALU.mult,
                          [pS, Aa[d], trib], [smt])
                    P.act(qst_.ap()[:, 2 * h:2 * h + 2, :], qt.ap()[:, 2 * h:2 * h + 2, :], AF.Copy, [qt, C0[d]], [qst_], scale=C0[d].ap()[:, col:col + 1])
                for h in range(4):
                    pn = pnum[h // 2]
                    P.mm(pn.ap()[:, h % 2, :], smt.ap()[:, h, :], vt.ap()[:, h * 256:(h + 1) * 256], True, False, [smt, vt], [pn])
                    P.mm(pn.ap()[:, h % 2, :], qst_.ap()[:, 2 * h, :], Cbf[d].ap()[:, h, 0:256], False, False, [qst_, Cbt[d][h]], [pn])
                    P.mm(pn.ap()[:, h % 2, :], qst_.ap()[:, 2 * h + 1, :], Cbf[d].ap()[:, h, 256:512], False, True, [qst_, Cbt[d][h]], [pn])
                for h in range(4):
                    P.mm(pden[d].ap()[:, h:h + 1], smt.ap()[:, h, :], onesb.ap()[:, 0:1], True, False, [smt, onesb], [pden[d]])
                    P.mm(pden[d].ap()[:, h:h + 1], qst_.ap()[:, 2 * h, :], nbf[d].ap()[:, 2 * h:2 * h + 1], False, False, [qst_, nbf[d]], [pden[d]])
                    P.mm(pden[d].ap()[:, h:h + 1], qst_.ap()[:, 2 * h + 1, :], nbf[d].ap()[:, 2 * h + 1:2 * h + 2], False, True, [qst_, nbf[d]], [pden[d]])
            for h in range(4):
                col = k * 4 + h
                pd = pdc[h % 2]
                for dc in range(2):
                    P.mm(pd.ap()[:, dc * 256:(dc + 1) * 256], kpt.ap()[:, h * 256 + dc * 128:h * 256 + (dc + 1) * 128], vt.ap()[:, h * 256:(h + 1) * 256],
                         True, True, [kpt, vt], [pd])
                P.stt("dve", Cst[d].ap()[:, h, :], Cst[d].ap()[:, h, :], C0[d].ap()[:, col:col + 1], pd.ap(), ALU.mult, ALU.add,
                      [Cft[d][h], C0[d], pd], [Cft[d][h]])
                P.cp("act", Cbf[d].ap()[:, h, :], Cst[d].ap()[:, h, :], [Cft[d][h]], [Cbt[d][h]])
            for j in range(8):
                P.mm(pdn[d].ap()[:, j:j + 1], kpt.ap()[:, j * 128:(j + 1) * 128], onesb.ap()[:, 0:1], True, True, [kpt, onesb], [pdn[d]])
            P.tt("dve", ntmp[d].ap().rearrange("p (h two) -> p h two", two=2), nst[d].ap().rearrange("p (h two) -> p h two", two=2),
                 C0[d].ap()[:, k * 4:(k + 1) * 4].unsqueeze(2).to_broadcast([128, 4, 2]), ALU.mult, [nst[d], C0[d]], [ntmp[d]])
            P.tt("dve", nst[d].ap(), ntmp[d].ap(), pdn[d].ap(), ALU.add, [ntmp[d], pdn[d]], [nst[d]])
            P.cp("dve", nbf[d].ap(), nst[d].ap(), [nst[d]], [nbf[d]])
            if not lat:
                continue
            ds_ = dsb[d]
            hb = hbuf[d][sl]
            P.cp("dve", ds_.ap()[:, 0:4], pden[d].ap(), [pden[d]], [ds_])
            P.stt("dve", ds_.ap()[:, 4:8], ds_.ap()[:, 0:4], -1.0, ds_.ap()[:, 0:4], ALU.mult, ALU.max, [ds_], [ds_])
            P.tt("dve", ds_.ap()[:, 8:12], ds_.ap()[:, 4:8], Ee[d].ap()[:, k * 4:(k + 1) * 4], ALU.max, [ds_, Ee[d]], [ds_])
            P.recip(ds_.ap()[:, 12:16], ds_.ap()[:, 8:12], [ds_], [ds_])
            for h in range(4):
                pn = pnum[h // 2]
                if h % 2:
                    P.act(hb.ap()[:, h * 256:(h + 1) * 256], pn.ap()[:, h % 2, :], AF.Copy, [pn, ds_], [hb], scale=ds_.ap()[:, 12 + h:13 + h])
                else:
                    P.ts("dve", hb.ap()[:, h * 256:(h + 1) * 256], pn.ap()[:, h % 2, :], ds_.ap()[:, 12 + h:13 + h], None, ALU.mult, None, [pn, ds_], [hb])
            li_ = tile - NCT
            ko = step_of[1 - d][tile]
            if ko > k:
                P.dma("pool", hdir[d, li_ * 128:(li_ + 1) * 128, :], hb.ap(), r=[hb], w=[k_hdir])
                continue
            fi = fcount[0]; fcount[0] += 1
            ho = hoth[fi % 2]; og = osg[fi % 2]; ym_ = ymt[fi % 2]; fs = fst[fi % 2]
            P.dma("sp", ho.ap(), hdir[1 - d, li_ * 128:(li_ + 1) * 128, :], r=[k_hdir], w=[ho])
            P.dma("sp", og.ap(), osig[li_ * 128:(li_ + 1) * 128, :], r=[k_osig], w=[og])
            P.tt("dve", ho.ap(), ho.ap(), hb.ap(), ALU.add, [ho, hb], [ho])
            for h in range(4):
                P.act(fjunk.ap(), ho.ap()[:, h * 256:(h + 1) * 256], AF.Square, [ho], [fjunk, fs], accum_out=fs.ap()[:, h:h + 1])
            P.act(fs.ap()[:, 4:8], fs.ap()[:, 0:4], AF.Sqrt, [fs], [fs], bias=EPS, scale=1.0 / 256)
            P.recip(fs.ap()[:, 8:12], fs.ap()[:, 4:8], [fs], [fs])
            P.tt("dve", ho.ap().rearrange("p (h e) -> p h e", h=4), ho.ap().rearrange("p (h e) -> p h e", h=4),
                 fs.ap()[:, 8:12].unsqueeze(2).to_broadcast([128, 4, 256]), ALU.mult, [ho, fs], [ho])
            P.tt("dve", ho.ap(), ho.ap(), mng.ap(), ALU.mult, [ho, mng], [ho])
            P.tt("dve", ym_.ap(), ho.ap(), og.ap(), ALU.mult, [ho, og], [ym_])
            for j in range(8):
                P.tp(pT3.ap()[:, j, :], ym_.ap()[:, j * 128:(j + 1) * 128], identb.ap(), [ym_, identb], [pT3])
            blk = li_ // GRP
            stg = ymstage[d][(fin_cnt[d] // GRP) % 2]
            P.cp("act", stg.ap()[:, :, (li_ % GRP) * 128:(li_ % GRP + 1) * 128], pT3.ap(), [pT3], [stg])
            fin_cnt[d] += 1
            if fin_cnt[d] % GRP == 0:
                for j in range(8):
                    P.dma("pool", ymT[j * 128:(j + 1) * 128, blk * GRP * 128:(blk + 1) * GRP * 128], stg.ap()[:, j, :], r=[stg], w=[k_ymT], key=stg)
    P.barrier()
    P.reset()
    if stop_after == 3:
        P.finish()
        return nc

    P.phase = "p3"
    KT = P.sb("KT", [128, 2, NT], BF16)
    Vr = P.sb("Vr", [128, NTL, 2, 132], BF16)
    P.memset("dve", Vr.ap()[:, :, :, 128:129], 1.0, [Vr])
    for h in range(2):
        P.dma("sp", KT.ap()[:, h, :], kaT[h * 128:(h + 1) * 128, :], r=[k_kaT], w=[KT])
        P.dma("sp", Vr.ap()[:, :, h, 0:128], va[:, h * 128:(h + 1) * 128].rearrange("(t p) e -> p t e", p=128), r=[k_va], w=[Vr])
    QB = 512
    qin = [P.sb("qin%d" % i, [128, 8, QB], BF16) for i in range(2)]
    Pt = [P.sb("Pt%d" % i, [128, QB], BF16) for i in range(3)]
    yat = [[P.sb("yat%d%d" % (i, j), [128, D], BF16) for j in range(4)] for i in range(2)]
    arec = [P.sb("arec%d" % i, [128, 4], F32) for i in range(2)]
    yastage = [P.sb("yast%d" % i, [128, 8, QB], BF16) for i in range(2)]
    pSa = [P.ps("pSa%d" % i, [128, QB], F32, bank=(0, 1, 7)[i]) for i in range(3)]
    pacc1 = [P.ps("pacc%d" % j, [128, 129], F32, bank=2 + j) for j in range(4)]
    pacc = [pacc1, pacc1]
    pT4 = P.ps("pT4", [128, 8, 128], BF16, bank=6)
    sc_att = 128.0 ** -0.5
    its = [(qb, h, kt_) for qb in range(S // QB) for h in range(8) for kt_ in range(NTL)]
    NI = len(its)
    DEPTH = 2

    def issue_qk(i):
        qb, h, kt_ = its[i]
        qi = qin[qb % 2]
        if h == 0 and kt_ == 0:
            for hh in range(8):
                P.dma("sp", qi.ap()[:, hh, :], qaT[hh * 128:(hh + 1) * 128, qb * QB:(qb + 1) * QB], r=[k_qaT], w=[qi])
        ps_ = pSa[i % 3]; pt_ = Pt[i % 3]
        P.mm(ps_.ap(), KT.ap()[:, h // 4, kt_ * 128:(kt_ + 1) * 128], qi.ap()[:, h, :], True, True, [KT, qi], [ps_])
        P.act(pt_.ap(), ps_.ap(), AF.Exp, [ps_], [pt_], scale=sc_att)

    def issue_pv(i):
        qb, h, kt_ = its[i]
        kvh = h // 4
        pt_ = Pt[i % 3]
        acc_ = pacc[h % 2]
        yt = yat[qb % 2]
        for j in range(4):
            P.mm(acc_[j].ap(), pt_.ap()[:, j * 128:(j + 1) * 128], Vr.ap()[:, kt_, kvh, 0:129], kt_ == 0, kt_ == NTL - 1, [pt_, Vr], [acc_[j]])
        if kt_ != NTL - 1:
            return
        ar = arec[h % 2]
        for j in range(4):
            P.recip(ar.ap()[:, j:j + 1], acc_[j].ap()[:, 128:129], [acc_[j]], [ar])
            P.ts("dve", yt[j].ap()[:, h * 128:(h + 1) * 128], acc_[j].ap()[:, 0:128], ar.ap()[:, j:j + 1], None, ALU.mult, None, [acc_[j], ar], [yt[j]])
        if h != 7:
            return
        stg = yastage[qb % 2]
        for j in range(4):
            for hh in range(8):
                P.tp(pT4.ap()[:, hh, :], yt[j].ap()[:, hh * 128:(hh + 1) * 128], identb.ap(), [yt[j], identb], [pT4])
            P.cp("dve", stg.ap()[:, :, j * 128:(j + 1) * 128], pT4.ap(), [pT4], [stg])
        for hh in range(8):
            P.dma("pool", yaT[hh * 128:(hh + 1) * 128, qb * QB:(qb + 1) * QB], stg.ap()[:, hh, :], r=[stg], w=[k_yaT], key=stg)

    for i in range(-DEPTH, NI):
        if i + DEPTH < NI:
            issue_qk(i + DEPTH)
        if i >= 0:
            issue_pv(i)
    P.barrier()
    P.reset()
    if stop_after == 4:
        P.finish()
        return nc

    P.phase = "p4a"
    wpa = P.sb("wpa", [128, 8, D], BF16); wpb = P.sb("wpb", [128, 8, D], BF16); wo = P.sb("wo", [128, 8, D], BF16)
    for wt_, src in ((wpa, w_pa), (wpb, w_pb), (wo, w_o)):
        for hh in range(2):
            P.dma("pool", wt_.ap()[:, :, hh * 512:(hh + 1) * 512], src[:, hh * 512:(hh + 1) * 512].rearrange("(k p) f -> p k f", p=128), w=[wt_])
    ymin = [P.sb("ymin%d" % i, [128, 8, 512], BF16) for i in range(2)]
    yain = [P.sb("yain%d" % i, [128, 8, 512], BF16) for i in range(2)]
    gin = [P.sb("gin%d" % i, [128, 16, 512], BF16) for i in range(2)]
    uT = [P.sb("uT%d" % i, [128, 8, 512], BF16) for i in range(2)]
    u1 = [P.sb("u1_%d" % i, [128, 512], F32) for i in range(2)]
    u2 = [P.sb("u2_%d" % i, [128, 512], F32) for i in range(2)]
    xin = [P.sb("xin%d" % i, [128, D], F32) for i in range(3)]
    ytmp = [P.sb("ytmp%d" % i, [128, D], F32) for i in range(2)]
    x1t = [P.sb("x1t%d" % i, [128, D], F32) for i in range(2)]
    pA = [P.ps("pA%d" % i, [128, 512], F32, bank=i) for i in range(2)]
    pB = [P.ps("pB%d" % i, [128, 512], F32, bank=2 + i) for i in range(2)]
    pY = [P.ps("pY%d" % i, [128, 512], F32, bank=4 + i) for i in range(4)]
    xc = 0
    for b in range(S // 512):
        ym_ = ymin[b % 2]; ya_ = yain[b % 2]; g_ = gin[b % 2]; u_ = uT[b % 2]
        c0_ = b * 512
        P.dma("sp", ym_.ap(), ymT[:, c0_:c0_ + 512].rearrange("(j p) t -> p j t", p=128), r=[k_ymT], w=[ym_])
        P.dma("sp", ya_.ap(), yaT[:, c0_:c0_ + 512].rearrange("(j p) t -> p j t", p=128), r=[k_yaT], w=[ya_])
        P.dma("sp", g_.ap(), gT[:, c0_:c0_ + 512].rearrange("(j p) t -> p j t", p=128), r=[k_gT], w=[g_])
        for fc in range(8):
            pa = pA[fc % 2]; pb_ = pB[fc % 2]; a1 = u1[fc % 2]; a2 = u2[fc % 2]
            for kc in range(8):
                P.mm(pa.ap(), wpa.ap()[:, kc, fc * 128:(fc + 1) * 128], ym_.ap()[:, kc, :], kc == 0, kc == 7, [wpa, ym_], [pa])
            for kc in range(8):
                P.mm(pb_.ap(), wpb.ap()[:, kc, fc * 128:(fc + 1) * 128], ya_.ap()[:, kc, :], kc == 0, kc == 7, [wpb, ya_], [pb_])
            P.tt("dve", a1.ap(), pa.ap(), g_.ap()[:, fc, :], ALU.mult, [pa, g_], [a1])
            P.tt("dve", a2.ap(), pb_.ap(), g_.ap()[:, 8 + fc, :], ALU.mult, [pb_, g_], [a2])
            P.tt("pool", u_.ap()[:, fc, :], a1.ap(), a2.ap(), ALU.add, [a1, a2], [u_])
        for j in range(4):
            xi = xin[xc % 3]; yt_ = ytmp[xc % 2]; xo = x1t[xc % 2]; xc += 1
            r0 = c0_ + j * 128
            P.dma("sp", xi.ap(), x[r0:r0 + 128, :], w=[xi])
            for hh in range(2):
                py = pY[(j * 2 + hh) % 4]
                for kc in range(8):
                    P.mm(py.ap(), u_.ap()[:, kc, j * 128:(j + 1) * 128], wo.ap()[:, kc, hh * 512:(hh + 1) * 512], kc == 0, kc == 7, [u_, wo], [py])
                P.tt("dve", yt_.ap()[:, hh * 512:(hh + 1) * 512], py.ap(), G1bc.ap()[:, hh * 512:(hh + 1) * 512], ALU.mult, [py, G1bc], [yt_])
            P.tt("pool", xo.ap(), yt_.ap(), xi.ap(), ALU.add, [yt_, xi], [xo])
            P.dma("pool", x1d[r0:r0 + 128, :], xo.ap(), r=[xo], w=[k_x1], key=xo)
    P.barrier()
    P.reset()
    if stop_after == 5:
        P.finish()
        return nc

    P.phase = "p4b"
    wg = P.sb("wg", [128, 8, DFF], BF16); wu = P.sb("wu", [128, 8, DFF], BF16); wd = P.sb("wd", [128, 22, D], BF16)
    for wt_, src in ((wg, w_g), (wu, w_u)):
        for c0_ in range(0, DFF, 704):
            P.dma("pool", wt_.ap()[:, :, c0_:c0_ + 704], src[:, c0_:c0_ + 704].rearrange("(k p) f -> p k f", p=128), w=[wt_])
    for hh in range(2):
        P.dma("pool", wd.ap()[:, :, hh * 512:(hh + 1) * 512], w_d[:, hh * 512:(hh + 1) * 512].rearrange("(k p) f -> p k f", p=128), w=[wd])
    fgb = G1bc
    P.dma("sp", fgb.ap(), final_g.partition_broadcast(128), w=[fgb])
    TB = 256
    NJ = TB // 128
    x1in = [P.sb("x1in%d" % i, [128, D], F32) for i in range(2 * NJ)]
    st2 = [P.sb("st2_%d" % i, [128, 4], F32) for i in range(3)]
    junk2 = P.sb("junk2", [128, D], BF16)
    xn2 = [P.sb("xn2_%d" % i, [128, D], BF16) for i in range(2)]
    tf2 = [P.sb("tf2_%d" % i, [128, 8, 128], F32) for i in range(1)] * 2
    h2T = [P.sb("h2T%d" % i, [128, 8, TB], BF16) for i in range(2)]
    aT = [P.sb("aT%d" % i, [128, 22, TB], BF16) for i in range(1)] * 2
    sg = [P.sb("sg%d" % i, [128, TB], F32) for i in range(2)]
    ftmp = [P.sb("ftmp%d" % i, [128, D], F32) for i in range(1)] * 2
    pT5 = [P.ps("pT5_%d" % i, [128, 8, 128], BF16, bank=i) for i in range(2)]
    pG = [P.ps("pG%d" % i, [128, TB], F32, bank=2 + i) for i in range(2)]
    pU = [P.ps("pU%d" % i, [128, TB], F32, bank=4 + i) for i in range(2)]
    pD = [P.ps("pD%d" % i, [128, 512], F32, bank=6 + i) for i in range(2)] * 2
    tcn = [0]
    NB4 = S // TB
    xsets = {}

    def prologue_a(b):
        xs_ = []
        for j in range(NJ):
            r0 = b * TB + j * 128
            xi = x1in[(b % 2) * NJ + j]; ss = st2[tcn[0] % 3]; xb = xn2[j]; tcn[0] += 1
            xs_.append(xi)
            P.dma("sp", xi.ap(), x1d[r0:r0 + 128, :], r=[k_x1], w=[xi])
            P.act(junk2.ap(), xi.ap(), AF.Square, [xi], [junk2, ss], accum_out=ss.ap()[:, 0:1])
            rstd_ops(ss, D)
            P.ts("dve", xb.ap(), xi.ap(), ss.ap()[:, 2:3], None, ALU.mult, None, [xi, ss], [xb])
        xsets[b] = xs_

    def prologue_b(b):
        h2 = h2T[b % 2]
        for j in range(NJ):
            xb = xn2[j]; pt = pT5[j % 2]; tf = tf2[j % 2]
            for kc in range(8):
                P.tp(pt.ap()[:, kc, :], xb.ap()[:, kc * 128:(kc + 1) * 128], identb.ap(), [xb, identb], [pt])
            P.tt("dve", tf.ap(), pt.ap(), A2.ap().unsqueeze(2).to_broadcast([128, 8, 128]), ALU.mult, [pt, A2], [tf])
            P.tt("pool", h2.ap()[:, :, j * 128:(j + 1) * 128], tf.ap(), modT.ap()[:, 24:32, 0:1].to_broadcast([128, 8, 128]), ALU.add, [tf, modT], [h2])

    def gateup(b):
        h2 = h2T[b % 2]; a_ = aT[b % 2]
        for fc in range(22):
            pg = pG[fc % 2]; pu = pU[fc % 2]; s_ = sg[fc % 2]
            for kc in range(8):
                P.mm(pg.ap(), wg.ap()[:, kc, fc * 128:(fc + 1) * 128], h2.ap()[:, kc, :], kc == 0, kc == 7, [wg, h2], [pg])
            for kc in range(8):
                P.mm(pu.ap(), wu.ap()[:, kc, fc * 128:(fc + 1) * 128], h2.ap()[:, kc, :], kc == 0, kc == 7, [wu, h2], [pu])
            P.act(s_.ap(), pg.ap(), AF.Silu, [pg], [s_])
            P.tt("dve", a_.ap()[:, fc, :], pu.ap(), s_.ap(), ALU.mult, [pu, s_], [a_])

    def down_tail(b):
        a_ = aT[b % 2]
        xs_ = xsets.pop(b)
        for j in range(NJ):
            r0 = b * TB + j * 128
            xi = xs_[j]; ft = ftmp[j % 2]; x2 = xi; o_ = xi; ss = st2[tcn[0] % 3]; tcn[0] += 1
            for hh in range(2):
                pd_ = pD[(j * 2 + hh) % 4]
                for fc in range(22):
                    P.mm(pd_.ap(), a_.ap()[:, fc, j * 128:(j + 1) * 128], wd.ap()[:, fc, hh * 512:(hh + 1) * 512], fc == 0, fc == 21, [a_, wd], [pd_])
                P.tt("dve", ft.ap()[:, hh * 512:(hh + 1) * 512], pd_.ap(), G2bc.ap()[:, hh * 512:(hh + 1) * 512], ALU.mult, [pd_, G2bc], [ft])
            P.tt("pool", x2.ap(), ft.ap(), xi.ap(), ALU.add, [ft, xi], [x2])
            P.act(junk2.ap(), x2.ap(), AF.Square, [x2], [junk2, ss], accum_out=ss.ap()[:, 0:1])
            rstd_ops(ss, D)
            P.ts("dve", ft.ap(), x2.ap(), ss.ap()[:, 2:3], None, ALU.mult, None, [x2, ss], [ft])
            P.tt("pool", o_.ap(), ft.ap(), fgb.ap(), ALU.mult, [ft, fgb], [o_])
            P.dma("sp", out[r0:r0 + 128, :], o_.ap(), r=[o_], w=[k_out])

    prologue_a(0)
    prologue_b(0)
    for b in range(NB4):
        if b + 1 < NB4:
            prologue_a(b + 1)
        gateup(b)
        if b + 1 < NB4:
            prologue_b(b + 1)
        down_tail(b)
    P.finish()
    return nc


def make_consts(S):
    cst = np.zeros((128, 512), np.float32)
    cst[:, 0:128] = np.eye(128, dtype=np.float32)
    s = np.arange(128)[:, None]
    t = np.arange(128)[None, :]
    cst[:, 128:256] = (s <= t).astype(np.float32)
    cst[:, 256:384] = (s >= t).astype(np.float32)
    cst[0, 384:512] = 1.0
    rows = S // 64
    row = np.repeat(np.arange(rows, dtype=np.float32), 64)
    col = np.tile(np.arange(64, dtype=np.float32), rows)
    inv = (np.float32(10000.0) ** (-np.arange(32, dtype=np.float32) / np.float32(32))).astype(np.float32)
    ang = np.concatenate([row[:, None] * inv, col[:, None] * inv], axis=-1).astype(np.float32)
    rope = np.concatenate([np.cos(ang), np.sin(ang)], axis=-1).astype(np.float32)
    return cst, rope


def core_inputs(b, inp, S, cst, rope):
    f = lambda a: np.ascontiguousarray(a, dtype=np.float32)
    return {
        "x": f(inp["x"][b, :S]), "c": f(inp["c"][b]), "ctx": f(inp["ctx"][b]), "c_ctx": f(inp["c_ctx"]),
        "w_mod": f(inp["w_mod"][0]), "b_mod": f(inp["b_mod"][0]), "norm1_g": f(inp["norm1_g"][0]),
        "norm2_g": f(inp["norm2_g"][0]), "w_in": f(inp["w_in"][0]), "gate_b": f(inp["gate_b"][0]),
        "conv_w": f(inp["conv_w"][0]), "conv_b": f(inp["conv_b"][0]), "m_norm_g": f(inp["m_norm_g"][0]),
        "q_norm_g": f(inp["q_norm_g"][0]), "k_norm_g": f(inp["k_norm_g"][0]), "w_pa": f(inp["w_pa"][0]),
        "w_pb": f(inp["w_pb"][0]), "w_o": f(inp["w_o"][0]), "w_ffn_gate": f(inp["w_ffn_gate"][0]),
        "w_ffn_up": f(inp["w_ffn_up"][0]), "w_ffn_down": f(inp["w_ffn_down"][0]), "final_g": f(inp["final_g"]),
        "cst": cst, "rope": rope,
    }


_CACHE = {}


def kernel(**inputs):
    S = inputs["x"].shape[1]
    B = inputs["x"].shape[0]
    if S not in _CACHE:
        _CACHE[S] = build(S)
    nc = _CACHE[S]
    cst, rope = make_consts(S)
    in_maps = [core_inputs(b, inputs, S, cst, rope) for b in range(B)]
    res = run_bass_kernel_spmd(nc, in_maps, core_ids=list(range(B)))
    return np.stack([np.asarray(r["out"], dtype=np.float32) for r in res.results], axis=0)
```
